# Optimizing a Trainium2 kernel written in Bass

```python
import jax, jax.numpy as jnp
from jax import lax
import numpy as np

D_MODEL = 1024
BATCH = 2
SEQ = 16384
DEPTH = 2

HEAD_DIM = D_MODEL // 16
RET_HEADS = 4
RET_DK = HEAD_DIM
RET_DV = HEAD_DIM
RET_CHUNK = 128
MLA_HEADS = 4
MLA_NOPE = 2 * HEAD_DIM
MLA_ROPE = HEAD_DIM
MLA_DV = 2 * HEAD_DIM
MLA_Q_RANK = 3 * D_MODEL // 8
MLA_KV_RANK = D_MODEL // 4
MLA_BLOCK = 128
GLA_HEADS = 4
GLA_DK = HEAD_DIM
GLA_DV = HEAD_DIM
GLA_GATE_RANK = 16
GLA_TAU = 16.0
GLA_CHUNK = 128
D_MIX = RET_HEADS * RET_DV + MLA_HEADS * MLA_DV + GLA_HEADS * GLA_DV
IN_SPLITS = (
    RET_HEADS * RET_DK, RET_HEADS * RET_DK, RET_HEADS * RET_DV, RET_HEADS * RET_DV,
    MLA_Q_RANK, MLA_KV_RANK, MLA_ROPE,
    GLA_HEADS * GLA_DK, GLA_HEADS * GLA_DK, GLA_HEADS * GLA_DV, GLA_GATE_RANK,
    GLA_HEADS * GLA_DV,
)
IN_COLS = sum(IN_SPLITS)
D_FF = ((8 * D_MODEL // 3 + 127) // 128) * 128
ROPE_THETA = 10000.0
EPS = 1e-6

kernel_name = "hybrid_retention_mla_gla_macaron"


def rms_norm(x, g):
    xf = x.astype(jnp.float32)
    y = xf * lax.rsqrt(jnp.mean(xf * xf, axis=-1, keepdims=True) + EPS)
    return (y * g.astype(jnp.float32)).astype(x.dtype)


def rope_tables(positions, dim):
    inv = ROPE_THETA ** (-jnp.arange(0, dim, 2, dtype=jnp.float32) / dim)
    ang = positions.astype(jnp.float32)[..., None] * inv
    return jnp.cos(ang), jnp.sin(ang)


def apply_rope(x, cos, sin):
    x1, x2 = jnp.split(x, 2, axis=-1)
    c = cos[:, :, None, :].astype(x.dtype)
    s = sin[:, :, None, :].astype(x.dtype)
    return jnp.concatenate([x1 * c - x2 * s, x1 * s + x2 * c], axis=-1)


def swiglu(x, w_gate_up, w_down):
    g, u = jnp.split(x @ w_gate_up, 2, axis=-1)
    return (jax.nn.silu(g) * u) @ w_down


def to_chunks(x, c):
    b, s, h, d = x.shape
    return x.reshape(b, s // c, c, h, d).transpose(1, 0, 3, 2, 4)


def from_chunks(o):
    n, b, h, c, d = o.shape
    return o.transpose(1, 0, 3, 2, 4).reshape(b, n * c, h * d)


def retention(q, k, v):
    out_dtype = v.dtype
    b, s, h, dk = q.shape
    dv = v.shape[-1]
    c = RET_CHUNK
    q = q.astype(jnp.float32)
    k = k.astype(jnp.float32) * (dk ** -0.5)
    v = v.astype(jnp.float32)
    log_g = jnp.log1p(-jnp.power(2.0, -5.0 - jnp.arange(h, dtype=jnp.float32)))
    idx = jnp.arange(c, dtype=jnp.float32)
    diff = idx[:, None] - idx[None, :]
    intra = jnp.where(diff >= 0, jnp.exp(log_g[:, None, None] * jnp.maximum(diff, 0.0)), 0.0)
    xi = jnp.exp(log_g[:, None] * (idx + 1.0))[:, :, None]
    zeta = jnp.exp(log_g[:, None] * (c - 1.0 - idx))[:, :, None]
    chunk_decay = jnp.exp(log_g * c)[:, None, None]

    def step(state, inp):
        qi, ki, vi = inp
        sc = jnp.einsum('bhtd,bhsd->bhts', qi, ki) * intra
        o = jnp.einsum('bhts,bhsv->bhtv', sc, vi) + jnp.einsum('bhtd,bhdv->bhtv', qi, state) * xi
        state = state * chunk_decay + jnp.einsum('bhsd,bhsv->bhdv', ki, vi * zeta)
        return state, o

    state0 = jnp.zeros((b, h, dk, dv), jnp.float32)
    _, o = lax.scan(step, state0, (to_chunks(q, c), to_chunks(k, c), to_chunks(v, c)))
    return from_chunks(o).astype(out_dtype)


def gated_linear_attention(q, k, v, log_a):
    out_dtype = v.dtype
    b, s, h, dk = q.shape
    dv = v.shape[-1]
    c = GLA_CHUNK
    q = q.astype(jnp.float32) * (dk ** -0.5)
    k = k.astype(jnp.float32)
    v = v.astype(jnp.float32)
    log_a = log_a.astype(jnp.float32)
    mask = jnp.tril(jnp.ones((c, c), dtype=bool))[:, :, None]

    def step(state, inp):
        qi, ki, vi, ai = inp
        bcum = jnp.cumsum(ai, axis=2)
        rel = bcum[:, :, :, None, :] - bcum[:, :, None, :, :]
        dec = jnp.exp(jnp.where(mask, rel, -jnp.inf))
        sc = jnp.sum(qi[:, :, :, None, :] * ki[:, :, None, :, :] * dec, axis=-1)
        o = jnp.einsum('bhts,bhsv->bhtv', sc, vi) + jnp.einsum('bhtd,bhdv->bhtv', qi * jnp.exp(bcum), state)
        b_last = bcum[:, :, -1:, :]
        state = jnp.exp(b_last[:, :, 0, :])[..., None] * state + jnp.einsum(
            'bhsd,bhsv->bhdv', ki * jnp.exp(b_last - bcum), vi)
        return state, o

    state0 = jnp.zeros((b, h, dk, dv), jnp.float32)
    _, o = lax.scan(step, state0, (to_chunks(q, c), to_chunks(k, c), to_chunks(v, c), to_chunks(log_a, c)))
    return from_chunks(o).astype(out_dtype)


def causal_block_attention(q, k, v):
    b, s, h, d = q.shape
    dv = v.shape[-1]
    nb = s // MLA_BLOCK
    scale = d ** -0.5
    qb = q.reshape(b, nb, MLA_BLOCK, h, d).transpose(1, 0, 3, 2, 4)
    kh = k.transpose(0, 2, 1, 3)
    vh = v.transpose(0, 2, 1, 3)
    kpos = jnp.arange(s)

    def one_block(args):
        i, qi = args
        sc = jnp.einsum('bhqd,bhkd->bhqk', qi, kh).astype(jnp.float32) * scale
        qpos = i * MLA_BLOCK + jnp.arange(MLA_BLOCK)
        sc = jnp.where(kpos[None, :] <= qpos[:, None], sc, -jnp.inf)
        p = jax.nn.softmax(sc, axis=-1).astype(vh.dtype)
        return jnp.einsum('bhqk,bhkv->bhqv', p, vh)

    o = lax.map(one_block, (jnp.arange(nb), qb))
    return o.transpose(1, 0, 3, 2, 4).reshape(b, s, h * dv)


def hybrid_mixer(h, cos, sin, w_in, ret_out_norm,
                 mla_q_norm, mla_w_uq, mla_kv_norm, mla_w_ukv,
                 mla_q_nope_norm, mla_q_rope_norm, mla_k_nope_norm, mla_k_rope_norm,
                 gla_w_gate_up, gla_gate_bias, gla_out_norm, w_out):
    b, s, _ = h.shape
    proj = h @ w_in
    points = []
    acc = 0
    for sz in IN_SPLITS[:-1]:
        acc += sz
        points.append(acc)
    (ret_q, ret_k, ret_v, ret_g, mla_cq, mla_ckv, mla_kr,
     gla_q, gla_k, gla_v, gla_a_low, gla_r) = jnp.split(proj, points, axis=-1)

    rq = apply_rope(ret_q.reshape(b, s, RET_HEADS, RET_DK), cos, sin)
    rk = apply_rope(ret_k.reshape(b, s, RET_HEADS, RET_DK), cos, sin)
    rv = ret_v.reshape(b, s, RET_HEADS, RET_DV)
    ry = retention(rq, rk, rv).reshape(b, s, RET_HEADS, RET_DV)
    ry = rms_norm(ry, ret_out_norm.reshape(RET_HEADS, RET_DV)).reshape(b, s, RET_HEADS * RET_DV)
    ret_out = jax.nn.silu(ret_g) * ry

    cq = rms_norm(mla_cq, mla_q_norm)
    qf = (cq @ mla_w_uq).reshape(b, s, MLA_HEADS, MLA_NOPE + MLA_ROPE)
    ckv = rms_norm(mla_ckv, mla_kv_norm)
    kv = (ckv @ mla_w_ukv).reshape(b, s, MLA_HEADS, MLA_NOPE + MLA_DV)
    q_nope = rms_norm(qf[..., :MLA_NOPE], mla_q_nope_norm)
    q_rope = apply_rope(rms_norm(qf[..., MLA_NOPE:], mla_q_rope_norm), cos, sin)
    k_nope = rms_norm(kv[..., :MLA_NOPE], mla_k_nope_norm)
    mv = kv[..., MLA_NOPE:]
    k_rope = apply_rope(rms_norm(mla_kr, mla_k_rope_norm).reshape(b, s, 1, MLA_ROPE), cos, sin)
    k_rope = jnp.broadcast_to(k_rope, (b, s, MLA_HEADS, MLA_ROPE))
    mq = jnp.concatenate([q_nope, q_rope], axis=-1)
    mk = jnp.concatenate([k_nope, k_rope], axis=-1)
    mla_out = causal_block_attention(mq, mk, mv)

    gq = gla_q.reshape(b, s, GLA_HEADS, GLA_DK)
    gk = gla_k.reshape(b, s, GLA_HEADS, GLA_DK)
    gv = gla_v.reshape(b, s, GLA_HEADS, GLA_DV)
    log_a = jax.nn.log_sigmoid((gla_a_low @ gla_w_gate_up + gla_gate_bias).astype(jnp.float32)) / GLA_TAU
    log_a = log_a.reshape(b, s, GLA_HEADS, GLA_DK)
    gy = gated_linear_attention(gq, gk, gv, log_a).reshape(b, s, GLA_HEADS, GLA_DV)
    gy = rms_norm(gy, gla_out_norm.reshape(GLA_HEADS, GLA_DV)).reshape(b, s, GLA_HEADS * GLA_DV)
    gla_out = jax.nn.silu(gla_r) * gy

    return jnp.concatenate([ret_out, mla_out, gla_out], axis=-1) @ w_out


def setup_inputs(seed: int = 0) -> dict:
    key = jax.random.key(seed)
    ks = iter(jax.random.split(key, 32))

    def dense(shape):
        return jax.random.normal(next(ks), shape, jnp.float32) * (shape[-2] ** -0.5)

    def gain(n):
        return 1.0 + 0.02 * jax.random.normal(next(ks), (DEPTH, n), jnp.float32)

    x = jax.random.normal(next(ks), (BATCH, SEQ, D_MODEL), jnp.float32)
    positions = jnp.broadcast_to(jnp.arange(SEQ, dtype=jnp.int32), (BATCH, SEQ))
    return {
        "x": x,
        "positions": positions,
        "ffn1_norm": gain(D_MODEL),
        "ffn1_w_gate_up": dense((DEPTH, D_MODEL, 2 * D_FF)),
        "ffn1_w_down": dense((DEPTH, D_FF, D_MODEL)),
        "mix_norm": gain(D_MODEL),
        "w_in": dense((DEPTH, D_MODEL, IN_COLS)),
        "ret_out_norm": gain(RET_HEADS * RET_DV),
        "mla_q_norm": gain(MLA_Q_RANK),
        "mla_w_uq": dense((DEPTH, MLA_Q_RANK, MLA_HEADS * (MLA_NOPE + MLA_ROPE))),
        "mla_kv_norm": gain(MLA_KV_RANK),
        "mla_w_ukv": dense((DEPTH, MLA_KV_RANK, MLA_HEADS * (MLA_NOPE + MLA_DV))),
        "mla_q_nope_norm": gain(MLA_NOPE),
        "mla_q_rope_norm": gain(MLA_ROPE),
        "mla_k_nope_norm": gain(MLA_NOPE),
        "mla_k_rope_norm": gain(MLA_ROPE),
        "gla_w_gate_up": dense((DEPTH, GLA_GATE_RANK, GLA_HEADS * GLA_DK)),
        "gla_gate_bias": 0.1 * jax.random.normal(next(ks), (DEPTH, GLA_HEADS * GLA_DK), jnp.float32),
        "gla_out_norm": gain(GLA_HEADS * GLA_DV),
        "w_out": dense((DEPTH, D_MIX, D_MODEL)),
        "ffn2_norm": gain(D_MODEL),
        "ffn2_w_gate_up": dense((DEPTH, D_MODEL, 2 * D_FF)),
        "ffn2_w_down": dense((DEPTH, D_FF, D_MODEL)),
    }


def reference(x, positions, ffn1_norm, ffn1_w_gate_up, ffn1_w_down, mix_norm, w_in, ret_out_norm,
              mla_q_norm, mla_w_uq, mla_kv_norm, mla_w_ukv,
              mla_q_nope_norm, mla_q_rope_norm, mla_k_nope_norm, mla_k_rope_norm,
              gla_w_gate_up, gla_gate_bias, gla_out_norm, w_out,
              ffn2_norm, ffn2_w_gate_up, ffn2_w_down):
    cos, sin = rope_tables(positions, HEAD_DIM)
    for l in range(DEPTH):
        x = x + 0.5 * swiglu(rms_norm(x, ffn1_norm[l]), ffn1_w_gate_up[l], ffn1_w_down[l])
        h = rms_norm(x, mix_norm[l])
        x = x + hybrid_mixer(h, cos, sin, w_in[l], ret_out_norm[l],
                             mla_q_norm[l], mla_w_uq[l], mla_kv_norm[l], mla_w_ukv[l],
                             mla_q_nope_norm[l], mla_q_rope_norm[l], mla_k_nope_norm[l], mla_k_rope_norm[l],
                             gla_w_gate_up[l], gla_gate_bias[l], gla_out_norm[l], w_out[l])
        x = x + 0.5 * swiglu(rms_norm(x, ffn2_norm[l]), ffn2_w_gate_up[l], ffn2_w_down[l])
    return x
```

```python
import numpy as np
import ml_dtypes
import concourse.bass as bass
import concourse.mybir as mybir
from concourse.bass_utils import run_bass_kernel_spmd

F32 = mybir.dt.float32
BF16 = mybir.dt.bfloat16
I32 = mybir.dt.int32
AF = mybir.ActivationFunctionType
ALU = mybir.AluOpType

D = 1024
B = 2
S = 16384
DEPTH = 2
DFF = 2816
NCORE = 8
TOK = B * S // NCORE
TT = 512
NT = TOK // TT
EPS = 1e-6
IN_COLS = 2768
C_RQ, C_RK, C_RV, C_RG = 0, 256, 512, 768
C_CQ, C_CKV, C_KR = 1024, 1408, 1664
C_GQ, C_GK, C_GV, C_GA, C_GR = 1728, 1984, 2240, 2496, 2512


class Buf:
    __slots__ = ("name", "writers", "readers", "sem", "dcount", "excl")

    def __init__(self, name, excl=False):
        self.name = name
        self.excl = excl
        self.writers = {}
        self.readers = {}
        self.sem = None
        self.dcount = 0


class Op:
    __slots__ = ("eng", "fn", "deps", "needs_inc", "sem", "count", "is_dma", "idx")

    def __init__(self, eng, fn, is_dma):
        self.eng = eng
        self.fn = fn
        self.deps = []
        self.needs_inc = False
        self.sem = None
        self.count = 0
        self.is_dma = is_dma


ENGS = ("pe", "act", "dve", "pool", "sp")
ROT = 30000


class Prog:
    def __init__(self, nc):
        self.nc = nc
        self.ops = {e: [] for e in ENGS}
        self.nops = 0
        self.dma_sems = []
        self.out_dmas = []

    def buf(self, name):
        return Buf(name)

    def bufs(self, name, n, excl=False):
        return [Buf(f"{name}{i}", excl) for i in range(n)]

    def _dep(self, op, prod, kind):
        if prod is None or prod is op:
            return
        if not prod.is_dma and prod.eng == op.eng and not op.is_dma:
            if op.eng == "pe" or kind != "raw":
                return
        prod.needs_inc = True
        op.deps.append(prod)

    def add(self, eng, fn, reads=(), writes=(), dma_buf=None, is_out=False):
        is_dma = dma_buf is not None
        op = Op(eng, fn, is_dma)
        op.idx = self.nops
        self.nops += 1
        for b in reads:
            for w in b.writers.values():
                self._dep(op, w, "raw")
            if b.excl:
                for r in b.readers.values():
                    self._dep(op, r, "war")
        for b in writes:
            for w in b.writers.values():
                self._dep(op, w, "waw")
            for r in b.readers.values():
                self._dep(op, r, "war")
        if is_dma:
            if dma_buf.sem is None:
                dma_buf.sem = self.nc.semaphore(f"d{len(self.dma_sems)}_{dma_buf.name}").__enter__()
                self.dma_sems.append(dma_buf.sem)
            dma_buf.dcount += 16
            op.sem = dma_buf.sem
            op.count = dma_buf.dcount
            op.needs_inc = True
            key = ("dma", id(dma_buf))
            if is_out:
                self.out_dmas.append(op)
        else:
            key = eng
        for b in reads:
            b.readers[key] = op
        for b in writes:
            b.writers = {key: op}
            b.readers = {}
        self.ops[eng].append(op)
        return op

    def I(self, eng, meth, reads, writes, *args, **kw):
        return self.add(eng, lambda e: getattr(e, meth)(*args, **kw), reads=reads, writes=writes)

    def dma(self, eng, out, in_, reads, writes, dma_buf, partial=False, is_out=False):
        fn = lambda e: e.dma_start(out=out, in_=in_)
        if partial:
            return self.add_partial_write(eng, fn, reads, writes, dma_buf)
        return self.add(eng, fn, reads, writes, dma_buf, is_out)

    def add_partial_write(self, eng, fn, reads=(), writes=(), dma_buf=None):
        saved = [(b, dict(b.writers), dict(b.readers)) for b in writes]
        op = self.add(eng, fn, reads, writes, dma_buf)
        for b, w, r in saved:
            key = ("dma", id(dma_buf)) if dma_buf is not None else eng
            w = dict(w)
            w[key] = op
            b.writers = w
            b.readers = r
        return op

    def emit(self):
        nc = self.nc
        eng_sems = {}
        for e in ENGS:
            cnt = 0
            sems = []
            for op in self.ops[e]:
                if op.is_dma or not op.needs_inc:
                    continue
                k = cnt // ROT
                if k >= len(sems):
                    sems.append(nc.semaphore(f"c_{e}{k}").__enter__())
                op.sem = sems[k]
                op.count = cnt % ROT + 1
                cnt += 1
            eng_sems[e] = sems
        final_waits = [(op.sem, op.count) for op in self.out_dmas]
        fw = {}
        for s, c in final_waits:
            fw[id(s)] = (s, max(c, fw.get(id(s), (s, 0))[1]))

        def run(engname, eng):
            waited = {}
            for op in self.ops[engname]:
                need = {}
                for p in op.deps:
                    k = id(p.sem)
                    if p.count > need.get(k, (None, 0))[1]:
                        need[k] = (p.sem, p.count)
                for k, (s, c) in need.items():
                    if waited.get(k, 0) >= c:
                        continue
                    eng.wait_ge(s, c)
                    waited[k] = c
                ins = op.fn(eng)
                if op.needs_inc:
                    ins.then_inc(op.sem, 16 if op.is_dma else 1)
            if engname == "sp":
                for s, c in fw.values():
                    eng.wait_ge(s, c)

        with nc.Block() as block:
            @block.tensor
            def _(t):
                run("pe", t)

            @block.scalar
            def _(t):
                run("act", t)

            @block.vector
            def _(t):
                run("dve", t)

            @block.gpsimd
            def _(t):
                run("pool", t)

            @block.sync
            def _(t):
                run("sp", t)


def _mm(P, out_ap, lhsT, rhs, start, stop, reads, writes):
    return P.I("pe", "matmul", reads, writes, out_ap, lhsT, rhs, start=start, stop=stop)


class TokCtx:
    pass


def alloc(nc, name, shape, dt):
    return nc.sbuf_tensor("s_" + name, shape, dt).__enter__()


def ffn_phase(P, nc, C, x_src, x_dst, wgu_d, wd_d, gain_d, src_bufs=None, dst_bufs=None):
    WA, WB = C.WA, C.WB
    wgu = WA[:, 0:8 * 5632].rearrange("p (k c) -> p k c", k=8)
    wd = WB[:, 0:22 * 1024].rearrange("p (k c) -> p k c", k=22)
    for k in range(8):
        P.dma("pool", wgu[:, k, :], wgu_d[k * 128:(k + 1) * 128, :], [], [C.bWA], C.bWA, partial=(k > 0))
    wd_v = wd_d.rearrange("(k p) n -> p k n", p=128)
    for k0 in range(0, 22, 11):
        P.dma("pool", wd[:, k0:k0 + 11, :], wd_v[:, k0:k0 + 11, :], [], [C.bWB], C.bWB, partial=(k0 > 0))
    P.dma("sp", C.gain[:, 0:8], gain_d, [], [C.bgain], C.bgain)

    x_src_v = x_src.rearrange("(k p) t -> p k t", p=128)
    x_dst_v = x_dst.rearrange("(k p) t -> p k t", p=128)

    def load(i):
        s = i % 2
        P.dma("sp", C.xt[s][:], x_src_v[:, :, i * TT:(i + 1) * TT], [src_bufs[i]] if src_bufs else [], [C.bx[s]], C.bx[s])

    load(0)
    for i in range(NT):
        s = i % 2
        if i + 1 < NT:
            load(i + 1)
        xt = C.xt[s]
        bx = C.bx[s]
        yb = C.bps[6]
        ps = C.ps[6]
        for k in range(8):
            q = k % 2
            P.I("act", "activation", [bx], [C.bsq[q]], out=C.sq[q][:], in_=xt[:, k, :], func=AF.Square)
            _mm(P, ps[:], C.ones[:], C.sq[q][:], k == 0, k == 7, [C.bsq[q], C.bones], [yb])
        P.I("act", "activation", [yb, C.bconst], [C.brstd], out=C.rstd[:], in_=ps[:], func=AF.Ln,
            bias=C.epsc[:, 0:1], scale=1.0 / D)
        P.I("act", "activation", [C.brstd], [C.brstd], out=C.rstd[:], in_=C.rstd[:], func=AF.Exp, scale=-0.5)
        for k in range(8):
            P.I("dve", "scalar_tensor_tensor", [bx, C.brstd, C.bgain], [C.bhT], out=C.hT[:, k, :], in0=xt[:, k, :],
                scalar=C.gain[:, k:k + 1], in1=C.rstd[:], op0=ALU.mult, op1=ALU.mult)
        for j in range(22):
            r = j % 3
            gb, ub = C.bps[r], C.bps[3 + r]
            gp, up = C.ps[r], C.ps[3 + r]
            for k in range(8):
                _mm(P, gp[:], wgu[:, k, j * 128:(j + 1) * 128], C.hT[:, k, :], k == 0, k == 7, [C.bWA, C.bhT], [gb])
            for k in range(8):
                _mm(P, up[:], wgu[:, k, DFF + j * 128:DFF + (j + 1) * 128], C.hT[:, k, :], k == 0, k == 7,
                    [C.bWA, C.bhT], [ub])
            q = j % 2
            P.I("act", "activation", [gb], [C.bstmp[q]], out=C.stmp[q][:], in_=gp[:], func=AF.Silu)
            P.I("dve", "tensor_tensor", [ub, C.bstmp[q]], [C.bact], out=C.act[:, j, :], in0=up[:], in1=C.stmp[q][:],
                op=ALU.mult)
        for m in range(8):
            r = 6 + (m % 2)
            yb, yp = C.bps[r], C.ps[r]
            for k in range(22):
                _mm(P, yp[:], wd[:, k, m * 128:(m + 1) * 128], C.act[:, k, :], k == 0, k == 21, [C.bWB, C.bact], [yb])
            P.I("dve", "scalar_tensor_tensor", [yb, bx], [bx], out=xt[:, m, :], in0=yp[:], scalar=0.5, in1=xt[:, m, :],
                op0=ALU.mult, op1=ALU.add)
        P.dma("sp", x_dst_v[:, :, i * TT:(i + 1) * TT], xt[:], [bx], [dst_bufs[i]] if dst_bufs else [], bx, is_out=True)


def tok_ctx(P, nc):
    C = TokCtx()
    C.WA = alloc(nc, "WA", [128, 8 * 5632], BF16)
    C.WB = alloc(nc, "WB", [128, 22 * 1024], BF16)
    C.bWA, C.bWB = P.buf("WA"), P.buf("WB")
    C.xt = [alloc(nc, f"xt{i}", [128, 8, TT], F32) for i in range(2)]
    C.bx = P.bufs("x", 2)
    C.hT = alloc(nc, "hT", [128, 8, TT], BF16)
    C.bhT = P.buf("hT")
    C.act = alloc(nc, "act", [128, 22, TT], BF16)
    C.bact = P.buf("act")
    C.sq = [alloc(nc, f"sq{i}", [128, TT], BF16) for i in range(2)]
    C.bsq = P.bufs("sq", 2)
    C.rstd = alloc(nc, "rstd", [128, TT], F32)
    C.brstd = P.buf("rstd")
    C.stmp = [alloc(nc, f"stmp{i}", [128, TT], BF16) for i in range(2)]
    C.bstmp = P.bufs("stmp", 2)
    C.gain = alloc(nc, "gain", [128, 32], F32)
    C.bgain = P.buf("gain")
    C.ones = alloc(nc, "ones", [128, 128], BF16)
    C.bones = P.buf("ones")
    C.epsc = alloc(nc, "epsc", [128, 4], F32)
    C.vec = alloc(nc, "vec", [128, NVEC], F32)
    C.bvec = P.buf("vec")
    C.ones2 = alloc(nc, "ones2", [128, 128], BF16)
    C.bones2 = C.bones
    C.bxd = P.bufs("xd", NT)
    C.bconst = P.buf("const")
    C.ps = [nc.psum_tensor(f"ps{i}", [128, 512], F32).__enter__() for i in range(8)]
    C.bps = P.bufs("ps", 8, excl=True)
    P.I("dve", "memset", [], [C.bones], C.ones[:], 1.0)
    P.I("dve", "memset", [], [C.bconst], C.epsc[:], EPS)
    P.I("dve", "memset", [C.bconst], [C.bconst], C.epsc[:, 1:2], 1.0)
    P.I("dve", "memset", [C.bconst], [C.bconst], C.epsc[:, 2:3], float(np.log(0.125)))
    P.I("pool", "memset", [], [C.bones], C.ones2[:], 0.0)
    P.I("pool", "memset", [C.bones], [C.bones], C.ones2[0:64, 0:64], 1.0)
    P.I("pool", "memset", [C.bones], [C.bones], C.ones2[64:128, 64:128], 1.0)
    return C


def build_k1_test():
    nc = bass.Bass("TRN2", target_bir_lowering=False)
    xT = nc.dram_tensor("xT", [D, TOK], F32, kind="ExternalInput").ap()
    wgu = nc.dram_tensor("wgu", [D, 2 * DFF], F32, kind="ExternalInput").ap()
    wd = nc.dram_tensor("wd", [DFF, D], F32, kind="ExternalInput").ap()
    g1 = nc.dram_tensor("g1", [128, 8], F32, kind="ExternalInput").ap()
    x1T = nc.dram_tensor("x1T", [D, TOK], F32, kind="ExternalOutput").ap()
    P = Prog(nc)
    C = tok_ctx(P, nc)
    ffn_phase(P, nc, C, xT, x1T, wgu, wd, g1)
    P.emit()
    return nc


NQG = S // 512
NCH = S // 128
SCALE_MLA = 192.0 ** -0.5


def build_k2(phases=(1, 2)):
    nc = bass.Bass("TRN2", target_bir_lowering=False)

    def din(name, shape, dt):
        return nc.dram_tensor(name, shape, dt, kind="ExternalInput").ap()

    if 2 in phases:
        qn_d = din("qn", [128, S], BF16)
        qr_d = din("qr", [64, S], BF16)
        kn_d = din("kn", [128, S], BF16)
        kr_d = din("kr", [64, S], BF16)
        vm_d = din("vm", [S, 128], BF16)
        mo_d = nc.dram_tensor("mo", [S, 128], F32, kind="ExternalOutput").ap()
    lin_d = {}
    for X in (("r", "g") if 1 in phases else ()):
        lin_d[X] = dict(q=din(X + "q", [64, S], BF16), k=din(X + "k", [64, S], BF16),
                        kt=din(X + "kt", [S, 64], BF16), v=din(X + "v", [S, 64], BF16),
                        dec=din(X + "dec", [64, NCH], F32))
    mask_d = din("mask", [128, 128], BF16)
    lo_d = {X: nc.dram_tensor(X + "o", [S, 64], F32, kind="ExternalOutput").ap() for X in (("r", "g") if 1 in phases else ())}

    P = Prog(nc)
    ps = [nc.psum_tensor(f"ps{i}", [128, 512], F32).__enter__() for i in range(8)]
    bps = P.bufs("ps", 8, excl=True)
    mask = alloc(nc, "mask", [128, 128], BF16)
    bmask = P.buf("mask")
    P.dma("sp", mask[:], mask_d, [], [bmask], bmask)

    L = {}
    for X in (("r", "g") if 1 in phases else ()):
        o = TokCtx()
        o.q = [alloc(nc, f"{X}q{i}", [64, 512], BF16) for i in range(2)]
        o.k = [alloc(nc, f"{X}k{i}", [64, 512], BF16) for i in range(2)]
        o.kt = [alloc(nc, f"{X}kt{i}", [128, 4, 64], BF16) for i in range(2)]
        o.v = [alloc(nc, f"{X}v{i}", [128, 4, 64], BF16) for i in range(2)]
        o.bin = P.bufs(X + "in", 2)
        o.dec = alloc(nc, X + "dec", [64, NCH], F32)
        o.bdec = P.buf(X + "dec")
        o.scm = [alloc(nc, f"{X}scm{i}", [128, 128], BF16) for i in range(2)]
        o.bscm = P.bufs(X + "scm", 2)
        o.st = alloc(nc, X + "st", [64, 64], F32)
        o.tmp = alloc(nc, X + "tmp", [64, 64], F32)
        o.stb = alloc(nc, X + "stb", [64, 64], BF16)
        o.bst, o.btmp, o.bstb = P.buf(X + "st"), P.buf(X + "tmp"), P.buf(X + "stb")
        o.osb = [alloc(nc, f"{X}osb{i}", [128, 4, 64], F32) for i in range(2)]
        o.bosb = P.bufs(X + "osb", 2)
        L[X] = o
        P.dma("sp", o.dec[:], lin_d[X]["dec"], [], [o.bdec], o.bdec)
        P.I("dve", "memset", [], [o.bst], o.st[:], 0.0)
        P.I("dve", "memset", [], [o.bstb], o.stb[:], 0.0)

    def lin_load(g):
        s = g % 2
        for X in ("r", "g"):
            o, d = L[X], lin_d[X]
            t0 = g * 512
            P.dma("sp", o.q[s][:], d["q"][:, t0:t0 + 512], [], [o.bin[s]], o.bin[s])
            P.dma("sp", o.k[s][:], d["k"][:, t0:t0 + 512], [], [o.bin[s]], o.bin[s], partial=True)
            P.dma("sp", o.kt[s][:], d["kt"][t0:t0 + 512, :].rearrange("(n p) d -> p n d", p=128), [], [o.bin[s]], o.bin[s],
                  partial=True)
            P.dma("sp", o.v[s][:], d["v"][t0:t0 + 512, :].rearrange("(n p) d -> p n d", p=128), [], [o.bin[s]], o.bin[s],
                  partial=True)

    if 1 in phases:
        lin_load(0)
    for g in range(NQG if 1 in phases else 0):
        s = g % 2
        if g + 1 < NQG:
            lin_load(g + 1)
        for c in range(4):
            n = g * 4 + c
            cs = slice(c * 128, (c + 1) * 128)
            for xi, X in enumerate(("r", "g")):
                o = L[X]
                sb = xi * 2 + (n % 2)
                ob = 4 + xi * 2 + (n % 2)
                m2 = n % 2
                _mm(P, ps[sb][:, 0:128], o.k[s][:, cs], o.q[s][:, cs], True, True, [o.bin[s]], [bps[sb]])
                P.I("dve", "tensor_tensor", [bps[sb], bmask], [o.bscm[m2]], out=o.scm[m2][:], in0=ps[sb][:, 0:128],
                    in1=mask[:], op=ALU.mult)
                _mm(P, ps[ob][:, 0:64], o.scm[m2][:], o.v[s][:, c, :], True, False, [o.bscm[m2], o.bin[s]], [bps[ob]])
                _mm(P, ps[ob][:, 0:64], o.q[s][:, cs], o.stb[:], False, True, [o.bin[s], o.bstb], [bps[ob]])
                _mm(P, ps[ob][0:64, 64:128], o.kt[s][:, c, :], o.v[s][:, c, :], True, True, [o.bin[s]], [bps[ob]])
                P.I("act", "copy", [bps[ob]], [o.bosb[s]], out=o.osb[s][:, c, :], in_=ps[ob][:, 0:64])
                P.I("dve", "tensor_tensor", [bps[ob], o.bst], [o.btmp], out=o.tmp[:], in0=ps[ob][0:64, 64:128], in1=o.st[:],
                    op=ALU.add)
                P.I("dve", "tensor_scalar", [o.btmp, o.bdec], [o.bst], out=o.st[:], in0=o.tmp[:], scalar1=o.dec[:, n:n + 1],
                    scalar2=None, op0=ALU.mult)
                P.I("dve", "tensor_copy", [o.bst], [o.bstb], out=o.stb[:], in_=o.st[:])
        for X in ("r", "g"):
            o = L[X]
            P.dma("sp", lo_d[X][g * 512:(g + 1) * 512, :].rearrange("(n p) d -> p n d", p=128), o.osb[s][:],
                  [o.bosb[s]], [], o.bosb[s], is_out=True)

    if 2 not in phases:
        P.emit()
        return nc
    kn = alloc(nc, "kn", [128, S], BF16)
    kr = alloc(nc, "kr", [64, S], BF16)
    V = alloc(nc, "V", [128, NCH, 130], BF16)
    bkv = P.bufs("kv", NQG)
    kvsem = P.bufs("kvsem", 4)
    bvones = P.buf("vones")
    P.I("pool", "memset", [], [bvones], V[:, :, 128:130], 1.0)
    qn = [alloc(nc, f"qn{i}", [128, 512], BF16) for i in range(2)]
    qr = [alloc(nc, f"qr{i}", [64, 512], BF16) for i in range(2)]
    bq = P.bufs("q", 2)
    pt = [alloc(nc, f"pt{i}", [128, 512], BF16) for i in range(3)]
    bpt = P.bufs("pt", 3)
    rec = [alloc(nc, f"rec{i}", [128, 1], F32) for i in range(2)]
    brec = P.bufs("rec", 2)
    mosb = [alloc(nc, f"mosb{i}", [128, 4, 128], F32) for i in range(2)]
    bmosb = P.bufs("mosb", 2)
    vm_v = vm_d.rearrange("(n p) d -> p n d", p=128)

    def kv_load(g):
        t0 = g * 512
        sb = kvsem[g % 4]
        rd = [bkv[g - 4]] if g >= 4 else []
        P.dma("sp", kn[:, t0:t0 + 512], kn_d[:, t0:t0 + 512], rd, [bkv[g]], sb)
        P.dma("sp", kr[:, t0:t0 + 512], kr_d[:, t0:t0 + 512], [], [bkv[g]], sb, partial=True)
        P.dma("sp", V[:, 4 * g:4 * g + 4, 0:128], vm_v[:, 4 * g:4 * g + 4, :], [], [bkv[g]], sb, partial=True)

    def q_load(g):
        s = g % 2
        t0 = g * 512
        P.dma("sp", qn[s][:], qn_d[:, t0:t0 + 512], [], [bq[s]], bq[s])
        P.dma("sp", qr[s][:], qr_d[:, t0:t0 + 512], [], [bq[s]], bq[s], partial=True)

    if 2 in phases:
        kv_load(0)
        q_load(0)
    blk = 0
    for g in range(NQG if 2 in phases else 0):
        s = g % 2
        if g + 1 < NQG:
            kv_load(g + 1)
            q_load(g + 1)
        nkb = 4 * g + 4
        for kb in range(nkb):
            j = kb - 4 * g
            c0 = 128 * j if j > 0 else 0
            r = blk % 3
            blk += 1
            ks = slice(kb * 128, (kb + 1) * 128)
            kvb = bkv[kb // 4]
            _mm(P, ps[r][:, c0:512], kn[:, ks], qn[s][:, c0:512], True, False, [kvb, bq[s]], [bps[r]])
            _mm(P, ps[r][:, c0:512], kr[:, ks], qr[s][:, c0:512], False, True, [kvb, bq[s]], [bps[r]])
            P.I("act", "activation", [bps[r]], [bpt[r]], out=pt[r][:, c0:512], in_=ps[r][:, c0:512], func=AF.Exp,
                scale=SCALE_MLA)
            if j >= 0:
                P.I("pool", "tensor_tensor", [bpt[r], bmask], [bpt[r]], out=pt[r][:, 128 * j:128 * j + 128],
                    in0=pt[r][:, 128 * j:128 * j + 128], in1=mask[:], op=ALU.mult)
            for qi in range(max(j, 0), 4):
                ab = 3 + qi
                _mm(P, ps[ab][:, 0:129], pt[r][:, qi * 128:(qi + 1) * 128], V[:, kb, 0:129], kb == 0, kb == 4 * g + qi,
                    [bpt[r], kvb, bvones], [bps[ab]])
        for qi in range(4):
            ab = 3 + qi
            q2 = qi % 2
            P.I("dve", "reciprocal", [bps[ab]], [brec[q2]], out=rec[q2][:], in_=ps[ab][:, 128:129])
            P.I("dve", "tensor_scalar", [bps[ab], brec[q2]], [bmosb[s]], out=mosb[s][:, qi, :], in0=ps[ab][:, 0:128],
                scalar1=rec[q2][:, 0:1], scalar2=None, op0=ALU.mult)
        P.dma("sp", mo_d[g * 512:(g + 1) * 512, :].rearrange("(n p) d -> p n d", p=128), mosb[s][:], [bmosb[s]], [], bmosb[s],
              is_out=True)
    P.emit()
    return nc


TWO_PI = 2.0 * np.pi
MAGIC = 12582912.0
CW1 = 6.28125
CW2 = TWO_PI - 6.28125
V_MIX = 0
V_CQ = 8
V_CKV = 11
V_QN = 13
V_KN = 14
V_QR = 15
V_QRS = 16
V_KR = 17
V_KRS = 18
V_INV = 19
V_RO = 20
V_GO = 22
NVEC = 24


def norm_from_psum(P, C, src_aps, src_bufs, K, nparts, ones_ap, n_norm, out_aps, out_buf, gain_cols, stat_bank):
    sb, sp = C.bps[stat_bank], C.ps[stat_bank]
    n = len(src_aps)
    for k in range(n):
        q = k % 2
        P.I("act", "activation", [src_bufs[k]], [C.bsq[q]], out=C.sq[q][0:nparts, :], in_=src_aps[k], func=AF.Square)
        _mm(P, sp[0:nparts, :], ones_ap, C.sq[q][0:nparts, :], k == 0, k == n - 1, [C.bsq[q], C.bones], [sb])
    P.I("act", "activation", [sb, C.bconst], [C.brstd], out=C.rstd[0:nparts, :], in_=sp[0:nparts, :], func=AF.Ln,
        bias=C.epsc[0:nparts, 0:1], scale=1.0 / n_norm)
    P.I("act", "activation", [C.brstd], [C.brstd], out=C.rstd[0:nparts, :], in_=C.rstd[0:nparts, :], func=AF.Exp, scale=-0.5)
    if out_aps is not None:
        for k in range(n):
            P.I("dve", "scalar_tensor_tensor", [src_bufs[k], C.brstd, C.bvec], [out_buf], out=out_aps[k], in0=src_aps[k],
                scalar=C.vec[0:nparts, gain_cols[k]:gain_cols[k] + 1], in1=C.rstd[0:nparts, :], op0=ALU.mult, op1=ALU.mult)


def proj_phase(P, nc, C, d):
    WA = C.WA
    off = [0]

    def carve(n):
        a = off[0]
        off[0] += n
        return WA[:, a:a + n]

    bW = C.bWA
    w_in = carve(8 * IN_COLS).rearrange("p (k c) -> p k c", k=8)
    w_sw = carve(8 * 576).rearrange("p (k c) -> p k c", k=8)
    wuq_n = carve(3 * 512).rearrange("p (k c) -> p k c", k=3)
    wuq_r = carve(3 * 256).rearrange("p (k c) -> p k c", k=3)
    wuq_rs = carve(3 * 256).rearrange("p (k c) -> p k c", k=3)
    wkv_k = carve(2 * 512).rearrange("p (k c) -> p k c", k=2)
    wkv_v = carve(2 * 512).rearrange("p (k c) -> p k c", k=2)
    first = [True]

    def wdma(out, in_):
        P.dma("pool", out, in_, [], [bW], bW, partial=not first[0])
        first[0] = False

    win_d = d["w_in"]
    for k in range(8):
        rows = slice(k * 128, (k + 1) * 128)
        wdma(w_in[:, k, :], win_d[rows, :])
        src = win_d[rows, 0:512].rearrange("p (h t c) -> p h t c", h=8, t=2)
        dst = w_sw[:, k, 0:512].rearrange("p (h t c) -> p h t c", h=8, t=2)
        wdma(dst[:, :, 0, :], src[:, :, 1, :])
        wdma(dst[:, :, 1, :], src[:, :, 0, :])
        wdma(w_sw[:, k, 512:544], win_d[rows, C_KR + 32:C_KR + 64])
        wdma(w_sw[:, k, 544:576], win_d[rows, C_KR:C_KR + 32])
    for k in range(3):
        rows = slice(k * 128, (k + 1) * 128)
        src = d["w_uq"][rows, :].rearrange("p (h c) -> p h c", h=4)
        wdma(wuq_n[:, k, :].rearrange("p (h c) -> p h c", h=4), src[:, :, 0:128])
        wdma(wuq_r[:, k, :].rearrange("p (h c) -> p h c", h=4), src[:, :, 128:192])
        dsts = wuq_rs[:, k, :].rearrange("p (h c) -> p h c", h=4)
        wdma(dsts[:, :, 0:32], src[:, :, 160:192])
        wdma(dsts[:, :, 32:64], src[:, :, 128:160])
    for k in range(2):
        rows = slice(k * 128, (k + 1) * 128)
        src = d["w_ukv"][rows, :].rearrange("p (h c) -> p h c", h=4)
        wdma(wkv_k[:, k, :].rearrange("p (h c) -> p h c", h=4), src[:, :, 0:128])
        wdma(wkv_v[:, k, :].rearrange("p (h c) -> p h c", h=4), src[:, :, 128:256])

    def alias():
        b = Buf("al")
        for o in (C.bWA, C.bWB, C.bact):
            for k, v in o.writers.items():
                if k not in b.writers or b.writers[k].idx < v.idx:
                    b.writers[k] = v
            for k, v in o.readers.items():
                if k not in b.readers or b.readers[k].idx < v.idx:
                    b.readers[k] = v
        return b

    assert off[0] % 2 == 0
    WAf = WA.bitcast(F32)
    WBf = C.WB.bitcast(F32)
    offa = [off[0] // 2]
    offb = [0]

    def cf(n, region="b"):
        o_, t_ = (offb, WBf) if region == "b" else (offa, WAf)
        a = o_[0]
        o_[0] += n
        return t_[:, a:a + n]

    xitab = cf(4 * TT, "a").rearrange("p (k c) -> p k c", k=4); bxi = alias()
    Eq = [cf(TT, "a") for _ in range(2)]; Ek = [cf(TT, "a") for _ in range(2)]; bE = [alias() for _ in range(2)]
    assert offa[0] <= 8 * 5632 // 2, offa[0]
    pos = cf(TT); bpos = alias()
    ang = cf(TT); bang = alias()
    tk = cf(TT); btk = alias()
    r1 = cf(TT); br1 = alias()
    Ssb = cf(TT); bS = alias()
    Csb = cf(TT); bC = alias()
    GCq = cf(TT); GSq = cf(TT); bGq = alias()
    GCk = cf(TT); GSk = cf(TT); bGk = alias()
    t1 = [cf(TT) for _ in range(2)]; bt1 = [alias() for _ in range(2)]
    t2 = [cf(TT) for _ in range(2)]; bt2 = [alias() for _ in range(2)]
    sgs = [cf(TT) for _ in range(2)]; bsgs = [alias() for _ in range(2)]
    alow = cf(TT); balow = alias()
    gw = cf(256); bgw = alias()
    tri = cf(128); btri = alias()
    zl = cf(256); bzl = alias()
    decs = cf(8).rearrange("p (k c) -> p k c", k=2); bdecs = alias()
    assert offb[0] <= 22 * 1024 // 2, offb[0]

    nb = [0]

    def stage_bf(n):
        a = nb[0]
        nb[0] += n
        assert nb[0] <= 22
        return C.act[:, a:a + n, :], alias()

    cqn, bcqn = stage_bf(3)
    ckvn, bckvn = stage_bf(2)
    vst, bvst = stage_bf(4)
    ostage = [stage_bf(1) for _ in range(8)]
    nst = [0]

    def next_stage():
        a = ostage[nst[0] % len(ostage)]
        nst[0] += 1
        return a[0][:, 0, :], a[1]

    P.dma("sp", C.vec[:], d["vecs"], [], [C.bvec], C.bvec)
    P.dma("sp", xitab, d["xitab"].rearrange("p (k c) -> p k c", k=4), [], [bxi], bxi)
    P.dma("sp", gw[0:17, :], d["gw"], [], [bgw], bgw)
    P.dma("sp", tri, d["tri"], [], [btri], btri)
    P.I("pool", "memset", [], [balow], alow[0:32, :], 1.0)

    x_v = d["x1T"].rearrange("(k p) t -> p k t", p=128)

    def load(i):
        s = i % 2
        P.dma("sp", C.xt[s][:], x_v[:, :, i * TT:(i + 1) * TT], [C.bxd[i]], [C.bx[s]], C.bx[s])

    load(0)
    bank = [0]

    def nb_():
        b = bank[0] % 6
        bank[0] += 1
        return b

    def proj_chunk(wt, col0, M=128):
        b = nb_()
        for k in range(8):
            _mm(P, C.ps[b][0:M, :], wt[:, k, col0:col0 + M], C.hT[:, k, :], k == 0, k == 7, [bW, C.bhT], [C.bps[b]])
        return b

    def store(dram_ap, sb_ap, buf):
        P.dma("sp", dram_ap, sb_ap, [buf], [], buf, is_out=True)

    def rope_combine(bx_, bs_, M, cos_ap, sin_ap, rbufs, post_ap, post_bufs, out_ap, out_buf, q):
        P.I("dve", "tensor_tensor", [C.bps[bx_]] + rbufs, [bt1[q]], out=t1[q][0:M, :], in0=C.ps[bx_][0:M, :], in1=cos_ap,
            op=ALU.mult)
        P.I("dve", "tensor_tensor", [C.bps[bs_]] + rbufs, [bt2[q]], out=t2[q][0:M, :], in0=C.ps[bs_][0:M, :], in1=sin_ap,
            op=ALU.mult)
        P.I("pool", "tensor_tensor", [bt1[q], bt2[q]], [bt1[q]], out=t1[q][0:M, :], in0=t1[q][0:M, :], in1=t2[q][0:M, :],
            op=ALU.add)
        P.I("pool", "tensor_tensor", [bt1[q]] + post_bufs, [out_buf], out=out_ap, in0=t1[q][0:M, :], in1=post_ap, op=ALU.mult)

    for i in range(NT):
        s = i % 2
        tsl = slice(i * TT, (i + 1) * TT)
        if i + 1 < NT:
            load(i + 1)
        xt, bx = C.xt[s], C.bx[s]
        norm_from_psum(P, C, [xt[:, k, :] for k in range(8)], [bx] * 8, 128, 128, C.ones[:], D,
                       [C.hT[:, k, :] for k in range(8)], C.bhT, [V_MIX + k for k in range(8)], 6)
        P.dma("sp", pos, d["posf"][:, tsl], [], [bpos], bpos)
        for (dst, bd, shift) in ((Ssb, bS, 0.0), (Csb, bC, 0.5 * np.pi)):
            P.I("pool", "tensor_scalar", [bpos, C.bvec], [bang], out=ang, in0=pos, scalar1=C.vec[:, V_INV:V_INV + 1],
                scalar2=shift, op0=ALU.mult, op1=ALU.add)
            P.I("pool", "tensor_scalar", [bang], [btk], out=tk, in0=ang, scalar1=1.0 / TWO_PI, scalar2=MAGIC,
                op0=ALU.mult, op1=ALU.add)
            P.I("pool", "tensor_scalar", [btk], [btk], out=tk, in0=tk, scalar1=-MAGIC, scalar2=None, op0=ALU.add)
            P.I("dve", "scalar_tensor_tensor", [btk, bang], [br1], out=r1, in0=tk, scalar=-CW1, in1=ang,
                op0=ALU.mult, op1=ALU.add)
            P.I("dve", "scalar_tensor_tensor", [btk, br1], [br1], out=r1, in0=tk, scalar=-CW2, in1=r1,
                op0=ALU.mult, op1=ALU.add)
            P.I("pool", "tensor_scalar", [br1], [br1], out=r1, in0=r1, scalar1=-np.pi, scalar2=np.pi, op0=ALU.max, op1=ALU.min)
            P.I("act", "activation", [br1], [bd], out=dst, in_=r1, func=AF.Sin)
        P.I("pool", "tensor_scalar", [bC, C.bvec], [bGq], out=GCq, in0=Csb, scalar1=C.vec[:, V_QR:V_QR + 1], scalar2=None,
            op0=ALU.mult)
        P.I("pool", "tensor_scalar", [bS, C.bvec], [bGq], out=GSq, in0=Ssb, scalar1=C.vec[:, V_QRS:V_QRS + 1], scalar2=None,
            op0=ALU.mult)
        P.I("pool", "tensor_scalar", [bC, C.bvec], [bGk], out=GCk[0:64, :], in0=Csb[0:64, :], scalar1=C.vec[0:64, V_KR:V_KR + 1],
            scalar2=None, op0=ALU.mult)
        P.I("pool", "tensor_scalar", [bS, C.bvec], [bGk], out=GSk[0:64, :], in0=Ssb[0:64, :],
            scalar1=C.vec[0:64, V_KRS:V_KRS + 1], scalar2=None, op0=ALU.mult)

        for (c0, tab0, dname) in ((C_RQ, 0, "rqT"), (C_RK, 2, "rkT")):
            for cc in range(2):
                bxp = proj_chunk(w_in, c0 + cc * 128)
                bsp = proj_chunk(w_sw, c0 + cc * 128)
                o_ap, o_b = next_stage()
                rope_combine(bxp, bsp, 128, Csb, Ssb, [bC, bS], xitab[:, tab0 + cc, :], [bxi], o_ap, o_b, cc)
                store(d[dname][cc * 128:(cc + 1) * 128, tsl], o_ap, o_b)
        for (c0, r0) in ((C_RG, 0), (C_GR, 256)):
            for cc in range(2):
                bp = proj_chunk(w_in, c0 + cc * 128)
                P.I("act", "activation", [C.bps[bp]], [bsgs[cc]], out=sgs[cc], in_=C.ps[bp][:], func=AF.Silu)
                store(d["sgT"][r0 + cc * 128:r0 + (cc + 1) * 128, tsl], sgs[cc], bsgs[cc])
        for sub in range(4):
            b = nb_()
            for (c0, o0) in ((C_RV, 0), (C_GV, 256)):
                for k in range(8):
                    _mm(P, C.ps[b][:, o0:o0 + 256], C.hT[:, k, sub * 128:(sub + 1) * 128], w_in[:, k, c0:c0 + 256], k == 0,
                        k == 7, [bW, C.bhT], [C.bps[b]])
            P.I("act", "copy", [C.bps[b]], [bvst], out=vst[:, sub, :], in_=C.ps[b][:])
        store(d["vtok"][i * TT:(i + 1) * TT, :].rearrange("(n p) c -> p n c", p=128), vst, bvst)
        bq_ = [proj_chunk(w_in, C_CQ + k * 128) for k in range(3)]
        norm_from_psum(P, C, [C.ps[b][:] for b in bq_], [C.bps[b] for b in bq_], 128, 128, C.ones[:], 384,
                       [cqn[:, k, :] for k in range(3)], bcqn, [V_CQ + k for k in range(3)], 6)
        for h in range(4):
            b = nb_()
            for k in range(3):
                _mm(P, C.ps[b][:], wuq_n[:, k, h * 128:(h + 1) * 128], cqn[:, k, :], k == 0, k == 2, [bW, bcqn], [C.bps[b]])
            o_ap, o_b = next_stage()
            norm_from_psum(P, C, [C.ps[b][:]], [C.bps[b]], 128, 128, C.ones[:], 128, [o_ap], o_b, [V_QN], 7)
            store(d["qnT"][h * 128:(h + 1) * 128, tsl], o_ap, o_b)
        for cc in range(2):
            b1, b2 = nb_(), nb_()
            for (b, wt) in ((b1, wuq_r), (b2, wuq_rs)):
                for k in range(3):
                    _mm(P, C.ps[b][:], wt[:, k, cc * 128:(cc + 1) * 128], cqn[:, k, :], k == 0, k == 2, [bW, bcqn], [C.bps[b]])
            norm_from_psum(P, C, [C.ps[b1][:]], [C.bps[b1]], 128, 128, C.ones2[:], 64, None, None, None, 7)
            o_ap, o_b = next_stage()
            rope_combine(b1, b2, 128, GCq, GSq, [bGq], C.rstd[:], [C.brstd], o_ap, o_b, cc)
            store(d["qrT"][cc * 128:(cc + 1) * 128, tsl], o_ap, o_b)
        bk_ = [proj_chunk(w_in, C_CKV + k * 128) for k in range(2)]
        norm_from_psum(P, C, [C.ps[b][:] for b in bk_], [C.bps[b] for b in bk_], 128, 128, C.ones[:], 256,
                       [ckvn[:, k, :] for k in range(2)], bckvn, [V_CKV + k for k in range(2)], 6)
        for h in range(4):
            b = nb_()
            for k in range(2):
                _mm(P, C.ps[b][:], wkv_k[:, k, h * 128:(h + 1) * 128], ckvn[:, k, :], k == 0, k == 1, [bW, bckvn], [C.bps[b]])
            o_ap, o_b = next_stage()
            norm_from_psum(P, C, [C.ps[b][:]], [C.bps[b]], 128, 128, C.ones[:], 128, [o_ap], o_b, [V_KN], 7)
            store(d["knT"][h * 128:(h + 1) * 128, tsl], o_ap, o_b)
        for sub in range(4):
            b = nb_()
            for k in range(2):
                _mm(P, C.ps[b][:], ckvn[:, k, sub * 128:(sub + 1) * 128], wkv_v[:, k, :], k == 0, k == 1, [bW, bckvn], [C.bps[b]])
            o_ap, o_b = next_stage()
            P.I("act", "copy", [C.bps[b]], [o_b], out=o_ap, in_=C.ps[b][:])
            store(d["vmtok"][i * TT + sub * 128:i * TT + (sub + 1) * 128, :], o_ap, o_b)
        b1 = proj_chunk(w_in, C_KR, M=64)
        b2 = proj_chunk(w_sw, 512, M=64)
        norm_from_psum(P, C, [C.ps[b1][0:64, :]], [C.bps[b1]], 64, 64, C.ones[0:64, 0:64], 64, None, None, None, 7)
        o_ap, o_b = next_stage()
        rope_combine(b1, b2, 64, GCk[0:64, :], GSk[0:64, :], [bGk], C.rstd[0:64, :], [C.brstd], o_ap[0:64, :], o_b, 0)
        store(d["krT"][:, tsl], o_ap[0:64, :], o_b)
        ba = proj_chunk(w_in, C_GA, M=16)
        P.I("act", "copy", [C.bps[ba]], [balow], out=alow[0:16, :], in_=C.ps[ba][0:16, :])
        bc = [nb_(), nb_()]
        for sub in range(4):
            bz = 7
            P.I("pe", "matmul", [balow, bgw], [C.bps[bz]], C.ps[bz][:, 0:256], alow[0:17, sub * 128:(sub + 1) * 128], gw[0:17, :],
                start=True, stop=True)
            P.I("act", "activation", [C.bps[bz]], [bzl], out=zl, in_=C.ps[bz][:, 0:256], func=AF.Exp, scale=-1.0)
            P.I("act", "activation", [bzl, C.bconst], [bzl], out=zl, in_=zl, func=AF.Ln, bias=C.epsc[:, 1:2], scale=1.0)
            for c in range(2):
                P.I("pe", "matmul", [bzl, btri], [C.bps[bc[c]]], C.ps[bc[c]][:, sub * 128:(sub + 1) * 128],
                    zl[:, c * 128:(c + 1) * 128], tri, start=True, stop=True)
        for c in range(2):
            pb = C.ps[bc[c]]
            P.I("act", "activation", [C.bps[bc[c]], C.bconst], [bE[c]], out=Eq[c], in_=pb[:], func=AF.Exp,
                bias=C.epsc[:, 2:3], scale=1.0)
            P.I("act", "activation", [C.bps[bc[c]]], [bE[c]], out=Ek[c], in_=pb[:], func=AF.Exp, scale=-1.0)
            P.I("act", "activation", [C.bps[bc[c]]], [bdecs], out=decs[:, c, :],
                in_=pb[:].rearrange("p (n t) -> p n t", t=128)[:, :, 127], func=AF.Exp)
        store(d["gdec"][:, i * 4:(i + 1) * 4].rearrange("(c p) n -> p c n", p=128), decs, bdecs)
        for (c0, E, dname) in ((C_GQ, Eq, "gqT"), (C_GK, Ek, "gkT")):
            for cc in range(2):
                bp = proj_chunk(w_in, c0 + cc * 128)
                o_ap, o_b = next_stage()
                P.I("dve", "tensor_tensor", [C.bps[bp], bE[cc]], [o_b], out=o_ap, in0=C.ps[bp][:], in1=E[cc], op=ALU.mult)
                store(d[dname][cc * 128:(cc + 1) * 128, tsl], o_ap, o_b)


K1_OUTS = dict(rqT=([256, TOK], BF16), rkT=([256, TOK], BF16), sgT=([512, TOK], F32), vtok=([TOK, 512], BF16),
               qnT=([512, TOK], BF16), qrT=([256, TOK], BF16), knT=([512, TOK], BF16), vmtok=([TOK, 512], BF16),
               krT=([64, TOK], BF16), gdec=([256, TOK // 128], F32), gqT=([256, TOK], BF16), gkT=([256, TOK], BF16),
               x1T=([D, TOK], F32))


def build_k1(with_ffn=True):
    nc = bass.Bass("TRN2", target_bir_lowering=False)

    def din(name, shape, dt=F32):
        return nc.dram_tensor(name, shape, dt, kind="ExternalInput").ap()

    xT = din("xT", [D, TOK])
    wgu = din("wgu", [D, 2 * DFF])
    wd = din("wd", [DFF, D])
    g1 = din("g1", [128, 8])
    d = dict(w_in=din("w_in", [D, IN_COLS]), w_uq=din("w_uq", [384, 768]), w_ukv=din("w_ukv", [256, 1024]),
             vecs=din("vecs", [128, NVEC]), xitab=din("xitab", [128, 4 * TT]), gw=din("gw", [17, 256]),
             tri=din("tri", [128, 128]), posf=din("posf", [128, TOK]))
    for name, (shape, dt) in K1_OUTS.items():
        d[name] = nc.dram_tensor(name, shape, dt, kind="ExternalOutput").ap()
    P = Prog(nc)
    C = tok_ctx(P, nc)
    if with_ffn:
        ffn_phase(P, nc, C, xT, d["x1T"], wgu, wd, g1, dst_bufs=C.bxd)
    else:
        d["x1T"] = xT
    proj_phase(P, nc, C, d)
    P.emit()
    return nc


def post_phase(P, nc, C, d):
    WA = C.WA
    bW = C.bWA
    w_out = WA[:, 0:8 * 1024].rearrange("p (k c) -> p k c", k=8)
    P.dma("pool", w_out, d["w_out"].rearrange("(k p) n -> p k n", p=128), [], [bW], bW)
    WBf = C.WB.bitcast(F32)
    offb = [0]

    def cf(n):
        a = offb[0]
        offb[0] += n
        return WBf[:, a:a + n]

    ot = [cf(TT) for _ in range(2)]; bot = P.bufs("ot", 2)
    sg = [cf(TT) for _ in range(2)]; bsg = P.bufs("sg", 2)
    zt = [cf(TT) for _ in range(2)]; bzt = P.bufs("zt", 2)
    cat = C.act[:, 0:8, :]
    bcat = P.buf("cat")
    C.post_wb = bot + bsg + bzt
    C.post_act = [bcat]
    P.dma("sp", C.vec[:], d["vecs"], [], [C.bvec], C.bvec)
    x_v = d["x1T"].rearrange("(k p) t -> p k t", p=128)
    x_o = d["x2T"].rearrange("(k p) t -> p k t", p=128)

    def load(i):
        s = i % 2
        P.dma("sp", C.xt[s][:], x_v[:, :, i * TT:(i + 1) * TT], [], [C.bx[s]], C.bx[s])

    load(0)
    n2 = 0
    for i in range(NT):
        s = i % 2
        tsl = slice(i * TT, (i + 1) * TT)
        if i + 1 < NT:
            load(i + 1)
        xt, bx = C.xt[s], C.bx[s]
        for (src, sg0, gcol, cat0) in (("roT", 0, V_RO, 0), ("goT", 256, V_GO, 6)):
            for cc in range(2):
                q = n2 % 2
                n2 += 1
                P.dma("sp", ot[q], d[src][cc * 128:(cc + 1) * 128, tsl], [], [bot[q]], bot[q])
                P.dma("sp", sg[q], d["sgT"][sg0 + cc * 128:sg0 + (cc + 1) * 128, tsl], [], [bsg[q]], bsg[q])
                norm_from_psum(P, C, [ot[q]], [bot[q]], 128, 128, C.ones2[:], 64, [zt[q]], bzt[q], [gcol + cc], 7)
                P.I("pool", "tensor_tensor", [bzt[q], bsg[q]], [bcat], out=cat[:, cat0 + cc, :], in0=zt[q], in1=sg[q], op=ALU.mult)
        for cc in range(4):
            q = n2 % 2
            n2 += 1
            P.dma("sp", ot[q], d["moT"][cc * 128:(cc + 1) * 128, tsl], [], [bot[q]], bot[q])
            P.I("act", "copy", [bot[q]], [bcat], out=cat[:, 2 + cc, :], in_=ot[q])
        for m in range(8):
            r = m % 6
            for k in range(8):
                _mm(P, C.ps[r][:], w_out[:, k, m * 128:(m + 1) * 128], cat[:, k, :], k == 0, k == 7, [bW, bcat], [C.bps[r]])
            P.I("dve", "tensor_tensor", [C.bps[r], bx], [bx], out=xt[:, m, :], in0=C.ps[r][:], in1=xt[:, m, :], op=ALU.add)
        P.dma("sp", x_o[:, :, tsl], xt[:], [bx], [C.bxd[i]], bx, is_out=True)


def build_k3():
    nc = bass.Bass("TRN2", target_bir_lowering=False)

    def din(name, shape, dt=F32):
        return nc.dram_tensor(name, shape, dt, kind="ExternalInput").ap()

    d = dict(x1T=din("x1T", [D, TOK]), roT=din("roT", [256, TOK]), goT=din("goT", [256, TOK]), moT=din("moT", [512, TOK]),
             sgT=din("sgT", [512, TOK]), w_out=din("w_out", [D, D]), vecs=din("vecs", [128, NVEC]))
    wgu = din("wgu", [D, 2 * DFF])
    wd = din("wd", [DFF, D])
    g2 = din("g2", [128, 8])
    d["x2T"] = nc.dram_tensor("x2T", [D, TOK], F32, kind="ExternalOutput").ap()
    x3T = nc.dram_tensor("x3T", [D, TOK], F32, kind="ExternalOutput").ap()
    P = Prog(nc)
    C = tok_ctx(P, nc)
    post_phase(P, nc, C, d)
    for dst, srcs in ((C.bWB, C.post_wb), (C.bact, C.post_act)):
        for o in srcs:
            for k, v in o.writers.items():
                if k not in dst.writers or dst.writers[k].idx < v.idx:
                    dst.writers[k] = v
            for k, v in o.readers.items():
                if k not in dst.readers or dst.readers[k].idx < v.idx:
                    dst.readers[k] = v
    ffn_phase(P, nc, C, d["x2T"], x3T, wgu, wd, g2, src_bufs=C.bxd)
    P.emit()
    return nc


_BF = ml_dtypes.bfloat16
_CACHE = {}


def _get(name, fn):
    if name not in _CACHE:
        _CACHE[name] = fn()
    return _CACHE[name]


def _swap(g):
    return np.concatenate([g[32:], g[:32]])


def _consts():
    p = np.arange(128)
    inv = (10000.0 ** (-(np.arange(0, 64, 2, dtype=np.float32)) / 64.0)).astype(np.float32)
    inv_signed = np.where((p % 64) < 32, -1.0, 1.0).astype(np.float32) * inv[p % 32]
    t = (np.arange(TT) % 128 + 1).astype(np.float64)
    xitab = np.zeros((128, 4, TT), np.float32)
    for k in range(4):
        for half in range(2):
            h = (k % 2) * 2 + half
            lg = np.log1p(-2.0 ** (-5.0 - h))
            row = np.exp(lg * t) if k < 2 else np.exp(-lg * t) * 0.125
            xitab[half * 64:(half + 1) * 64, k, :] = row[None, :]
    s_, t_ = np.meshgrid(np.arange(128), np.arange(128), indexing="ij")
    tri = np.where(s_ <= t_, -1.0 / 16.0, 0.0).astype(np.float32)
    mask = np.where(s_ <= t_, 1.0, 0.0).astype(_BF)
    rdec = np.zeros((4, 64, NCH), np.float32)
    for h in range(4):
        rdec[h] = np.exp(np.log1p(-2.0 ** (-5.0 - h)) * 128.0)
    return dict(inv_signed=inv_signed, xitab=xitab.reshape(128, 4 * TT), tri=tri, mask=mask, rdec=rdec)


def _vecs(inp, l, cst):
    v = np.zeros((128, NVEC), np.float32)
    v[:, V_MIX:V_MIX + 8] = inp["mix_norm"][l].reshape(8, 128).T
    v[:, V_CQ:V_CQ + 3] = inp["mla_q_norm"][l].reshape(3, 128).T
    v[:, V_CKV:V_CKV + 2] = inp["mla_kv_norm"][l].reshape(2, 128).T
    v[:, V_QN] = inp["mla_q_nope_norm"][l]
    v[:, V_KN] = inp["mla_k_nope_norm"][l]
    gq = inp["mla_q_rope_norm"][l]
    gk = inp["mla_k_rope_norm"][l]
    v[:, V_QR] = np.tile(gq, 2)
    v[:, V_QRS] = np.tile(_swap(gq), 2)
    v[:64, V_KR] = gk
    v[:64, V_KRS] = _swap(gk)
    v[:, V_INV] = cst["inv_signed"]
    v[:, V_RO:V_RO + 2] = inp["ret_out_norm"][l].reshape(2, 128).T
    v[:, V_GO:V_GO + 2] = inp["gla_out_norm"][l].reshape(2, 128).T
    return v


def _run(nc, in_maps):
    res = run_bass_kernel_spmd(nc, in_maps, core_ids=list(range(NCORE)))
    return res.results


def _cat_tok(res, name, b):
    return np.concatenate([res[b * 4 + q][name] for q in range(4)], axis=1)


def _cat_rows(res, name, b):
    return np.concatenate([res[b * 4 + q][name] for q in range(4)], axis=0)


def kernel(**inp):
    inp = {k: np.asarray(v) for k, v in inp.items()}
    cst = _get("cst", _consts)
    k1 = _get("k1", build_k1)
    k2a = _get("k2a", lambda: build_k2(phases=(1,)))
    k2b = _get("k2b", lambda: build_k2(phases=(2,)))
    k3 = _get("k3", build_k3)
    x = inp["x"]
    posf = inp["positions"].astype(np.float32)
    xT = [np.ascontiguousarray(x[c // 4, (c % 4) * TOK:(c % 4 + 1) * TOK, :].T) for c in range(NCORE)]
    for l in range(DEPTH):
        vecs = _vecs(inp, l, cst)
        gw = np.concatenate([inp["gla_w_gate_up"][l], inp["gla_gate_bias"][l][None, :]], axis=0)
        ims = []
        for c in range(NCORE):
            b, q = c // 4, c % 4
            ims.append(dict(xT=xT[c], wgu=inp["ffn1_w_gate_up"][l], wd=inp["ffn1_w_down"][l],
                            g1=np.ascontiguousarray(inp["ffn1_norm"][l].reshape(8, 128).T),
                            w_in=inp["w_in"][l], w_uq=inp["mla_w_uq"][l], w_ukv=inp["mla_w_ukv"][l], vecs=vecs,
                            xitab=cst["xitab"], gw=gw, tri=cst["tri"],
                            posf=np.ascontiguousarray(np.broadcast_to(posf[b, q * TOK:(q + 1) * TOK][None, :], (128, TOK)))))
        r1 = _run(k1, ims)
        ima, imb = [], []
        for b in range(B):
            rq, rk = _cat_tok(r1, "rqT", b), _cat_tok(r1, "rkT", b)
            gq, gk = _cat_tok(r1, "gqT", b), _cat_tok(r1, "gkT", b)
            vt = _cat_rows(r1, "vtok", b)
            qn, qr = _cat_tok(r1, "qnT", b), _cat_tok(r1, "qrT", b)
            kn, kr = _cat_tok(r1, "knT", b), _cat_tok(r1, "krT", b)
            vm = _cat_rows(r1, "vmtok", b)
            gdec = _cat_tok(r1, "gdec", b)
            for h in range(4):
                hs = slice(h * 64, (h + 1) * 64)
                ima.append(dict(rq=np.ascontiguousarray(rq[hs]), rk=np.ascontiguousarray(rk[hs]),
                                rkt=np.ascontiguousarray(rk[hs].T), rv=np.ascontiguousarray(vt[:, h * 64:(h + 1) * 64]),
                                rdec=cst["rdec"][h],
                                gq=np.ascontiguousarray(gq[hs]), gk=np.ascontiguousarray(gk[hs]),
                                gkt=np.ascontiguousarray(gk[hs].T),
                                gv=np.ascontiguousarray(vt[:, 256 + h * 64:256 + (h + 1) * 64]),
                                gdec=np.ascontiguousarray(gdec[hs]), mask=cst["mask"]))
                imb.append(dict(qn=np.ascontiguousarray(qn[h * 128:(h + 1) * 128]), qr=np.ascontiguousarray(qr[hs]),
                                kn=np.ascontiguousarray(kn[h * 128:(h + 1) * 128]), kr=kr,
                                vm=np.ascontiguousarray(vm[:, h * 128:(h + 1) * 128]), mask=cst["mask"]))
        z = dict(qn=np.zeros((128, S), _BF), qr=np.zeros((64, S), _BF), kn=np.zeros((128, S), _BF), kr=np.zeros((64, S), _BF),
                 vm=np.zeros((S, 128), _BF))
        za = {k: np.zeros(v.shape, v.dtype) for k, v in ima[0].items() if k != "mask"}
        r2a = _run(k2a, ima)
        r2b = _run(k2b, imb)
        im3 = []
        for c in range(NCORE):
            b, q = c // 4, c % 4
            ts = slice(q * TOK, (q + 1) * TOK)
            roT = np.concatenate([r2a[b * 4 + h]["ro"][ts].T for h in range(4)], axis=0)
            goT = np.concatenate([r2a[b * 4 + h]["go"][ts].T for h in range(4)], axis=0)
            moT = np.concatenate([r2b[b * 4 + h]["mo"][ts].T for h in range(4)], axis=0)
            im3.append(dict(x1T=r1[c]["x1T"], roT=np.ascontiguousarray(roT), goT=np.ascontiguousarray(goT),
                            moT=np.ascontiguousarray(moT), sgT=r1[c]["sgT"], w_out=inp["w_out"][l], vecs=vecs,
                            wgu=inp["ffn2_w_gate_up"][l], wd=inp["ffn2_w_down"][l],
                            g2=np.ascontiguousarray(inp["ffn2_norm"][l].reshape(8, 128).T)))
        r3 = _run(k3, im3)
        xT = [r3[c]["x3T"] for c in range(NCORE)]
        if _CACHE.get("debug") is not None:
            _CACHE["debug"].append(dict(r1=r1, r2a=r2a, r2b=r2b, r3=r3))
    out = np.empty((B, S, D), np.float32)
    for c in range(NCORE):
        out[c // 4, (c % 4) * TOK:(c % 4 + 1) * TOK, :] = xT[c].T
    return out
```

```python
import numpy as np
import ml_dtypes
import concourse.bass as bass
import concourse.mybir as mybir
from concourse.bass_utils import run_bass_kernel_spmd

F32 = mybir.dt.float32
BF16 = mybir.dt.bfloat16
I32 = mybir.dt.int32
AF = mybir.ActivationFunctionType
ALU = mybir.AluOpType

D = 1024
B = 2
S = 16384
DEPTH = 2
DFF = 2816
NCORE = 8
TOK = B * S // NCORE
TT = 512
NT = TOK // TT
EPS = 1e-6
IN_COLS = 2768
C_RQ, C_RK, C_RV, C_RG = 0, 256, 512, 768
C_CQ, C_CKV, C_KR = 1024, 1408, 1664
C_GQ, C_GK, C_GV, C_GA, C_GR = 1728, 1984, 2240, 2496, 2512


class Buf:
    __slots__ = ("name", "writers", "readers", "sem", "dcount", "excl")

    def __init__(self, name, excl=False):
        self.name = name
        self.excl = excl
        self.writers = {}
        self.readers = {}
        self.sem = None
        self.dcount = 0


class Op:
    __slots__ = ("eng", "fn", "deps", "needs_inc", "sem", "count", "is_dma", "idx")

    def __init__(self, eng, fn, is_dma):
        self.eng = eng
        self.fn = fn
        self.deps = []
        self.needs_inc = False
        self.sem = None
        self.count = 0
        self.is_dma = is_dma


ENGS = ("pe", "act", "dve", "pool", "sp")
ROT = 30000


class Prog:
    def __init__(self, nc):
        self.nc = nc
        self.ops = {e: [] for e in ENGS}
        self.nops = 0
        self.dma_sems = []
        self.out_dmas = []

    def buf(self, name):
        return Buf(name)

    def bufs(self, name, n, excl=False):
        return [Buf(f"{name}{i}", excl) for i in range(n)]

    def _dep(self, op, prod, kind):
        if prod is None or prod is op:
            return
        if not prod.is_dma and prod.eng == op.eng and not op.is_dma:
            if op.eng == "pe" or kind != "raw":
                return
        prod.needs_inc = True
        op.deps.append(prod)

    def add(self, eng, fn, reads=(), writes=(), dma_buf=None, is_out=False):
        is_dma = dma_buf is not None
        op = Op(eng, fn, is_dma)
        op.idx = self.nops
        self.nops += 1
        for b in reads:
            for w in b.writers.values():
                self._dep(op, w, "raw")
            if b.excl:
                for r in b.readers.values():
                    self._dep(op, r, "war")
        for b in writes:
            for w in b.writers.values():
                self._dep(op, w, "waw")
            for r in b.readers.values():
                self._dep(op, r, "war")
        if is_dma:
            if dma_buf.sem is None:
                dma_buf.sem = self.nc.semaphore(f"d{len(self.dma_sems)}_{dma_buf.name}").__enter__()
                self.dma_sems.append(dma_buf.sem)
            dma_buf.dcount += 16
            op.sem = dma_buf.sem
            op.count = dma_buf.dcount
            op.needs_inc = True
            key = ("dma", id(dma_buf))
            if is_out:
                self.out_dmas.append(op)
        else:
            key = eng
        for b in reads:
            b.readers[key] = op
        for b in writes:
            b.writers = {key: op}
            b.readers = {}
        self.ops[eng].append(op)
        return op

    def I(self, eng, meth, reads, writes, *args, **kw):
        return self.add(eng, lambda e: getattr(e, meth)(*args, **kw), reads=reads, writes=writes)

    def dma(self, eng, out, in_, reads, writes, dma_buf, partial=False, is_out=False):
        fn = lambda e: e.dma_start(out=out, in_=in_)
        if partial:
            return self.add_partial_write(eng, fn, reads, writes, dma_buf)
        return self.add(eng, fn, reads, writes, dma_buf, is_out)

    def add_partial_write(self, eng, fn, reads=(), writes=(), dma_buf=None):
        saved = [(b, dict(b.writers), dict(b.readers)) for b in writes]
        op = self.add(eng, fn, reads, writes, dma_buf)
        for b, w, r in saved:
            key = ("dma", id(dma_buf)) if dma_buf is not None else eng
            w = dict(w)
            w[key] = op
            b.writers = w
            b.readers = r
        return op

    def emit(self):
        nc = self.nc
        eng_sems = {}
        for e in ENGS:
            cnt = 0
            sems = []
            for op in self.ops[e]:
                if op.is_dma or not op.needs_inc:
                    continue
                k = cnt // ROT
                if k >= len(sems):
                    sems.append(nc.semaphore(f"c_{e}{k}").__enter__())
                op.sem = sems[k]
                op.count = cnt % ROT + 1
                cnt += 1
            eng_sems[e] = sems
        final_waits = [(op.sem, op.count) for op in self.out_dmas]
        fw = {}
        for s, c in final_waits:
            fw[id(s)] = (s, max(c, fw.get(id(s), (s, 0))[1]))

        def run(engname, eng):
            waited = {}
            for op in self.ops[engname]:
                need = {}
                for p in op.deps:
                    k = id(p.sem)
                    if p.count > need.get(k, (None, 0))[1]:
                        need[k] = (p.sem, p.count)
                for k, (s, c) in need.items():
                    if waited.get(k, 0) >= c:
                        continue
                    eng.wait_ge(s, c)
                    waited[k] = c
                ins = op.fn(eng)
                if op.needs_inc:
                    ins.then_inc(op.sem, 16 if op.is_dma else 1)
            if engname == "sp":
                for s, c in fw.values():
                    eng.wait_ge(s, c)

        with nc.Block() as block:
            @block.tensor
            def _(t):
                run("pe", t)

            @block.scalar
            def _(t):
                run("act", t)

            @block.vector
            def _(t):
                run("dve", t)

            @block.gpsimd
            def _(t):
                run("pool", t)

            @block.sync
            def _(t):
                run("sp", t)


def _mm(P, out_ap, lhsT, rhs, start, stop, reads, writes):
    return P.I("pe", "matmul", reads, writes, out_ap, lhsT, rhs, start=start, stop=stop)


class TokCtx:
    pass


def alloc(nc, name, shape, dt):
    return nc.sbuf_tensor("s_" + name, shape, dt).__enter__()


def ffn_phase(P, nc, C, x_src, x_dst, wgu_d, wd_d, gain_d, src_bufs=None, dst_bufs=None):
    WA, WB = C.WA, C.WB
    wgu = WA[:, 0:8 * 5632].rearrange("p (k c) -> p k c", k=8)
    wd = WB[:, 0:22 * 1024].rearrange("p (k c) -> p k c", k=22)
    for k in range(8):
        P.dma("pool", wgu[:, k, :], wgu_d[k * 128:(k + 1) * 128, :], [], [C.bWA], C.bWA, partial=(k > 0))
    wd_v = wd_d.rearrange("(k p) n -> p k n", p=128)
    for k0 in range(0, 22, 11):
        P.dma("pool", wd[:, k0:k0 + 11, :], wd_v[:, k0:k0 + 11, :], [], [C.bWB], C.bWB, partial=(k0 > 0))
    P.dma("sp", C.gain[:, 0:8], gain_d, [], [C.bgain], C.bgain)

    x_src_v = x_src.rearrange("(k p) t -> p k t", p=128)
    x_dst_v = x_dst.rearrange("(k p) t -> p k t", p=128)

    def load(i):
        s = i % 2
        P.dma("sp", C.xt[s][:], x_src_v[:, :, i * TT:(i + 1) * TT], [src_bufs[i]] if src_bufs else [], [C.bx[s]], C.bx[s])

    load(0)
    for i in range(NT):
        s = i % 2
        if i + 1 < NT:
            load(i + 1)
        xt = C.xt[s]
        bx = C.bx[s]
        yb = C.bps[6]
        ps = C.ps[6]
        for k in range(8):
            q = k % 2
            P.I("act", "activation", [bx], [C.bsq[q]], out=C.sq[q][:], in_=xt[:, k, :], func=AF.Square)
            _mm(P, ps[:], C.ones[:], C.sq[q][:], k == 0, k == 7, [C.bsq[q], C.bones], [yb])
        P.I("act", "activation", [yb, C.bconst], [C.brstd], out=C.rstd[:], in_=ps[:], func=AF.Ln,
            bias=C.epsc[:, 0:1], scale=1.0 / D)
        P.I("act", "activation", [C.brstd], [C.brstd], out=C.rstd[:], in_=C.rstd[:], func=AF.Exp, scale=-0.5)
        for k in range(8):
            P.I("dve", "scalar_tensor_tensor", [bx, C.brstd, C.bgain], [C.bhT], out=C.hT[:, k, :], in0=xt[:, k, :],
                scalar=C.gain[:, k:k + 1], in1=C.rstd[:], op0=ALU.mult, op1=ALU.mult)
        for j in range(22):
            r = j % 3
            gb, ub = C.bps[r], C.bps[3 + r]
            gp, up = C.ps[r], C.ps[3 + r]
            for k in range(8):
                _mm(P, gp[:], wgu[:, k, j * 128:(j + 1) * 128], C.hT[:, k, :], k == 0, k == 7, [C.bWA, C.bhT], [gb])
            for k in range(8):
                _mm(P, up[:], wgu[:, k, DFF + j * 128:DFF + (j + 1) * 128], C.hT[:, k, :], k == 0, k == 7,
                    [C.bWA, C.bhT], [ub])
            q = j % 2
            P.I("act", "activation", [gb], [C.bstmp[q]], out=C.stmp[q][:], in_=gp[:], func=AF.Silu)
            P.I("dve", "tensor_tensor", [ub, C.bstmp[q]], [C.bact], out=C.act[:, j, :], in0=up[:], in1=C.stmp[q][:],
                op=ALU.mult)
        for m in range(8):
            r = 6 + (m % 2)
            yb, yp = C.bps[r], C.ps[r]
            for k in range(22):
                _mm(P, yp[:], wd[:, k, m * 128:(m + 1) * 128], C.act[:, k, :], k == 0, k == 21, [C.bWB, C.bact], [yb])
            P.I("dve", "scalar_tensor_tensor", [yb, bx], [bx], out=xt[:, m, :], in0=yp[:], scalar=0.5, in1=xt[:, m, :],
                op0=ALU.mult, op1=ALU.add)
        P.dma("sp", x_dst_v[:, :, i * TT:(i + 1) * TT], xt[:], [bx], [dst_bufs[i]] if dst_bufs else [], bx, is_out=True)


def tok_ctx(P, nc):
    C = TokCtx()
    C.WA = alloc(nc, "WA", [128, 8 * 5632], BF16)
    C.WB = alloc(nc, "WB", [128, 22 * 1024], BF16)
    C.bWA, C.bWB = P.buf("WA"), P.buf("WB")
    C.xt = [alloc(nc, f"xt{i}", [128, 8, TT], F32) for i in range(2)]
    C.bx = P.bufs("x", 2)
    C.hT = alloc(nc, "hT", [128, 8, TT], BF16)
    C.bhT = P.buf("hT")
    C.act = alloc(nc, "act", [128, 22, TT], BF16)
    C.bact = P.buf("act")
    C.sq = [alloc(nc, f"sq{i}", [128, TT], BF16) for i in range(2)]
    C.bsq = P.bufs("sq", 2)
    C.rstd = alloc(nc, "rstd", [128, TT], F32)
    C.brstd = P.buf("rstd")
    C.stmp = [alloc(nc, f"stmp{i}", [128, TT], BF16) for i in range(2)]
    C.bstmp = P.bufs("stmp", 2)
    C.gain = alloc(nc, "gain", [128, 32], F32)
    C.bgain = P.buf("gain")
    C.ones = alloc(nc, "ones", [128, 128], BF16)
    C.bones = P.buf("ones")
    C.epsc = alloc(nc, "epsc", [128, 4], F32)
    C.vec = alloc(nc, "vec", [128, NVEC], F32)
    C.bvec = P.buf("vec")
    C.ones2 = alloc(nc, "ones2", [128, 128], BF16)
    C.bones2 = C.bones
    C.bxd = P.bufs("xd", NT)
    C.bconst = P.buf("const")
    C.ps = [nc.psum_tensor(f"ps{i}", [128, 512], F32).__enter__() for i in range(8)]
    C.bps = P.bufs("ps", 8, excl=True)
    P.I("dve", "memset", [], [C.bones], C.ones[:], 1.0)
    P.I("dve", "memset", [], [C.bconst], C.epsc[:], EPS)
    P.I("dve", "memset", [C.bconst], [C.bconst], C.epsc[:, 1:2], 1.0)
    P.I("dve", "memset", [C.bconst], [C.bconst], C.epsc[:, 2:3], float(np.log(0.125)))
    P.I("pool", "memset", [], [C.bones], C.ones2[:], 0.0)
    P.I("pool", "memset", [C.bones], [C.bones], C.ones2[0:64, 0:64], 1.0)
    P.I("pool", "memset", [C.bones], [C.bones], C.ones2[64:128, 64:128], 1.0)
    return C


def build_k1_test():
    nc = bass.Bass("TRN2", target_bir_lowering=False)
    xT = nc.dram_tensor("xT", [D, TOK], F32, kind="ExternalInput").ap()
    wgu = nc.dram_tensor("wgu", [D, 2 * DFF], F32, kind="ExternalInput").ap()
    wd = nc.dram_tensor("wd", [DFF, D], F32, kind="ExternalInput").ap()
    g1 = nc.dram_tensor("g1", [128, 8], F32, kind="ExternalInput").ap()
    x1T = nc.dram_tensor("x1T", [D, TOK], F32, kind="ExternalOutput").ap()
    P = Prog(nc)
    C = tok_ctx(P, nc)
    ffn_phase(P, nc, C, xT, x1T, wgu, wd, g1)
    P.emit()
    return nc


NQG = S // 512
NCH = S // 128
SCALE_MLA = 192.0 ** -0.5


def build_k2(phases=(1, 2)):
    nc = bass.Bass("TRN2", target_bir_lowering=False)

    def din(name, shape, dt):
        return nc.dram_tensor(name, shape, dt, kind="ExternalInput").ap()

    if 2 in phases:
        qn_d = din("qn", [128, S], BF16)
        qr_d = din("qr", [64, S], BF16)
        kn_d = din("kn", [128, S], BF16)
        kr_d = din("kr", [64, S], BF16)
        vm_d = din("vm", [S, 128], BF16)
        mo_d = nc.dram_tensor("mo", [S, 128], F32, kind="ExternalOutput").ap()
    lin_d = {}
    for X in (("r", "g") if 1 in phases else ()):
        lin_d[X] = dict(q=din(X + "q", [64, S], BF16), k=din(X + "k", [64, S], BF16),
                        kt=din(X + "kt", [S, 64], BF16), v=din(X + "v", [S, 64], BF16),
                        dec=din(X + "dec", [64, NCH], F32))
    mask_d = din("mask", [128, 128], BF16)
    lo_d = {X: nc.dram_tensor(X + "o", [S, 64], F32, kind="ExternalOutput").ap() for X in (("r", "g") if 1 in phases else ())}

    P = Prog(nc)
    ps = [nc.psum_tensor(f"ps{i}", [128, 512], F32).__enter__() for i in range(8)]
    bps = P.bufs("ps", 8, excl=True)
    mask = alloc(nc, "mask", [128, 128], BF16)
    bmask = P.buf("mask")
    P.dma("sp", mask[:], mask_d, [], [bmask], bmask)

    L = {}
    for X in (("r", "g") if 1 in phases else ()):
        o = TokCtx()
        o.q = [alloc(nc, f"{X}q{i}", [64, 512], BF16) for i in range(2)]
        o.k = [alloc(nc, f"{X}k{i}", [64, 512], BF16) for i in range(2)]
        o.kt = [alloc(nc, f"{X}kt{i}", [128, 4, 64], BF16) for i in range(2)]
        o.v = [alloc(nc, f"{X}v{i}", [128, 4, 64], BF16) for i in range(2)]
        o.bin = P.bufs(X + "in", 2)
        o.dec = alloc(nc, X + "dec", [64, NCH], F32)
        o.bdec = P.buf(X + "dec")
        o.scm = [alloc(nc, f"{X}scm{i}", [128, 128], BF16) for i in range(2)]
        o.bscm = P.bufs(X + "scm", 2)
        o.st = alloc(nc, X + "st", [64, 64], F32)
        o.tmp = alloc(nc, X + "tmp", [64, 64], F32)
        o.stb = alloc(nc, X + "stb", [64, 64], BF16)
        o.bst, o.btmp, o.bstb = P.buf(X + "st"), P.buf(X + "tmp"), P.buf(X + "stb")
        o.osb = [alloc(nc, f"{X}osb{i}", [128, 4, 64], F32) for i in range(2)]
        o.bosb = P.bufs(X + "osb", 2)
        L[X] = o
        P.dma("sp", o.dec[:], lin_d[X]["dec"], [], [o.bdec], o.bdec)
        P.I("dve", "memset", [], [o.bst], o.st[:], 0.0)
        P.I("dve", "memset", [], [o.bstb], o.stb[:], 0.0)

    def lin_load(g):
        s = g % 2
        for X in ("r", "g"):
            o, d = L[X], lin_d[X]
            t0 = g * 512
            P.dma("sp", o.q[s][:], d["q"][:, t0:t0 + 512], [], [o.bin[s]], o.bin[s])
            P.dma("sp", o.k[s][:], d["k"][:, t0:t0 + 512], [], [o.bin[s]], o.bin[s], partial=True)
            P.dma("sp", o.kt[s][:], d["kt"][t0:t0 + 512, :].rearrange("(n p) d -> p n d", p=128), [], [o.bin[s]], o.bin[s],
                  partial=True)
            P.dma("sp", o.v[s][:], d["v"][t0:t0 + 512, :].rearrange("(n p) d -> p n d", p=128), [], [o.bin[s]], o.bin[s],
                  partial=True)

    if 1 in phases:
        lin_load(0)
    for g in range(NQG if 1 in phases else 0):
        s = g % 2
        if g + 1 < NQG:
            lin_load(g + 1)
        for c in range(4):
            n = g * 4 + c
            cs = slice(c * 128, (c + 1) * 128)
            for xi, X in enumerate(("r", "g")):
                o = L[X]
                sb = xi * 2 + (n % 2)
                ob = 4 + xi * 2 + (n % 2)
                m2 = n % 2
                _mm(P, ps[sb][:, 0:128], o.k[s][:, cs], o.q[s][:, cs], True, True, [o.bin[s]], [bps[sb]])
                P.I("dve", "tensor_tensor", [bps[sb], bmask], [o.bscm[m2]], out=o.scm[m2][:], in0=ps[sb][:, 0:128],
                    in1=mask[:], op=ALU.mult)
                _mm(P, ps[ob][:, 0:64], o.scm[m2][:], o.v[s][:, c, :], True, False, [o.bscm[m2], o.bin[s]], [bps[ob]])
                _mm(P, ps[ob][:, 0:64], o.q[s][:, cs], o.stb[:], False, True, [o.bin[s], o.bstb], [bps[ob]])
                _mm(P, ps[ob][0:64, 64:128], o.kt[s][:, c, :], o.v[s][:, c, :], True, True, [o.bin[s]], [bps[ob]])
                P.I("act", "copy", [bps[ob]], [o.bosb[s]], out=o.osb[s][:, c, :], in_=ps[ob][:, 0:64])
                P.I("dve", "tensor_tensor", [bps[ob], o.bst], [o.btmp], out=o.tmp[:], in0=ps[ob][0:64, 64:128], in1=o.st[:],
                    op=ALU.add)
                P.I("dve", "tensor_scalar", [o.btmp, o.bdec], [o.bst], out=o.st[:], in0=o.tmp[:], scalar1=o.dec[:, n:n + 1],
                    scalar2=None, op0=ALU.mult)
                P.I("dve", "tensor_copy", [o.bst], [o.bstb], out=o.stb[:], in_=o.st[:])
        for X in ("r", "g"):
            o = L[X]
            P.dma("sp", lo_d[X][g * 512:(g + 1) * 512, :].rearrange("(n p) d -> p n d", p=128), o.osb[s][:],
                  [o.bosb[s]], [], o.bosb[s], is_out=True)

    if 2 not in phases:
        P.emit()
        return nc
    kn = alloc(nc, "kn", [128, S], BF16)
    kr = alloc(nc, "kr", [64, S], BF16)
    V = alloc(nc, "V", [128, NCH, 130], BF16)
    bkv = P.bufs("kv", NQG)
    kvsem = P.bufs("kvsem", 4)
    bvones = P.buf("vones")
    P.I("pool", "memset", [], [bvones], V[:, :, 128:130], 1.0)
    qn = [alloc(nc, f"qn{i}", [128, 512], BF16) for i in range(2)]
    qr = [alloc(nc, f"qr{i}", [64, 512], BF16) for i in range(2)]
    bq = P.bufs("q", 2)
    pt = [alloc(nc, f"pt{i}", [128, 512], BF16) for i in range(3)]
    bpt = P.bufs("pt", 3)
    rec = [alloc(nc, f"rec{i}", [128, 1], F32) for i in range(2)]
    brec = P.bufs("rec", 2)
    mosb = [alloc(nc, f"mosb{i}", [128, 4, 128], F32) for i in range(2)]
    bmosb = P.bufs("mosb", 2)
    vm_v = vm_d.rearrange("(n p) d -> p n d", p=128)

    def kv_load(g):
        t0 = g * 512
        sb = kvsem[g % 4]
        rd = [bkv[g - 4]] if g >= 4 else []
        P.dma("sp", kn[:, t0:t0 + 512], kn_d[:, t0:t0 + 512], rd, [bkv[g]], sb)
        P.dma("sp", kr[:, t0:t0 + 512], kr_d[:, t0:t0 + 512], [], [bkv[g]], sb, partial=True)
        P.dma("sp", V[:, 4 * g:4 * g + 4, 0:128], vm_v[:, 4 * g:4 * g + 4, :], [], [bkv[g]], sb, partial=True)

    def q_load(g):
        s = g % 2
        t0 = g * 512
        P.dma("sp", qn[s][:], qn_d[:, t0:t0 + 512], [], [bq[s]], bq[s])
        P.dma("sp", qr[s][:], qr_d[:, t0:t0 + 512], [], [bq[s]], bq[s], partial=True)

    LOOK = 2
    blocks = [(g, kb) for g in range(NQG) for kb in range(4 * g + 4)]
    nblk = len(blocks)
    kv_load(0)
    q_load(0)

    def emit_sc(i):
        g, kb = blocks[i]
        s = g % 2
        if kb == 0 and g + 1 < NQG:
            kv_load(g + 1)
            q_load(g + 1)
        j = kb - 4 * g
        c0 = 128 * j if j > 0 else 0
        r = i % 3
        ks = slice(kb * 128, (kb + 1) * 128)
        kvb = bkv[kb // 4]
        _mm(P, ps[r][:, c0:512], kn[:, ks], qn[s][:, c0:512], True, False, [kvb, bq[s]], [bps[r]])
        _mm(P, ps[r][:, c0:512], kr[:, ks], qr[s][:, c0:512], False, True, [kvb, bq[s]], [bps[r]])
        P.I("act", "activation", [bps[r]], [bpt[r]], out=pt[r][:, c0:512], in_=ps[r][:, c0:512], func=AF.Exp,
            scale=SCALE_MLA)
        if j >= 0:
            P.I("pool", "tensor_tensor", [bpt[r], bmask], [bpt[r]], out=pt[r][:, 128 * j:128 * j + 128],
                in0=pt[r][:, 128 * j:128 * j + 128], in1=mask[:], op=ALU.mult)

    def emit_pv(i):
        g, kb = blocks[i]
        s = g % 2
        j = kb - 4 * g
        r = i % 3
        kvb = bkv[kb // 4]
        for qi in range(max(j, 0), 4):
            ab = 3 + qi
            _mm(P, ps[ab][:, 0:129], pt[r][:, qi * 128:(qi + 1) * 128], V[:, kb, 0:129], kb == 0, kb == 4 * g + qi,
                [bpt[r], kvb, bvones], [bps[ab]])
        if kb == 4 * g + 3:
            for qi in range(4):
                ab = 3 + qi
                q2 = qi % 2
                P.I("dve", "reciprocal", [bps[ab]], [brec[q2]], out=rec[q2][:], in_=ps[ab][:, 128:129])
                P.I("dve", "tensor_scalar", [bps[ab], brec[q2]], [bmosb[s]], out=mosb[s][:, qi, :], in0=ps[ab][:, 0:128],
                    scalar1=rec[q2][:, 0:1], scalar2=None, op0=ALU.mult)
            P.dma("sp", mo_d[g * 512:(g + 1) * 512, :].rearrange("(n p) d -> p n d", p=128), mosb[s][:], [bmosb[s]], [],
                  bmosb[s], is_out=True)

    for i in range(nblk + LOOK):
        if i < nblk:
            emit_sc(i)
        if i >= LOOK:
            emit_pv(i - LOOK)
    P.emit()
    return nc


TWO_PI = 2.0 * np.pi
MAGIC = 12582912.0
CW1 = 6.28125
CW2 = TWO_PI - 6.28125
V_MIX = 0
V_CQ = 8
V_CKV = 11
V_QN = 13
V_KN = 14
V_QR = 15
V_QRS = 16
V_KR = 17
V_KRS = 18
V_INV = 19
V_RO = 20
V_GO = 22
NVEC = 24


def norm_from_psum(P, C, src_aps, src_bufs, K, nparts, ones_ap, n_norm, out_aps, out_buf, gain_cols, stat_bank):
    sb, sp = C.bps[stat_bank], C.ps[stat_bank]
    n = len(src_aps)
    for k in range(n):
        q = k % 2
        P.I("act", "activation", [src_bufs[k]], [C.bsq[q]], out=C.sq[q][0:nparts, :], in_=src_aps[k], func=AF.Square)
        _mm(P, sp[0:nparts, :], ones_ap, C.sq[q][0:nparts, :], k == 0, k == n - 1, [C.bsq[q], C.bones], [sb])
    P.I("act", "activation", [sb, C.bconst], [C.brstd], out=C.rstd[0:nparts, :], in_=sp[0:nparts, :], func=AF.Ln,
        bias=C.epsc[0:nparts, 0:1], scale=1.0 / n_norm)
    P.I("act", "activation", [C.brstd], [C.brstd], out=C.rstd[0:nparts, :], in_=C.rstd[0:nparts, :], func=AF.Exp, scale=-0.5)
    if out_aps is not None:
        for k in range(n):
            P.I("dve", "scalar_tensor_tensor", [src_bufs[k], C.brstd, C.bvec], [out_buf], out=out_aps[k], in0=src_aps[k],
                scalar=C.vec[0:nparts, gain_cols[k]:gain_cols[k] + 1], in1=C.rstd[0:nparts, :], op0=ALU.mult, op1=ALU.mult)


def proj_phase(P, nc, C, d):
    WA = C.WA
    off = [0]

    def carve(n):
        a = off[0]
        off[0] += n
        return WA[:, a:a + n]

    bW = C.bWA
    w_in = carve(8 * IN_COLS).rearrange("p (k c) -> p k c", k=8)
    w_sw = carve(8 * 576).rearrange("p (k c) -> p k c", k=8)
    wuq_n = carve(3 * 512).rearrange("p (k c) -> p k c", k=3)
    wuq_r = carve(3 * 256).rearrange("p (k c) -> p k c", k=3)
    wuq_rs = carve(3 * 256).rearrange("p (k c) -> p k c", k=3)
    wkv_k = carve(2 * 512).rearrange("p (k c) -> p k c", k=2)
    wkv_v = carve(2 * 512).rearrange("p (k c) -> p k c", k=2)
    first = [True]

    def wdma(out, in_):
        P.dma("pool", out, in_, [], [bW], bW, partial=not first[0])
        first[0] = False

    win_d = d["w_in"]
    for k in range(8):
        rows = slice(k * 128, (k + 1) * 128)
        wdma(w_in[:, k, :], win_d[rows, :])
        src = win_d[rows, 0:512].rearrange("p (h t c) -> p h t c", h=8, t=2)
        dst = w_sw[:, k, 0:512].rearrange("p (h t c) -> p h t c", h=8, t=2)
        wdma(dst[:, :, 0, :], src[:, :, 1, :])
        wdma(dst[:, :, 1, :], src[:, :, 0, :])
        wdma(w_sw[:, k, 512:544], win_d[rows, C_KR + 32:C_KR + 64])
        wdma(w_sw[:, k, 544:576], win_d[rows, C_KR:C_KR + 32])
    for k in range(3):
        rows = slice(k * 128, (k + 1) * 128)
        src = d["w_uq"][rows, :].rearrange("p (h c) -> p h c", h=4)
        wdma(wuq_n[:, k, :].rearrange("p (h c) -> p h c", h=4), src[:, :, 0:128])
        wdma(wuq_r[:, k, :].rearrange("p (h c) -> p h c", h=4), src[:, :, 128:192])
        dsts = wuq_rs[:, k, :].rearrange("p (h c) -> p h c", h=4)
        wdma(dsts[:, :, 0:32], src[:, :, 160:192])
        wdma(dsts[:, :, 32:64], src[:, :, 128:160])
    for k in range(2):
        rows = slice(k * 128, (k + 1) * 128)
        src = d["w_ukv"][rows, :].rearrange("p (h c) -> p h c", h=4)
        wdma(wkv_k[:, k, :].rearrange("p (h c) -> p h c", h=4), src[:, :, 0:128])
        wdma(wkv_v[:, k, :].rearrange("p (h c) -> p h c", h=4), src[:, :, 128:256])

    def alias():
        b = Buf("al")
        for o in (C.bWA, C.bWB, C.bact):
            for k, v in o.writers.items():
                if k not in b.writers or b.writers[k].idx < v.idx:
                    b.writers[k] = v
            for k, v in o.readers.items():
                if k not in b.readers or b.readers[k].idx < v.idx:
                    b.readers[k] = v
        return b

    assert off[0] % 2 == 0
    WAf = WA.bitcast(F32)
    WBf = C.WB.bitcast(F32)
    offa = [off[0] // 2]
    offb = [0]

    def cf(n, region="b"):
        o_, t_ = (offb, WBf) if region == "b" else (offa, WAf)
        a = o_[0]
        o_[0] += n
        return t_[:, a:a + n]

    xitab = cf(4 * TT, "a").rearrange("p (k c) -> p k c", k=4); bxi = alias()
    Eq = [cf(TT, "a") for _ in range(2)]; Ek = [cf(TT, "a") for _ in range(2)]; bE = [alias() for _ in range(2)]
    assert offa[0] <= 8 * 5632 // 2, offa[0]
    pos = cf(TT); bpos = alias()
    ang = cf(TT); bang = alias()
    tk = cf(TT); btk = alias()
    r1 = cf(TT); br1 = alias()
    Ssb = cf(TT); bS = alias()
    Csb = cf(TT); bC = alias()
    GCq = cf(TT); GSq = cf(TT); bGq = alias()
    GCk = cf(TT); GSk = cf(TT); bGk = alias()
    t1 = [cf(TT) for _ in range(2)]; bt1 = [alias() for _ in range(2)]
    t2 = [cf(TT) for _ in range(2)]; bt2 = [alias() for _ in range(2)]
    sgs = [cf(TT) for _ in range(2)]; bsgs = [alias() for _ in range(2)]
    alow = cf(TT); balow = alias()
    gw = cf(256); bgw = alias()
    tri = cf(128); btri = alias()
    zl = cf(256); bzl = alias()
    decs = cf(8).rearrange("p (k c) -> p k c", k=2); bdecs = alias()
    assert offb[0] <= 22 * 1024 // 2, offb[0]

    nb = [0]

    def stage_bf(n):
        a = nb[0]
        nb[0] += n
        assert nb[0] <= 22
        return C.act[:, a:a + n, :], alias()

    cqn, bcqn = stage_bf(3)
    ckvn, bckvn = stage_bf(2)
    vst, bvst = stage_bf(4)
    ostage = [stage_bf(1) for _ in range(8)]
    nst = [0]

    def next_stage():
        a = ostage[nst[0] % len(ostage)]
        nst[0] += 1
        return a[0][:, 0, :], a[1]

    P.dma("sp", C.vec[:], d["vecs"], [], [C.bvec], C.bvec)
    P.dma("sp", xitab, d["xitab"].rearrange("p (k c) -> p k c", k=4), [], [bxi], bxi)
    P.dma("sp", gw[0:17, :], d["gw"], [], [bgw], bgw)
    P.dma("sp", tri, d["tri"], [], [btri], btri)
    P.I("pool", "memset", [], [balow], alow[0:32, :], 1.0)

    x_v = d["x1T"].rearrange("(k p) t -> p k t", p=128)

    def load(i):
        s = i % 2
        P.dma("sp", C.xt[s][:], x_v[:, :, i * TT:(i + 1) * TT], [C.bxd[i]], [C.bx[s]], C.bx[s])

    load(0)
    bank = [0]

    def nb_():
        b = bank[0] % 6
        bank[0] += 1
        return b

    def proj_chunk(wt, col0, M=128):
        b = nb_()
        for k in range(8):
            _mm(P, C.ps[b][0:M, :], wt[:, k, col0:col0 + M], C.hT[:, k, :], k == 0, k == 7, [bW, C.bhT], [C.bps[b]])
        return b

    def store(dram_ap, sb_ap, buf):
        P.dma("sp", dram_ap, sb_ap, [buf], [], buf, is_out=True)

    def rope_combine(bx_, bs_, M, cos_ap, sin_ap, rbufs, post_ap, post_bufs, out_ap, out_buf, q):
        P.I("dve", "tensor_tensor", [C.bps[bx_]] + rbufs, [bt1[q]], out=t1[q][0:M, :], in0=C.ps[bx_][0:M, :], in1=cos_ap,
            op=ALU.mult)
        P.I("dve", "tensor_tensor", [C.bps[bs_]] + rbufs, [bt2[q]], out=t2[q][0:M, :], in0=C.ps[bs_][0:M, :], in1=sin_ap,
            op=ALU.mult)
        P.I("pool", "tensor_tensor", [bt1[q], bt2[q]], [bt1[q]], out=t1[q][0:M, :], in0=t1[q][0:M, :], in1=t2[q][0:M, :],
            op=ALU.add)
        P.I("pool", "tensor_tensor", [bt1[q]] + post_bufs, [out_buf], out=out_ap, in0=t1[q][0:M, :], in1=post_ap, op=ALU.mult)

    for i in range(NT):
        s = i % 2
        tsl = slice(i * TT, (i + 1) * TT)
        if i + 1 < NT:
            load(i + 1)
        xt, bx = C.xt[s], C.bx[s]
        norm_from_psum(P, C, [xt[:, k, :] for k in range(8)], [bx] * 8, 128, 128, C.ones[:], D,
                       [C.hT[:, k, :] for k in range(8)], C.bhT, [V_MIX + k for k in range(8)], 6)
        P.dma("sp", pos, d["posf"][:, tsl], [], [bpos], bpos)
        for (dst, bd, shift) in ((Ssb, bS, 0.0), (Csb, bC, 0.5 * np.pi)):
            P.I("pool", "tensor_scalar", [bpos, C.bvec], [bang], out=ang, in0=pos, scalar1=C.vec[:, V_INV:V_INV + 1],
                scalar2=shift, op0=ALU.mult, op1=ALU.add)
            P.I("pool", "tensor_scalar", [bang], [btk], out=tk, in0=ang, scalar1=1.0 / TWO_PI, scalar2=MAGIC,
                op0=ALU.mult, op1=ALU.add)
            P.I("pool", "tensor_scalar", [btk], [btk], out=tk, in0=tk, scalar1=-MAGIC, scalar2=None, op0=ALU.add)
            P.I("dve", "scalar_tensor_tensor", [btk, bang], [br1], out=r1, in0=tk, scalar=-CW1, in1=ang,
                op0=ALU.mult, op1=ALU.add)
            P.I("dve", "scalar_tensor_tensor", [btk, br1], [br1], out=r1, in0=tk, scalar=-CW2, in1=r1,
                op0=ALU.mult, op1=ALU.add)
            P.I("pool", "tensor_scalar", [br1], [br1], out=r1, in0=r1, scalar1=-np.pi, scalar2=np.pi, op0=ALU.max, op1=ALU.min)
            P.I("act", "activation", [br1], [bd], out=dst, in_=r1, func=AF.Sin)
        P.I("pool", "tensor_scalar", [bC, C.bvec], [bGq], out=GCq, in0=Csb, scalar1=C.vec[:, V_QR:V_QR + 1], scalar2=None,
            op0=ALU.mult)
        P.I("pool", "tensor_scalar", [bS, C.bvec], [bGq], out=GSq, in0=Ssb, scalar1=C.vec[:, V_QRS:V_QRS + 1], scalar2=None,
            op0=ALU.mult)
        P.I("pool", "tensor_scalar", [bC, C.bvec], [bGk], out=GCk[0:64, :], in0=Csb[0:64, :], scalar1=C.vec[0:64, V_KR:V_KR + 1],
            scalar2=None, op0=ALU.mult)
        P.I("pool", "tensor_scalar", [bS, C.bvec], [bGk], out=GSk[0:64, :], in0=Ssb[0:64, :],
            scalar1=C.vec[0:64, V_KRS:V_KRS + 1], scalar2=None, op0=ALU.mult)

        for (c0, tab0, dname) in ((C_RQ, 0, "rqT"), (C_RK, 2, "rkT")):
            for cc in range(2):
                bxp = proj_chunk(w_in, c0 + cc * 128)
                bsp = proj_chunk(w_sw, c0 + cc * 128)
                o_ap, o_b = next_stage()
                rope_combine(bxp, bsp, 128, Csb, Ssb, [bC, bS], xitab[:, tab0 + cc, :], [bxi], o_ap, o_b, cc)
                store(d[dname][cc * 128:(cc + 1) * 128, tsl], o_ap, o_b)
        for (c0, r0) in ((C_RG, 0), (C_GR, 256)):
            for cc in range(2):
                bp = proj_chunk(w_in, c0 + cc * 128)
                P.I("act", "activation", [C.bps[bp]], [bsgs[cc]], out=sgs[cc], in_=C.ps[bp][:], func=AF.Silu)
                store(d["sgT"][r0 + cc * 128:r0 + (cc + 1) * 128, tsl], sgs[cc], bsgs[cc])
        for sub in range(4):
            b = nb_()
            for (c0, o0) in ((C_RV, 0), (C_GV, 256)):
                for k in range(8):
                    _mm(P, C.ps[b][:, o0:o0 + 256], C.hT[:, k, sub * 128:(sub + 1) * 128], w_in[:, k, c0:c0 + 256], k == 0,
                        k == 7, [bW, C.bhT], [C.bps[b]])
            P.I("act", "copy", [C.bps[b]], [bvst], out=vst[:, sub, :], in_=C.ps[b][:])
        store(d["vtok"][i * TT:(i + 1) * TT, :].rearrange("(n p) c -> p n c", p=128), vst, bvst)
        bq_ = [proj_chunk(w_in, C_CQ + k * 128) for k in range(3)]
        norm_from_psum(P, C, [C.ps[b][:] for b in bq_], [C.bps[b] for b in bq_], 128, 128, C.ones[:], 384,
                       [cqn[:, k, :] for k in range(3)], bcqn, [V_CQ + k for k in range(3)], 6)
        for h in range(4):
            b = nb_()
            for k in range(3):
                _mm(P, C.ps[b][:], wuq_n[:, k, h * 128:(h + 1) * 128], cqn[:, k, :], k == 0, k == 2, [bW, bcqn], [C.bps[b]])
            o_ap, o_b = next_stage()
            norm_from_psum(P, C, [C.ps[b][:]], [C.bps[b]], 128, 128, C.ones[:], 128, [o_ap], o_b, [V_QN], 7)
            store(d["qnT"][h * 128:(h + 1) * 128, tsl], o_ap, o_b)
        for cc in range(2):
            b1, b2 = nb_(), nb_()
            for (b, wt) in ((b1, wuq_r), (b2, wuq_rs)):
                for k in range(3):
                    _mm(P, C.ps[b][:], wt[:, k, cc * 128:(cc + 1) * 128], cqn[:, k, :], k == 0, k == 2, [bW, bcqn], [C.bps[b]])
            norm_from_psum(P, C, [C.ps[b1][:]], [C.bps[b1]], 128, 128, C.ones2[:], 64, None, None, None, 7)
            o_ap, o_b = next_stage()
            rope_combine(b1, b2, 128, GCq, GSq, [bGq], C.rstd[:], [C.brstd], o_ap, o_b, cc)
            store(d["qrT"][cc * 128:(cc + 1) * 128, tsl], o_ap, o_b)
        bk_ = [proj_chunk(w_in, C_CKV + k * 128) for k in range(2)]
        norm_from_psum(P, C, [C.ps[b][:] for b in bk_], [C.bps[b] for b in bk_], 128, 128, C.ones[:], 256,
                       [ckvn[:, k, :] for k in range(2)], bckvn, [V_CKV + k for k in range(2)], 6)
        for h in range(4):
            b = nb_()
            for k in range(2):
                _mm(P, C.ps[b][:], wkv_k[:, k, h * 128:(h + 1) * 128], ckvn[:, k, :], k == 0, k == 1, [bW, bckvn], [C.bps[b]])
            o_ap, o_b = next_stage()
            norm_from_psum(P, C, [C.ps[b][:]], [C.bps[b]], 128, 128, C.ones[:], 128, [o_ap], o_b, [V_KN], 7)
            store(d["knT"][h * 128:(h + 1) * 128, tsl], o_ap, o_b)
        for sub in range(4):
            b = nb_()
            for k in range(2):
                _mm(P, C.ps[b][:], ckvn[:, k, sub * 128:(sub + 1) * 128], wkv_v[:, k, :], k == 0, k == 1, [bW, bckvn], [C.bps[b]])
            o_ap, o_b = next_stage()
            P.I("act", "copy", [C.bps[b]], [o_b], out=o_ap, in_=C.ps[b][:])
            store(d["vmtok"][i * TT + sub * 128:i * TT + (sub + 1) * 128, :], o_ap, o_b)
        b1 = proj_chunk(w_in, C_KR, M=64)
        b2 = proj_chunk(w_sw, 512, M=64)
        norm_from_psum(P, C, [C.ps[b1][0:64, :]], [C.bps[b1]], 64, 64, C.ones[0:64, 0:64], 64, None, None, None, 7)
        o_ap, o_b = next_stage()
        rope_combine(b1, b2, 64, GCk[0:64, :], GSk[0:64, :], [bGk], C.rstd[0:64, :], [C.brstd], o_ap[0:64, :], o_b, 0)
        store(d["krT"][:, tsl], o_ap[0:64, :], o_b)
        ba = proj_chunk(w_in, C_GA, M=16)
        P.I("act", "copy", [C.bps[ba]], [balow], out=alow[0:16, :], in_=C.ps[ba][0:16, :])
        bc = [nb_(), nb_()]
        for sub in range(4):
            bz = 7
            P.I("pe", "matmul", [balow, bgw], [C.bps[bz]], C.ps[bz][:, 0:256], alow[0:17, sub * 128:(sub + 1) * 128], gw[0:17, :],
                start=True, stop=True)
            P.I("act", "activation", [C.bps[bz]], [bzl], out=zl, in_=C.ps[bz][:, 0:256], func=AF.Exp, scale=-1.0)
            P.I("act", "activation", [bzl, C.bconst], [bzl], out=zl, in_=zl, func=AF.Ln, bias=C.epsc[:, 1:2], scale=1.0)
            for c in range(2):
                P.I("pe", "matmul", [bzl, btri], [C.bps[bc[c]]], C.ps[bc[c]][:, sub * 128:(sub + 1) * 128],
                    zl[:, c * 128:(c + 1) * 128], tri, start=True, stop=True)
        for c in range(2):
            pb = C.ps[bc[c]]
            P.I("act", "activation", [C.bps[bc[c]], C.bconst], [bE[c]], out=Eq[c], in_=pb[:], func=AF.Exp,
                bias=C.epsc[:, 2:3], scale=1.0)
            P.I("act", "activation", [C.bps[bc[c]]], [bE[c]], out=Ek[c], in_=pb[:], func=AF.Exp, scale=-1.0)
            P.I("act", "activation", [C.bps[bc[c]]], [bdecs], out=decs[:, c, :],
                in_=pb[:].rearrange("p (n t) -> p n t", t=128)[:, :, 127], func=AF.Exp)
        store(d["gdec"][:, i * 4:(i + 1) * 4].rearrange("(c p) n -> p c n", p=128), decs, bdecs)
        for (c0, E, dname) in ((C_GQ, Eq, "gqT"), (C_GK, Ek, "gkT")):
            for cc in range(2):
                bp = proj_chunk(w_in, c0 + cc * 128)
                o_ap, o_b = next_stage()
                P.I("dve", "tensor_tensor", [C.bps[bp], bE[cc]], [o_b], out=o_ap, in0=C.ps[bp][:], in1=E[cc], op=ALU.mult)
                store(d[dname][cc * 128:(cc + 1) * 128, tsl], o_ap, o_b)


K1_OUTS = dict(rqT=([256, TOK], BF16), rkT=([256, TOK], BF16), sgT=([512, TOK], F32), vtok=([TOK, 512], BF16),
               qnT=([512, TOK], BF16), qrT=([256, TOK], BF16), knT=([512, TOK], BF16), vmtok=([TOK, 512], BF16),
               krT=([64, TOK], BF16), gdec=([256, TOK // 128], F32), gqT=([256, TOK], BF16), gkT=([256, TOK], BF16),
               x1T=([D, TOK], F32))


def build_k1(with_ffn=True):
    nc = bass.Bass("TRN2", target_bir_lowering=False)

    def din(name, shape, dt=F32):
        return nc.dram_tensor(name, shape, dt, kind="ExternalInput").ap()

    xT = din("xT", [D, TOK])
    wgu = din("wgu", [D, 2 * DFF])
    wd = din("wd", [DFF, D])
    g1 = din("g1", [128, 8])
    d = dict(w_in=din("w_in", [D, IN_COLS]), w_uq=din("w_uq", [384, 768]), w_ukv=din("w_ukv", [256, 1024]),
             vecs=din("vecs", [128, NVEC]), xitab=din("xitab", [128, 4 * TT]), gw=din("gw", [17, 256]),
             tri=din("tri", [128, 128]), posf=din("posf", [128, TOK]))
    for name, (shape, dt) in K1_OUTS.items():
        d[name] = nc.dram_tensor(name, shape, dt, kind="ExternalOutput").ap()
    P = Prog(nc)
    C = tok_ctx(P, nc)
    if with_ffn:
        ffn_phase(P, nc, C, xT, d["x1T"], wgu, wd, g1, dst_bufs=C.bxd)
    else:
        d["x1T"] = xT
    proj_phase(P, nc, C, d)
    P.emit()
    return nc


def post_phase(P, nc, C, d):
    WA = C.WA
    bW = C.bWA
    w_out = WA[:, 0:8 * 1024].rearrange("p (k c) -> p k c", k=8)
    P.dma("pool", w_out, d["w_out"].rearrange("(k p) n -> p k n", p=128), [], [bW], bW)
    WBf = C.WB.bitcast(F32)
    offb = [0]

    def cf(n):
        a = offb[0]
        offb[0] += n
        return WBf[:, a:a + n]

    ot = [cf(TT) for _ in range(2)]; bot = P.bufs("ot", 2)
    sg = [cf(TT) for _ in range(2)]; bsg = P.bufs("sg", 2)
    zt = [cf(TT) for _ in range(2)]; bzt = P.bufs("zt", 2)
    cat = C.act[:, 0:8, :]
    bcat = P.buf("cat")
    C.post_wb = bot + bsg + bzt
    C.post_act = [bcat]
    P.dma("sp", C.vec[:], d["vecs"], [], [C.bvec], C.bvec)
    x_v = d["x1T"].rearrange("(k p) t -> p k t", p=128)
    x_o = d["x2T"].rearrange("(k p) t -> p k t", p=128)

    def load(i):
        s = i % 2
        P.dma("sp", C.xt[s][:], x_v[:, :, i * TT:(i + 1) * TT], [], [C.bx[s]], C.bx[s])

    load(0)
    n2 = 0
    for i in range(NT):
        s = i % 2
        tsl = slice(i * TT, (i + 1) * TT)
        if i + 1 < NT:
            load(i + 1)
        xt, bx = C.xt[s], C.bx[s]
        for (src, sg0, gcol, cat0) in (("roT", 0, V_RO, 0), ("goT", 256, V_GO, 6)):
            for cc in range(2):
                q = n2 % 2
                n2 += 1
                P.dma("sp", ot[q], d[src][cc * 128:(cc + 1) * 128, tsl], [], [bot[q]], bot[q])
                P.dma("sp", sg[q], d["sgT"][sg0 + cc * 128:sg0 + (cc + 1) * 128, tsl], [], [bsg[q]], bsg[q])
                norm_from_psum(P, C, [ot[q]], [bot[q]], 128, 128, C.ones2[:], 64, [zt[q]], bzt[q], [gcol + cc], 7)
                P.I("pool", "tensor_tensor", [bzt[q], bsg[q]], [bcat], out=cat[:, cat0 + cc, :], in0=zt[q], in1=sg[q], op=ALU.mult)
        for cc in range(4):
            q = n2 % 2
            n2 += 1
            P.dma("sp", ot[q], d["moT"][cc * 128:(cc + 1) * 128, tsl], [], [bot[q]], bot[q])
            P.I("act", "copy", [bot[q]], [bcat], out=cat[:, 2 + cc, :], in_=ot[q])
        for m in range(8):
            r = m % 6
            for k in range(8):
                _mm(P, C.ps[r][:], w_out[:, k, m * 128:(m + 1) * 128], cat[:, k, :], k == 0, k == 7, [bW, bcat], [C.bps[r]])
            P.I("dve", "tensor_tensor", [C.bps[r], bx], [bx], out=xt[:, m, :], in0=C.ps[r][:], in1=xt[:, m, :], op=ALU.add)
        P.dma("sp", x_o[:, :, tsl], xt[:], [bx], [C.bxd[i]], bx, is_out=True)


def build_k3():
    nc = bass.Bass("TRN2", target_bir_lowering=False)

    def din(name, shape, dt=F32):
        return nc.dram_tensor(name, shape, dt, kind="ExternalInput").ap()

    d = dict(x1T=din("x1T", [D, TOK]), roT=din("roT", [256, TOK]), goT=din("goT", [256, TOK]), moT=din("moT", [512, TOK]),
             sgT=din("sgT", [512, TOK]), w_out=din("w_out", [D, D]), vecs=din("vecs", [128, NVEC]))
    wgu = din("wgu", [D, 2 * DFF])
    wd = din("wd", [DFF, D])
    g2 = din("g2", [128, 8])
    d["x2T"] = nc.dram_tensor("x2T", [D, TOK], F32, kind="ExternalOutput").ap()
    x3T = nc.dram_tensor("x3T", [D, TOK], F32, kind="ExternalOutput").ap()
    P = Prog(nc)
    C = tok_ctx(P, nc)
    post_phase(P, nc, C, d)
    for dst, srcs in ((C.bWB, C.post_wb), (C.bact, C.post_act)):
        for o in srcs:
            for k, v in o.writers.items():
                if k not in dst.writers or dst.writers[k].idx < v.idx:
                    dst.writers[k] = v
            for k, v in o.readers.items():
                if k not in dst.readers or dst.readers[k].idx < v.idx:
                    dst.readers[k] = v
    ffn_phase(P, nc, C, d["x2T"], x3T, wgu, wd, g2, src_bufs=C.bxd)
    P.emit()
    return nc


_BF = ml_dtypes.bfloat16
_CACHE = {}


def _get(name, fn):
    if name not in _CACHE:
        _CACHE[name] = fn()
    return _CACHE[name]


def _swap(g):
    return np.concatenate([g[32:], g[:32]])


def _consts():
    p = np.arange(128)
    inv = (10000.0 ** (-(np.arange(0, 64, 2, dtype=np.float32)) / 64.0)).astype(np.float32)
    inv_signed = np.where((p % 64) < 32, -1.0, 1.0).astype(np.float32) * inv[p % 32]
    t = (np.arange(TT) % 128 + 1).astype(np.float64)
    xitab = np.zeros((128, 4, TT), np.float32)
    for k in range(4):
        for half in range(2):
            h = (k % 2) * 2 + half
            lg = np.log1p(-2.0 ** (-5.0 - h))
            row = np.exp(lg * t) if k < 2 else np.exp(-lg * t) * 0.125
            xitab[half * 64:(half + 1) * 64, k, :] = row[None, :]
    s_, t_ = np.meshgrid(np.arange(128), np.arange(128), indexing="ij")
    tri = np.where(s_ <= t_, -1.0 / 16.0, 0.0).astype(np.float32)
    mask = np.where(s_ <= t_, 1.0, 0.0).astype(_BF)
    rdec = np.zeros((4, 64, NCH), np.float32)
    for h in range(4):
        rdec[h] = np.exp(np.log1p(-2.0 ** (-5.0 - h)) * 128.0)
    return dict(inv_signed=inv_signed, xitab=xitab.reshape(128, 4 * TT), tri=tri, mask=mask, rdec=rdec)


def _vecs(inp, l, cst):
    v = np.zeros((128, NVEC), np.float32)
    v[:, V_MIX:V_MIX + 8] = inp["mix_norm"][l].reshape(8, 128).T
    v[:, V_CQ:V_CQ + 3] = inp["mla_q_norm"][l].reshape(3, 128).T
    v[:, V_CKV:V_CKV + 2] = inp["mla_kv_norm"][l].reshape(2, 128).T
    v[:, V_QN] = inp["mla_q_nope_norm"][l]
    v[:, V_KN] = inp["mla_k_nope_norm"][l]
    gq = inp["mla_q_rope_norm"][l]
    gk = inp["mla_k_rope_norm"][l]
    v[:, V_QR] = np.tile(gq, 2)
    v[:, V_QRS] = np.tile(_swap(gq), 2)
    v[:64, V_KR] = gk
    v[:64, V_KRS] = _swap(gk)
    v[:, V_INV] = cst["inv_signed"]
    v[:, V_RO:V_RO + 2] = inp["ret_out_norm"][l].reshape(2, 128).T
    v[:, V_GO:V_GO + 2] = inp["gla_out_norm"][l].reshape(2, 128).T
    return v


def _run(nc, in_maps):
    res = run_bass_kernel_spmd(nc, in_maps, core_ids=list(range(NCORE)))
    return res.results


def _cat_tok(res, name, b):
    return np.concatenate([res[b * 4 + q][name] for q in range(4)], axis=1)


def _cat_rows(res, name, b):
    return np.concatenate([res[b * 4 + q][name] for q in range(4)], axis=0)


def kernel(**inp):
    inp = {k: np.asarray(v) for k, v in inp.items()}
    cst = _get("cst", _consts)
    k1 = _get("k1", build_k1)
    k2a = _get("k2a", lambda: build_k2(phases=(1,)))
    k2b = _get("k2b", lambda: build_k2(phases=(2,)))
    k3 = _get("k3", build_k3)
    x = inp["x"]
    posf = inp["positions"].astype(np.float32)
    xT = [np.ascontiguousarray(x[c // 4, (c % 4) * TOK:(c % 4 + 1) * TOK, :].T) for c in range(NCORE)]
    for l in range(DEPTH):
        vecs = _vecs(inp, l, cst)
        gw = np.concatenate([inp["gla_w_gate_up"][l], inp["gla_gate_bias"][l][None, :]], axis=0)
        ims = []
        for c in range(NCORE):
            b, q = c // 4, c % 4
            ims.append(dict(xT=xT[c], wgu=inp["ffn1_w_gate_up"][l], wd=inp["ffn1_w_down"][l],
                            g1=np.ascontiguousarray(inp["ffn1_norm"][l].reshape(8, 128).T),
                            w_in=inp["w_in"][l], w_uq=inp["mla_w_uq"][l], w_ukv=inp["mla_w_ukv"][l], vecs=vecs,
                            xitab=cst["xitab"], gw=gw, tri=cst["tri"],
                            posf=np.ascontiguousarray(np.broadcast_to(posf[b, q * TOK:(q + 1) * TOK][None, :], (128, TOK)))))
        r1 = _run(k1, ims)
        ima, imb = [], []
        for b in range(B):
            rq, rk = _cat_tok(r1, "rqT", b), _cat_tok(r1, "rkT", b)
            gq, gk = _cat_tok(r1, "gqT", b), _cat_tok(r1, "gkT", b)
            vt = _cat_rows(r1, "vtok", b)
            qn, qr = _cat_tok(r1, "qnT", b), _cat_tok(r1, "qrT", b)
            kn, kr = _cat_tok(r1, "knT", b), _cat_tok(r1, "krT", b)
            vm = _cat_rows(r1, "vmtok", b)
            gdec = _cat_tok(r1, "gdec", b)
            for h in range(4):
                hs = slice(h * 64, (h + 1) * 64)
                ima.append(dict(rq=np.ascontiguousarray(rq[hs]), rk=np.ascontiguousarray(rk[hs]),
                                rkt=np.ascontiguousarray(rk[hs].T), rv=np.ascontiguousarray(vt[:, h * 64:(h + 1) * 64]),
                                rdec=cst["rdec"][h],
                                gq=np.ascontiguousarray(gq[hs]), gk=np.ascontiguousarray(gk[hs]),
                                gkt=np.ascontiguousarray(gk[hs].T),
                                gv=np.ascontiguousarray(vt[:, 256 + h * 64:256 + (h + 1) * 64]),
                                gdec=np.ascontiguousarray(gdec[hs]), mask=cst["mask"]))
                imb.append(dict(qn=np.ascontiguousarray(qn[h * 128:(h + 1) * 128]), qr=np.ascontiguousarray(qr[hs]),
                                kn=np.ascontiguousarray(kn[h * 128:(h + 1) * 128]), kr=kr,
                                vm=np.ascontiguousarray(vm[:, h * 128:(h + 1) * 128]), mask=cst["mask"]))
        z = dict(qn=np.zeros((128, S), _BF), qr=np.zeros((64, S), _BF), kn=np.zeros((128, S), _BF), kr=np.zeros((64, S), _BF),
                 vm=np.zeros((S, 128), _BF))
        za = {k: np.zeros(v.shape, v.dtype) for k, v in ima[0].items() if k != "mask"}
        r2a = _run(k2a, ima)
        r2b = _run(k2b, imb)
        im3 = []
        for c in range(NCORE):
            b, q = c // 4, c % 4
            ts = slice(q * TOK, (q + 1) * TOK)
            roT = np.concatenate([r2a[b * 4 + h]["ro"][ts].T for h in range(4)], axis=0)
            goT = np.concatenate([r2a[b * 4 + h]["go"][ts].T for h in range(4)], axis=0)
            moT = np.concatenate([r2b[b * 4 + h]["mo"][ts].T for h in range(4)], axis=0)
            im3.append(dict(x1T=r1[c]["x1T"], roT=np.ascontiguousarray(roT), goT=np.ascontiguousarray(goT),
                            moT=np.ascontiguousarray(moT), sgT=r1[c]["sgT"], w_out=inp["w_out"][l], vecs=vecs,
                            wgu=inp["ffn2_w_gate_up"][l], wd=inp["ffn2_w_down"][l],
                            g2=np.ascontiguousarray(inp["ffn2_norm"][l].reshape(8, 128).T)))
        r3 = _run(k3, im3)
        xT = [r3[c]["x3T"] for c in range(NCORE)]
        if _CACHE.get("debug") is not None:
            _CACHE["debug"].append(dict(r1=r1, r2a=r2a, r2b=r2b, r3=r3))
    out = np.empty((B, S, D), np.float32)
    for c in range(NCORE):
        out[c // 4, (c % 4) * TOK:(c % 4 + 1) * TOK, :] = xT[c].T
    return out
```

```python
import numpy as np
import ml_dtypes
import concourse.bass as bass
import concourse.mybir as mybir
from concourse.bass_utils import run_bass_kernel_spmd

F32 = mybir.dt.float32
BF16 = mybir.dt.bfloat16
I32 = mybir.dt.int32
AF = mybir.ActivationFunctionType
ALU = mybir.AluOpType

D = 1024
B = 2
S = 16384
DEPTH = 2
DFF = 2816
NCORE = 8
TOK = B * S // NCORE
TT = 512
NT = TOK // TT
EPS = 1e-6
IN_COLS = 2768
C_RQ, C_RK, C_RV, C_RG = 0, 256, 512, 768
C_CQ, C_CKV, C_KR = 1024, 1408, 1664
C_GQ, C_GK, C_GV, C_GA, C_GR = 1728, 1984, 2240, 2496, 2512


class Buf:
    __slots__ = ("name", "writers", "readers", "sem", "dcount", "excl")

    def __init__(self, name, excl=False):
        self.name = name
        self.excl = excl
        self.writers = {}
        self.readers = {}
        self.sem = None
        self.dcount = 0


class Op:
    __slots__ = ("eng", "fn", "deps", "needs_inc", "sem", "count", "is_dma", "idx")

    def __init__(self, eng, fn, is_dma):
        self.eng = eng
        self.fn = fn
        self.deps = []
        self.needs_inc = False
        self.sem = None
        self.count = 0
        self.is_dma = is_dma


ENGS = ("pe", "act", "dve", "pool", "sp")
ROT = 30000


class Prog:
    def __init__(self, nc):
        self.nc = nc
        self.ops = {e: [] for e in ENGS}
        self.nops = 0
        self.dma_sems = []
        self.out_dmas = []

    def buf(self, name):
        return Buf(name)

    def bufs(self, name, n, excl=False):
        return [Buf(f"{name}{i}", excl) for i in range(n)]

    def _dep(self, op, prod, kind):
        if prod is None or prod is op:
            return
        if not prod.is_dma and prod.eng == op.eng and not op.is_dma:
            if op.eng == "pe" or kind != "raw":
                return
        prod.needs_inc = True
        op.deps.append(prod)

    def add(self, eng, fn, reads=(), writes=(), dma_buf=None, is_out=False):
        is_dma = dma_buf is not None
        op = Op(eng, fn, is_dma)
        op.idx = self.nops
        self.nops += 1
        for b in reads:
            for w in b.writers.values():
                self._dep(op, w, "raw")
            if b.excl:
                for r in b.readers.values():
                    self._dep(op, r, "war")
        for b in writes:
            for w in b.writers.values():
                self._dep(op, w, "waw")
            for r in b.readers.values():
                self._dep(op, r, "war")
        if is_dma:
            if dma_buf.sem is None:
                dma_buf.sem = self.nc.semaphore(f"d{len(self.dma_sems)}_{dma_buf.name}").__enter__()
                self.dma_sems.append(dma_buf.sem)
            dma_buf.dcount += 16
            op.sem = dma_buf.sem
            op.count = dma_buf.dcount
            op.needs_inc = True
            key = ("dma", id(dma_buf))
            if is_out:
                self.out_dmas.append(op)
        else:
            key = eng
        for b in reads:
            b.readers[key] = op
        for b in writes:
            b.writers = {key: op}
            b.readers = {}
        self.ops[eng].append(op)
        return op

    def I(self, eng, meth, reads, writes, *args, **kw):
        return self.add(eng, lambda e: getattr(e, meth)(*args, **kw), reads=reads, writes=writes)

    def dma(self, eng, out, in_, reads, writes, dma_buf, partial=False, is_out=False):
        fn = lambda e: e.dma_start(out=out, in_=in_)
        if partial:
            return self.add_partial_write(eng, fn, reads, writes, dma_buf)
        return self.add(eng, fn, reads, writes, dma_buf, is_out)

    def add_partial_write(self, eng, fn, reads=(), writes=(), dma_buf=None):
        saved = [(b, dict(b.writers), dict(b.readers)) for b in writes]
        op = self.add(eng, fn, reads, writes, dma_buf)
        for b, w, r in saved:
            key = ("dma", id(dma_buf)) if dma_buf is not None else eng
            w = dict(w)
            w[key] = op
            b.writers = w
            b.readers = r
        return op

    def emit(self):
        nc = self.nc
        eng_sems = {}
        for e in ENGS:
            cnt = 0
            sems = []
            for op in self.ops[e]:
                if op.is_dma or not op.needs_inc:
                    continue
                k = cnt // ROT
                if k >= len(sems):
                    sems.append(nc.semaphore(f"c_{e}{k}").__enter__())
                op.sem = sems[k]
                op.count = cnt % ROT + 1
                cnt += 1
            eng_sems[e] = sems
        final_waits = [(op.sem, op.count) for op in self.out_dmas]
        fw = {}
        for s, c in final_waits:
            fw[id(s)] = (s, max(c, fw.get(id(s), (s, 0))[1]))

        def run(engname, eng):
            waited = {}
            for op in self.ops[engname]:
                need = {}
                for p in op.deps:
                    k = id(p.sem)
                    if p.count > need.get(k, (None, 0))[1]:
                        need[k] = (p.sem, p.count)
                for k, (s, c) in need.items():
                    if waited.get(k, 0) >= c:
                        continue
                    eng.wait_ge(s, c)
                    waited[k] = c
                ins = op.fn(eng)
                if op.needs_inc:
                    ins.then_inc(op.sem, 16 if op.is_dma else 1)
            if engname == "sp":
                for s, c in fw.values():
                    eng.wait_ge(s, c)

        with nc.Block() as block:
            @block.tensor
            def _(t):
                run("pe", t)

            @block.scalar
            def _(t):
                run("act", t)

            @block.vector
            def _(t):
                run("dve", t)

            @block.gpsimd
            def _(t):
                run("pool", t)

            @block.sync
            def _(t):
                run("sp", t)


def _mm(P, out_ap, lhsT, rhs, start, stop, reads, writes):
    return P.I("pe", "matmul", reads, writes, out_ap, lhsT, rhs, start=start, stop=stop)


class TokCtx:
    pass


def alloc(nc, name, shape, dt):
    return nc.sbuf_tensor("s_" + name, shape, dt).__enter__()


def ffn_phase(P, nc, C, x_src, x_dst, wgu_d, wd_d, gain_d, src_bufs=None, dst_bufs=None):
    WA, WB = C.WA, C.WB
    wgu = WA[:, 0:8 * 5632].rearrange("p (k c) -> p k c", k=8)
    wd = WB[:, 0:22 * 1024].rearrange("p (k c) -> p k c", k=22)
    for k in range(8):
        P.dma("pool", wgu[:, k, :], wgu_d[k * 128:(k + 1) * 128, :], [], [C.bWA], C.bWA, partial=(k > 0))
    wd_v = wd_d.rearrange("(k p) n -> p k n", p=128)
    for k0 in range(0, 22, 11):
        P.dma("pool", wd[:, k0:k0 + 11, :], wd_v[:, k0:k0 + 11, :], [], [C.bWB], C.bWB, partial=(k0 > 0))
    P.dma("sp", C.gain[:, 0:8], gain_d, [], [C.bgain], C.bgain)

    x_src_v = x_src.rearrange("(k p) t -> p k t", p=128)
    x_dst_v = x_dst.rearrange("(k p) t -> p k t", p=128)

    def load(i):
        s = i % 2
        P.dma("sp", C.xt[s][:], x_src_v[:, :, i * TT:(i + 1) * TT], [src_bufs[i]] if src_bufs else [], [C.bx[s]], C.bx[s])

    load(0)
    for i in range(NT):
        s = i % 2
        if i + 1 < NT:
            load(i + 1)
        xt = C.xt[s]
        bx = C.bx[s]
        yb = C.bps[6]
        ps = C.ps[6]
        for k in range(8):
            q = k % 2
            P.I("act", "activation", [bx], [C.bsq[q]], out=C.sq[q][:], in_=xt[:, k, :], func=AF.Square)
            _mm(P, ps[:], C.ones[:], C.sq[q][:], k == 0, k == 7, [C.bsq[q], C.bones], [yb])
        P.I("act", "activation", [yb, C.bconst], [C.brstd], out=C.rstd[:], in_=ps[:], func=AF.Ln,
            bias=C.epsc[:, 0:1], scale=1.0 / D)
        P.I("act", "activation", [C.brstd], [C.brstd], out=C.rstd[:], in_=C.rstd[:], func=AF.Exp, scale=-0.5)
        for k in range(8):
            P.I("dve", "scalar_tensor_tensor", [bx, C.brstd, C.bgain], [C.bhT], out=C.hT[:, k, :], in0=xt[:, k, :],
                scalar=C.gain[:, k:k + 1], in1=C.rstd[:], op0=ALU.mult, op1=ALU.mult)
        for j in range(22):
            r = j % 3
            gb, ub = C.bps[r], C.bps[3 + r]
            gp, up = C.ps[r], C.ps[3 + r]
            for k in range(8):
                _mm(P, gp[:], wgu[:, k, j * 128:(j + 1) * 128], C.hT[:, k, :], k == 0, k == 7, [C.bWA, C.bhT], [gb])
            for k in range(8):
                _mm(P, up[:], wgu[:, k, DFF + j * 128:DFF + (j + 1) * 128], C.hT[:, k, :], k == 0, k == 7,
                    [C.bWA, C.bhT], [ub])
            q = j % 2
            P.I("act", "activation", [gb], [C.bstmp[q]], out=C.stmp[q][:], in_=gp[:], func=AF.Silu)
            P.I("dve", "tensor_tensor", [ub, C.bstmp[q]], [C.bact], out=C.act[:, j, :], in0=up[:], in1=C.stmp[q][:],
                op=ALU.mult)
        for m in range(8):
            r = 6 + (m % 2)
            yb, yp = C.bps[r], C.ps[r]
            for k in range(22):
                _mm(P, yp[:], wd[:, k, m * 128:(m + 1) * 128], C.act[:, k, :], k == 0, k == 21, [C.bWB, C.bact], [yb])
            P.I("dve", "scalar_tensor_tensor", [yb, bx], [bx], out=xt[:, m, :], in0=yp[:], scalar=0.5, in1=xt[:, m, :],
                op0=ALU.mult, op1=ALU.add)
        P.dma("sp", x_dst_v[:, :, i * TT:(i + 1) * TT], xt[:], [bx], [dst_bufs[i]] if dst_bufs else [], bx, is_out=True)


def tok_ctx(P, nc):
    C = TokCtx()
    C.WA = alloc(nc, "WA", [128, 8 * 5632], BF16)
    C.WB = alloc(nc, "WB", [128, 22 * 1024], BF16)
    C.bWA, C.bWB = P.buf("WA"), P.buf("WB")
    C.xt = [alloc(nc, f"xt{i}", [128, 8, TT], F32) for i in range(2)]
    C.bx = P.bufs("x", 2)
    C.hT = alloc(nc, "hT", [128, 8, TT], BF16)
    C.bhT = P.buf("hT")
    C.act = alloc(nc, "act", [128, 22, TT], BF16)
    C.bact = P.buf("act")
    C.sq = [alloc(nc, f"sq{i}", [128, TT], BF16) for i in range(2)]
    C.bsq = P.bufs("sq", 2)
    C.rstd = alloc(nc, "rstd", [128, TT], F32)
    C.brstd = P.buf("rstd")
    C.stmp = [alloc(nc, f"stmp{i}", [128, TT], BF16) for i in range(2)]
    C.bstmp = P.bufs("stmp", 2)
    C.gain = alloc(nc, "gain", [128, 32], F32)
    C.bgain = P.buf("gain")
    C.ones = alloc(nc, "ones", [128, 128], BF16)
    C.bones = P.buf("ones")
    C.epsc = alloc(nc, "epsc", [128, 4], F32)
    C.vec = alloc(nc, "vec", [128, NVEC], F32)
    C.bvec = P.buf("vec")
    C.ones2 = alloc(nc, "ones2", [128, 128], BF16)
    C.bones2 = C.bones
    C.bxd = P.bufs("xd", NT)
    C.bconst = P.buf("const")
    C.ps = [nc.psum_tensor(f"ps{i}", [128, 512], F32).__enter__() for i in range(8)]
    C.bps = P.bufs("ps", 8, excl=True)
    P.I("dve", "memset", [], [C.bones], C.ones[:], 1.0)
    P.I("dve", "memset", [], [C.bconst], C.epsc[:], EPS)
    P.I("dve", "memset", [C.bconst], [C.bconst], C.epsc[:, 1:2], 1.0)
    P.I("dve", "memset", [C.bconst], [C.bconst], C.epsc[:, 2:3], float(np.log(0.125)))
    P.I("pool", "memset", [], [C.bones], C.ones2[:], 0.0)
    P.I("pool", "memset", [C.bones], [C.bones], C.ones2[0:64, 0:64], 1.0)
    P.I("pool", "memset", [C.bones], [C.bones], C.ones2[64:128, 64:128], 1.0)
    return C


def build_k1_test():
    nc = bass.Bass("TRN2", target_bir_lowering=False)
    xT = nc.dram_tensor("xT", [D, TOK], F32, kind="ExternalInput").ap()
    wgu = nc.dram_tensor("wgu", [D, 2 * DFF], F32, kind="ExternalInput").ap()
    wd = nc.dram_tensor("wd", [DFF, D], F32, kind="ExternalInput").ap()
    g1 = nc.dram_tensor("g1", [128, 8], F32, kind="ExternalInput").ap()
    x1T = nc.dram_tensor("x1T", [D, TOK], F32, kind="ExternalOutput").ap()
    P = Prog(nc)
    C = tok_ctx(P, nc)
    ffn_phase(P, nc, C, xT, x1T, wgu, wd, g1)
    P.emit()
    return nc


NQG = S // 512
NCH = S // 128
SCALE_MLA = 192.0 ** -0.5


def build_k2(phases=(1, 2)):
    nc = bass.Bass("TRN2", target_bir_lowering=False)

    def din(name, shape, dt):
        return nc.dram_tensor(name, shape, dt, kind="ExternalInput").ap()

    if 2 in phases:
        qn_d = din("qn", [128, S], BF16)
        qr_d = din("qr", [64, S], BF16)
        kn_d = din("kn", [128, S], BF16)
        kr_d = din("kr", [64, S], BF16)
        vm_d = din("vm", [S, 128], BF16)
        mo_d = nc.dram_tensor("moT", [128, S], F32, kind="ExternalOutput").ap()
    if 1 in phases:
        lqk_d = din("lqk", [64, 4, S], BF16)
        lkv_d = din("lkv", [S, 256], BF16)
        ldec_d = din("ldec", [64, 2, NCH], F32)
        lo_d = nc.dram_tensor("lo", [S, 128], F32, kind="ExternalOutput").ap()
    mask_d = din("mask", [128, 128], BF16)

    P = Prog(nc)
    ps = [nc.psum_tensor(f"ps{i}", [128, 512], F32).__enter__() for i in range(8)]
    bps = P.bufs("ps", 8, excl=True)
    mask = alloc(nc, "mask", [128, 128], BF16)
    bmask = P.buf("mask")
    P.dma("sp", mask[:], mask_d, [], [bmask], bmask)

    L = {}
    NB = 3
    if 1 in phases:
        qk = [alloc(nc, f"lqk{i}", [64, 4, 512], BF16) for i in range(NB)]
        kv = [alloc(nc, f"lkv{i}", [128, 4, 256], BF16) for i in range(NB)]
        bin_ = P.bufs("lin", NB)
        dec = alloc(nc, "ldec", [64, 2, NCH], F32)
        bdec = P.buf("ldec")
        osb = [alloc(nc, f"losb{i}", [128, 4, 128], F32) for i in range(2)]
        bosb = P.bufs("losb", 2)
        P.dma("sp", dec[:], ldec_d, [], [bdec], bdec)
    for xi, X in enumerate(("r", "g") if 1 in phases else ()):
        o = TokCtx()
        o.qi, o.ki, o.kti, o.vi, o.oi = 2 * xi, 2 * xi + 1, 128 * xi, 128 * xi + 64, 64 * xi
        o.scm = [alloc(nc, f"{X}scm{i}", [128, 128], BF16) for i in range(2)]
        o.bscm = P.bufs(X + "scm", 2)
        o.st = alloc(nc, X + "st", [64, 64], F32)
        o.tmp = alloc(nc, X + "tmp", [64, 64], F32)
        o.stb = alloc(nc, X + "stb", [64, 64], BF16)
        o.bst, o.btmp, o.bstb = P.buf(X + "st"), P.buf(X + "tmp"), P.buf(X + "stb")
        o.xi = xi
        L[X] = o
        P.I("dve", "memset", [], [o.bst], o.st[:], 0.0)
        P.I("dve", "memset", [], [o.bstb], o.stb[:], 0.0)

    def lin_load(g):
        s = g % NB
        t0 = g * 512
        P.dma("sp", qk[s][:], lqk_d[:, :, t0:t0 + 512], [], [bin_[s]], bin_[s])
        P.dma("act", kv[s][:], lkv_d[t0:t0 + 512, :].rearrange("(n p) d -> p n d", p=128), [], [bin_[s]], bin_[s],
              partial=True)

    def lin_A(n):
        g, c = n // 4, n % 4
        s = g % NB
        cs = slice(c * 128, (c + 1) * 128)
        for xi, X in enumerate(("r", "g")):
            o = L[X]
            sb = xi * 2 + (n % 2)
            m2 = n % 2
            _mm(P, ps[sb][:, 0:128], qk[s][:, o.ki, cs], qk[s][:, o.qi, cs], True, True, [bin_[s]], [bps[sb]])
            P.I("dve", "tensor_tensor", [bps[sb], bmask], [o.bscm[m2]], out=o.scm[m2][:], in0=ps[sb][:, 0:128],
                in1=mask[:], op=ALU.mult)

    def lin_C(n):
        g, c = n // 4, n % 4
        s = g % NB
        so = g % 2
        cs = slice(c * 128, (c + 1) * 128)
        for xi, X in enumerate(("r", "g")):
            o = L[X]
            ob = 4 + xi * 2 + (n % 2)
            m2 = n % 2
            vv = kv[s][:, c, o.vi:o.vi + 64]
            _mm(P, ps[ob][:, 0:64], o.scm[m2][:], vv, True, False, [o.bscm[m2], bin_[s]], [bps[ob]])
            _mm(P, ps[ob][:, 0:64], qk[s][:, o.qi, cs], o.stb[:], False, True, [bin_[s], o.bstb], [bps[ob]])
            _mm(P, ps[ob][0:64, 64:128], kv[s][:, c, o.kti:o.kti + 64], vv, True, True, [bin_[s]], [bps[ob]])
            P.I("dve", "scalar_tensor_tensor", [bps[ob], bdec, o.btmp], [o.bstb], out=o.stb[:], in0=ps[ob][0:64, 64:128],
                scalar=dec[:, xi, n:n + 1], in1=o.tmp[:], op0=ALU.mult, op1=ALU.add)
            P.I("dve", "scalar_tensor_tensor", [bps[ob], bdec, o.btmp], [o.bst], out=o.st[:], in0=ps[ob][0:64, 64:128],
                scalar=dec[:, xi, n:n + 1], in1=o.tmp[:], op0=ALU.mult, op1=ALU.add)
            P.I("act", "copy", [bps[ob]], [bosb[so]], out=osb[so][:, c, o.oi:o.oi + 64], in_=ps[ob][:, 0:64])
            if n + 1 < NCH:
                P.I("dve", "tensor_scalar", [o.bst, bdec], [o.btmp], out=o.tmp[:], in0=o.st[:],
                    scalar1=dec[:, xi, n + 1:n + 2], scalar2=None, op0=ALU.mult)
        if c == 3:
            P.dma("sp", lo_d[g * 512:(g + 1) * 512, :].rearrange("(n p) d -> p n d", p=128), osb[so][:],
                  [bosb[so]], [], bosb[so], is_out=True)

    if 1 in phases:
        for X in ("r", "g"):
            P.I("dve", "memset", [], [L[X].btmp], L[X].tmp[:], 0.0)
        for g0 in range(NB):
            lin_load(g0)
        lin_A(0)
        for n in range(NCH):
            if n + 1 < NCH:
                lin_A(n + 1)
            lin_C(n)
            if n % 4 == 3 and n // 4 + NB < NQG:
                lin_load(n // 4 + NB)

    if 2 not in phases:
        P.emit()
        return nc
    kn = alloc(nc, "kn", [128, S], BF16)
    kr = alloc(nc, "kr", [64, S], BF16)
    V = alloc(nc, "V", [128, NCH, 128], BF16)
    bkv = P.bufs("kv", NQG)
    kvsem = P.bufs("kvsem", 4)
    qn = [alloc(nc, f"qn{i}", [128, 512], BF16) for i in range(2)]
    qr = [alloc(nc, f"qr{i}", [64, 512], BF16) for i in range(2)]
    bq = P.bufs("q", 2)
    NSC = 4
    pt = [alloc(nc, f"pt{i}", [128, 512], BF16) for i in range(NSC)]
    bpt = P.bufs("pt", NSC)
    psum_t = [alloc(nc, f"ptsum{i}", [128, 512], F32) for i in range(2)]
    bpsum = P.bufs("ptsum", 2)
    rec = [alloc(nc, f"rec{i}", [128, 512], F32) for i in range(2)]
    brec = P.bufs("rec", 2)
    mosb = [alloc(nc, f"mosb{i}", [128, 512], F32) for i in range(2)]
    bmosb = P.bufs("mosb", 2)
    onesf = alloc(nc, "onesf", [128, 128], F32)
    bonesf = P.buf("onesf")
    P.I("pool", "memset", [], [bonesf], onesf[:], 1.0)
    vm_v = vm_d.rearrange("(n p) d -> p n d", p=128)

    def kv_load(g):
        t0 = g * 512
        sb = kvsem[g % 4]
        rd = [bkv[g - 4]] if g >= 4 else []
        P.dma("sp", kn[:, t0:t0 + 512], kn_d[:, t0:t0 + 512], rd, [bkv[g]], sb)
        P.dma("sp", kr[:, t0:t0 + 512], kr_d[:, t0:t0 + 512], [], [bkv[g]], sb, partial=True)
        P.dma("sp", V[:, 4 * g:4 * g + 4, 0:128], vm_v[:, 4 * g:4 * g + 4, :], [], [bkv[g]], sb, partial=True)

    def q_load(g):
        s = g % 2
        t0 = g * 512
        P.dma("sp", qn[s][:], qn_d[:, t0:t0 + 512], [], [bq[s]], bq[s])
        P.dma("sp", qr[s][:], qr_d[:, t0:t0 + 512], [], [bq[s]], bq[s], partial=True)

    LOOK = 3
    SUMB = 6
    blocks = [(g, kb) for g in range(NQG) for kb in range(4 * g + 4)]
    nblk = len(blocks)
    kv_load(0)
    q_load(0)

    def emit_sc(i):
        g, kb = blocks[i]
        s = g % 2
        if kb == 0 and g + 1 < NQG:
            kv_load(g + 1)
            q_load(g + 1)
        j = kb - 4 * g
        c0 = 128 * j if j > 0 else 0
        r = i % NSC
        ks = slice(kb * 128, (kb + 1) * 128)
        kvb = bkv[kb // 4]
        _mm(P, ps[r][:, c0:512], kn[:, ks], qn[s][:, c0:512], True, False, [kvb, bq[s]], [bps[r]])
        _mm(P, ps[r][:, c0:512], kr[:, ks], qr[s][:, c0:512], False, True, [kvb, bq[s]], [bps[r]])
        P.I("act", "activation", [bps[r]], [bpt[r]], out=pt[r][:, c0:512], in_=ps[r][:, c0:512], func=AF.Exp,
            scale=SCALE_MLA)
        if j >= 0:
            P.I("pool", "tensor_tensor", [bpt[r], bmask], [bpt[r]], out=pt[r][:, 128 * j:128 * j + 128],
                in0=pt[r][:, 128 * j:128 * j + 128], in1=mask[:], op=ALU.mult)
        if kb == 0:
            P.I("dve", "tensor_copy", [bpt[r]], [bpsum[s]], out=psum_t[s][:], in_=pt[r][:])
        else:
            P.I("dve", "tensor_tensor", [bpt[r], bpsum[s]], [bpsum[s]], out=psum_t[s][:, c0:512], in0=psum_t[s][:, c0:512],
                in1=pt[r][:, c0:512], op=ALU.add)

    def emit_pv(i):
        g, kb = blocks[i]
        s = g % 2
        j = kb - 4 * g
        c0 = 128 * j if j > 0 else 0
        r = i % NSC
        kvb = bkv[kb // 4]
        ab = 4 + s
        _mm(P, ps[ab][:, c0:512], V[:, kb, 0:128], pt[r][:, c0:512], kb == 0, kb == 4 * g + 3, [bpt[r], kvb], [bps[ab]])
        if kb == 4 * g + 3:
            P.I("pe", "matmul", [bpsum[s], bonesf], [bps[SUMB]], ps[SUMB][:], onesf[:], psum_t[s][:], start=True, stop=True)
            P.I("dve", "reciprocal", [bps[SUMB]], [brec[s]], out=rec[s][:], in_=ps[SUMB][:])
            P.I("dve", "tensor_tensor", [bps[ab], brec[s]], [bmosb[s]], out=mosb[s][:], in0=ps[ab][:], in1=rec[s][:], op=ALU.mult)
            P.dma("sp", mo_d[:, g * 512:(g + 1) * 512], mosb[s][:], [bmosb[s]], [], bmosb[s], is_out=True)

    for i in range(nblk + LOOK):
        if i < nblk:
            emit_sc(i)
        if i >= LOOK:
            emit_pv(i - LOOK)
    P.emit()
    return nc


TWO_PI = 2.0 * np.pi
MAGIC = 12582912.0
CW1 = 6.28125
CW2 = TWO_PI - 6.28125
V_MIX = 0
V_CQ = 8
V_CKV = 11
V_QN = 13
V_KN = 14
V_QR = 15
V_QRS = 16
V_KR = 17
V_KRS = 18
V_INV = 19
V_RO = 20
V_GO = 22
NVEC = 24


def norm_from_psum(P, C, src_aps, src_bufs, K, nparts, ones_ap, n_norm, out_aps, out_buf, gain_cols, stat_bank):
    sb, sp = C.bps[stat_bank], C.ps[stat_bank]
    n = len(src_aps)
    for k in range(n):
        q = k % 2
        P.I("act", "activation", [src_bufs[k]], [C.bsq[q]], out=C.sq[q][0:nparts, :], in_=src_aps[k], func=AF.Square)
        _mm(P, sp[0:nparts, :], ones_ap, C.sq[q][0:nparts, :], k == 0, k == n - 1, [C.bsq[q], C.bones], [sb])
    P.I("act", "activation", [sb, C.bconst], [C.brstd], out=C.rstd[0:nparts, :], in_=sp[0:nparts, :], func=AF.Ln,
        bias=C.epsc[0:nparts, 0:1], scale=1.0 / n_norm)
    P.I("act", "activation", [C.brstd], [C.brstd], out=C.rstd[0:nparts, :], in_=C.rstd[0:nparts, :], func=AF.Exp, scale=-0.5)
    if out_aps is not None:
        for k in range(n):
            P.I("dve", "scalar_tensor_tensor", [src_bufs[k], C.brstd, C.bvec], [out_buf], out=out_aps[k], in0=src_aps[k],
                scalar=C.vec[0:nparts, gain_cols[k]:gain_cols[k] + 1], in1=C.rstd[0:nparts, :], op0=ALU.mult, op1=ALU.mult)


def proj_phase(P, nc, C, d):
    WA = C.WA
    off = [0]

    def carve(n):
        a = off[0]
        off[0] += n
        return WA[:, a:a + n]

    bW = C.bWA
    w_in = carve(8 * IN_COLS).rearrange("p (k c) -> p k c", k=8)
    w_sw = carve(8 * 576).rearrange("p (k c) -> p k c", k=8)
    wuq_n = carve(3 * 512).rearrange("p (k c) -> p k c", k=3)
    wuq_r = carve(3 * 256).rearrange("p (k c) -> p k c", k=3)
    wuq_rs = carve(3 * 256).rearrange("p (k c) -> p k c", k=3)
    wkv_k = carve(2 * 512).rearrange("p (k c) -> p k c", k=2)
    wkv_v = carve(2 * 512).rearrange("p (k c) -> p k c", k=2)
    first = [True]

    def wdma(out, in_):
        P.dma("pool", out, in_, [], [bW], bW, partial=not first[0])
        first[0] = False

    win_d = d["w_in"]
    for k in range(8):
        rows = slice(k * 128, (k + 1) * 128)
        wdma(w_in[:, k, :], win_d[rows, :])
        src = win_d[rows, 0:512].rearrange("p (h t c) -> p h t c", h=8, t=2)
        dst = w_sw[:, k, 0:512].rearrange("p (h t c) -> p h t c", h=8, t=2)
        wdma(dst[:, :, 0, :], src[:, :, 1, :])
        wdma(dst[:, :, 1, :], src[:, :, 0, :])
        wdma(w_sw[:, k, 512:544], win_d[rows, C_KR + 32:C_KR + 64])
        wdma(w_sw[:, k, 544:576], win_d[rows, C_KR:C_KR + 32])
    for k in range(3):
        rows = slice(k * 128, (k + 1) * 128)
        src = d["w_uq"][rows, :].rearrange("p (h c) -> p h c", h=4)
        wdma(wuq_n[:, k, :].rearrange("p (h c) -> p h c", h=4), src[:, :, 0:128])
        wdma(wuq_r[:, k, :].rearrange("p (h c) -> p h c", h=4), src[:, :, 128:192])
        dsts = wuq_rs[:, k, :].rearrange("p (h c) -> p h c", h=4)
        wdma(dsts[:, :, 0:32], src[:, :, 160:192])
        wdma(dsts[:, :, 32:64], src[:, :, 128:160])
    for k in range(2):
        rows = slice(k * 128, (k + 1) * 128)
        src = d["w_ukv"][rows, :].rearrange("p (h c) -> p h c", h=4)
        wdma(wkv_k[:, k, :].rearrange("p (h c) -> p h c", h=4), src[:, :, 0:128])
        wdma(wkv_v[:, k, :].rearrange("p (h c) -> p h c", h=4), src[:, :, 128:256])

    def alias():
        b = Buf("al")
        for o in (C.bWA, C.bWB, C.bact):
            for k, v in o.writers.items():
                if k not in b.writers or b.writers[k].idx < v.idx:
                    b.writers[k] = v
            for k, v in o.readers.items():
                if k not in b.readers or b.readers[k].idx < v.idx:
                    b.readers[k] = v
        return b

    assert off[0] % 2 == 0
    WAf = WA.bitcast(F32)
    WBf = C.WB.bitcast(F32)
    offa = [off[0] // 2]
    offb = [0]

    def cf(n, region="b"):
        o_, t_ = (offb, WBf) if region == "b" else (offa, WAf)
        a = o_[0]
        o_[0] += n
        return t_[:, a:a + n]

    xitab = cf(4 * TT, "a").rearrange("p (k c) -> p k c", k=4); bxi = alias()
    Eq = [cf(TT, "a") for _ in range(2)]; Ek = [cf(TT, "a") for _ in range(2)]; bE = [alias() for _ in range(2)]
    assert offa[0] <= 8 * 5632 // 2, offa[0]
    pos = cf(TT); bpos = alias()
    ang = cf(TT); bang = alias()
    tk = cf(TT); btk = alias()
    r1 = cf(TT); br1 = alias()
    Ssb = cf(TT); bS = alias()
    Csb = cf(TT); bC = alias()
    GCq = cf(TT); GSq = cf(TT); bGq = alias()
    GCk = cf(TT); GSk = cf(TT); bGk = alias()
    t1 = [cf(TT) for _ in range(2)]; bt1 = [alias() for _ in range(2)]
    t2 = [cf(TT) for _ in range(2)]; bt2 = [alias() for _ in range(2)]
    sgs = [cf(TT) for _ in range(2)]; bsgs = [alias() for _ in range(2)]
    alow = cf(TT); balow = alias()
    gw = cf(256); bgw = alias()
    tri = cf(128); btri = alias()
    zl = cf(256); bzl = alias()
    decs = cf(8).rearrange("p (k c) -> p k c", k=2); bdecs = alias()
    assert offb[0] <= 22 * 1024 // 2, offb[0]

    nb = [0]

    def stage_bf(n):
        a = nb[0]
        nb[0] += n
        assert nb[0] <= 22
        return C.act[:, a:a + n, :], alias()

    cqn, bcqn = stage_bf(3)
    ckvn, bckvn = stage_bf(2)
    vst, bvst = stage_bf(4)
    ostage = [stage_bf(1) for _ in range(8)]
    nst = [0]

    def next_stage():
        a = ostage[nst[0] % len(ostage)]
        nst[0] += 1
        return a[0][:, 0, :], a[1]

    P.dma("sp", C.vec[:], d["vecs"], [], [C.bvec], C.bvec)
    P.dma("sp", xitab, d["xitab"].rearrange("p (k c) -> p k c", k=4), [], [bxi], bxi)
    P.dma("sp", gw[0:17, :], d["gw"], [], [bgw], bgw)
    P.dma("sp", tri, d["tri"], [], [btri], btri)
    P.I("pool", "memset", [], [balow], alow[0:32, :], 1.0)

    x_v = d["x1T"].rearrange("(k p) t -> p k t", p=128)

    def load(i):
        s = i % 2
        P.dma("sp", C.xt[s][:], x_v[:, :, i * TT:(i + 1) * TT], [C.bxd[i]], [C.bx[s]], C.bx[s])

    load(0)
    bank = [0]

    def nb_():
        b = bank[0] % 6
        bank[0] += 1
        return b

    def proj_chunk(wt, col0, M=128):
        b = nb_()
        for k in range(8):
            _mm(P, C.ps[b][0:M, :], wt[:, k, col0:col0 + M], C.hT[:, k, :], k == 0, k == 7, [bW, C.bhT], [C.bps[b]])
        return b

    def store(dram_ap, sb_ap, buf):
        P.dma("sp", dram_ap, sb_ap, [buf], [], buf, is_out=True)

    def rope_combine(bx_, bs_, M, cos_ap, sin_ap, rbufs, post_ap, post_bufs, out_ap, out_buf, q):
        P.I("dve", "tensor_tensor", [C.bps[bx_]] + rbufs, [bt1[q]], out=t1[q][0:M, :], in0=C.ps[bx_][0:M, :], in1=cos_ap,
            op=ALU.mult)
        P.I("dve", "tensor_tensor", [C.bps[bs_]] + rbufs, [bt2[q]], out=t2[q][0:M, :], in0=C.ps[bs_][0:M, :], in1=sin_ap,
            op=ALU.mult)
        P.I("pool", "tensor_tensor", [bt1[q], bt2[q]], [bt1[q]], out=t1[q][0:M, :], in0=t1[q][0:M, :], in1=t2[q][0:M, :],
            op=ALU.add)
        P.I("pool", "tensor_tensor", [bt1[q]] + post_bufs, [out_buf], out=out_ap, in0=t1[q][0:M, :], in1=post_ap, op=ALU.mult)

    for i in range(NT):
        s = i % 2
        tsl = slice(i * TT, (i + 1) * TT)
        if i + 1 < NT:
            load(i + 1)
        xt, bx = C.xt[s], C.bx[s]
        norm_from_psum(P, C, [xt[:, k, :] for k in range(8)], [bx] * 8, 128, 128, C.ones[:], D,
                       [C.hT[:, k, :] for k in range(8)], C.bhT, [V_MIX + k for k in range(8)], 6)
        P.dma("sp", pos, d["posf"][:, tsl], [], [bpos], bpos)
        for (dst, bd, shift) in ((Ssb, bS, 0.0), (Csb, bC, 0.5 * np.pi)):
            P.I("pool", "tensor_scalar", [bpos, C.bvec], [bang], out=ang, in0=pos, scalar1=C.vec[:, V_INV:V_INV + 1],
                scalar2=shift, op0=ALU.mult, op1=ALU.add)
            P.I("pool", "tensor_scalar", [bang], [btk], out=tk, in0=ang, scalar1=1.0 / TWO_PI, scalar2=MAGIC,
                op0=ALU.mult, op1=ALU.add)
            P.I("pool", "tensor_scalar", [btk], [btk], out=tk, in0=tk, scalar1=-MAGIC, scalar2=None, op0=ALU.add)
            P.I("dve", "scalar_tensor_tensor", [btk, bang], [br1], out=r1, in0=tk, scalar=-CW1, in1=ang,
                op0=ALU.mult, op1=ALU.add)
            P.I("dve", "scalar_tensor_tensor", [btk, br1], [br1], out=r1, in0=tk, scalar=-CW2, in1=r1,
                op0=ALU.mult, op1=ALU.add)
            P.I("pool", "tensor_scalar", [br1], [br1], out=r1, in0=r1, scalar1=-np.pi, scalar2=np.pi, op0=ALU.max, op1=ALU.min)
            P.I("act", "activation", [br1], [bd], out=dst, in_=r1, func=AF.Sin)
        P.I("pool", "tensor_scalar", [bC, C.bvec], [bGq], out=GCq, in0=Csb, scalar1=C.vec[:, V_QR:V_QR + 1], scalar2=None,
            op0=ALU.mult)
        P.I("pool", "tensor_scalar", [bS, C.bvec], [bGq], out=GSq, in0=Ssb, scalar1=C.vec[:, V_QRS:V_QRS + 1], scalar2=None,
            op0=ALU.mult)
        P.I("pool", "tensor_scalar", [bC, C.bvec], [bGk], out=GCk[0:64, :], in0=Csb[0:64, :], scalar1=C.vec[0:64, V_KR:V_KR + 1],
            scalar2=None, op0=ALU.mult)
        P.I("pool", "tensor_scalar", [bS, C.bvec], [bGk], out=GSk[0:64, :], in0=Ssb[0:64, :],
            scalar1=C.vec[0:64, V_KRS:V_KRS + 1], scalar2=None, op0=ALU.mult)

        for (c0, tab0, dname) in ((C_RQ, 0, "rqT"), (C_RK, 2, "rkT")):
            for cc in range(2):
                bxp = proj_chunk(w_in, c0 + cc * 128)
                bsp = proj_chunk(w_sw, c0 + cc * 128)
                o_ap, o_b = next_stage()
                rope_combine(bxp, bsp, 128, Csb, Ssb, [bC, bS], xitab[:, tab0 + cc, :], [bxi], o_ap, o_b, cc)
                store(d[dname][cc * 128:(cc + 1) * 128, tsl], o_ap, o_b)
        for (c0, r0) in ((C_RG, 0), (C_GR, 256)):
            for cc in range(2):
                bp = proj_chunk(w_in, c0 + cc * 128)
                P.I("act", "activation", [C.bps[bp]], [bsgs[cc]], out=sgs[cc], in_=C.ps[bp][:], func=AF.Silu)
                store(d["sgT"][r0 + cc * 128:r0 + (cc + 1) * 128, tsl], sgs[cc], bsgs[cc])
        for sub in range(4):
            b = nb_()
            for (c0, o0) in ((C_RV, 0), (C_GV, 256)):
                for k in range(8):
                    _mm(P, C.ps[b][:, o0:o0 + 256], C.hT[:, k, sub * 128:(sub + 1) * 128], w_in[:, k, c0:c0 + 256], k == 0,
                        k == 7, [bW, C.bhT], [C.bps[b]])
            P.I("act", "copy", [C.bps[b]], [bvst], out=vst[:, sub, :], in_=C.ps[b][:])
        store(d["vtok"][i * TT:(i + 1) * TT, :].rearrange("(n p) c -> p n c", p=128), vst, bvst)
        bq_ = [proj_chunk(w_in, C_CQ + k * 128) for k in range(3)]
        norm_from_psum(P, C, [C.ps[b][:] for b in bq_], [C.bps[b] for b in bq_], 128, 128, C.ones[:], 384,
                       [cqn[:, k, :] for k in range(3)], bcqn, [V_CQ + k for k in range(3)], 6)
        for h in range(4):
            b = nb_()
            for k in range(3):
                _mm(P, C.ps[b][:], wuq_n[:, k, h * 128:(h + 1) * 128], cqn[:, k, :], k == 0, k == 2, [bW, bcqn], [C.bps[b]])
            o_ap, o_b = next_stage()
            norm_from_psum(P, C, [C.ps[b][:]], [C.bps[b]], 128, 128, C.ones[:], 128, [o_ap], o_b, [V_QN], 7)
            store(d["qnT"][h * 128:(h + 1) * 128, tsl], o_ap, o_b)
        for cc in range(2):
            b1, b2 = nb_(), nb_()
            for (b, wt) in ((b1, wuq_r), (b2, wuq_rs)):
                for k in range(3):
                    _mm(P, C.ps[b][:], wt[:, k, cc * 128:(cc + 1) * 128], cqn[:, k, :], k == 0, k == 2, [bW, bcqn], [C.bps[b]])
            norm_from_psum(P, C, [C.ps[b1][:]], [C.bps[b1]], 128, 128, C.ones2[:], 64, None, None, None, 7)
            o_ap, o_b = next_stage()
            rope_combine(b1, b2, 128, GCq, GSq, [bGq], C.rstd[:], [C.brstd], o_ap, o_b, cc)
            store(d["qrT"][cc * 128:(cc + 1) * 128, tsl], o_ap, o_b)
        bk_ = [proj_chunk(w_in, C_CKV + k * 128) for k in range(2)]
        norm_from_psum(P, C, [C.ps[b][:] for b in bk_], [C.bps[b] for b in bk_], 128, 128, C.ones[:], 256,
                       [ckvn[:, k, :] for k in range(2)], bckvn, [V_CKV + k for k in range(2)], 6)
        for h in range(4):
            b = nb_()
            for k in range(2):
                _mm(P, C.ps[b][:], wkv_k[:, k, h * 128:(h + 1) * 128], ckvn[:, k, :], k == 0, k == 1, [bW, bckvn], [C.bps[b]])
            o_ap, o_b = next_stage()
            norm_from_psum(P, C, [C.ps[b][:]], [C.bps[b]], 128, 128, C.ones[:], 128, [o_ap], o_b, [V_KN], 7)
            store(d["knT"][h * 128:(h + 1) * 128, tsl], o_ap, o_b)
        for sub in range(4):
            b = nb_()
            for k in range(2):
                _mm(P, C.ps[b][:], ckvn[:, k, sub * 128:(sub + 1) * 128], wkv_v[:, k, :], k == 0, k == 1, [bW, bckvn], [C.bps[b]])
            o_ap, o_b = next_stage()
            P.I("act", "copy", [C.bps[b]], [o_b], out=o_ap, in_=C.ps[b][:])
            store(d["vmtok"][i * TT + sub * 128:i * TT + (sub + 1) * 128, :], o_ap, o_b)
        b1 = proj_chunk(w_in, C_KR, M=64)
        b2 = proj_chunk(w_sw, 512, M=64)
        norm_from_psum(P, C, [C.ps[b1][0:64, :]], [C.bps[b1]], 64, 64, C.ones[0:64, 0:64], 64, None, None, None, 7)
        o_ap, o_b = next_stage()
        rope_combine(b1, b2, 64, GCk[0:64, :], GSk[0:64, :], [bGk], C.rstd[0:64, :], [C.brstd], o_ap[0:64, :], o_b, 0)
        store(d["krT"][:, tsl], o_ap[0:64, :], o_b)
        ba = proj_chunk(w_in, C_GA, M=16)
        P.I("act", "copy", [C.bps[ba]], [balow], out=alow[0:16, :], in_=C.ps[ba][0:16, :])
        bc = [nb_(), nb_()]
        for sub in range(4):
            bz = 7
            P.I("pe", "matmul", [balow, bgw], [C.bps[bz]], C.ps[bz][:, 0:256], alow[0:17, sub * 128:(sub + 1) * 128], gw[0:17, :],
                start=True, stop=True)
            P.I("act", "activation", [C.bps[bz]], [bzl], out=zl, in_=C.ps[bz][:, 0:256], func=AF.Exp, scale=-1.0)
            P.I("act", "activation", [bzl, C.bconst], [bzl], out=zl, in_=zl, func=AF.Ln, bias=C.epsc[:, 1:2], scale=1.0)
            for c in range(2):
                P.I("pe", "matmul", [bzl, btri], [C.bps[bc[c]]], C.ps[bc[c]][:, sub * 128:(sub + 1) * 128],
                    zl[:, c * 128:(c + 1) * 128], tri, start=True, stop=True)
        for c in range(2):
            pb = C.ps[bc[c]]
            P.I("act", "activation", [C.bps[bc[c]], C.bconst], [bE[c]], out=Eq[c], in_=pb[:], func=AF.Exp,
                bias=C.epsc[:, 2:3], scale=1.0)
            P.I("act", "activation", [C.bps[bc[c]]], [bE[c]], out=Ek[c], in_=pb[:], func=AF.Exp, scale=-1.0)
            P.I("act", "activation", [C.bps[bc[c]]], [bdecs], out=decs[:, c, :],
                in_=pb[:].rearrange("p (n t) -> p n t", t=128)[:, :, 127], func=AF.Exp)
        store(d["gdec"][:, i * 4:(i + 1) * 4].rearrange("(c p) n -> p c n", p=128), decs, bdecs)
        for (c0, E, dname) in ((C_GQ, Eq, "gqT"), (C_GK, Ek, "gkT")):
            for cc in range(2):
                bp = proj_chunk(w_in, c0 + cc * 128)
                o_ap, o_b = next_stage()
                P.I("dve", "tensor_tensor", [C.bps[bp], bE[cc]], [o_b], out=o_ap, in0=C.ps[bp][:], in1=E[cc], op=ALU.mult)
                store(d[dname][cc * 128:(cc + 1) * 128, tsl], o_ap, o_b)


K1_OUTS = dict(rqT=([256, TOK], BF16), rkT=([256, TOK], BF16), sgT=([512, TOK], F32), vtok=([TOK, 512], BF16),
               qnT=([512, TOK], BF16), qrT=([256, TOK], BF16), knT=([512, TOK], BF16), vmtok=([TOK, 512], BF16),
               krT=([64, TOK], BF16), gdec=([256, TOK // 128], F32), gqT=([256, TOK], BF16), gkT=([256, TOK], BF16),
               x1T=([D, TOK], F32))


def build_k1(with_ffn=True):
    nc = bass.Bass("TRN2", target_bir_lowering=False)

    def din(name, shape, dt=F32):
        return nc.dram_tensor(name, shape, dt, kind="ExternalInput").ap()

    xT = din("xT", [D, TOK])
    wgu = din("wgu", [D, 2 * DFF])
    wd = din("wd", [DFF, D])
    g1 = din("g1", [128, 8])
    d = dict(w_in=din("w_in", [D, IN_COLS]), w_uq=din("w_uq", [384, 768]), w_ukv=din("w_ukv", [256, 1024]),
             vecs=din("vecs", [128, NVEC]), xitab=din("xitab", [128, 4 * TT]), gw=din("gw", [17, 256]),
             tri=din("tri", [128, 128]), posf=din("posf", [128, TOK]))
    for name, (shape, dt) in K1_OUTS.items():
        d[name] = nc.dram_tensor(name, shape, dt, kind="ExternalOutput").ap()
    P = Prog(nc)
    C = tok_ctx(P, nc)
    if with_ffn:
        ffn_phase(P, nc, C, xT, d["x1T"], wgu, wd, g1, dst_bufs=C.bxd)
    else:
        d["x1T"] = xT
    proj_phase(P, nc, C, d)
    P.emit()
    return nc


def post_phase(P, nc, C, d):
    WA = C.WA
    bW = C.bWA
    w_out = WA[:, 0:8 * 1024].rearrange("p (k c) -> p k c", k=8)
    P.dma("pool", w_out, d["w_out"].rearrange("(k p) n -> p k n", p=128), [], [bW], bW)
    WBf = C.WB.bitcast(F32)
    offb = [0]

    def cf(n):
        a = offb[0]
        offb[0] += n
        return WBf[:, a:a + n]

    ot = [cf(TT) for _ in range(2)]; bot = P.bufs("ot", 2)
    sg = [cf(TT) for _ in range(2)]; bsg = P.bufs("sg", 2)
    zt = [cf(TT) for _ in range(2)]; bzt = P.bufs("zt", 2)
    cat = C.act[:, 0:8, :]
    bcat = P.buf("cat")
    C.post_wb = bot + bsg + bzt
    C.post_act = [bcat]
    P.dma("sp", C.vec[:], d["vecs"], [], [C.bvec], C.bvec)
    x_v = d["x1T"].rearrange("(k p) t -> p k t", p=128)
    x_o = d["x2T"].rearrange("(k p) t -> p k t", p=128)

    def load(i):
        s = i % 2
        P.dma("sp", C.xt[s][:], x_v[:, :, i * TT:(i + 1) * TT], [], [C.bx[s]], C.bx[s])

    load(0)
    n2 = 0
    for i in range(NT):
        s = i % 2
        tsl = slice(i * TT, (i + 1) * TT)
        if i + 1 < NT:
            load(i + 1)
        xt, bx = C.xt[s], C.bx[s]
        for (src, sg0, gcol, cat0) in (("roT", 0, V_RO, 0), ("goT", 256, V_GO, 6)):
            for cc in range(2):
                q = n2 % 2
                n2 += 1
                P.dma("sp", ot[q], d[src][cc * 128:(cc + 1) * 128, tsl], [], [bot[q]], bot[q])
                P.dma("sp", sg[q], d["sgT"][sg0 + cc * 128:sg0 + (cc + 1) * 128, tsl], [], [bsg[q]], bsg[q])
                norm_from_psum(P, C, [ot[q]], [bot[q]], 128, 128, C.ones2[:], 64, [zt[q]], bzt[q], [gcol + cc], 7)
                P.I("pool", "tensor_tensor", [bzt[q], bsg[q]], [bcat], out=cat[:, cat0 + cc, :], in0=zt[q], in1=sg[q], op=ALU.mult)
        for cc in range(4):
            q = n2 % 2
            n2 += 1
            P.dma("sp", ot[q], d["moT"][cc * 128:(cc + 1) * 128, tsl], [], [bot[q]], bot[q])
            P.I("act", "copy", [bot[q]], [bcat], out=cat[:, 2 + cc, :], in_=ot[q])
        for m in range(8):
            r = m % 6
            for k in range(8):
                _mm(P, C.ps[r][:], w_out[:, k, m * 128:(m + 1) * 128], cat[:, k, :], k == 0, k == 7, [bW, bcat], [C.bps[r]])
            P.I("dve", "tensor_tensor", [C.bps[r], bx], [bx], out=xt[:, m, :], in0=C.ps[r][:], in1=xt[:, m, :], op=ALU.add)
        P.dma("sp", x_o[:, :, tsl], xt[:], [bx], [C.bxd[i]], bx, is_out=True)


def build_k3():
    nc = bass.Bass("TRN2", target_bir_lowering=False)

    def din(name, shape, dt=F32):
        return nc.dram_tensor(name, shape, dt, kind="ExternalInput").ap()

    d = dict(x1T=din("x1T", [D, TOK]), roT=din("roT", [256, TOK]), goT=din("goT", [256, TOK]), moT=din("moT", [512, TOK]),
             sgT=din("sgT", [512, TOK]), w_out=din("w_out", [D, D]), vecs=din("vecs", [128, NVEC]))
    wgu = din("wgu", [D, 2 * DFF])
    wd = din("wd", [DFF, D])
    g2 = din("g2", [128, 8])
    d["x2T"] = nc.dram_tensor("x2T", [D, TOK], F32, kind="ExternalOutput").ap()
    x3T = nc.dram_tensor("x3T", [D, TOK], F32, kind="ExternalOutput").ap()
    P = Prog(nc)
    C = tok_ctx(P, nc)
    post_phase(P, nc, C, d)
    for dst, srcs in ((C.bWB, C.post_wb), (C.bact, C.post_act)):
        for o in srcs:
            for k, v in o.writers.items():
                if k not in dst.writers or dst.writers[k].idx < v.idx:
                    dst.writers[k] = v
            for k, v in o.readers.items():
                if k not in dst.readers or dst.readers[k].idx < v.idx:
                    dst.readers[k] = v
    ffn_phase(P, nc, C, d["x2T"], x3T, wgu, wd, g2, src_bufs=C.bxd)
    P.emit()
    return nc


_BF = ml_dtypes.bfloat16
_CACHE = {}


def _get(name, fn):
    if name not in _CACHE:
        _CACHE[name] = fn()
    return _CACHE[name]


def _swap(g):
    return np.concatenate([g[32:], g[:32]])


def _consts():
    p = np.arange(128)
    inv = (10000.0 ** (-(np.arange(0, 64, 2, dtype=np.float32)) / 64.0)).astype(np.float32)
    inv_signed = np.where((p % 64) < 32, -1.0, 1.0).astype(np.float32) * inv[p % 32]
    t = (np.arange(TT) % 128 + 1).astype(np.float64)
    xitab = np.zeros((128, 4, TT), np.float32)
    for k in range(4):
        for half in range(2):
            h = (k % 2) * 2 + half
            lg = np.log1p(-2.0 ** (-5.0 - h))
            row = np.exp(lg * t) if k < 2 else np.exp(-lg * t) * 0.125
            xitab[half * 64:(half + 1) * 64, k, :] = row[None, :]
    s_, t_ = np.meshgrid(np.arange(128), np.arange(128), indexing="ij")
    tri = np.where(s_ <= t_, -1.0 / 16.0, 0.0).astype(np.float32)
    mask = np.where(s_ <= t_, 1.0, 0.0).astype(_BF)
    rdec = np.zeros((4, 64, NCH), np.float32)
    for h in range(4):
        rdec[h] = np.exp(np.log1p(-2.0 ** (-5.0 - h)) * 128.0)
    return dict(inv_signed=inv_signed, xitab=xitab.reshape(128, 4 * TT), tri=tri, mask=mask, rdec=rdec)


def _vecs(inp, l, cst):
    v = np.zeros((128, NVEC), np.float32)
    v[:, V_MIX:V_MIX + 8] = inp["mix_norm"][l].reshape(8, 128).T
    v[:, V_CQ:V_CQ + 3] = inp["mla_q_norm"][l].reshape(3, 128).T
    v[:, V_CKV:V_CKV + 2] = inp["mla_kv_norm"][l].reshape(2, 128).T
    v[:, V_QN] = inp["mla_q_nope_norm"][l]
    v[:, V_KN] = inp["mla_k_nope_norm"][l]
    gq = inp["mla_q_rope_norm"][l]
    gk = inp["mla_k_rope_norm"][l]
    v[:, V_QR] = np.tile(gq, 2)
    v[:, V_QRS] = np.tile(_swap(gq), 2)
    v[:64, V_KR] = gk
    v[:64, V_KRS] = _swap(gk)
    v[:, V_INV] = cst["inv_signed"]
    v[:, V_RO:V_RO + 2] = inp["ret_out_norm"][l].reshape(2, 128).T
    v[:, V_GO:V_GO + 2] = inp["gla_out_norm"][l].reshape(2, 128).T
    return v


def _run(nc, in_maps):
    res = run_bass_kernel_spmd(nc, in_maps, core_ids=list(range(NCORE)))
    return res.results


def _cat_tok(res, name, b):
    return np.concatenate([res[b * 4 + q][name] for q in range(4)], axis=1)


def _cat_rows(res, name, b):
    return np.concatenate([res[b * 4 + q][name] for q in range(4)], axis=0)


def kernel(**inp):
    inp = {k: np.asarray(v) for k, v in inp.items()}
    cst = _get("cst", _consts)
    k1 = _get("k1", build_k1)
    k2a = _get("k2a", lambda: build_k2(phases=(1,)))
    k2b = _get("k2b", lambda: build_k2(phases=(2,)))
    k3 = _get("k3", build_k3)
    x = inp["x"]
    posf = inp["positions"].astype(np.float32)
    xT = [np.ascontiguousarray(x[c // 4, (c % 4) * TOK:(c % 4 + 1) * TOK, :].T) for c in range(NCORE)]
    for l in range(DEPTH):
        vecs = _vecs(inp, l, cst)
        gw = np.concatenate([inp["gla_w_gate_up"][l], inp["gla_gate_bias"][l][None, :]], axis=0)
        ims = []
        for c in range(NCORE):
            b, q = c // 4, c % 4
            ims.append(dict(xT=xT[c], wgu=inp["ffn1_w_gate_up"][l], wd=inp["ffn1_w_down"][l],
                            g1=np.ascontiguousarray(inp["ffn1_norm"][l].reshape(8, 128).T),
                            w_in=inp["w_in"][l], w_uq=inp["mla_w_uq"][l], w_ukv=inp["mla_w_ukv"][l], vecs=vecs,
                            xitab=cst["xitab"], gw=gw, tri=cst["tri"],
                            posf=np.ascontiguousarray(np.broadcast_to(posf[b, q * TOK:(q + 1) * TOK][None, :], (128, TOK)))))
        r1 = _run(k1, ims)
        ima, imb = [], []
        for b in range(B):
            rq, rk = _cat_tok(r1, "rqT", b), _cat_tok(r1, "rkT", b)
            gq, gk = _cat_tok(r1, "gqT", b), _cat_tok(r1, "gkT", b)
            vt = _cat_rows(r1, "vtok", b)
            qn, qr = _cat_tok(r1, "qnT", b), _cat_tok(r1, "qrT", b)
            kn, kr = _cat_tok(r1, "knT", b), _cat_tok(r1, "krT", b)
            vm = _cat_rows(r1, "vmtok", b)
            gdec = _cat_tok(r1, "gdec", b)
            for h in range(4):
                hs = slice(h * 64, (h + 1) * 64)
                ima.append(dict(lqk=np.ascontiguousarray(np.stack([rq[hs], rk[hs], gq[hs], gk[hs]], axis=1)),
                                lkv=np.ascontiguousarray(np.concatenate([rk[hs].T, vt[:, h * 64:(h + 1) * 64], gk[hs].T,
                                                                         vt[:, 256 + h * 64:256 + (h + 1) * 64]], axis=1)),
                                ldec=np.ascontiguousarray(np.stack([cst["rdec"][h], gdec[hs]], axis=1)), mask=cst["mask"]))
                imb.append(dict(qn=np.ascontiguousarray(qn[h * 128:(h + 1) * 128]), qr=np.ascontiguousarray(qr[hs]),
                                kn=np.ascontiguousarray(kn[h * 128:(h + 1) * 128]), kr=kr,
                                vm=np.ascontiguousarray(vm[:, h * 128:(h + 1) * 128]), mask=cst["mask"]))
        r2a = _run(k2a, ima)
        r2b = _run(k2b, imb)
        im3 = []
        for c in range(NCORE):
            b, q = c // 4, c % 4
            ts = slice(q * TOK, (q + 1) * TOK)
            roT = np.concatenate([r2a[b * 4 + h]["lo"][ts, 0:64].T for h in range(4)], axis=0)
            goT = np.concatenate([r2a[b * 4 + h]["lo"][ts, 64:128].T for h in range(4)], axis=0)
            moT = np.concatenate([r2b[b * 4 + h]["moT"][:, ts] for h in range(4)], axis=0)
            im3.append(dict(x1T=r1[c]["x1T"], roT=np.ascontiguousarray(roT), goT=np.ascontiguousarray(goT),
                            moT=np.ascontiguousarray(moT), sgT=r1[c]["sgT"], w_out=inp["w_out"][l], vecs=vecs,
                            wgu=inp["ffn2_w_gate_up"][l], wd=inp["ffn2_w_down"][l],
                            g2=np.ascontiguousarray(inp["ffn2_norm"][l].reshape(8, 128).T)))
        r3 = _run(k3, im3)
        xT = [r3[c]["x3T"] for c in range(NCORE)]
        if _CACHE.get("debug") is not None:
            _CACHE["debug"].append(dict(r1=r1, r2a=r2a, r2b=r2b, r3=r3))
    out = np.empty((B, S, D), np.float32)
    for c in range(NCORE):
        out[c // 4, (c % 4) * TOK:(c % 4 + 1) * TOK, :] = xT[c].T
    return out
```

```python
import numpy as np
import ml_dtypes
import concourse.bass as bass
import concourse.mybir as mybir
from concourse.bass_utils import run_bass_kernel_spmd

F32 = mybir.dt.float32
BF16 = mybir.dt.bfloat16
I32 = mybir.dt.int32
AF = mybir.ActivationFunctionType
ALU = mybir.AluOpType

D = 1024
B = 2
S = 16384
DEPTH = 2
DFF = 2816
NCORE = 8
TOK = B * S // NCORE
TT = 512
NT = TOK // TT
EPS = 1e-6
IN_COLS = 2768
C_RQ, C_RK, C_RV, C_RG = 0, 256, 512, 768
C_CQ, C_CKV, C_KR = 1024, 1408, 1664
C_GQ, C_GK, C_GV, C_GA, C_GR = 1728, 1984, 2240, 2496, 2512


class Buf:
    __slots__ = ("name", "writers", "readers", "sem", "dcount", "excl")

    def __init__(self, name, excl=False):
        self.name = name
        self.excl = excl
        self.writers = {}
        self.readers = {}
        self.sem = None
        self.dcount = 0


class Op:
    __slots__ = ("eng", "fn", "deps", "needs_inc", "sem", "count", "is_dma", "idx")

    def __init__(self, eng, fn, is_dma):
        self.eng = eng
        self.fn = fn
        self.deps = []
        self.needs_inc = False
        self.sem = None
        self.count = 0
        self.is_dma = is_dma


ENGS = ("pe", "act", "dve", "pool", "sp")
ROT = 30000


class Prog:
    def __init__(self, nc):
        self.nc = nc
        self.ops = {e: [] for e in ENGS}
        self.nops = 0
        self.dma_sems = []
        self.out_dmas = []

    def buf(self, name):
        return Buf(name)

    def bufs(self, name, n, excl=False):
        return [Buf(f"{name}{i}", excl) for i in range(n)]

    def _dep(self, op, prod, kind):
        if prod is None or prod is op:
            return
        if not prod.is_dma and prod.eng == op.eng and not op.is_dma:
            if op.eng == "pe" or kind != "raw":
                return
        prod.needs_inc = True
        op.deps.append(prod)

    def add(self, eng, fn, reads=(), writes=(), dma_buf=None, is_out=False):
        is_dma = dma_buf is not None
        op = Op(eng, fn, is_dma)
        op.idx = self.nops
        self.nops += 1
        for b in reads:
            for w in b.writers.values():
                self._dep(op, w, "raw")
            if b.excl:
                for r in b.readers.values():
                    self._dep(op, r, "war")
        for b in writes:
            for w in b.writers.values():
                self._dep(op, w, "waw")
            for r in b.readers.values():
                self._dep(op, r, "war")
        if is_dma:
            if dma_buf.sem is None:
                dma_buf.sem = self.nc.semaphore(f"d{len(self.dma_sems)}_{dma_buf.name}").__enter__()
                self.dma_sems.append(dma_buf.sem)
            dma_buf.dcount += 16
            op.sem = dma_buf.sem
            op.count = dma_buf.dcount
            op.needs_inc = True
            key = ("dma", id(dma_buf))
            if is_out:
                self.out_dmas.append(op)
        else:
            key = eng
        for b in reads:
            b.readers[key] = op
        for b in writes:
            b.writers = {key: op}
            b.readers = {}
        self.ops[eng].append(op)
        return op

    def I(self, eng, meth, reads, writes, *args, **kw):
        return self.add(eng, lambda e: getattr(e, meth)(*args, **kw), reads=reads, writes=writes)

    def dma(self, eng, out, in_, reads, writes, dma_buf, partial=False, is_out=False):
        fn = lambda e: e.dma_start(out=out, in_=in_)
        if partial:
            return self.add_partial_write(eng, fn, reads, writes, dma_buf)
        return self.add(eng, fn, reads, writes, dma_buf, is_out)

    def add_partial_write(self, eng, fn, reads=(), writes=(), dma_buf=None):
        saved = [(b, dict(b.writers), dict(b.readers)) for b in writes]
        op = self.add(eng, fn, reads, writes, dma_buf)
        for b, w, r in saved:
            key = ("dma", id(dma_buf)) if dma_buf is not None else eng
            w = dict(w)
            w[key] = op
            b.writers = w
            b.readers = r
        return op

    def emit(self):
        nc = self.nc
        eng_sems = {}
        for e in ENGS:
            cnt = 0
            sems = []
            for op in self.ops[e]:
                if op.is_dma or not op.needs_inc:
                    continue
                k = cnt // ROT
                if k >= len(sems):
                    sems.append(nc.semaphore(f"c_{e}{k}").__enter__())
                op.sem = sems[k]
                op.count = cnt % ROT + 1
                cnt += 1
            eng_sems[e] = sems
        final_waits = [(op.sem, op.count) for op in self.out_dmas]
        fw = {}
        for s, c in final_waits:
            fw[id(s)] = (s, max(c, fw.get(id(s), (s, 0))[1]))

        def run(engname, eng):
            waited = {}
            for op in self.ops[engname]:
                need = {}
                for p in op.deps:
                    k = id(p.sem)
                    if p.count > need.get(k, (None, 0))[1]:
                        need[k] = (p.sem, p.count)
                for k, (s, c) in need.items():
                    if waited.get(k, 0) >= c:
                        continue
                    eng.wait_ge(s, c)
                    waited[k] = c
                ins = op.fn(eng)
                if op.needs_inc:
                    ins.then_inc(op.sem, 16 if op.is_dma else 1)
            if engname == "sp":
                for s, c in fw.values():
                    eng.wait_ge(s, c)

        with nc.Block() as block:
            @block.tensor
            def _(t):
                run("pe", t)

            @block.scalar
            def _(t):
                run("act", t)

            @block.vector
            def _(t):
                run("dve", t)

            @block.gpsimd
            def _(t):
                run("pool", t)

            @block.sync
            def _(t):
                run("sp", t)


def _mm(P, out_ap, lhsT, rhs, start, stop, reads, writes):
    return P.I("pe", "matmul", reads, writes, out_ap, lhsT, rhs, start=start, stop=stop)


class TokCtx:
    pass


def alloc(nc, name, shape, dt):
    return nc.sbuf_tensor("s_" + name, shape, dt).__enter__()


def ffn_phase(P, nc, C, x_src, x_dst, wgu_d, wd_d, gain_d, src_bufs=None, dst_bufs=None):
    WA, WB = C.WA, C.WB
    wgu = WA[:, 0:8 * 5632].rearrange("p (k c) -> p k c", k=8)
    wd = WB[:, 0:22 * 1024].rearrange("p (k c) -> p k c", k=22)
    for k in range(8):
        P.dma("pool", wgu[:, k, :], wgu_d[k * 128:(k + 1) * 128, :], [], [C.bWA], C.bWA, partial=(k > 0))
    wd_v = wd_d.rearrange("(k p) n -> p k n", p=128)
    for k0 in range(0, 22, 11):
        P.dma("pool", wd[:, k0:k0 + 11, :], wd_v[:, k0:k0 + 11, :], [], [C.bWB], C.bWB, partial=(k0 > 0))
    P.dma("sp", C.gain[:, 0:8], gain_d, [], [C.bgain], C.bgain)

    x_src_v = x_src.rearrange("(k p) t -> p k t", p=128)
    x_dst_v = x_dst.rearrange("(k p) t -> p k t", p=128)

    def load(i):
        s = i % 2
        P.dma("sp", C.xt[s][:], x_src_v[:, :, i * TT:(i + 1) * TT], [src_bufs[i]] if src_bufs else [], [C.bx[s]], C.bx[s])

    load(0)
    for i in range(NT):
        s = i % 2
        if i + 1 < NT:
            load(i + 1)
        xt = C.xt[s]
        bx = C.bx[s]
        yb = C.bps[6]
        ps = C.ps[6]
        for k in range(8):
            q = k % 2
            P.I("act", "activation", [bx], [C.bsq[q]], out=C.sq[q][:], in_=xt[:, k, :], func=AF.Square)
            _mm(P, ps[:], C.ones[:], C.sq[q][:], k == 0, k == 7, [C.bsq[q], C.bones], [yb])
        P.I("act", "activation", [yb, C.bconst], [C.brstd], out=C.rstd[:], in_=ps[:], func=AF.Ln,
            bias=C.epsc[:, 0:1], scale=1.0 / D)
        P.I("act", "activation", [C.brstd], [C.brstd], out=C.rstd[:], in_=C.rstd[:], func=AF.Exp, scale=-0.5)
        for k in range(8):
            P.I("dve", "scalar_tensor_tensor", [bx, C.brstd, C.bgain], [C.bhT], out=C.hT[:, k, :], in0=xt[:, k, :],
                scalar=C.gain[:, k:k + 1], in1=C.rstd[:], op0=ALU.mult, op1=ALU.mult)
        for j in range(22):
            r = j % 3
            gb, ub = C.bps[r], C.bps[3 + r]
            gp, up = C.ps[r], C.ps[3 + r]
            for k in range(8):
                _mm(P, gp[:], wgu[:, k, j * 128:(j + 1) * 128], C.hT[:, k, :], k == 0, k == 7, [C.bWA, C.bhT], [gb])
            for k in range(8):
                _mm(P, up[:], wgu[:, k, DFF + j * 128:DFF + (j + 1) * 128], C.hT[:, k, :], k == 0, k == 7,
                    [C.bWA, C.bhT], [ub])
            q = j % 2
            P.I("act", "activation", [gb], [C.bstmp[q]], out=C.stmp[q][:], in_=gp[:], func=AF.Silu)
            P.I("dve", "tensor_tensor", [ub, C.bstmp[q]], [C.bact], out=C.act[:, j, :], in0=up[:], in1=C.stmp[q][:],
                op=ALU.mult)
        for m in range(8):
            r = 6 + (m % 2)
            yb, yp = C.bps[r], C.ps[r]
            for k in range(22):
                _mm(P, yp[:], wd[:, k, m * 128:(m + 1) * 128], C.act[:, k, :], k == 0, k == 21, [C.bWB, C.bact], [yb])
            P.I("dve", "scalar_tensor_tensor", [yb, bx], [bx], out=xt[:, m, :], in0=yp[:], scalar=0.5, in1=xt[:, m, :],
                op0=ALU.mult, op1=ALU.add)
        P.dma("sp", x_dst_v[:, :, i * TT:(i + 1) * TT], xt[:], [bx], [dst_bufs[i]] if dst_bufs else [], bx, is_out=True)


def tok_ctx(P, nc):
    C = TokCtx()
    C.WA = alloc(nc, "WA", [128, 8 * 5632], BF16)
    C.WB = alloc(nc, "WB", [128, 22 * 1024], BF16)
    C.bWA, C.bWB = P.buf("WA"), P.buf("WB")
    C.xt = [alloc(nc, f"xt{i}", [128, 8, TT], F32) for i in range(2)]
    C.bx = P.bufs("x", 2)
    C.hT = alloc(nc, "hT", [128, 8, TT], BF16)
    C.bhT = P.buf("hT")
    C.act = alloc(nc, "act", [128, 22, TT], BF16)
    C.bact = P.buf("act")
    C.sq = [alloc(nc, f"sq{i}", [128, TT], BF16) for i in range(2)]
    C.bsq = P.bufs("sq", 2)
    C.rstd = alloc(nc, "rstd", [128, TT], F32)
    C.brstd = P.buf("rstd")
    C.stmp = [alloc(nc, f"stmp{i}", [128, TT], BF16) for i in range(2)]
    C.bstmp = P.bufs("stmp", 2)
    C.gain = alloc(nc, "gain", [128, 32], F32)
    C.bgain = P.buf("gain")
    C.ones = alloc(nc, "ones", [128, 128], BF16)
    C.bones = P.buf("ones")
    C.epsc = alloc(nc, "epsc", [128, 4], F32)
    C.vec = alloc(nc, "vec", [128, NVEC], F32)
    C.bvec = P.buf("vec")
    C.ones2 = alloc(nc, "ones2", [128, 128], BF16)
    C.bones2 = C.bones
    C.bxd = P.bufs("xd", NT)
    C.bconst = P.buf("const")
    C.ps = [nc.psum_tensor(f"ps{i}", [128, 512], F32).__enter__() for i in range(8)]
    C.bps = P.bufs("ps", 8, excl=True)
    P.I("dve", "memset", [], [C.bones], C.ones[:], 1.0)
    P.I("dve", "memset", [], [C.bconst], C.epsc[:], EPS)
    P.I("dve", "memset", [C.bconst], [C.bconst], C.epsc[:, 1:2], 1.0)
    P.I("dve", "memset", [C.bconst], [C.bconst], C.epsc[:, 2:3], float(np.log(0.125)))
    P.I("pool", "memset", [], [C.bones], C.ones2[:], 0.0)
    P.I("pool", "memset", [C.bones], [C.bones], C.ones2[0:64, 0:64], 1.0)
    P.I("pool", "memset", [C.bones], [C.bones], C.ones2[64:128, 64:128], 1.0)
    return C


def build_k1_test():
    nc = bass.Bass("TRN2", target_bir_lowering=False)
    xT = nc.dram_tensor("xT", [D, TOK], F32, kind="ExternalInput").ap()
    wgu = nc.dram_tensor("wgu", [D, 2 * DFF], F32, kind="ExternalInput").ap()
    wd = nc.dram_tensor("wd", [DFF, D], F32, kind="ExternalInput").ap()
    g1 = nc.dram_tensor("g1", [128, 8], F32, kind="ExternalInput").ap()
    x1T = nc.dram_tensor("x1T", [D, TOK], F32, kind="ExternalOutput").ap()
    P = Prog(nc)
    C = tok_ctx(P, nc)
    ffn_phase(P, nc, C, xT, x1T, wgu, wd, g1)
    P.emit()
    return nc


NQG = S // 512
NCH = S // 128
SCALE_MLA = 192.0 ** -0.5


def build_k2(phases=(1, 2)):
    nc = bass.Bass("TRN2", target_bir_lowering=False)

    def din(name, shape, dt):
        return nc.dram_tensor(name, shape, dt, kind="ExternalInput").ap()

    if 2 in phases:
        qn_d = din("qn", [128, S], BF16)
        qr_d = din("qr", [64, S], BF16)
        kn_d = din("kn", [128, S], BF16)
        kr_d = din("kr", [64, S], BF16)
        vm_d = din("vm", [S, 128], BF16)
        mo_d = nc.dram_tensor("moT", [128, S], F32, kind="ExternalOutput").ap()
    if 1 in phases:
        lqk_d = din("lqk", [64, 4, S], BF16)
        lkv_d = din("lkv", [S, 256], BF16)
        ldec_d = din("ldec", [64, 2, NCH], F32)
        lo_d = nc.dram_tensor("lo", [S, 128], F32, kind="ExternalOutput").ap()
    mask_d = din("mask", [128, 128], BF16)

    P = Prog(nc)
    ps = [nc.psum_tensor(f"ps{i}", [128, 512], F32).__enter__() for i in range(8)]
    bps = P.bufs("ps", 8, excl=True)
    mask = alloc(nc, "mask", [128, 128], BF16)
    bmask = P.buf("mask")
    P.dma("sp", mask[:], mask_d, [], [bmask], bmask)

    L = {}
    NB = 3
    if 1 in phases:
        qk = [alloc(nc, f"lqk{i}", [64, 4, 512], BF16) for i in range(NB)]
        kv = [alloc(nc, f"lkv{i}", [128, 4, 256], BF16) for i in range(NB)]
        bin_ = P.bufs("lin", NB)
        dec = alloc(nc, "ldec", [64, 2, NCH], F32)
        bdec = P.buf("ldec")
        osb = [alloc(nc, f"losb{i}", [128, 4, 128], F32) for i in range(2)]
        bosb = P.bufs("losb", 2)
        P.dma("sp", dec[:], ldec_d, [], [bdec], bdec)
    for xi, X in enumerate(("r", "g") if 1 in phases else ()):
        o = TokCtx()
        o.qi, o.ki, o.kti, o.vi, o.oi = 2 * xi, 2 * xi + 1, 128 * xi, 128 * xi + 64, 64 * xi
        o.scm = [alloc(nc, f"{X}scm{i}", [128, 128], BF16) for i in range(2)]
        o.bscm = P.bufs(X + "scm", 2)
        o.st = alloc(nc, X + "st", [64, 64], F32)
        o.tmp = alloc(nc, X + "tmp", [64, 64], F32)
        o.stb = alloc(nc, X + "stb", [64, 64], BF16)
        o.bst, o.btmp, o.bstb = P.buf(X + "st"), P.buf(X + "tmp"), P.buf(X + "stb")
        o.xi = xi
        L[X] = o
        P.I("dve", "memset", [], [o.bst], o.st[:], 0.0)
        P.I("dve", "memset", [], [o.bstb], o.stb[:], 0.0)

    def lin_load(g):
        s = g % NB
        t0 = g * 512
        P.dma("sp", qk[s][:], lqk_d[:, :, t0:t0 + 512], [], [bin_[s]], bin_[s])
        P.dma("act", kv[s][:], lkv_d[t0:t0 + 512, :].rearrange("(n p) d -> p n d", p=128), [], [bin_[s]], bin_[s],
              partial=True)

    def lin_A(n):
        g, c = n // 4, n % 4
        s = g % NB
        cs = slice(c * 128, (c + 1) * 128)
        for xi, X in enumerate(("r", "g")):
            o = L[X]
            sb = xi * 2 + (n % 2)
            m2 = n % 2
            _mm(P, ps[sb][:, 0:128], qk[s][:, o.ki, cs], qk[s][:, o.qi, cs], True, True, [bin_[s]], [bps[sb]])
            P.I("dve", "tensor_tensor", [bps[sb], bmask], [o.bscm[m2]], out=o.scm[m2][:], in0=ps[sb][:, 0:128],
                in1=mask[:], op=ALU.mult)

    def lin_C(n):
        g, c = n // 4, n % 4
        s = g % NB
        so = g % 2
        cs = slice(c * 128, (c + 1) * 128)
        for xi, X in enumerate(("r", "g")):
            o = L[X]
            ob = 4 + xi * 2 + (n % 2)
            m2 = n % 2
            vv = kv[s][:, c, o.vi:o.vi + 64]
            _mm(P, ps[ob][:, 0:64], o.scm[m2][:], vv, True, False, [o.bscm[m2], bin_[s]], [bps[ob]])
            _mm(P, ps[ob][:, 0:64], qk[s][:, o.qi, cs], o.stb[:], False, True, [bin_[s], o.bstb], [bps[ob]])
            _mm(P, ps[ob][0:64, 64:128], kv[s][:, c, o.kti:o.kti + 64], vv, True, True, [bin_[s]], [bps[ob]])
            P.I("dve", "scalar_tensor_tensor", [bps[ob], bdec, o.btmp], [o.bstb], out=o.stb[:], in0=ps[ob][0:64, 64:128],
                scalar=dec[:, xi, n:n + 1], in1=o.tmp[:], op0=ALU.mult, op1=ALU.add)
            P.I("dve", "scalar_tensor_tensor", [bps[ob], bdec, o.btmp], [o.bst], out=o.st[:], in0=ps[ob][0:64, 64:128],
                scalar=dec[:, xi, n:n + 1], in1=o.tmp[:], op0=ALU.mult, op1=ALU.add)
            P.I("act", "copy", [bps[ob]], [bosb[so]], out=osb[so][:, c, o.oi:o.oi + 64], in_=ps[ob][:, 0:64])
            if n + 1 < NCH:
                P.I("dve", "tensor_scalar", [o.bst, bdec], [o.btmp], out=o.tmp[:], in0=o.st[:],
                    scalar1=dec[:, xi, n + 1:n + 2], scalar2=None, op0=ALU.mult)
        if c == 3:
            P.dma("sp", lo_d[g * 512:(g + 1) * 512, :].rearrange("(n p) d -> p n d", p=128), osb[so][:],
                  [bosb[so]], [], bosb[so], is_out=True)

    if 1 in phases:
        for X in ("r", "g"):
            P.I("dve", "memset", [], [L[X].btmp], L[X].tmp[:], 0.0)
        for g0 in range(NB):
            lin_load(g0)
        lin_A(0)
        for n in range(NCH):
            if n + 1 < NCH:
                lin_A(n + 1)
            lin_C(n)
            if n % 4 == 3 and n // 4 + NB < NQG:
                lin_load(n // 4 + NB)

    if 2 not in phases:
        P.emit()
        return nc
    kn = alloc(nc, "kn", [128, S], BF16)
    kr = alloc(nc, "kr", [64, S], BF16)
    V = alloc(nc, "V", [128, NCH, 128], BF16)
    bkv = P.bufs("kv", NQG)
    kvsem = P.bufs("kvsem", 4)
    qn = [alloc(nc, f"qn{i}", [128, 512], BF16) for i in range(2)]
    qr = [alloc(nc, f"qr{i}", [64, 512], BF16) for i in range(2)]
    bq = P.bufs("q", 2)
    NSC = 4
    pt = [alloc(nc, f"pt{i}", [128, 512], BF16) for i in range(NSC)]
    bpt = P.bufs("pt", NSC)
    psum_t = [alloc(nc, f"ptsum{i}", [128, 512], F32) for i in range(2)]
    bpsum = P.bufs("ptsum", 2)
    rec = [alloc(nc, f"rec{i}", [128, 512], F32) for i in range(2)]
    brec = P.bufs("rec", 2)
    mosb = [alloc(nc, f"mosb{i}", [128, 512], F32) for i in range(2)]
    bmosb = P.bufs("mosb", 2)
    onesf = alloc(nc, "onesf", [128, 128], F32)
    bonesf = P.buf("onesf")
    P.I("pool", "memset", [], [bonesf], onesf[:], 1.0)
    vm_v = vm_d.rearrange("(n p) d -> p n d", p=128)

    def kv_load(g):
        t0 = g * 512
        sb = kvsem[g % 4]
        rd = [bkv[g - 4]] if g >= 4 else []
        P.dma("sp", kn[:, t0:t0 + 512], kn_d[:, t0:t0 + 512], rd, [bkv[g]], sb)
        P.dma("sp", kr[:, t0:t0 + 512], kr_d[:, t0:t0 + 512], [], [bkv[g]], sb, partial=True)
        P.dma("sp", V[:, 4 * g:4 * g + 4, 0:128], vm_v[:, 4 * g:4 * g + 4, :], [], [bkv[g]], sb, partial=True)

    def q_load(g):
        s = g % 2
        t0 = g * 512
        P.dma("sp", qn[s][:], qn_d[:, t0:t0 + 512], [], [bq[s]], bq[s])
        P.dma("sp", qr[s][:], qr_d[:, t0:t0 + 512], [], [bq[s]], bq[s], partial=True)

    LOOK = 3
    SUMB = 6
    blocks = [(g, kb) for g in range(NQG) for kb in range(4 * g + 4)]
    nblk = len(blocks)
    kv_load(0)
    q_load(0)

    def emit_sc(i):
        g, kb = blocks[i]
        s = g % 2
        if kb == 0 and g + 1 < NQG:
            kv_load(g + 1)
            q_load(g + 1)
        j = kb - 4 * g
        c0 = 128 * j if j > 0 else 0
        r = i % NSC
        ks = slice(kb * 128, (kb + 1) * 128)
        kvb = bkv[kb // 4]
        _mm(P, ps[r][:, c0:512], kn[:, ks], qn[s][:, c0:512], True, False, [kvb, bq[s]], [bps[r]])
        _mm(P, ps[r][:, c0:512], kr[:, ks], qr[s][:, c0:512], False, True, [kvb, bq[s]], [bps[r]])
        P.I("act", "activation", [bps[r]], [bpt[r]], out=pt[r][:, c0:512], in_=ps[r][:, c0:512], func=AF.Exp,
            scale=SCALE_MLA)
        if j >= 0:
            P.I("pool", "tensor_tensor", [bpt[r], bmask], [bpt[r]], out=pt[r][:, 128 * j:128 * j + 128],
                in0=pt[r][:, 128 * j:128 * j + 128], in1=mask[:], op=ALU.mult)
        if kb == 0:
            P.I("dve", "tensor_copy", [bpt[r]], [bpsum[s]], out=psum_t[s][:], in_=pt[r][:])
        else:
            P.I("dve", "tensor_tensor", [bpt[r], bpsum[s]], [bpsum[s]], out=psum_t[s][:, c0:512], in0=psum_t[s][:, c0:512],
                in1=pt[r][:, c0:512], op=ALU.add)

    def emit_pv(i):
        g, kb = blocks[i]
        s = g % 2
        j = kb - 4 * g
        c0 = 128 * j if j > 0 else 0
        r = i % NSC
        kvb = bkv[kb // 4]
        ab = 4 + s
        _mm(P, ps[ab][:, c0:512], V[:, kb, 0:128], pt[r][:, c0:512], kb == 0, kb == 4 * g + 3, [bpt[r], kvb], [bps[ab]])
        if kb == 4 * g + 3:
            P.I("pe", "matmul", [bpsum[s], bonesf], [bps[SUMB]], ps[SUMB][:], onesf[:], psum_t[s][:], start=True, stop=True)
            P.I("dve", "reciprocal", [bps[SUMB]], [brec[s]], out=rec[s][:], in_=ps[SUMB][:])
            P.I("dve", "tensor_tensor", [bps[ab], brec[s]], [bmosb[s]], out=mosb[s][:], in0=ps[ab][:], in1=rec[s][:], op=ALU.mult)
            P.dma("sp", mo_d[:, g * 512:(g + 1) * 512], mosb[s][:], [bmosb[s]], [], bmosb[s], is_out=True)

    for i in range(nblk + LOOK):
        if i < nblk:
            emit_sc(i)
        if i >= LOOK:
            emit_pv(i - LOOK)
    P.emit()
    return nc


TWO_PI = 2.0 * np.pi
MAGIC = 12582912.0
CW1 = 6.28125
CW2 = TWO_PI - 6.28125
V_MIX = 0
V_CQ = 8
V_CKV = 11
V_QN = 13
V_KN = 14
V_QR = 15
V_QRS = 16
V_KR = 17
V_KRS = 18
V_INV = 19
V_RO = 20
V_GO = 22
NVEC = 24


def norm_from_psum(P, C, src_aps, src_bufs, K, nparts, ones_ap, n_norm, out_aps, out_buf, gain_cols, stat_bank):
    sb, sp = C.bps[stat_bank], C.ps[stat_bank]
    n = len(src_aps)
    for k in range(n):
        q = k % 2
        P.I("act", "activation", [src_bufs[k]], [C.bsq[q]], out=C.sq[q][0:nparts, :], in_=src_aps[k], func=AF.Square)
        _mm(P, sp[0:nparts, :], ones_ap, C.sq[q][0:nparts, :], k == 0, k == n - 1, [C.bsq[q], C.bones], [sb])
    P.I("act", "activation", [sb, C.bconst], [C.brstd], out=C.rstd[0:nparts, :], in_=sp[0:nparts, :], func=AF.Ln,
        bias=C.epsc[0:nparts, 0:1], scale=1.0 / n_norm)
    P.I("act", "activation", [C.brstd], [C.brstd], out=C.rstd[0:nparts, :], in_=C.rstd[0:nparts, :], func=AF.Exp, scale=-0.5)
    if out_aps is not None:
        for k in range(n):
            P.I("dve", "scalar_tensor_tensor", [src_bufs[k], C.brstd, C.bvec], [out_buf], out=out_aps[k], in0=src_aps[k],
                scalar=C.vec[0:nparts, gain_cols[k]:gain_cols[k] + 1], in1=C.rstd[0:nparts, :], op0=ALU.mult, op1=ALU.mult)


def norm_gen(P, C, src_aps, src_bufs, nparts, ones_ap, n_norm, out_aps, out_buf, gain_cols, stat_bank):
    sb, sp = C.bps[stat_bank], C.ps[stat_bank]
    n = len(src_aps)
    if n <= 2:
        for k in range(n):
            P.I("act", "activation", [src_bufs[k]], [C.bsq[k]], out=C.sq[k][0:nparts, :], in_=src_aps[k], func=AF.Square)
        yield None
        for k in range(n):
            _mm(P, sp[0:nparts, :], ones_ap, C.sq[k][0:nparts, :], k == 0, k == n - 1, [C.bsq[k], C.bones], [sb])
    else:
        yield None
        for k in range(n):
            q = k % 2
            P.I("act", "activation", [src_bufs[k]], [C.bsq[q]], out=C.sq[q][0:nparts, :], in_=src_aps[k], func=AF.Square)
            _mm(P, sp[0:nparts, :], ones_ap, C.sq[q][0:nparts, :], k == 0, k == n - 1, [C.bsq[q], C.bones], [sb])
    P.I("act", "activation", [sb, C.bconst], [C.brstd], out=C.rstd[0:nparts, :], in_=sp[0:nparts, :], func=AF.Ln,
        bias=C.epsc[0:nparts, 0:1], scale=1.0 / n_norm)
    P.I("act", "activation", [C.brstd], [C.brstd], out=C.rstd[0:nparts, :], in_=C.rstd[0:nparts, :], func=AF.Exp, scale=-0.5)
    if out_aps is not None:
        for k in range(n):
            P.I("dve", "scalar_tensor_tensor", [src_bufs[k], C.brstd, C.bvec], [out_buf], out=out_aps[k], in0=src_aps[k],
                scalar=C.vec[0:nparts, gain_cols[k]:gain_cols[k] + 1], in1=C.rstd[0:nparts, :], op0=ALU.mult, op1=ALU.mult)
    yield None


def proj_phase(P, nc, C, d):
    WA = C.WA
    off = [0]

    def carve(n):
        a = off[0]
        off[0] += n
        return WA[:, a:a + n]

    bW = C.bWA
    w_in = carve(8 * IN_COLS).rearrange("p (k c) -> p k c", k=8)
    w_sw = carve(8 * 576).rearrange("p (k c) -> p k c", k=8)
    wuq_n = carve(3 * 512).rearrange("p (k c) -> p k c", k=3)
    wuq_r = carve(3 * 256).rearrange("p (k c) -> p k c", k=3)
    wuq_rs = carve(3 * 256).rearrange("p (k c) -> p k c", k=3)
    wkv_k = carve(2 * 512).rearrange("p (k c) -> p k c", k=2)
    wkv_v = carve(2 * 512).rearrange("p (k c) -> p k c", k=2)
    first = [True]

    def wdma(out, in_):
        P.dma("pool", out, in_, [], [bW], bW, partial=not first[0])
        first[0] = False

    win_d = d["w_in"]
    for k in range(8):
        rows = slice(k * 128, (k + 1) * 128)
        wdma(w_in[:, k, :], win_d[rows, :])
        src = win_d[rows, 0:512].rearrange("p (h t c) -> p h t c", h=8, t=2)
        dst = w_sw[:, k, 0:512].rearrange("p (h t c) -> p h t c", h=8, t=2)
        wdma(dst[:, :, 0, :], src[:, :, 1, :])
        wdma(dst[:, :, 1, :], src[:, :, 0, :])
        wdma(w_sw[:, k, 512:544], win_d[rows, C_KR + 32:C_KR + 64])
        wdma(w_sw[:, k, 544:576], win_d[rows, C_KR:C_KR + 32])
    for k in range(3):
        rows = slice(k * 128, (k + 1) * 128)
        src = d["w_uq"][rows, :].rearrange("p (h c) -> p h c", h=4)
        wdma(wuq_n[:, k, :].rearrange("p (h c) -> p h c", h=4), src[:, :, 0:128])
        wdma(wuq_r[:, k, :].rearrange("p (h c) -> p h c", h=4), src[:, :, 128:192])
        dsts = wuq_rs[:, k, :].rearrange("p (h c) -> p h c", h=4)
        wdma(dsts[:, :, 0:32], src[:, :, 160:192])
        wdma(dsts[:, :, 32:64], src[:, :, 128:160])
    for k in range(2):
        rows = slice(k * 128, (k + 1) * 128)
        src = d["w_ukv"][rows, :].rearrange("p (h c) -> p h c", h=4)
        wdma(wkv_k[:, k, :].rearrange("p (h c) -> p h c", h=4), src[:, :, 0:128])
        wdma(wkv_v[:, k, :].rearrange("p (h c) -> p h c", h=4), src[:, :, 128:256])

    def alias():
        b = Buf("al")
        for o in (C.bWA, C.bWB, C.bact):
            for k, v in o.writers.items():
                if k not in b.writers or b.writers[k].idx < v.idx:
                    b.writers[k] = v
            for k, v in o.readers.items():
                if k not in b.readers or b.readers[k].idx < v.idx:
                    b.readers[k] = v
        return b

    assert off[0] % 2 == 0
    WAf = WA.bitcast(F32)
    WBf = C.WB.bitcast(F32)
    offa = [off[0] // 2]
    offb = [0]

    def cf(n, region="b"):
        o_, t_ = (offb, WBf) if region == "b" else (offa, WAf)
        a = o_[0]
        o_[0] += n
        return t_[:, a:a + n]

    xitab = cf(4 * TT, "a").rearrange("p (k c) -> p k c", k=4); bxi = alias()
    Eq = [cf(TT, "a") for _ in range(2)]; Ek = [cf(TT, "a") for _ in range(2)]; bE = [alias() for _ in range(2)]
    assert offa[0] <= 8 * 5632 // 2, offa[0]
    pos = cf(TT); bpos = alias()
    ang = cf(TT); bang = alias()
    tk = cf(TT); btk = alias()
    r1 = cf(TT); br1 = alias()
    Ssb = cf(TT); bS = alias()
    Csb = cf(TT); bC = alias()
    GCq = cf(TT); GSq = cf(TT); bGq = alias()
    GCk = cf(TT); GSk = cf(TT); bGk = alias()
    t1 = [cf(TT) for _ in range(2)]; bt1 = [alias() for _ in range(2)]
    t2 = [cf(TT) for _ in range(2)]; bt2 = [alias() for _ in range(2)]
    sgs = [cf(TT) for _ in range(2)]; bsgs = [alias() for _ in range(2)]
    alow = cf(TT); balow = alias()
    gw = cf(256); bgw = alias()
    tri = cf(128); btri = alias()
    zl = cf(256); bzl = alias()
    decs = cf(8).rearrange("p (k c) -> p k c", k=2); bdecs = alias()
    assert offb[0] <= 22 * 1024 // 2, offb[0]

    nb = [0]

    def stage_bf(n):
        a = nb[0]
        nb[0] += n
        assert nb[0] <= 22
        return C.act[:, a:a + n, :], alias()

    cqn, bcqn = stage_bf(3)
    ckvn, bckvn = stage_bf(2)
    vst, bvst = stage_bf(4)
    ostage = [stage_bf(1) for _ in range(8)]
    nst = [0]

    def next_stage():
        a = ostage[nst[0] % len(ostage)]
        nst[0] += 1
        return a[0][:, 0, :], a[1]

    P.dma("sp", C.vec[:], d["vecs"], [], [C.bvec], C.bvec)
    P.dma("sp", xitab, d["xitab"].rearrange("p (k c) -> p k c", k=4), [], [bxi], bxi)
    P.dma("sp", gw[0:17, :], d["gw"], [], [bgw], bgw)
    P.dma("sp", tri, d["tri"], [], [btri], btri)
    P.I("pool", "memset", [], [balow], alow[0:32, :], 1.0)

    x_v = d["x1T"].rearrange("(k p) t -> p k t", p=128)

    def load(i):
        s = i % 2
        P.dma("sp", C.xt[s][:], x_v[:, :, i * TT:(i + 1) * TT], [C.bxd[i]], [C.bx[s]], C.bx[s])

    load(0)
    bank = [0]

    lane = [0]
    bank2 = [0]

    def nb_():
        if lane[0] == 0:
            b = bank[0] % 3
            bank[0] += 1
        else:
            b = 3 + bank2[0] % 3
            bank2[0] += 1
        return b

    def proj_chunk(wt, col0, M=128):
        b = nb_()
        for k in range(8):
            _mm(P, C.ps[b][0:M, :], wt[:, k, col0:col0 + M], C.hT[:, k, :], k == 0, k == 7, [bW, C.bhT], [C.bps[b]])
        return b

    def store(dram_ap, sb_ap, buf):
        P.dma("sp", dram_ap, sb_ap, [buf], [], buf, is_out=True)

    def rope_combine(bx_, bs_, M, cos_ap, sin_ap, rbufs, post_ap, post_bufs, out_ap, out_buf, q):
        P.I("dve", "tensor_tensor", [C.bps[bx_]] + rbufs, [bt1[q]], out=t1[q][0:M, :], in0=C.ps[bx_][0:M, :], in1=cos_ap,
            op=ALU.mult)
        P.I("dve", "tensor_tensor", [C.bps[bs_]] + rbufs, [bt2[q]], out=t2[q][0:M, :], in0=C.ps[bs_][0:M, :], in1=sin_ap,
            op=ALU.mult)
        P.I("dve", "tensor_tensor", [bt1[q], bt2[q]], [bt1[q]], out=t1[q][0:M, :], in0=t1[q][0:M, :], in1=t2[q][0:M, :],
            op=ALU.add)
        P.I("dve", "tensor_tensor", [bt1[q]] + post_bufs, [out_buf], out=out_ap, in0=t1[q][0:M, :], in1=post_ap, op=ALU.mult)

    for i in range(NT):
        s = i % 2
        tsl = slice(i * TT, (i + 1) * TT)
        if i + 1 < NT:
            load(i + 1)
        xt, bx = C.xt[s], C.bx[s]
        norm_from_psum(P, C, [xt[:, k, :] for k in range(8)], [bx] * 8, 128, 128, C.ones[:], D,
                       [C.hT[:, k, :] for k in range(8)], C.bhT, [V_MIX + k for k in range(8)], 6)
        P.dma("sp", pos, d["posf"][:, tsl], [], [bpos], bpos)
        for (dst, bd, shift) in ((Ssb, bS, 0.0), (Csb, bC, 0.5 * np.pi)):
            P.I("dve", "tensor_scalar", [bpos, C.bvec], [bang], out=ang, in0=pos, scalar1=C.vec[:, V_INV:V_INV + 1],
                scalar2=shift, op0=ALU.mult, op1=ALU.add)
            P.I("dve", "tensor_scalar", [bang], [btk], out=tk, in0=ang, scalar1=1.0 / TWO_PI, scalar2=MAGIC,
                op0=ALU.mult, op1=ALU.add)
            P.I("dve", "tensor_scalar", [btk], [btk], out=tk, in0=tk, scalar1=-MAGIC, scalar2=None, op0=ALU.add)
            P.I("dve", "scalar_tensor_tensor", [btk, bang], [br1], out=r1, in0=tk, scalar=-CW1, in1=ang,
                op0=ALU.mult, op1=ALU.add)
            P.I("dve", "scalar_tensor_tensor", [btk, br1], [br1], out=r1, in0=tk, scalar=-CW2, in1=r1,
                op0=ALU.mult, op1=ALU.add)
            P.I("dve", "tensor_scalar", [br1], [br1], out=r1, in0=r1, scalar1=-np.pi, scalar2=np.pi, op0=ALU.max, op1=ALU.min)
            P.I("act", "activation", [br1], [bd], out=dst, in_=r1, func=AF.Sin)
        P.I("pool", "tensor_scalar", [bC, C.bvec], [bGq], out=GCq, in0=Csb, scalar1=C.vec[:, V_QR:V_QR + 1], scalar2=None,
            op0=ALU.mult)
        P.I("pool", "tensor_scalar", [bS, C.bvec], [bGq], out=GSq, in0=Ssb, scalar1=C.vec[:, V_QRS:V_QRS + 1], scalar2=None,
            op0=ALU.mult)
        P.I("pool", "tensor_scalar", [bC, C.bvec], [bGk], out=GCk[0:64, :], in0=Csb[0:64, :], scalar1=C.vec[0:64, V_KR:V_KR + 1],
            scalar2=None, op0=ALU.mult)
        P.I("pool", "tensor_scalar", [bS, C.bvec], [bGk], out=GSk[0:64, :], in0=Ssb[0:64, :],
            scalar1=C.vec[0:64, V_KRS:V_KRS + 1], scalar2=None, op0=ALU.mult)

        def u_ret(c0, tab0, dname, cc):
            def f():
                bxp = proj_chunk(w_in, c0 + cc * 128)
                bsp = proj_chunk(w_sw, c0 + cc * 128)
                o_ap, o_b = next_stage()
                rope_combine(bxp, bsp, 128, Csb, Ssb, [bC, bS], xitab[:, tab0 + cc, :], [bxi], o_ap, o_b, 1)
                store(d[dname][cc * 128:(cc + 1) * 128, tsl], o_ap, o_b)
            return f

        def u_sg(c0, r0, cc):
            def f():
                bp = proj_chunk(w_in, c0 + cc * 128)
                P.I("act", "activation", [C.bps[bp]], [bsgs[cc]], out=sgs[cc], in_=C.ps[bp][:], func=AF.Silu)
                store(d["sgT"][r0 + cc * 128:r0 + (cc + 1) * 128, tsl], sgs[cc], bsgs[cc])
            return f

        def u_v(sub):
            def f():
                b = nb_()
                for (c0, o0) in ((C_RV, 0), (C_GV, 256)):
                    for k in range(8):
                        _mm(P, C.ps[b][:, o0:o0 + 256], C.hT[:, k, sub * 128:(sub + 1) * 128], w_in[:, k, c0:c0 + 256], k == 0,
                            k == 7, [bW, C.bhT], [C.bps[b]])
                P.I("act", "copy", [C.bps[b]], [bvst], out=vst[:, sub, :], in_=C.ps[b][:])
                if sub == 3:
                    store(d["vtok"][i * TT:(i + 1) * TT, :].rearrange("(n p) c -> p n c", p=128), vst, bvst)
            return f

        def u_vm(sub):
            def f():
                b = nb_()
                for k in range(2):
                    _mm(P, C.ps[b][:], ckvn[:, k, sub * 128:(sub + 1) * 128], wkv_v[:, k, :], k == 0, k == 1, [bW, bckvn],
                        [C.bps[b]])
                o_ap, o_b = next_stage()
                P.I("act", "copy", [C.bps[b]], [o_b], out=o_ap, in_=C.ps[b][:])
                store(d["vmtok"][i * TT + sub * 128:i * TT + (sub + 1) * 128, :], o_ap, o_b)
            return f

        def u_gqk(c0, E, dname, cc):
            def f():
                bp = proj_chunk(w_in, c0 + cc * 128)
                o_ap, o_b = next_stage()
                P.I("dve", "tensor_tensor", [C.bps[bp], bE[cc]], [o_b], out=o_ap, in0=C.ps[bp][:], in1=E[cc], op=ALU.mult)
                store(d[dname][cc * 128:(cc + 1) * 128, tsl], o_ap, o_b)
            return f

        fill = [u_ret(C_RQ, 0, "rqT", 0), u_ret(C_RQ, 0, "rqT", 1), u_ret(C_RK, 2, "rkT", 0), u_ret(C_RK, 2, "rkT", 1)]
        fill += [u_sg(C_RG, 0, 0), u_sg(C_RG, 0, 1), u_sg(C_GR, 256, 0), u_sg(C_GR, 256, 1)]
        fill += [u_v(sub) for sub in range(4)]
        after = {"ckvn": [u_vm(sub) for sub in range(4)],
                 "E": [u_gqk(C_GQ, Eq, "gqT", 0), u_gqk(C_GQ, Eq, "gqT", 1), u_gqk(C_GK, Ek, "gkT", 0), u_gqk(C_GK, Ek, "gkT", 1)]}

        def lane1():
            ba = proj_chunk(w_in, C_GA, M=16)
            P.I("act", "copy", [C.bps[ba]], [balow], out=alow[0:16, :], in_=C.ps[ba][0:16, :])
            yield None
            bc = [nb_(), nb_()]
            for sub in range(4):
                bz = 7
                P.I("pe", "matmul", [balow, bgw], [C.bps[bz]], C.ps[bz][:, 0:256], alow[0:17, sub * 128:(sub + 1) * 128],
                    gw[0:17, :], start=True, stop=True)
                P.I("act", "activation", [C.bps[bz]], [bzl], out=zl, in_=C.ps[bz][:, 0:256], func=AF.Exp, scale=-1.0)
                P.I("act", "activation", [bzl, C.bconst], [bzl], out=zl, in_=zl, func=AF.Ln, bias=C.epsc[:, 1:2], scale=1.0)
                yield None
                for c in range(2):
                    P.I("pe", "matmul", [bzl, btri], [C.bps[bc[c]]], C.ps[bc[c]][:, sub * 128:(sub + 1) * 128],
                        zl[:, c * 128:(c + 1) * 128], tri, start=True, stop=True)
            for c in range(2):
                pb = C.ps[bc[c]]
                P.I("act", "activation", [C.bps[bc[c]], C.bconst], [bE[c]], out=Eq[c], in_=pb[:], func=AF.Exp,
                    bias=C.epsc[:, 2:3], scale=1.0)
                P.I("act", "activation", [C.bps[bc[c]]], [bE[c]], out=Ek[c], in_=pb[:], func=AF.Exp, scale=-1.0)
                P.I("act", "activation", [C.bps[bc[c]]], [bdecs], out=decs[:, c, :],
                    in_=pb[:].rearrange("p (n t) -> p n t", t=128)[:, :, 127], func=AF.Exp)
            store(d["gdec"][:, i * 4:(i + 1) * 4].rearrange("(c p) n -> p c n", p=128), decs, bdecs)
            yield "E"
            bk_ = [proj_chunk(w_in, C_CKV + k * 128) for k in range(2)]
            yield from norm_gen(P, C, [C.ps[b][:] for b in bk_], [C.bps[b] for b in bk_], 128, C.ones[:], 256,
                                [ckvn[:, k, :] for k in range(2)], bckvn, [V_CKV + k for k in range(2)], 6)
            yield "ckvn"
            bq_ = [proj_chunk(w_in, C_CQ + k * 128) for k in range(3)]
            yield from norm_gen(P, C, [C.ps[b][:] for b in bq_], [C.bps[b] for b in bq_], 128, C.ones[:], 384,
                                [cqn[:, k, :] for k in range(3)], bcqn, [V_CQ + k for k in range(3)], 6)
            for h in range(4):
                for (wt, src_t, src_b, nk, gcol, dname) in ((wuq_n, cqn, bcqn, 3, V_QN, "qnT"), (wkv_k, ckvn, bckvn, 2, V_KN, "knT")):
                    b = nb_()
                    for k in range(nk):
                        _mm(P, C.ps[b][:], wt[:, k, h * 128:(h + 1) * 128], src_t[:, k, :], k == 0, k == nk - 1, [bW, src_b],
                            [C.bps[b]])
                    o_ap, o_b = next_stage()
                    yield from norm_gen(P, C, [C.ps[b][:]], [C.bps[b]], 128, C.ones[:], 128, [o_ap], o_b, [gcol], 7)
                    store(d[dname][h * 128:(h + 1) * 128, tsl], o_ap, o_b)
            for cc in range(2):
                b1, b2 = nb_(), nb_()
                for (b, wt) in ((b1, wuq_r), (b2, wuq_rs)):
                    for k in range(3):
                        _mm(P, C.ps[b][:], wt[:, k, cc * 128:(cc + 1) * 128], cqn[:, k, :], k == 0, k == 2, [bW, bcqn], [C.bps[b]])
                yield from norm_gen(P, C, [C.ps[b1][:]], [C.bps[b1]], 128, C.ones2[:], 64, None, None, None, 7)
                o_ap, o_b = next_stage()
                rope_combine(b1, b2, 128, GCq, GSq, [bGq], C.rstd[:], [C.brstd], o_ap, o_b, 0)
                store(d["qrT"][cc * 128:(cc + 1) * 128, tsl], o_ap, o_b)
            b1 = proj_chunk(w_in, C_KR, M=64)
            b2 = proj_chunk(w_sw, 512, M=64)
            yield from norm_gen(P, C, [C.ps[b1][0:64, :]], [C.bps[b1]], 64, C.ones[0:64, 0:64], 64, None, None, None, 7)
            o_ap, o_b = next_stage()
            rope_combine(b1, b2, 64, GCk[0:64, :], GSk[0:64, :], [bGk], C.rstd[0:64, :], [C.brstd], o_ap[0:64, :], o_b, 0)
            store(d["krT"][:, tsl], o_ap[0:64, :], o_b)

        lane[0] = 0
        for tag in lane1():
            if tag is not None:
                fill += after.pop(tag)
            if fill:
                lane[0] = 1
                fill.pop(0)()
                lane[0] = 0
        lane[0] = 1
        for tag in list(after):
            fill += after.pop(tag)
        while fill:
            fill.pop(0)()
        lane[0] = 0


K1_OUTS = dict(rqT=([256, TOK], BF16), rkT=([256, TOK], BF16), sgT=([512, TOK], F32), vtok=([TOK, 512], BF16),
               qnT=([512, TOK], BF16), qrT=([256, TOK], BF16), knT=([512, TOK], BF16), vmtok=([TOK, 512], BF16),
               krT=([64, TOK], BF16), gdec=([256, TOK // 128], F32), gqT=([256, TOK], BF16), gkT=([256, TOK], BF16),
               x1T=([D, TOK], F32))


def build_k1(with_ffn=True):
    nc = bass.Bass("TRN2", target_bir_lowering=False)

    def din(name, shape, dt=F32):
        return nc.dram_tensor(name, shape, dt, kind="ExternalInput").ap()

    xT = din("xT", [D, TOK])
    wgu = din("wgu", [D, 2 * DFF])
    wd = din("wd", [DFF, D])
    g1 = din("g1", [128, 8])
    d = dict(w_in=din("w_in", [D, IN_COLS]), w_uq=din("w_uq", [384, 768]), w_ukv=din("w_ukv", [256, 1024]),
             vecs=din("vecs", [128, NVEC]), xitab=din("xitab", [128, 4 * TT]), gw=din("gw", [17, 256]),
             tri=din("tri", [128, 128]), posf=din("posf", [128, TOK]))
    for name, (shape, dt) in K1_OUTS.items():
        d[name] = nc.dram_tensor(name, shape, dt, kind="ExternalOutput").ap()
    P = Prog(nc)
    C = tok_ctx(P, nc)
    if with_ffn:
        ffn_phase(P, nc, C, xT, d["x1T"], wgu, wd, g1, dst_bufs=C.bxd)
    else:
        d["x1T"] = xT
    proj_phase(P, nc, C, d)
    P.emit()
    return nc


def post_phase(P, nc, C, d):
    WA = C.WA
    bW = C.bWA
    w_out = WA[:, 0:8 * 1024].rearrange("p (k c) -> p k c", k=8)
    P.dma("pool", w_out, d["w_out"].rearrange("(k p) n -> p k n", p=128), [], [bW], bW)
    WBf = C.WB.bitcast(F32)
    offb = [0]

    def cf(n):
        a = offb[0]
        offb[0] += n
        return WBf[:, a:a + n]

    ot = [cf(TT) for _ in range(8)]; bot = P.bufs("ot", 8)
    sg = [cf(TT) for _ in range(4)]; bsg = P.bufs("sg", 4)
    zt = [cf(TT) for _ in range(2)]; bzt = P.bufs("zt", 2)
    cat = C.act[:, 0:8, :]
    bcat = P.buf("cat")
    C.post_wb = bot + bsg + bzt
    C.post_act = [bcat]
    P.dma("sp", C.vec[:], d["vecs"], [], [C.bvec], C.bvec)
    x_v = d["x1T"].rearrange("(k p) t -> p k t", p=128)
    x_o = d["x2T"].rearrange("(k p) t -> p k t", p=128)
    srcs = [("roT", 0), ("roT", 1), ("moT", 0), ("moT", 1), ("moT", 2), ("moT", 3), ("goT", 0), ("goT", 1)]

    def load(i):
        s = i % 2
        tsl = slice(i * TT, (i + 1) * TT)
        P.dma("sp", C.xt[s][:], x_v[:, :, tsl], [], [C.bx[s]], C.bx[s])
        for c, (name, cc) in enumerate(srcs):
            P.dma("act" if c % 2 else "sp", ot[c], d[name][cc * 128:(cc + 1) * 128, tsl], [], [bot[c]], bot[c])
        for c in range(4):
            r0 = (0, 128, 256, 384)[c]
            P.dma("act" if c % 2 else "sp", sg[c], d["sgT"][r0:r0 + 128, tsl], [], [bsg[c]], bsg[c])

    load(0)
    for i in range(NT):
        s = i % 2
        tsl = slice(i * TT, (i + 1) * TT)
        xt, bx = C.xt[s], C.bx[s]
        for n2, (c, gcol, sgi) in enumerate(((0, V_RO, 0), (1, V_RO + 1, 1), (6, V_GO, 2), (7, V_GO + 1, 3))):
            q = n2 % 2
            norm_from_psum(P, C, [ot[c]], [bot[c]], 128, 128, C.ones2[:], 64, [zt[q]], bzt[q], [gcol], 7)
            P.I("dve", "tensor_tensor", [bzt[q], bsg[sgi]], [bcat], out=cat[:, c, :], in0=zt[q], in1=sg[sgi], op=ALU.mult)
        for c in range(2, 6):
            P.I("act", "copy", [bot[c]], [bcat], out=cat[:, c, :], in_=ot[c])
        if i + 1 < NT:
            load(i + 1)
        for m in range(8):
            r = m % 6
            for k in range(8):
                _mm(P, C.ps[r][:], w_out[:, k, m * 128:(m + 1) * 128], cat[:, k, :], k == 0, k == 7, [bW, bcat], [C.bps[r]])
            P.I("dve", "tensor_tensor", [C.bps[r], bx], [bx], out=xt[:, m, :], in0=C.ps[r][:], in1=xt[:, m, :], op=ALU.add)
        P.dma("sp", x_o[:, :, tsl], xt[:], [bx], [C.bxd[i]], bx, is_out=True)


def build_k3():
    nc = bass.Bass("TRN2", target_bir_lowering=False)

    def din(name, shape, dt=F32):
        return nc.dram_tensor(name, shape, dt, kind="ExternalInput").ap()

    d = dict(x1T=din("x1T", [D, TOK]), roT=din("roT", [256, TOK]), goT=din("goT", [256, TOK]), moT=din("moT", [512, TOK]),
             sgT=din("sgT", [512, TOK]), w_out=din("w_out", [D, D]), vecs=din("vecs", [128, NVEC]))
    wgu = din("wgu", [D, 2 * DFF])
    wd = din("wd", [DFF, D])
    g2 = din("g2", [128, 8])
    d["x2T"] = nc.dram_tensor("x2T", [D, TOK], F32, kind="ExternalOutput").ap()
    x3T = nc.dram_tensor("x3T", [D, TOK], F32, kind="ExternalOutput").ap()
    P = Prog(nc)
    C = tok_ctx(P, nc)
    post_phase(P, nc, C, d)
    for dst, srcs in ((C.bWB, C.post_wb), (C.bact, C.post_act)):
        for o in srcs:
            for k, v in o.writers.items():
                if k not in dst.writers or dst.writers[k].idx < v.idx:
                    dst.writers[k] = v
            for k, v in o.readers.items():
                if k not in dst.readers or dst.readers[k].idx < v.idx:
                    dst.readers[k] = v
    ffn_phase(P, nc, C, d["x2T"], x3T, wgu, wd, g2, src_bufs=C.bxd)
    P.emit()
    return nc


_BF = ml_dtypes.bfloat16
_CACHE = {}


def _get(name, fn):
    if name not in _CACHE:
        _CACHE[name] = fn()
    return _CACHE[name]


def _swap(g):
    return np.concatenate([g[32:], g[:32]])


def _consts():
    p = np.arange(128)
    inv = (10000.0 ** (-(np.arange(0, 64, 2, dtype=np.float32)) / 64.0)).astype(np.float32)
    inv_signed = np.where((p % 64) < 32, -1.0, 1.0).astype(np.float32) * inv[p % 32]
    t = (np.arange(TT) % 128 + 1).astype(np.float64)
    xitab = np.zeros((128, 4, TT), np.float32)
    for k in range(4):
        for half in range(2):
            h = (k % 2) * 2 + half
            lg = np.log1p(-2.0 ** (-5.0 - h))
            row = np.exp(lg * t) if k < 2 else np.exp(-lg * t) * 0.125
            xitab[half * 64:(half + 1) * 64, k, :] = row[None, :]
    s_, t_ = np.meshgrid(np.arange(128), np.arange(128), indexing="ij")
    tri = np.where(s_ <= t_, -1.0 / 16.0, 0.0).astype(np.float32)
    mask = np.where(s_ <= t_, 1.0, 0.0).astype(_BF)
    rdec = np.zeros((4, 64, NCH), np.float32)
    for h in range(4):
        rdec[h] = np.exp(np.log1p(-2.0 ** (-5.0 - h)) * 128.0)
    return dict(inv_signed=inv_signed, xitab=xitab.reshape(128, 4 * TT), tri=tri, mask=mask, rdec=rdec)


def _vecs(inp, l, cst):
    v = np.zeros((128, NVEC), np.float32)
    v[:, V_MIX:V_MIX + 8] = inp["mix_norm"][l].reshape(8, 128).T
    v[:, V_CQ:V_CQ + 3] = inp["mla_q_norm"][l].reshape(3, 128).T
    v[:, V_CKV:V_CKV + 2] = inp["mla_kv_norm"][l].reshape(2, 128).T
    v[:, V_QN] = inp["mla_q_nope_norm"][l]
    v[:, V_KN] = inp["mla_k_nope_norm"][l]
    gq = inp["mla_q_rope_norm"][l]
    gk = inp["mla_k_rope_norm"][l]
    v[:, V_QR] = np.tile(gq, 2)
    v[:, V_QRS] = np.tile(_swap(gq), 2)
    v[:64, V_KR] = gk
    v[:64, V_KRS] = _swap(gk)
    v[:, V_INV] = cst["inv_signed"]
    v[:, V_RO:V_RO + 2] = inp["ret_out_norm"][l].reshape(2, 128).T
    v[:, V_GO:V_GO + 2] = inp["gla_out_norm"][l].reshape(2, 128).T
    return v


def _run(nc, in_maps):
    res = run_bass_kernel_spmd(nc, in_maps, core_ids=list(range(NCORE)))
    return res.results


def _cat_tok(res, name, b):
    return np.concatenate([res[b * 4 + q][name] for q in range(4)], axis=1)


def _cat_rows(res, name, b):
    return np.concatenate([res[b * 4 + q][name] for q in range(4)], axis=0)


def kernel(**inp):
    inp = {k: np.asarray(v) for k, v in inp.items()}
    cst = _get("cst", _consts)
    k1 = _get("k1", build_k1)
    k2a = _get("k2a", lambda: build_k2(phases=(1,)))
    k2b = _get("k2b", lambda: build_k2(phases=(2,)))
    k3 = _get("k3", build_k3)
    x = inp["x"]
    posf = inp["positions"].astype(np.float32)
    xT = [np.ascontiguousarray(x[c // 4, (c % 4) * TOK:(c % 4 + 1) * TOK, :].T) for c in range(NCORE)]
    for l in range(DEPTH):
        vecs = _vecs(inp, l, cst)
        gw = np.concatenate([inp["gla_w_gate_up"][l], inp["gla_gate_bias"][l][None, :]], axis=0)
        ims = []
        for c in range(NCORE):
            b, q = c // 4, c % 4
            ims.append(dict(xT=xT[c], wgu=inp["ffn1_w_gate_up"][l], wd=inp["ffn1_w_down"][l],
                            g1=np.ascontiguousarray(inp["ffn1_norm"][l].reshape(8, 128).T),
                            w_in=inp["w_in"][l], w_uq=inp["mla_w_uq"][l], w_ukv=inp["mla_w_ukv"][l], vecs=vecs,
                            xitab=cst["xitab"], gw=gw, tri=cst["tri"],
                            posf=np.ascontiguousarray(np.broadcast_to(posf[b, q * TOK:(q + 1) * TOK][None, :], (128, TOK)))))
        r1 = _run(k1, ims)
        ima, imb = [], []
        for b in range(B):
            rq, rk = _cat_tok(r1, "rqT", b), _cat_tok(r1, "rkT", b)
            gq, gk = _cat_tok(r1, "gqT", b), _cat_tok(r1, "gkT", b)
            vt = _cat_rows(r1, "vtok", b)
            qn, qr = _cat_tok(r1, "qnT", b), _cat_tok(r1, "qrT", b)
            kn, kr = _cat_tok(r1, "knT", b), _cat_tok(r1, "krT", b)
            vm = _cat_rows(r1, "vmtok", b)
            gdec = _cat_tok(r1, "gdec", b)
            for h in range(4):
                hs = slice(h * 64, (h + 1) * 64)
                ima.append(dict(lqk=np.ascontiguousarray(np.stack([rq[hs], rk[hs], gq[hs], gk[hs]], axis=1)),
                                lkv=np.ascontiguousarray(np.concatenate([rk[hs].T, vt[:, h * 64:(h + 1) * 64], gk[hs].T,
                                                                         vt[:, 256 + h * 64:256 + (h + 1) * 64]], axis=1)),
                                ldec=np.ascontiguousarray(np.stack([cst["rdec"][h], gdec[hs]], axis=1)), mask=cst["mask"]))
                imb.append(dict(qn=np.ascontiguousarray(qn[h * 128:(h + 1) * 128]), qr=np.ascontiguousarray(qr[hs]),
                                kn=np.ascontiguousarray(kn[h * 128:(h + 1) * 128]), kr=kr,
                                vm=np.ascontiguousarray(vm[:, h * 128:(h + 1) * 128]), mask=cst["mask"]))
        r2a = _run(k2a, ima)
        r2b = _run(k2b, imb)
        im3 = []
        for c in range(NCORE):
            b, q = c // 4, c % 4
            ts = slice(q * TOK, (q + 1) * TOK)
            roT = np.concatenate([r2a[b * 4 + h]["lo"][ts, 0:64].T for h in range(4)], axis=0)
            goT = np.concatenate([r2a[b * 4 + h]["lo"][ts, 64:128].T for h in range(4)], axis=0)
            moT = np.concatenate([r2b[b * 4 + h]["moT"][:, ts] for h in range(4)], axis=0)
            im3.append(dict(x1T=r1[c]["x1T"], roT=np.ascontiguousarray(roT), goT=np.ascontiguousarray(goT),
                            moT=np.ascontiguousarray(moT), sgT=r1[c]["sgT"], w_out=inp["w_out"][l], vecs=vecs,
                            wgu=inp["ffn2_w_gate_up"][l], wd=inp["ffn2_w_down"][l],
                            g2=np.ascontiguousarray(inp["ffn2_norm"][l].reshape(8, 128).T)))
        r3 = _run(k3, im3)
        xT = [r3[c]["x3T"] for c in range(NCORE)]
        if _CACHE.get("debug") is not None:
            _CACHE["debug"].append(dict(r1=r1, r2a=r2a, r2b=r2b, r3=r3))
    out = np.empty((B, S, D), np.float32)
    for c in range(NCORE):
        out[c // 4, (c % 4) * TOK:(c % 4 + 1) * TOK, :] = xT[c].T
    return out
```

```python
import numpy as np
import ml_dtypes
import concourse.bass as bass
import concourse.mybir as mybir
from concourse.bass_utils import run_bass_kernel_spmd

F32 = mybir.dt.float32
BF16 = mybir.dt.bfloat16
I32 = mybir.dt.int32
AF = mybir.ActivationFunctionType
ALU = mybir.AluOpType

D = 1024
B = 2
S = 16384
DEPTH = 2
DFF = 2816
NCORE = 8
TOK = B * S // NCORE
TT = 512
NT = TOK // TT
EPS = 1e-6
IN_COLS = 2768
C_RQ, C_RK, C_RV, C_RG = 0, 256, 512, 768
C_CQ, C_CKV, C_KR = 1024, 1408, 1664
C_GQ, C_GK, C_GV, C_GA, C_GR = 1728, 1984, 2240, 2496, 2512


class Buf:
    __slots__ = ("name", "writers", "readers", "sem", "dcount", "excl")

    def __init__(self, name, excl=False):
        self.name = name
        self.excl = excl
        self.writers = {}
        self.readers = {}
        self.sem = None
        self.dcount = 0


class Op:
    __slots__ = ("eng", "fn", "deps", "needs_inc", "sem", "count", "is_dma", "idx")

    def __init__(self, eng, fn, is_dma):
        self.eng = eng
        self.fn = fn
        self.deps = []
        self.needs_inc = False
        self.sem = None
        self.count = 0
        self.is_dma = is_dma


ENGS = ("pe", "act", "dve", "pool", "sp")
ROT = 30000


class Prog:
    def __init__(self, nc):
        self.nc = nc
        self.ops = {e: [] for e in ENGS}
        self.nops = 0
        self.dma_sems = []
        self.out_dmas = []

    def buf(self, name):
        return Buf(name)

    def bufs(self, name, n, excl=False):
        return [Buf(f"{name}{i}", excl) for i in range(n)]

    def _dep(self, op, prod, kind):
        if prod is None or prod is op:
            return
        if not prod.is_dma and prod.eng == op.eng and not op.is_dma:
            if op.eng == "pe" or kind != "raw":
                return
        prod.needs_inc = True
        op.deps.append(prod)

    def add(self, eng, fn, reads=(), writes=(), dma_buf=None, is_out=False):
        is_dma = dma_buf is not None
        op = Op(eng, fn, is_dma)
        op.idx = self.nops
        self.nops += 1
        for b in reads:
            for w in b.writers.values():
                self._dep(op, w, "raw")
            if b.excl:
                for r in b.readers.values():
                    self._dep(op, r, "war")
        for b in writes:
            for w in b.writers.values():
                self._dep(op, w, "waw")
            for r in b.readers.values():
                self._dep(op, r, "war")
        if is_dma:
            if dma_buf.sem is None:
                dma_buf.sem = self.nc.semaphore(f"d{len(self.dma_sems)}_{dma_buf.name}").__enter__()
                self.dma_sems.append(dma_buf.sem)
            dma_buf.dcount += 16
            op.sem = dma_buf.sem
            op.count = dma_buf.dcount
            op.needs_inc = True
            key = ("dma", id(dma_buf))
            if is_out:
                self.out_dmas.append(op)
        else:
            key = eng
        for b in reads:
            b.readers[key] = op
        for b in writes:
            b.writers = {key: op}
            b.readers = {}
        self.ops[eng].append(op)
        return op

    def I(self, eng, meth, reads, writes, *args, **kw):
        return self.add(eng, lambda e: getattr(e, meth)(*args, **kw), reads=reads, writes=writes)

    def dma(self, eng, out, in_, reads, writes, dma_buf, partial=False, is_out=False):
        fn = lambda e: e.dma_start(out=out, in_=in_)
        if partial:
            return self.add_partial_write(eng, fn, reads, writes, dma_buf)
        return self.add(eng, fn, reads, writes, dma_buf, is_out)

    def add_partial_write(self, eng, fn, reads=(), writes=(), dma_buf=None):
        saved = [(b, dict(b.writers), dict(b.readers)) for b in writes]
        op = self.add(eng, fn, reads, writes, dma_buf)
        for b, w, r in saved:
            key = ("dma", id(dma_buf)) if dma_buf is not None else eng
            w = dict(w)
            w[key] = op
            b.writers = w
            b.readers = r
        return op

    def emit(self):
        nc = self.nc
        eng_sems = {}
        for e in ENGS:
            cnt = 0
            sems = []
            for op in self.ops[e]:
                if op.is_dma or not op.needs_inc:
                    continue
                k = cnt // ROT
                if k >= len(sems):
                    sems.append(nc.semaphore(f"c_{e}{k}").__enter__())
                op.sem = sems[k]
                op.count = cnt % ROT + 1
                cnt += 1
            eng_sems[e] = sems
        final_waits = [(op.sem, op.count) for op in self.out_dmas]
        fw = {}
        for s, c in final_waits:
            fw[id(s)] = (s, max(c, fw.get(id(s), (s, 0))[1]))

        def run(engname, eng):
            waited = {}
            for op in self.ops[engname]:
                need = {}
                for p in op.deps:
                    k = id(p.sem)
                    if p.count > need.get(k, (None, 0))[1]:
                        need[k] = (p.sem, p.count)
                for k, (s, c) in need.items():
                    if waited.get(k, 0) >= c:
                        continue
                    eng.wait_ge(s, c)
                    waited[k] = c
                ins = op.fn(eng)
                if op.needs_inc:
                    ins.then_inc(op.sem, 16 if op.is_dma else 1)
            if engname == "sp":
                for s, c in fw.values():
                    eng.wait_ge(s, c)

        with nc.Block() as block:
            @block.tensor
            def _(t):
                run("pe", t)

            @block.scalar
            def _(t):
                run("act", t)

            @block.vector
            def _(t):
                run("dve", t)

            @block.gpsimd
            def _(t):
                run("pool", t)

            @block.sync
            def _(t):
                run("sp", t)


def _mm(P, out_ap, lhsT, rhs, start, stop, reads, writes):
    return P.I("pe", "matmul", reads, writes, out_ap, lhsT, rhs, start=start, stop=stop)


class TokCtx:
    pass


def alloc(nc, name, shape, dt):
    return nc.sbuf_tensor("s_" + name, shape, dt).__enter__()


def ffn_phase(P, nc, C, x_src, x_dst, wgu_d, wd_d, gain_d, src_bufs=None, dst_bufs=None):
    WA, WB = C.WA, C.WB
    wgu = WA[:, 0:8 * 5632].rearrange("p (k c) -> p k c", k=8)
    wd = WB[:, 0:22 * 1024].rearrange("p (k c) -> p k c", k=22)
    for k in range(8):
        P.dma("pool", wgu[:, k, :], wgu_d[k * 128:(k + 1) * 128, :], [], [C.bWA], C.bWA, partial=(k > 0))
    wd_v = wd_d.rearrange("(k p) n -> p k n", p=128)
    for k0 in range(0, 22, 11):
        P.dma("pool", wd[:, k0:k0 + 11, :], wd_v[:, k0:k0 + 11, :], [], [C.bWB], C.bWB, partial=(k0 > 0))
    P.dma("sp", C.gain[:, 0:8], gain_d, [], [C.bgain], C.bgain)

    x_src_v = x_src.rearrange("(k p) t -> p k t", p=128)
    x_dst_v = x_dst.rearrange("(k p) t -> p k t", p=128)

    def load(i):
        s = i % 2
        P.dma("sp", C.xt[s][:], x_src_v[:, :, i * TT:(i + 1) * TT], [src_bufs[i]] if src_bufs else [], [C.bx[s]], C.bx[s])

    load(0)
    for i in range(NT):
        s = i % 2
        if i + 1 < NT:
            load(i + 1)
        xt = C.xt[s]
        bx = C.bx[s]
        yb = C.bps[6]
        ps = C.ps[6]
        for k in range(8):
            q = k % 2
            P.I("act", "activation", [bx], [C.bsq[q]], out=C.sq[q][:], in_=xt[:, k, :], func=AF.Square)
            _mm(P, ps[:], C.ones[:], C.sq[q][:], k == 0, k == 7, [C.bsq[q], C.bones], [yb])
        P.I("act", "activation", [yb, C.bconst], [C.brstd], out=C.rstd[:], in_=ps[:], func=AF.Ln,
            bias=C.epsc[:, 0:1], scale=1.0 / D)
        P.I("act", "activation", [C.brstd], [C.brstd], out=C.rstd[:], in_=C.rstd[:], func=AF.Exp, scale=-0.5)
        for k in range(8):
            P.I("dve", "scalar_tensor_tensor", [bx, C.brstd, C.bgain], [C.bhT], out=C.hT[:, k, :], in0=xt[:, k, :],
                scalar=C.gain[:, k:k + 1], in1=C.rstd[:], op0=ALU.mult, op1=ALU.mult)
        for j in range(22):
            r = j % 3
            gb, ub = C.bps[r], C.bps[3 + r]
            gp, up = C.ps[r], C.ps[3 + r]
            for k in range(8):
                _mm(P, gp[:], wgu[:, k, j * 128:(j + 1) * 128], C.hT[:, k, :], k == 0, k == 7, [C.bWA, C.bhT], [gb])
            for k in range(8):
                _mm(P, up[:], wgu[:, k, DFF + j * 128:DFF + (j + 1) * 128], C.hT[:, k, :], k == 0, k == 7,
                    [C.bWA, C.bhT], [ub])
            q = j % 2
            P.I("act", "activation", [gb], [C.bstmp[q]], out=C.stmp[q][:], in_=gp[:], func=AF.Silu)
            P.I("dve", "tensor_tensor", [ub, C.bstmp[q]], [C.bact], out=C.act[:, j, :], in0=up[:], in1=C.stmp[q][:],
                op=ALU.mult)
        for m in range(8):
            r = 6 + (m % 2)
            yb, yp = C.bps[r], C.ps[r]
            for k in range(22):
                _mm(P, yp[:], wd[:, k, m * 128:(m + 1) * 128], C.act[:, k, :], k == 0, k == 21, [C.bWB, C.bact], [yb])
            P.I("dve", "scalar_tensor_tensor", [yb, bx], [bx], out=xt[:, m, :], in0=yp[:], scalar=0.5, in1=xt[:, m, :],
                op0=ALU.mult, op1=ALU.add)
        P.dma("sp", x_dst_v[:, :, i * TT:(i + 1) * TT], xt[:], [bx], [dst_bufs[i]] if dst_bufs else [], bx, is_out=True)


def tok_ctx(P, nc):
    C = TokCtx()
    C.WA = alloc(nc, "WA", [128, 8 * 5632], BF16)
    C.WB = alloc(nc, "WB", [128, 22 * 1024], BF16)
    C.bWA, C.bWB = P.buf("WA"), P.buf("WB")
    C.xt = [alloc(nc, f"xt{i}", [128, 8, TT], F32) for i in range(2)]
    C.bx = P.bufs("x", 2)
    C.hT = alloc(nc, "hT", [128, 8, TT], BF16)
    C.bhT = P.buf("hT")
    C.act = alloc(nc, "act", [128, 22, TT], BF16)
    C.bact = P.buf("act")
    C.sq = [alloc(nc, f"sq{i}", [128, TT], BF16) for i in range(2)]
    C.bsq = P.bufs("sq", 2)
    C.rstd = alloc(nc, "rstd", [128, TT], F32)
    C.brstd = P.buf("rstd")
    C.stmp = [alloc(nc, f"stmp{i}", [128, TT], BF16) for i in range(2)]
    C.bstmp = P.bufs("stmp", 2)
    C.gain = alloc(nc, "gain", [128, 32], F32)
    C.bgain = P.buf("gain")
    C.ones = alloc(nc, "ones", [128, 128], BF16)
    C.bones = P.buf("ones")
    C.epsc = alloc(nc, "epsc", [128, 4], F32)
    C.vec = alloc(nc, "vec", [128, NVEC], F32)
    C.bvec = P.buf("vec")
    C.ones2 = alloc(nc, "ones2", [128, 128], BF16)
    C.bones2 = C.bones
    C.bxd = P.bufs("xd", NT)
    C.bconst = P.buf("const")
    C.ps = [nc.psum_tensor(f"ps{i}", [128, 512], F32).__enter__() for i in range(8)]
    C.bps = P.bufs("ps", 8, excl=True)
    P.I("dve", "memset", [], [C.bones], C.ones[:], 1.0)
    P.I("dve", "memset", [], [C.bconst], C.epsc[:], EPS)
    P.I("dve", "memset", [C.bconst], [C.bconst], C.epsc[:, 1:2], 1.0)
    P.I("dve", "memset", [C.bconst], [C.bconst], C.epsc[:, 2:3], float(np.log(0.125)))
    P.I("pool", "memset", [], [C.bones], C.ones2[:], 0.0)
    P.I("pool", "memset", [C.bones], [C.bones], C.ones2[0:64, 0:64], 1.0)
    P.I("pool", "memset", [C.bones], [C.bones], C.ones2[64:128, 64:128], 1.0)
    return C


def build_k1_test():
    nc = bass.Bass("TRN2", target_bir_lowering=False)
    xT = nc.dram_tensor("xT", [D, TOK], F32, kind="ExternalInput").ap()
    wgu = nc.dram_tensor("wgu", [D, 2 * DFF], F32, kind="ExternalInput").ap()
    wd = nc.dram_tensor("wd", [DFF, D], F32, kind="ExternalInput").ap()
    g1 = nc.dram_tensor("g1", [128, 8], F32, kind="ExternalInput").ap()
    x1T = nc.dram_tensor("x1T", [D, TOK], F32, kind="ExternalOutput").ap()
    P = Prog(nc)
    C = tok_ctx(P, nc)
    ffn_phase(P, nc, C, xT, x1T, wgu, wd, g1)
    P.emit()
    return nc


NQG = S // 512
NCH = S // 128
SCALE_MLA = 192.0 ** -0.5


def build_k2(phases=(1, 2)):
    nc = bass.Bass("TRN2", target_bir_lowering=False)

    def din(name, shape, dt):
        return nc.dram_tensor(name, shape, dt, kind="ExternalInput").ap()

    if 2 in phases:
        qn_d = din("qn", [128, S], BF16)
        qr_d = din("qr", [64, S], BF16)
        kn_d = din("kn", [128, S], BF16)
        kr_d = din("kr", [64, S], BF16)
        vm_d = din("vm", [S, 128], BF16)
        mo_d = nc.dram_tensor("moT", [128, S], F32, kind="ExternalOutput").ap()
    if 1 in phases:
        lqk_d = din("lqk", [64, 4, S], BF16)
        lkv_d = din("lkv", [S, 256], BF16)
        ldec_d = din("ldec", [64, 2, NCH], F32)
        lo_d = nc.dram_tensor("lo", [S, 128], F32, kind="ExternalOutput").ap()
    mask_d = din("mask", [128, 128], BF16)

    P = Prog(nc)
    ps = [nc.psum_tensor(f"ps{i}", [128, 512], F32).__enter__() for i in range(8)]
    bps = P.bufs("ps", 8, excl=True)
    mask = alloc(nc, "mask", [128, 128], BF16)
    bmask = P.buf("mask")
    P.dma("sp", mask[:], mask_d, [], [bmask], bmask)

    L = {}
    NB = 3
    if 1 in phases:
        qk = [alloc(nc, f"lqk{i}", [64, 4, 512], BF16) for i in range(NB)]
        kv = [alloc(nc, f"lkv{i}", [128, 4, 256], BF16) for i in range(NB)]
        bin_ = P.bufs("lin", NB)
        dec = alloc(nc, "ldec", [64, 2, NCH], F32)
        bdec = P.buf("ldec")
        osb = [alloc(nc, f"losb{i}", [128, 4, 128], F32) for i in range(2)]
        bosb = P.bufs("losb", 2)
        P.dma("sp", dec[:], ldec_d, [], [bdec], bdec)
    for xi, X in enumerate(("r", "g") if 1 in phases else ()):
        o = TokCtx()
        o.qi, o.ki, o.kti, o.vi, o.oi = 2 * xi, 2 * xi + 1, 128 * xi, 128 * xi + 64, 64 * xi
        o.scm = [alloc(nc, f"{X}scm{i}", [128, 128], BF16) for i in range(2)]
        o.bscm = P.bufs(X + "scm", 2)
        o.st = alloc(nc, X + "st", [64, 64], F32)
        o.tmp = alloc(nc, X + "tmp", [64, 64], F32)
        o.stb = alloc(nc, X + "stb", [64, 64], BF16)
        o.bst, o.btmp, o.bstb = P.buf(X + "st"), P.buf(X + "tmp"), P.buf(X + "stb")
        o.xi = xi
        L[X] = o
        P.I("dve", "memset", [], [o.bst], o.st[:], 0.0)
        P.I("dve", "memset", [], [o.bstb], o.stb[:], 0.0)

    def lin_load(g):
        s = g % NB
        t0 = g * 512
        P.dma("sp", qk[s][:], lqk_d[:, :, t0:t0 + 512], [], [bin_[s]], bin_[s])
        P.dma("act", kv[s][:], lkv_d[t0:t0 + 512, :].rearrange("(n p) d -> p n d", p=128), [], [bin_[s]], bin_[s],
              partial=True)

    def lin_A(n):
        g, c = n // 4, n % 4
        s = g % NB
        cs = slice(c * 128, (c + 1) * 128)
        for xi, X in enumerate(("r", "g")):
            o = L[X]
            sb = xi * 2 + (n % 2)
            m2 = n % 2
            _mm(P, ps[sb][:, 0:128], qk[s][:, o.ki, cs], qk[s][:, o.qi, cs], True, True, [bin_[s]], [bps[sb]])
            P.I("dve", "tensor_tensor", [bps[sb], bmask], [o.bscm[m2]], out=o.scm[m2][:], in0=ps[sb][:, 0:128],
                in1=mask[:], op=ALU.mult)

    def lin_C(n):
        g, c = n // 4, n % 4
        s = g % NB
        so = g % 2
        cs = slice(c * 128, (c + 1) * 128)
        for xi, X in enumerate(("r", "g")):
            o = L[X]
            ob = 4 + xi * 2 + (n % 2)
            m2 = n % 2
            vv = kv[s][:, c, o.vi:o.vi + 64]
            _mm(P, ps[ob][:, 0:64], o.scm[m2][:], vv, True, False, [o.bscm[m2], bin_[s]], [bps[ob]])
            _mm(P, ps[ob][:, 0:64], qk[s][:, o.qi, cs], o.stb[:], False, True, [bin_[s], o.bstb], [bps[ob]])
            _mm(P, ps[ob][0:64, 64:128], kv[s][:, c, o.kti:o.kti + 64], vv, True, True, [bin_[s]], [bps[ob]])
            P.I("dve", "scalar_tensor_tensor", [bps[ob], bdec, o.btmp], [o.bstb], out=o.stb[:], in0=ps[ob][0:64, 64:128],
                scalar=dec[:, xi, n:n + 1], in1=o.tmp[:], op0=ALU.mult, op1=ALU.add)
            P.I("dve", "scalar_tensor_tensor", [bps[ob], bdec, o.btmp], [o.bst], out=o.st[:], in0=ps[ob][0:64, 64:128],
                scalar=dec[:, xi, n:n + 1], in1=o.tmp[:], op0=ALU.mult, op1=ALU.add)
            P.I("act", "copy", [bps[ob]], [bosb[so]], out=osb[so][:, c, o.oi:o.oi + 64], in_=ps[ob][:, 0:64])
            if n + 1 < NCH:
                P.I("dve", "tensor_scalar", [o.bst, bdec], [o.btmp], out=o.tmp[:], in0=o.st[:],
                    scalar1=dec[:, xi, n + 1:n + 2], scalar2=None, op0=ALU.mult)
        if c == 3:
            P.dma("sp", lo_d[g * 512:(g + 1) * 512, :].rearrange("(n p) d -> p n d", p=128), osb[so][:],
                  [bosb[so]], [], bosb[so], is_out=True)

    if 1 in phases:
        for X in ("r", "g"):
            P.I("dve", "memset", [], [L[X].btmp], L[X].tmp[:], 0.0)
        for g0 in range(NB):
            lin_load(g0)
        lin_A(0)
        for n in range(NCH):
            if n + 1 < NCH:
                lin_A(n + 1)
            lin_C(n)
            if n % 4 == 3 and n // 4 + NB < NQG:
                lin_load(n // 4 + NB)

    if 2 not in phases:
        P.emit()
        return nc
    kn = alloc(nc, "kn", [128, S], BF16)
    kr = alloc(nc, "kr", [64, S], BF16)
    V = alloc(nc, "V", [128, NCH, 128], BF16)
    bkv = P.bufs("kv", NQG)
    kvsem = P.bufs("kvsem", 4)
    qn = [alloc(nc, f"qn{i}", [128, 512], BF16) for i in range(2)]
    qr = [alloc(nc, f"qr{i}", [64, 512], BF16) for i in range(2)]
    bq = P.bufs("q", 2)
    NSC = 4
    pt = [alloc(nc, f"pt{i}", [128, 512], BF16) for i in range(NSC)]
    bpt = P.bufs("pt", NSC)
    psum_t = [alloc(nc, f"ptsum{i}", [128, 512], F32) for i in range(2)]
    bpsum = P.bufs("ptsum", 2)
    rec = [alloc(nc, f"rec{i}", [128, 512], F32) for i in range(2)]
    brec = P.bufs("rec", 2)
    mosb = [alloc(nc, f"mosb{i}", [128, 512], F32) for i in range(2)]
    bmosb = P.bufs("mosb", 2)
    onesf = alloc(nc, "onesf", [128, 128], F32)
    bonesf = P.buf("onesf")
    P.I("pool", "memset", [], [bonesf], onesf[:], 1.0)
    vm_v = vm_d.rearrange("(n p) d -> p n d", p=128)

    def kv_load(g):
        t0 = g * 512
        sb = kvsem[g % 4]
        rd = [bkv[g - 4]] if g >= 4 else []
        P.dma("sp", kn[:, t0:t0 + 512], kn_d[:, t0:t0 + 512], rd, [bkv[g]], sb)
        P.dma("sp", kr[:, t0:t0 + 512], kr_d[:, t0:t0 + 512], [], [bkv[g]], sb, partial=True)
        P.dma("sp", V[:, 4 * g:4 * g + 4, 0:128], vm_v[:, 4 * g:4 * g + 4, :], [], [bkv[g]], sb, partial=True)

    def q_load(g):
        s = g % 2
        t0 = g * 512
        P.dma("sp", qn[s][:], qn_d[:, t0:t0 + 512], [], [bq[s]], bq[s])
        P.dma("sp", qr[s][:], qr_d[:, t0:t0 + 512], [], [bq[s]], bq[s], partial=True)

    LOOK = 3
    SUMB = 6
    blocks = [(g, kb) for g in range(NQG) for kb in range(4 * g + 4)]
    nblk = len(blocks)
    kv_load(0)
    q_load(0)

    def emit_sc(i):
        g, kb = blocks[i]
        s = g % 2
        if kb == 0 and g + 1 < NQG:
            kv_load(g + 1)
            q_load(g + 1)
        j = kb - 4 * g
        c0 = 128 * j if j > 0 else 0
        r = i % NSC
        ks = slice(kb * 128, (kb + 1) * 128)
        kvb = bkv[kb // 4]
        _mm(P, ps[r][:, c0:512], kn[:, ks], qn[s][:, c0:512], True, False, [kvb, bq[s]], [bps[r]])
        _mm(P, ps[r][:, c0:512], kr[:, ks], qr[s][:, c0:512], False, True, [kvb, bq[s]], [bps[r]])
        P.I("act", "activation", [bps[r]], [bpt[r]], out=pt[r][:, c0:512], in_=ps[r][:, c0:512], func=AF.Exp,
            scale=SCALE_MLA)
        if j >= 0:
            P.I("pool", "tensor_tensor", [bpt[r], bmask], [bpt[r]], out=pt[r][:, 128 * j:128 * j + 128],
                in0=pt[r][:, 128 * j:128 * j + 128], in1=mask[:], op=ALU.mult)
        if kb == 0:
            P.I("dve", "tensor_copy", [bpt[r]], [bpsum[s]], out=psum_t[s][:], in_=pt[r][:])
        else:
            P.I("dve", "tensor_tensor", [bpt[r], bpsum[s]], [bpsum[s]], out=psum_t[s][:, c0:512], in0=psum_t[s][:, c0:512],
                in1=pt[r][:, c0:512], op=ALU.add)

    def emit_pv(i):
        g, kb = blocks[i]
        s = g % 2
        j = kb - 4 * g
        c0 = 128 * j if j > 0 else 0
        r = i % NSC
        kvb = bkv[kb // 4]
        ab = 4 + s
        _mm(P, ps[ab][:, c0:512], V[:, kb, 0:128], pt[r][:, c0:512], kb == 0, kb == 4 * g + 3, [bpt[r], kvb], [bps[ab]])
        if kb == 4 * g + 3:
            P.I("pe", "matmul", [bpsum[s], bonesf], [bps[SUMB]], ps[SUMB][:], onesf[:], psum_t[s][:], start=True, stop=True)
            P.I("dve", "reciprocal", [bps[SUMB]], [brec[s]], out=rec[s][:], in_=ps[SUMB][:])
            P.I("dve", "tensor_tensor", [bps[ab], brec[s]], [bmosb[s]], out=mosb[s][:], in0=ps[ab][:], in1=rec[s][:], op=ALU.mult)
            P.dma("sp", mo_d[:, g * 512:(g + 1) * 512], mosb[s][:], [bmosb[s]], [], bmosb[s], is_out=True)

    for i in range(nblk + LOOK):
        if i < nblk:
            emit_sc(i)
        if i >= LOOK:
            emit_pv(i - LOOK)
    P.emit()
    return nc


TWO_PI = 2.0 * np.pi
MAGIC = 12582912.0
CW1 = 6.28125
CW2 = TWO_PI - 6.28125
V_MIX = 0
V_CQ = 8
V_CKV = 11
V_QN = 13
V_KN = 14
V_QR = 15
V_QRS = 16
V_KR = 17
V_KRS = 18
V_INV = 19
V_RO = 20
V_GO = 22
NVEC = 24


def norm_from_psum(P, C, src_aps, src_bufs, K, nparts, ones_ap, n_norm, out_aps, out_buf, gain_cols, stat_bank):
    sb, sp = C.bps[stat_bank], C.ps[stat_bank]
    n = len(src_aps)
    for k in range(n):
        q = k % 2
        P.I("act", "activation", [src_bufs[k]], [C.bsq[q]], out=C.sq[q][0:nparts, :], in_=src_aps[k], func=AF.Square)
        _mm(P, sp[0:nparts, :], ones_ap, C.sq[q][0:nparts, :], k == 0, k == n - 1, [C.bsq[q], C.bones], [sb])
    P.I("act", "activation", [sb, C.bconst], [C.brstd], out=C.rstd[0:nparts, :], in_=sp[0:nparts, :], func=AF.Ln,
        bias=C.epsc[0:nparts, 0:1], scale=1.0 / n_norm)
    P.I("act", "activation", [C.brstd], [C.brstd], out=C.rstd[0:nparts, :], in_=C.rstd[0:nparts, :], func=AF.Exp, scale=-0.5)
    if out_aps is not None:
        for k in range(n):
            P.I("dve", "scalar_tensor_tensor", [src_bufs[k], C.brstd, C.bvec], [out_buf], out=out_aps[k], in0=src_aps[k],
                scalar=C.vec[0:nparts, gain_cols[k]:gain_cols[k] + 1], in1=C.rstd[0:nparts, :], op0=ALU.mult, op1=ALU.mult)


def norm_gen(P, C, src_aps, src_bufs, nparts, ones_ap, n_norm, out_aps, out_buf, gain_cols, stat_bank):
    sb, sp = C.bps[stat_bank], C.ps[stat_bank]
    n = len(src_aps)
    if n <= 2:
        for k in range(n):
            P.I("act", "activation", [src_bufs[k]], [C.bsq[k]], out=C.sq[k][0:nparts, :], in_=src_aps[k], func=AF.Square)
        yield None
        for k in range(n):
            _mm(P, sp[0:nparts, :], ones_ap, C.sq[k][0:nparts, :], k == 0, k == n - 1, [C.bsq[k], C.bones], [sb])
    else:
        yield None
        for k in range(n):
            q = k % 2
            P.I("act", "activation", [src_bufs[k]], [C.bsq[q]], out=C.sq[q][0:nparts, :], in_=src_aps[k], func=AF.Square)
            _mm(P, sp[0:nparts, :], ones_ap, C.sq[q][0:nparts, :], k == 0, k == n - 1, [C.bsq[q], C.bones], [sb])
    P.I("act", "activation", [sb, C.bconst], [C.brstd], out=C.rstd[0:nparts, :], in_=sp[0:nparts, :], func=AF.Ln,
        bias=C.epsc[0:nparts, 0:1], scale=1.0 / n_norm)
    P.I("act", "activation", [C.brstd], [C.brstd], out=C.rstd[0:nparts, :], in_=C.rstd[0:nparts, :], func=AF.Exp, scale=-0.5)
    if out_aps is not None:
        for k in range(n):
            P.I("dve", "scalar_tensor_tensor", [src_bufs[k], C.brstd, C.bvec], [out_buf], out=out_aps[k], in0=src_aps[k],
                scalar=C.vec[0:nparts, gain_cols[k]:gain_cols[k] + 1], in1=C.rstd[0:nparts, :], op0=ALU.mult, op1=ALU.mult)
    yield None


def proj_phase(P, nc, C, d):
    WA = C.WA
    off = [0]

    def carve(n):
        a = off[0]
        off[0] += n
        return WA[:, a:a + n]

    bW = C.bWA
    w_in = carve(8 * IN_COLS).rearrange("p (k c) -> p k c", k=8)
    w_sw = carve(8 * 576).rearrange("p (k c) -> p k c", k=8)
    wuq_n = carve(3 * 512).rearrange("p (k c) -> p k c", k=3)
    wuq_r = carve(3 * 256).rearrange("p (k c) -> p k c", k=3)
    wuq_rs = carve(3 * 256).rearrange("p (k c) -> p k c", k=3)
    wkv_k = carve(2 * 512).rearrange("p (k c) -> p k c", k=2)
    wkv_v = carve(2 * 512).rearrange("p (k c) -> p k c", k=2)
    first = [True]

    def wdma(out, in_):
        P.dma("pool", out, in_, [], [bW], bW, partial=not first[0])
        first[0] = False

    win_d = d["w_in"]
    for k in range(8):
        rows = slice(k * 128, (k + 1) * 128)
        wdma(w_in[:, k, :], win_d[rows, :])
        src = win_d[rows, 0:512].rearrange("p (h t c) -> p h t c", h=8, t=2)
        dst = w_sw[:, k, 0:512].rearrange("p (h t c) -> p h t c", h=8, t=2)
        wdma(dst[:, :, 0, :], src[:, :, 1, :])
        wdma(dst[:, :, 1, :], src[:, :, 0, :])
        wdma(w_sw[:, k, 512:544], win_d[rows, C_KR + 32:C_KR + 64])
        wdma(w_sw[:, k, 544:576], win_d[rows, C_KR:C_KR + 32])
    for k in range(3):
        rows = slice(k * 128, (k + 1) * 128)
        src = d["w_uq"][rows, :].rearrange("p (h c) -> p h c", h=4)
        wdma(wuq_n[:, k, :].rearrange("p (h c) -> p h c", h=4), src[:, :, 0:128])
        wdma(wuq_r[:, k, :].rearrange("p (h c) -> p h c", h=4), src[:, :, 128:192])
        dsts = wuq_rs[:, k, :].rearrange("p (h c) -> p h c", h=4)
        wdma(dsts[:, :, 0:32], src[:, :, 160:192])
        wdma(dsts[:, :, 32:64], src[:, :, 128:160])
    for k in range(2):
        rows = slice(k * 128, (k + 1) * 128)
        src = d["w_ukv"][rows, :].rearrange("p (h c) -> p h c", h=4)
        wdma(wkv_k[:, k, :].rearrange("p (h c) -> p h c", h=4), src[:, :, 0:128])
        wdma(wkv_v[:, k, :].rearrange("p (h c) -> p h c", h=4), src[:, :, 128:256])

    def alias(*olds):
        b = Buf("al")
        for o in (olds or (C.bWA, C.bWB, C.bact)):
            for k, v in o.writers.items():
                if k not in b.writers or b.writers[k].idx < v.idx:
                    b.writers[k] = v
            for k, v in o.readers.items():
                if k not in b.readers or b.readers[k].idx < v.idx:
                    b.readers[k] = v
        return b

    assert off[0] % 2 == 0
    WAf = WA.bitcast(F32)
    WBf = C.WB.bitcast(F32)
    offa = [off[0] // 2]
    offb = [0]

    def cf(n, region="b"):
        o_, t_ = (offb, WBf) if region == "b" else (offa, WAf)
        a = o_[0]
        o_[0] += n
        return t_[:, a:a + n]

    xitab = cf(4 * TT, "a").rearrange("p (k c) -> p k c", k=4); bxi = alias()
    Eq = [cf(TT, "a") for _ in range(2)]; Ek = [cf(TT, "a") for _ in range(2)]; bE = [alias() for _ in range(2)]
    assert offa[0] <= 8 * 5632 // 2, offa[0]
    pos = cf(TT); bpos = alias()
    ang = cf(TT); bang = alias()
    tk = cf(TT); btk = alias()
    r1 = cf(TT); br1 = alias()
    Ssb = cf(TT); bS = alias()
    Csb = cf(TT); bC = alias()
    GCq = cf(TT); GSq = cf(TT); bGq = alias()
    GCk = cf(TT); GSk = cf(TT); bGk = alias()
    t1 = [cf(TT) for _ in range(2)]; bt1 = [alias() for _ in range(2)]
    t2 = [cf(TT) for _ in range(2)]; bt2 = [alias() for _ in range(2)]
    sgs = [cf(TT) for _ in range(2)]; bsgs = [alias() for _ in range(2)]
    alow = cf(TT); balow = alias()
    gw = cf(256); bgw = alias()
    tri = cf(128); btri = alias()
    zl = cf(256); bzl = alias()
    decs = cf(8).rearrange("p (k c) -> p k c", k=2); bdecs = alias()
    rstd2 = cf(TT); brstd2 = alias()
    sq2 = []
    for _ in range(2):
        a_ = offb[0]
        offb[0] += TT // 2
        sq2.append(C.WB[:, 2 * a_:2 * a_ + TT])
    bsq2 = [alias(), alias()]
    hT1 = WA[:, off[0] + 8 * TT * 2:off[0] + 8 * TT * 2 + 8 * TT].rearrange("p (k t) -> p k t", k=8)
    assert off[0] + 8 * TT * 2 + 8 * TT <= 8 * 5632
    hT_set = [(C.hT, C.bhT), (hT1, alias())]
    x1s = C.xt[1]
    TBS = [dict(pos=pos, bpos=bpos, S=Ssb, bS=bS, C=Csb, bC=bC, GCq=GCq, GSq=GSq, bGq=bGq, GCk=GCk, GSk=GSk, bGk=bGk),
           dict(pos=x1s[:, 0, :], bpos=alias(C.bx[1]), S=x1s[:, 1, :], bS=alias(C.bx[1]), C=x1s[:, 2, :], bC=alias(C.bx[1]),
                GCq=x1s[:, 3, :], GSq=x1s[:, 4, :], bGq=alias(C.bx[1]), GCk=x1s[:, 5, :], GSk=x1s[:, 6, :],
                bGk=alias(C.bx[1]))]
    assert offb[0] <= 22 * 1024 // 2, offb[0]

    nb = [0]

    def stage_bf(n):
        a = nb[0]
        nb[0] += n
        assert nb[0] <= 22
        return C.act[:, a:a + n, :], alias()

    cqn, bcqn = stage_bf(3)
    ckvn, bckvn = stage_bf(2)
    vst, bvst = stage_bf(4)
    ostage = [stage_bf(1) for _ in range(8)]
    nst = [0]

    def next_stage():
        a = ostage[nst[0] % len(ostage)]
        nst[0] += 1
        return a[0][:, 0, :], a[1]

    P.dma("sp", C.vec[:], d["vecs"], [], [C.bvec], C.bvec)
    P.dma("sp", xitab, d["xitab"].rearrange("p (k c) -> p k c", k=4), [], [bxi], bxi)
    P.dma("sp", gw[0:17, :], d["gw"], [], [bgw], bgw)
    P.dma("sp", tri, d["tri"], [], [btri], btri)
    P.I("pool", "memset", [], [balow], alow[0:32, :], 1.0)

    x_v = d["x1T"].rearrange("(k p) t -> p k t", p=128)

    def load(i):
        P.dma("sp", C.xt[0][:], x_v[:, :, i * TT:(i + 1) * TT], [C.bxd[i]], [C.bx[0]], C.bx[0])

    load(0)
    bank = [0]

    lane = [0]
    bank2 = [0]

    def nb_():
        if lane[0] == 0:
            b = bank[0] % 3
            bank[0] += 1
        else:
            b = 3 + bank2[0] % 3
            bank2[0] += 1
        return b

    def proj_chunk(wt, col0, M=128):
        b = nb_()
        for k in range(8):
            _mm(P, C.ps[b][0:M, :], wt[:, k, col0:col0 + M], HT[0][:, k, :], k == 0, k == 7, [bW, HT[1]], [C.bps[b]])
        return b

    def store(dram_ap, sb_ap, buf):
        P.dma("sp", dram_ap, sb_ap, [buf], [], buf, is_out=True)

    def rope_combine(bx_, bs_, M, cos_ap, sin_ap, rbufs, post_ap, post_bufs, out_ap, out_buf, q):
        P.I("dve", "tensor_tensor", [C.bps[bx_]] + rbufs, [bt1[q]], out=t1[q][0:M, :], in0=C.ps[bx_][0:M, :], in1=cos_ap,
            op=ALU.mult)
        P.I("dve", "tensor_tensor", [C.bps[bs_]] + rbufs, [bt2[q]], out=t2[q][0:M, :], in0=C.ps[bs_][0:M, :], in1=sin_ap,
            op=ALU.mult)
        P.I("dve", "tensor_tensor", [bt1[q], bt2[q]], [bt1[q]], out=t1[q][0:M, :], in0=t1[q][0:M, :], in1=t2[q][0:M, :],
            op=ALU.add)
        P.I("dve", "tensor_tensor", [bt1[q]] + post_bufs, [out_buf], out=out_ap, in0=t1[q][0:M, :], in1=post_ap, op=ALU.mult)

    def prep(i, t):
        xt, bx = C.xt[0], C.bx[0]
        tsl = slice(i * TT, (i + 1) * TT)
        hT_t, bhT_t = hT_set[t]
        T = TBS[t]
        for k in range(8):
            q = k % 2
            P.I("act", "activation", [bx], [bsq2[q]], out=sq2[q], in_=xt[:, k, :], func=AF.Square)
            _mm(P, C.ps[6][:], C.ones[:], sq2[q], k == 0, k == 7, [bsq2[q], C.bones], [C.bps[6]])
        P.I("act", "activation", [C.bps[6], C.bconst], [brstd2], out=rstd2, in_=C.ps[6][:], func=AF.Ln,
            bias=C.epsc[:, 0:1], scale=1.0 / D)
        P.I("act", "activation", [brstd2], [brstd2], out=rstd2, in_=rstd2, func=AF.Exp, scale=-0.5)
        for k in range(8):
            P.I("dve", "scalar_tensor_tensor", [bx, brstd2, C.bvec], [bhT_t], out=hT_t[:, k, :], in0=xt[:, k, :],
                scalar=C.vec[:, V_MIX + k:V_MIX + k + 1], in1=rstd2, op0=ALU.mult, op1=ALU.mult)
        P.dma("sp", T["pos"], d["posf"][:, tsl], [], [T["bpos"]], T["bpos"])
        for (dst, bd, shift) in ((T["S"], T["bS"], 0.0), (T["C"], T["bC"], 0.5 * np.pi)):
            P.I("dve", "tensor_scalar", [T["bpos"], C.bvec], [bang], out=ang, in0=T["pos"], scalar1=C.vec[:, V_INV:V_INV + 1],
                scalar2=shift, op0=ALU.mult, op1=ALU.add)
            P.I("dve", "tensor_scalar", [bang], [btk], out=tk, in0=ang, scalar1=1.0 / TWO_PI, scalar2=MAGIC,
                op0=ALU.mult, op1=ALU.add)
            P.I("dve", "tensor_scalar", [btk], [btk], out=tk, in0=tk, scalar1=-MAGIC, scalar2=None, op0=ALU.add)
            P.I("dve", "scalar_tensor_tensor", [btk, bang], [br1], out=r1, in0=tk, scalar=-CW1, in1=ang,
                op0=ALU.mult, op1=ALU.add)
            P.I("dve", "scalar_tensor_tensor", [btk, br1], [br1], out=r1, in0=tk, scalar=-CW2, in1=r1,
                op0=ALU.mult, op1=ALU.add)
            P.I("dve", "tensor_scalar", [br1], [br1], out=r1, in0=r1, scalar1=-np.pi, scalar2=np.pi, op0=ALU.max, op1=ALU.min)
            P.I("act", "activation", [br1], [bd], out=dst, in_=r1, func=AF.Sin)
        P.I("pool", "tensor_scalar", [T["bC"], C.bvec], [T["bGq"]], out=T["GCq"], in0=T["C"], scalar1=C.vec[:, V_QR:V_QR + 1],
            scalar2=None, op0=ALU.mult)
        P.I("pool", "tensor_scalar", [T["bS"], C.bvec], [T["bGq"]], out=T["GSq"], in0=T["S"], scalar1=C.vec[:, V_QRS:V_QRS + 1],
            scalar2=None, op0=ALU.mult)
        P.I("pool", "tensor_scalar", [T["bC"], C.bvec], [T["bGk"]], out=T["GCk"][0:64, :], in0=T["C"][0:64, :],
            scalar1=C.vec[0:64, V_KR:V_KR + 1], scalar2=None, op0=ALU.mult)
        P.I("pool", "tensor_scalar", [T["bS"], C.bvec], [T["bGk"]], out=T["GSk"][0:64, :], in0=T["S"][0:64, :],
            scalar1=C.vec[0:64, V_KRS:V_KRS + 1], scalar2=None, op0=ALU.mult)

    prep(0, 0)
    for i in range(NT):
        tsl = slice(i * TT, (i + 1) * TT)
        HT = hT_set[i % 2]
        TB = TBS[i % 2]
        if i + 1 < NT:
            load(i + 1)

        def u_ret(c0, tab0, dname, cc):
            def f():
                bxp = proj_chunk(w_in, c0 + cc * 128)
                bsp = proj_chunk(w_sw, c0 + cc * 128)
                o_ap, o_b = next_stage()
                rope_combine(bxp, bsp, 128, TB["C"], TB["S"], [TB["bC"], TB["bS"]], xitab[:, tab0 + cc, :], [bxi], o_ap, o_b, 1)
                store(d[dname][cc * 128:(cc + 1) * 128, tsl], o_ap, o_b)
            return f

        def u_sg(c0, r0, cc):
            def f():
                bp = proj_chunk(w_in, c0 + cc * 128)
                P.I("act", "activation", [C.bps[bp]], [bsgs[cc]], out=sgs[cc], in_=C.ps[bp][:], func=AF.Silu)
                store(d["sgT"][r0 + cc * 128:r0 + (cc + 1) * 128, tsl], sgs[cc], bsgs[cc])
            return f

        def u_v(sub):
            def f():
                b = nb_()
                for (c0, o0) in ((C_RV, 0), (C_GV, 256)):
                    for k in range(8):
                        _mm(P, C.ps[b][:, o0:o0 + 256], HT[0][:, k, sub * 128:(sub + 1) * 128], w_in[:, k, c0:c0 + 256],
                            k == 0, k == 7, [bW, HT[1]], [C.bps[b]])
                P.I("act", "copy", [C.bps[b]], [bvst], out=vst[:, sub, :], in_=C.ps[b][:])
                if sub == 3:
                    store(d["vtok"][i * TT:(i + 1) * TT, :].rearrange("(n p) c -> p n c", p=128), vst, bvst)
            return f

        def u_vm(sub):
            def f():
                b = nb_()
                for k in range(2):
                    _mm(P, C.ps[b][:], ckvn[:, k, sub * 128:(sub + 1) * 128], wkv_v[:, k, :], k == 0, k == 1, [bW, bckvn],
                        [C.bps[b]])
                o_ap, o_b = next_stage()
                P.I("act", "copy", [C.bps[b]], [o_b], out=o_ap, in_=C.ps[b][:])
                store(d["vmtok"][i * TT + sub * 128:i * TT + (sub + 1) * 128, :], o_ap, o_b)
            return f

        def u_gqk(c0, E, dname, cc):
            def f():
                bp = proj_chunk(w_in, c0 + cc * 128)
                o_ap, o_b = next_stage()
                P.I("dve", "tensor_tensor", [C.bps[bp], bE[cc]], [o_b], out=o_ap, in0=C.ps[bp][:], in1=E[cc], op=ALU.mult)
                store(d[dname][cc * 128:(cc + 1) * 128, tsl], o_ap, o_b)
            return f

        fill = [u_ret(C_RQ, 0, "rqT", 0), u_ret(C_RQ, 0, "rqT", 1), u_ret(C_RK, 2, "rkT", 0), u_ret(C_RK, 2, "rkT", 1)]
        fill += [u_sg(C_RG, 0, 0), u_sg(C_RG, 0, 1), u_sg(C_GR, 256, 0), u_sg(C_GR, 256, 1)]
        fill += [u_v(sub) for sub in range(4)]
        if i + 1 < NT:
            fill.insert(2, (lambda ii=i + 1: prep(ii, ii % 2)))
        after = {"ckvn": [u_vm(sub) for sub in range(4)],
                 "E": [u_gqk(C_GQ, Eq, "gqT", 0), u_gqk(C_GQ, Eq, "gqT", 1), u_gqk(C_GK, Ek, "gkT", 0), u_gqk(C_GK, Ek, "gkT", 1)]}

        def lane1():
            ba = proj_chunk(w_in, C_GA, M=16)
            P.I("act", "copy", [C.bps[ba]], [balow], out=alow[0:16, :], in_=C.ps[ba][0:16, :])
            yield None
            bc = [nb_(), nb_()]
            for sub in range(4):
                bz = 7
                P.I("pe", "matmul", [balow, bgw], [C.bps[bz]], C.ps[bz][:, 0:256], alow[0:17, sub * 128:(sub + 1) * 128],
                    gw[0:17, :], start=True, stop=True)
                P.I("act", "activation", [C.bps[bz]], [bzl], out=zl, in_=C.ps[bz][:, 0:256], func=AF.Exp, scale=-1.0)
                P.I("act", "activation", [bzl, C.bconst], [bzl], out=zl, in_=zl, func=AF.Ln, bias=C.epsc[:, 1:2], scale=1.0)
                yield None
                for c in range(2):
                    P.I("pe", "matmul", [bzl, btri], [C.bps[bc[c]]], C.ps[bc[c]][:, sub * 128:(sub + 1) * 128],
                        zl[:, c * 128:(c + 1) * 128], tri, start=True, stop=True)
            for c in range(2):
                pb = C.ps[bc[c]]
                P.I("act", "activation", [C.bps[bc[c]], C.bconst], [bE[c]], out=Eq[c], in_=pb[:], func=AF.Exp,
                    bias=C.epsc[:, 2:3], scale=1.0)
                P.I("act", "activation", [C.bps[bc[c]]], [bE[c]], out=Ek[c], in_=pb[:], func=AF.Exp, scale=-1.0)
                P.I("act", "activation", [C.bps[bc[c]]], [bdecs], out=decs[:, c, :],
                    in_=pb[:].rearrange("p (n t) -> p n t", t=128)[:, :, 127], func=AF.Exp)
            store(d["gdec"][:, i * 4:(i + 1) * 4].rearrange("(c p) n -> p c n", p=128), decs, bdecs)
            yield "E"
            bk_ = [proj_chunk(w_in, C_CKV + k * 128) for k in range(2)]
            yield from norm_gen(P, C, [C.ps[b][:] for b in bk_], [C.bps[b] for b in bk_], 128, C.ones[:], 256,
                                [ckvn[:, k, :] for k in range(2)], bckvn, [V_CKV + k for k in range(2)], 6)
            yield "ckvn"
            bq_ = [proj_chunk(w_in, C_CQ + k * 128) for k in range(3)]
            yield from norm_gen(P, C, [C.ps[b][:] for b in bq_], [C.bps[b] for b in bq_], 128, C.ones[:], 384,
                                [cqn[:, k, :] for k in range(3)], bcqn, [V_CQ + k for k in range(3)], 6)
            for h in range(4):
                for (wt, src_t, src_b, nk, gcol, dname) in ((wuq_n, cqn, bcqn, 3, V_QN, "qnT"), (wkv_k, ckvn, bckvn, 2, V_KN, "knT")):
                    b = nb_()
                    for k in range(nk):
                        _mm(P, C.ps[b][:], wt[:, k, h * 128:(h + 1) * 128], src_t[:, k, :], k == 0, k == nk - 1, [bW, src_b],
                            [C.bps[b]])
                    o_ap, o_b = next_stage()
                    yield from norm_gen(P, C, [C.ps[b][:]], [C.bps[b]], 128, C.ones[:], 128, [o_ap], o_b, [gcol], 7)
                    store(d[dname][h * 128:(h + 1) * 128, tsl], o_ap, o_b)
            for cc in range(2):
                b1, b2 = nb_(), nb_()
                for (b, wt) in ((b1, wuq_r), (b2, wuq_rs)):
                    for k in range(3):
                        _mm(P, C.ps[b][:], wt[:, k, cc * 128:(cc + 1) * 128], cqn[:, k, :], k == 0, k == 2, [bW, bcqn], [C.bps[b]])
                yield from norm_gen(P, C, [C.ps[b1][:]], [C.bps[b1]], 128, C.ones2[:], 64, None, None, None, 7)
                o_ap, o_b = next_stage()
                rope_combine(b1, b2, 128, TB["GCq"], TB["GSq"], [TB["bGq"]], C.rstd[:], [C.brstd], o_ap, o_b, 0)
                store(d["qrT"][cc * 128:(cc + 1) * 128, tsl], o_ap, o_b)
            b1 = proj_chunk(w_in, C_KR, M=64)
            b2 = proj_chunk(w_sw, 512, M=64)
            yield from norm_gen(P, C, [C.ps[b1][0:64, :]], [C.bps[b1]], 64, C.ones[0:64, 0:64], 64, None, None, None, 7)
            o_ap, o_b = next_stage()
            rope_combine(b1, b2, 64, TB["GCk"][0:64, :], TB["GSk"][0:64, :], [TB["bGk"]], C.rstd[0:64, :], [C.brstd], o_ap[0:64, :], o_b, 0)
            store(d["krT"][:, tsl], o_ap[0:64, :], o_b)

        lane[0] = 0
        for tag in lane1():
            if tag is not None:
                fill += after.pop(tag)
            if fill:
                lane[0] = 1
                fill.pop(0)()
                lane[0] = 0
        lane[0] = 1
        for tag in list(after):
            fill += after.pop(tag)
        while fill:
            fill.pop(0)()
        lane[0] = 0


K1_OUTS = dict(rqT=([256, TOK], BF16), rkT=([256, TOK], BF16), sgT=([512, TOK], F32), vtok=([TOK, 512], BF16),
               qnT=([512, TOK], BF16), qrT=([256, TOK], BF16), knT=([512, TOK], BF16), vmtok=([TOK, 512], BF16),
               krT=([64, TOK], BF16), gdec=([256, TOK // 128], F32), gqT=([256, TOK], BF16), gkT=([256, TOK], BF16),
               x1T=([D, TOK], F32))


def build_k1(with_ffn=True):
    nc = bass.Bass("TRN2", target_bir_lowering=False)

    def din(name, shape, dt=F32):
        return nc.dram_tensor(name, shape, dt, kind="ExternalInput").ap()

    xT = din("xT", [D, TOK])
    wgu = din("wgu", [D, 2 * DFF])
    wd = din("wd", [DFF, D])
    g1 = din("g1", [128, 8])
    d = dict(w_in=din("w_in", [D, IN_COLS]), w_uq=din("w_uq", [384, 768]), w_ukv=din("w_ukv", [256, 1024]),
             vecs=din("vecs", [128, NVEC]), xitab=din("xitab", [128, 4 * TT]), gw=din("gw", [17, 256]),
             tri=din("tri", [128, 128]), posf=din("posf", [128, TOK]))
    for name, (shape, dt) in K1_OUTS.items():
        d[name] = nc.dram_tensor(name, shape, dt, kind="ExternalOutput").ap()
    P = Prog(nc)
    C = tok_ctx(P, nc)
    if with_ffn:
        ffn_phase(P, nc, C, xT, d["x1T"], wgu, wd, g1, dst_bufs=C.bxd)
    else:
        d["x1T"] = xT
    proj_phase(P, nc, C, d)
    P.emit()
    return nc


def post_phase(P, nc, C, d):
    WA = C.WA
    bW = C.bWA
    w_out = WA[:, 0:8 * 1024].rearrange("p (k c) -> p k c", k=8)
    P.dma("pool", w_out, d["w_out"].rearrange("(k p) n -> p k n", p=128), [], [bW], bW)
    WBf = C.WB.bitcast(F32)
    offb = [0]

    def cf(n):
        a = offb[0]
        offb[0] += n
        return WBf[:, a:a + n]

    ot = [cf(TT) for _ in range(8)]; bot = P.bufs("ot", 8)
    sg = [cf(TT) for _ in range(4)]; bsg = P.bufs("sg", 4)
    zt = [cf(TT) for _ in range(2)]; bzt = P.bufs("zt", 2)
    cat = C.act[:, 0:8, :]
    bcat = P.buf("cat")
    C.post_wb = bot + bsg + bzt
    C.post_act = [bcat]
    P.dma("sp", C.vec[:], d["vecs"], [], [C.bvec], C.bvec)
    x_v = d["x1T"].rearrange("(k p) t -> p k t", p=128)
    x_o = d["x2T"].rearrange("(k p) t -> p k t", p=128)
    srcs = [("roT", 0), ("roT", 1), ("moT", 0), ("moT", 1), ("moT", 2), ("moT", 3), ("goT", 0), ("goT", 1)]

    def load(i):
        s = i % 2
        tsl = slice(i * TT, (i + 1) * TT)
        P.dma("sp", C.xt[s][:], x_v[:, :, tsl], [], [C.bx[s]], C.bx[s])
        for c, (name, cc) in enumerate(srcs):
            P.dma("act" if c % 2 else "sp", ot[c], d[name][cc * 128:(cc + 1) * 128, tsl], [], [bot[c]], bot[c])
        for c in range(4):
            r0 = (0, 128, 256, 384)[c]
            P.dma("act" if c % 2 else "sp", sg[c], d["sgT"][r0:r0 + 128, tsl], [], [bsg[c]], bsg[c])

    load(0)
    for i in range(NT):
        s = i % 2
        tsl = slice(i * TT, (i + 1) * TT)
        xt, bx = C.xt[s], C.bx[s]
        for n2, (c, gcol, sgi) in enumerate(((0, V_RO, 0), (1, V_RO + 1, 1), (6, V_GO, 2), (7, V_GO + 1, 3))):
            q = n2 % 2
            norm_from_psum(P, C, [ot[c]], [bot[c]], 128, 128, C.ones2[:], 64, [zt[q]], bzt[q], [gcol], 7)
            P.I("dve", "tensor_tensor", [bzt[q], bsg[sgi]], [bcat], out=cat[:, c, :], in0=zt[q], in1=sg[sgi], op=ALU.mult)
        for c in range(2, 6):
            P.I("act", "copy", [bot[c]], [bcat], out=cat[:, c, :], in_=ot[c])
        if i + 1 < NT:
            load(i + 1)
        for m in range(8):
            r = m % 6
            for k in range(8):
                _mm(P, C.ps[r][:], w_out[:, k, m * 128:(m + 1) * 128], cat[:, k, :], k == 0, k == 7, [bW, bcat], [C.bps[r]])
            P.I("dve", "tensor_tensor", [C.bps[r], bx], [bx], out=xt[:, m, :], in0=C.ps[r][:], in1=xt[:, m, :], op=ALU.add)
        P.dma("sp", x_o[:, :, tsl], xt[:], [bx], [C.bxd[i]], bx, is_out=True)


def build_k3():
    nc = bass.Bass("TRN2", target_bir_lowering=False)

    def din(name, shape, dt=F32):
        return nc.dram_tensor(name, shape, dt, kind="ExternalInput").ap()

    d = dict(x1T=din("x1T", [D, TOK]), roT=din("roT", [256, TOK]), goT=din("goT", [256, TOK]), moT=din("moT", [512, TOK]),
             sgT=din("sgT", [512, TOK]), w_out=din("w_out", [D, D]), vecs=din("vecs", [128, NVEC]))
    wgu = din("wgu", [D, 2 * DFF])
    wd = din("wd", [DFF, D])
    g2 = din("g2", [128, 8])
    d["x2T"] = nc.dram_tensor("x2T", [D, TOK], F32, kind="ExternalOutput").ap()
    x3T = nc.dram_tensor("x3T", [D, TOK], F32, kind="ExternalOutput").ap()
    P = Prog(nc)
    C = tok_ctx(P, nc)
    post_phase(P, nc, C, d)
    for dst, srcs in ((C.bWB, C.post_wb), (C.bact, C.post_act)):
        for o in srcs:
            for k, v in o.writers.items():
                if k not in dst.writers or dst.writers[k].idx < v.idx:
                    dst.writers[k] = v
            for k, v in o.readers.items():
                if k not in dst.readers or dst.readers[k].idx < v.idx:
                    dst.readers[k] = v
    ffn_phase(P, nc, C, d["x2T"], x3T, wgu, wd, g2, src_bufs=C.bxd)
    P.emit()
    return nc


_BF = ml_dtypes.bfloat16
_CACHE = {}


def _get(name, fn):
    if name not in _CACHE:
        _CACHE[name] = fn()
    return _CACHE[name]


def _swap(g):
    return np.concatenate([g[32:], g[:32]])


def _consts():
    p = np.arange(128)
    inv = (10000.0 ** (-(np.arange(0, 64, 2, dtype=np.float32)) / 64.0)).astype(np.float32)
    inv_signed = np.where((p % 64) < 32, -1.0, 1.0).astype(np.float32) * inv[p % 32]
    t = (np.arange(TT) % 128 + 1).astype(np.float64)
    xitab = np.zeros((128, 4, TT), np.float32)
    for k in range(4):
        for half in range(2):
            h = (k % 2) * 2 + half
            lg = np.log1p(-2.0 ** (-5.0 - h))
            row = np.exp(lg * t) if k < 2 else np.exp(-lg * t) * 0.125
            xitab[half * 64:(half + 1) * 64, k, :] = row[None, :]
    s_, t_ = np.meshgrid(np.arange(128), np.arange(128), indexing="ij")
    tri = np.where(s_ <= t_, -1.0 / 16.0, 0.0).astype(np.float32)
    mask = np.where(s_ <= t_, 1.0, 0.0).astype(_BF)
    rdec = np.zeros((4, 64, NCH), np.float32)
    for h in range(4):
        rdec[h] = np.exp(np.log1p(-2.0 ** (-5.0 - h)) * 128.0)
    return dict(inv_signed=inv_signed, xitab=xitab.reshape(128, 4 * TT), tri=tri, mask=mask, rdec=rdec)


def _vecs(inp, l, cst):
    v = np.zeros((128, NVEC), np.float32)
    v[:, V_MIX:V_MIX + 8] = inp["mix_norm"][l].reshape(8, 128).T
    v[:, V_CQ:V_CQ + 3] = inp["mla_q_norm"][l].reshape(3, 128).T
    v[:, V_CKV:V_CKV + 2] = inp["mla_kv_norm"][l].reshape(2, 128).T
    v[:, V_QN] = inp["mla_q_nope_norm"][l]
    v[:, V_KN] = inp["mla_k_nope_norm"][l]
    gq = inp["mla_q_rope_norm"][l]
    gk = inp["mla_k_rope_norm"][l]
    v[:, V_QR] = np.tile(gq, 2)
    v[:, V_QRS] = np.tile(_swap(gq), 2)
    v[:64, V_KR] = gk
    v[:64, V_KRS] = _swap(gk)
    v[:, V_INV] = cst["inv_signed"]
    v[:, V_RO:V_RO + 2] = inp["ret_out_norm"][l].reshape(2, 128).T
    v[:, V_GO:V_GO + 2] = inp["gla_out_norm"][l].reshape(2, 128).T
    return v


def _run(nc, in_maps):
    res = run_bass_kernel_spmd(nc, in_maps, core_ids=list(range(NCORE)))
    return res.results


def _cat_tok(res, name, b):
    return np.concatenate([res[b * 4 + q][name] for q in range(4)], axis=1)


def _cat_rows(res, name, b):
    return np.concatenate([res[b * 4 + q][name] for q in range(4)], axis=0)


def kernel(**inp):
    inp = {k: np.asarray(v) for k, v in inp.items()}
    cst = _get("cst", _consts)
    k1 = _get("k1", build_k1)
    k2a = _get("k2a", lambda: build_k2(phases=(1,)))
    k2b = _get("k2b", lambda: build_k2(phases=(2,)))
    k3 = _get("k3", build_k3)
    x = inp["x"]
    posf = inp["positions"].astype(np.float32)
    xT = [np.ascontiguousarray(x[c // 4, (c % 4) * TOK:(c % 4 + 1) * TOK, :].T) for c in range(NCORE)]
    for l in range(DEPTH):
        vecs = _vecs(inp, l, cst)
        gw = np.concatenate([inp["gla_w_gate_up"][l], inp["gla_gate_bias"][l][None, :]], axis=0)
        ims = []
        for c in range(NCORE):
            b, q = c // 4, c % 4
            ims.append(dict(xT=xT[c], wgu=inp["ffn1_w_gate_up"][l], wd=inp["ffn1_w_down"][l],
                            g1=np.ascontiguousarray(inp["ffn1_norm"][l].reshape(8, 128).T),
                            w_in=inp["w_in"][l], w_uq=inp["mla_w_uq"][l], w_ukv=inp["mla_w_ukv"][l], vecs=vecs,
                            xitab=cst["xitab"], gw=gw, tri=cst["tri"],
                            posf=np.ascontiguousarray(np.broadcast_to(posf[b, q * TOK:(q + 1) * TOK][None, :], (128, TOK)))))
        r1 = _run(k1, ims)
        ima, imb = [], []
        for b in range(B):
            rq, rk = _cat_tok(r1, "rqT", b), _cat_tok(r1, "rkT", b)
            gq, gk = _cat_tok(r1, "gqT", b), _cat_tok(r1, "gkT", b)
            vt = _cat_rows(r1, "vtok", b)
            qn, qr = _cat_tok(r1, "qnT", b), _cat_tok(r1, "qrT", b)
            kn, kr = _cat_tok(r1, "knT", b), _cat_tok(r1, "krT", b)
            vm = _cat_rows(r1, "vmtok", b)
            gdec = _cat_tok(r1, "gdec", b)
            for h in range(4):
                hs = slice(h * 64, (h + 1) * 64)
                ima.append(dict(lqk=np.ascontiguousarray(np.stack([rq[hs], rk[hs], gq[hs], gk[hs]], axis=1)),
                                lkv=np.ascontiguousarray(np.concatenate([rk[hs].T, vt[:, h * 64:(h + 1) * 64], gk[hs].T,
                                                                         vt[:, 256 + h * 64:256 + (h + 1) * 64]], axis=1)),
                                ldec=np.ascontiguousarray(np.stack([cst["rdec"][h], gdec[hs]], axis=1)), mask=cst["mask"]))
                imb.append(dict(qn=np.ascontiguousarray(qn[h * 128:(h + 1) * 128]), qr=np.ascontiguousarray(qr[hs]),
                                kn=np.ascontiguousarray(kn[h * 128:(h + 1) * 128]), kr=kr,
                                vm=np.ascontiguousarray(vm[:, h * 128:(h + 1) * 128]), mask=cst["mask"]))
        r2a = _run(k2a, ima)
        r2b = _run(k2b, imb)
        im3 = []
        for c in range(NCORE):
            b, q = c // 4, c % 4
            ts = slice(q * TOK, (q + 1) * TOK)
            roT = np.concatenate([r2a[b * 4 + h]["lo"][ts, 0:64].T for h in range(4)], axis=0)
            goT = np.concatenate([r2a[b * 4 + h]["lo"][ts, 64:128].T for h in range(4)], axis=0)
            moT = np.concatenate([r2b[b * 4 + h]["moT"][:, ts] for h in range(4)], axis=0)
            im3.append(dict(x1T=r1[c]["x1T"], roT=np.ascontiguousarray(roT), goT=np.ascontiguousarray(goT),
                            moT=np.ascontiguousarray(moT), sgT=r1[c]["sgT"], w_out=inp["w_out"][l], vecs=vecs,
                            wgu=inp["ffn2_w_gate_up"][l], wd=inp["ffn2_w_down"][l],
                            g2=np.ascontiguousarray(inp["ffn2_norm"][l].reshape(8, 128).T)))
        r3 = _run(k3, im3)
        xT = [r3[c]["x3T"] for c in range(NCORE)]
        if _CACHE.get("debug") is not None:
            _CACHE["debug"].append(dict(r1=r1, r2a=r2a, r2b=r2b, r3=r3))
    out = np.empty((B, S, D), np.float32)
    for c in range(NCORE):
        out[c // 4, (c % 4) * TOK:(c % 4 + 1) * TOK, :] = xT[c].T
    return out
```

```python
import numpy as np
import ml_dtypes
import concourse.bass as bass
import concourse.mybir as mybir
from concourse.bass_utils import run_bass_kernel_spmd

F32 = mybir.dt.float32
BF16 = mybir.dt.bfloat16
I32 = mybir.dt.int32
AF = mybir.ActivationFunctionType
ALU = mybir.AluOpType

D = 1024
B = 2
S = 16384
DEPTH = 2
DFF = 2816
NCORE = 8
TOK = B * S // NCORE
TT = 512
NT = TOK // TT
EPS = 1e-6
IN_COLS = 2768
C_RQ, C_RK, C_RV, C_RG = 0, 256, 512, 768
C_CQ, C_CKV, C_KR = 1024, 1408, 1664
C_GQ, C_GK, C_GV, C_GA, C_GR = 1728, 1984, 2240, 2496, 2512


class Buf:
    __slots__ = ("name", "writers", "readers", "sem", "dcount", "excl")

    def __init__(self, name, excl=False):
        self.name = name
        self.excl = excl
        self.writers = {}
        self.readers = {}
        self.sem = None
        self.dcount = 0


class Op:
    __slots__ = ("eng", "fn", "deps", "needs_inc", "sem", "count", "is_dma", "idx")

    def __init__(self, eng, fn, is_dma):
        self.eng = eng
        self.fn = fn
        self.deps = []
        self.needs_inc = False
        self.sem = None
        self.count = 0
        self.is_dma = is_dma


ENGS = ("pe", "act", "dve", "pool", "sp")
ROT = 30000


class Prog:
    def __init__(self, nc):
        self.nc = nc
        self.ops = {e: [] for e in ENGS}
        self.nops = 0
        self.dma_sems = []
        self.out_dmas = []

    def buf(self, name):
        return Buf(name)

    def bufs(self, name, n, excl=False):
        return [Buf(f"{name}{i}", excl) for i in range(n)]

    def _dep(self, op, prod, kind):
        if prod is None or prod is op:
            return
        if not prod.is_dma and prod.eng == op.eng and not op.is_dma:
            if op.eng == "pe" or kind != "raw":
                return
        prod.needs_inc = True
        op.deps.append(prod)

    def add(self, eng, fn, reads=(), writes=(), dma_buf=None, is_out=False):
        is_dma = dma_buf is not None
        op = Op(eng, fn, is_dma)
        op.idx = self.nops
        self.nops += 1
        for b in reads:
            for w in b.writers.values():
                self._dep(op, w, "raw")
            if b.excl:
                for r in b.readers.values():
                    self._dep(op, r, "war")
        for b in writes:
            for w in b.writers.values():
                self._dep(op, w, "waw")
            for r in b.readers.values():
                self._dep(op, r, "war")
        if is_dma:
            if dma_buf.sem is None:
                dma_buf.sem = self.nc.semaphore(f"d{len(self.dma_sems)}_{dma_buf.name}").__enter__()
                self.dma_sems.append(dma_buf.sem)
            dma_buf.dcount += 16
            op.sem = dma_buf.sem
            op.count = dma_buf.dcount
            op.needs_inc = True
            key = ("dma", id(dma_buf))
            if is_out:
                self.out_dmas.append(op)
        else:
            key = eng
        for b in reads:
            b.readers[key] = op
        for b in writes:
            b.writers = {key: op}
            b.readers = {}
        self.ops[eng].append(op)
        return op

    def I(self, eng, meth, reads, writes, *args, **kw):
        return self.add(eng, lambda e: getattr(e, meth)(*args, **kw), reads=reads, writes=writes)

    def dma(self, eng, out, in_, reads, writes, dma_buf, partial=False, is_out=False):
        fn = lambda e: e.dma_start(out=out, in_=in_)
        if partial:
            return self.add_partial_write(eng, fn, reads, writes, dma_buf)
        return self.add(eng, fn, reads, writes, dma_buf, is_out)

    def add_partial_write(self, eng, fn, reads=(), writes=(), dma_buf=None):
        saved = [(b, dict(b.writers), dict(b.readers)) for b in writes]
        op = self.add(eng, fn, reads, writes, dma_buf)
        for b, w, r in saved:
            key = ("dma", id(dma_buf)) if dma_buf is not None else eng
            w = dict(w)
            w[key] = op
            b.writers = w
            b.readers = r
        return op

    def emit(self):
        nc = self.nc
        eng_sems = {}
        for e in ENGS:
            cnt = 0
            sems = []
            for op in self.ops[e]:
                if op.is_dma or not op.needs_inc:
                    continue
                k = cnt // ROT
                if k >= len(sems):
                    sems.append(nc.semaphore(f"c_{e}{k}").__enter__())
                op.sem = sems[k]
                op.count = cnt % ROT + 1
                cnt += 1
            eng_sems[e] = sems
        final_waits = [(op.sem, op.count) for op in self.out_dmas]
        fw = {}
        for s, c in final_waits:
            fw[id(s)] = (s, max(c, fw.get(id(s), (s, 0))[1]))

        def run(engname, eng):
            waited = {}
            for op in self.ops[engname]:
                need = {}
                for p in op.deps:
                    k = id(p.sem)
                    if p.count > need.get(k, (None, 0))[1]:
                        need[k] = (p.sem, p.count)
                for k, (s, c) in need.items():
                    if waited.get(k, 0) >= c:
                        continue
                    eng.wait_ge(s, c)
                    waited[k] = c
                ins = op.fn(eng)
                if op.needs_inc:
                    ins.then_inc(op.sem, 16 if op.is_dma else 1)
            if engname == "sp":
                for s, c in fw.values():
                    eng.wait_ge(s, c)

        with nc.Block() as block:
            @block.tensor
            def _(t):
                run("pe", t)

            @block.scalar
            def _(t):
                run("act", t)

            @block.vector
            def _(t):
                run("dve", t)

            @block.gpsimd
            def _(t):
                run("pool", t)

            @block.sync
            def _(t):
                run("sp", t)


def _mm(P, out_ap, lhsT, rhs, start, stop, reads, writes):
    return P.I("pe", "matmul", reads, writes, out_ap, lhsT, rhs, start=start, stop=stop)


class TokCtx:
    pass


def alloc(nc, name, shape, dt):
    return nc.sbuf_tensor("s_" + name, shape, dt).__enter__()


def ffn_phase(P, nc, C, x_src, x_dst, wgu_d, wd_d, gain_d, src_bufs=None, dst_bufs=None):
    WA, WB = C.WA, C.WB
    wgu = WA[:, 0:8 * 5632].rearrange("p (k c) -> p k c", k=8)
    wd = WB[:, 0:22 * 1024].rearrange("p (k c) -> p k c", k=22)
    for k in range(8):
        P.dma("pool", wgu[:, k, :], wgu_d[k * 128:(k + 1) * 128, :], [], [C.bWA], C.bWA, partial=(k > 0))
    wd_v = wd_d.rearrange("(k p) n -> p k n", p=128)
    for k0 in range(0, 22, 11):
        P.dma("pool", wd[:, k0:k0 + 11, :], wd_v[:, k0:k0 + 11, :], [], [C.bWB], C.bWB, partial=(k0 > 0))
    P.dma("sp", C.gain[:, 0:8], gain_d, [], [C.bgain], C.bgain)

    x_src_v = x_src.rearrange("(k p) t -> p k t", p=128)
    x_dst_v = x_dst.rearrange("(k p) t -> p k t", p=128)

    def load(i):
        s = i % 2
        P.dma("sp", C.xt[s][:], x_src_v[:, :, i * TT:(i + 1) * TT], [src_bufs[i]] if src_bufs else [], [C.bx[s]], C.bx[s])

    load(0)
    for i in range(NT):
        s = i % 2
        if i + 1 < NT:
            load(i + 1)
        xt = C.xt[s]
        bx = C.bx[s]
        yb = C.bps[6]
        ps = C.ps[6]
        for k in range(8):
            q = k % 2
            P.I("act", "activation", [bx], [C.bsq[q]], out=C.sq[q][:], in_=xt[:, k, :], func=AF.Square)
            _mm(P, ps[:], C.ones[:], C.sq[q][:], k == 0, k == 7, [C.bsq[q], C.bones], [yb])
        P.I("act", "activation", [yb, C.bconst], [C.brstd], out=C.rstd[:], in_=ps[:], func=AF.Ln,
            bias=C.epsc[:, 0:1], scale=1.0 / D)
        P.I("act", "activation", [C.brstd], [C.brstd], out=C.rstd[:], in_=C.rstd[:], func=AF.Exp, scale=-0.5)
        for k in range(8):
            P.I("dve", "scalar_tensor_tensor", [bx, C.brstd, C.bgain], [C.bhT], out=C.hT[:, k, :], in0=xt[:, k, :],
                scalar=C.gain[:, k:k + 1], in1=C.rstd[:], op0=ALU.mult, op1=ALU.mult)
        for j in range(22):
            r = j % 3
            gb, ub = C.bps[r], C.bps[3 + r]
            gp, up = C.ps[r], C.ps[3 + r]
            for k in range(8):
                _mm(P, gp[:], wgu[:, k, j * 128:(j + 1) * 128], C.hT[:, k, :], k == 0, k == 7, [C.bWA, C.bhT], [gb])
            for k in range(8):
                _mm(P, up[:], wgu[:, k, DFF + j * 128:DFF + (j + 1) * 128], C.hT[:, k, :], k == 0, k == 7,
                    [C.bWA, C.bhT], [ub])
            q = j % 2
            P.I("act", "activation", [gb], [C.bstmp[q]], out=C.stmp[q][:], in_=gp[:], func=AF.Silu)
            P.I("dve", "tensor_tensor", [ub, C.bstmp[q]], [C.bact], out=C.act[:, j, :], in0=up[:], in1=C.stmp[q][:],
                op=ALU.mult)
        for m in range(8):
            r = 6 + (m % 2)
            yb, yp = C.bps[r], C.ps[r]
            for k in range(22):
                _mm(P, yp[:], wd[:, k, m * 128:(m + 1) * 128], C.act[:, k, :], k == 0, k == 21, [C.bWB, C.bact], [yb])
            P.I("dve", "scalar_tensor_tensor", [yb, bx], [bx], out=xt[:, m, :], in0=yp[:], scalar=0.5, in1=xt[:, m, :],
                op0=ALU.mult, op1=ALU.add)
        P.dma("sp", x_dst_v[:, :, i * TT:(i + 1) * TT], xt[:], [bx], [dst_bufs[i]] if dst_bufs else [], bx, is_out=True)


def tok_ctx(P, nc):
    C = TokCtx()
    C.WA = alloc(nc, "WA", [128, 8 * 5632], BF16)
    C.WB = alloc(nc, "WB", [128, 22 * 1024], BF16)
    C.bWA, C.bWB = P.buf("WA"), P.buf("WB")
    C.xt = [alloc(nc, f"xt{i}", [128, 8, TT], F32) for i in range(2)]
    C.bx = P.bufs("x", 2)
    C.hT = alloc(nc, "hT", [128, 8, TT], BF16)
    C.bhT = P.buf("hT")
    C.act = alloc(nc, "act", [128, 22, TT], BF16)
    C.bact = P.buf("act")
    C.sq = [alloc(nc, f"sq{i}", [128, TT], BF16) for i in range(2)]
    C.bsq = P.bufs("sq", 2)
    C.rstd = alloc(nc, "rstd", [128, TT], F32)
    C.brstd = P.buf("rstd")
    C.stmp = [alloc(nc, f"stmp{i}", [128, TT], BF16) for i in range(2)]
    C.bstmp = P.bufs("stmp", 2)
    C.gain = alloc(nc, "gain", [128, 32], F32)
    C.bgain = P.buf("gain")
    C.ones = alloc(nc, "ones", [128, 128], BF16)
    C.bones = P.buf("ones")
    C.epsc = alloc(nc, "epsc", [128, 4], F32)
    C.vec = alloc(nc, "vec", [128, NVEC], F32)
    C.bvec = P.buf("vec")
    C.ones2 = alloc(nc, "ones2", [128, 128], BF16)
    C.bones2 = C.bones
    C.bxd = P.bufs("xd", NT)
    C.bconst = P.buf("const")
    C.ps = [nc.psum_tensor(f"ps{i}", [128, 512], F32).__enter__() for i in range(8)]
    C.bps = P.bufs("ps", 8, excl=True)
    P.I("dve", "memset", [], [C.bones], C.ones[:], 1.0)
    P.I("dve", "memset", [], [C.bconst], C.epsc[:], EPS)
    P.I("dve", "memset", [C.bconst], [C.bconst], C.epsc[:, 1:2], 1.0)
    P.I("dve", "memset", [C.bconst], [C.bconst], C.epsc[:, 2:3], float(np.log(0.125)))
    P.I("pool", "memset", [], [C.bones], C.ones2[:], 0.0)
    P.I("pool", "memset", [C.bones], [C.bones], C.ones2[0:64, 0:64], 1.0)
    P.I("pool", "memset", [C.bones], [C.bones], C.ones2[64:128, 64:128], 1.0)
    return C


def build_k1_test():
    nc = bass.Bass("TRN2", target_bir_lowering=False)
    xT = nc.dram_tensor("xT", [D, TOK], F32, kind="ExternalInput").ap()
    wgu = nc.dram_tensor("wgu", [D, 2 * DFF], F32, kind="ExternalInput").ap()
    wd = nc.dram_tensor("wd", [DFF, D], F32, kind="ExternalInput").ap()
    g1 = nc.dram_tensor("g1", [128, 8], F32, kind="ExternalInput").ap()
    x1T = nc.dram_tensor("x1T", [D, TOK], F32, kind="ExternalOutput").ap()
    P = Prog(nc)
    C = tok_ctx(P, nc)
    ffn_phase(P, nc, C, xT, x1T, wgu, wd, g1)
    P.emit()
    return nc


NQG = S // 512
NCH = S // 128
SCALE_MLA = 192.0 ** -0.5


def build_k2(phases=(1, 2)):
    nc = bass.Bass("TRN2", target_bir_lowering=False)

    def din(name, shape, dt):
        return nc.dram_tensor(name, shape, dt, kind="ExternalInput").ap()

    if 2 in phases:
        qn_d = din("qn", [128, S], BF16)
        qr_d = din("qr", [64, S], BF16)
        kn_d = din("kn", [128, S], BF16)
        kr_d = din("kr", [64, S], BF16)
        vm_d = din("vm", [S, 128], BF16)
        mo_d = nc.dram_tensor("moT", [128, S], F32, kind="ExternalOutput").ap()
    if 1 in phases:
        lqk_d = din("lqk", [64, 4, S], BF16)
        lkv_d = din("lkv", [S, 256], BF16)
        ldec_d = din("ldec", [64, 2, NCH], F32)
        lo_d = nc.dram_tensor("lo", [S, 128], F32, kind="ExternalOutput").ap()
    mask_d = din("mask", [128, 128], BF16)

    P = Prog(nc)
    ps = [nc.psum_tensor(f"ps{i}", [128, 512], F32).__enter__() for i in range(8)]
    bps = P.bufs("ps", 8, excl=True)
    mask = alloc(nc, "mask", [128, 128], BF16)
    bmask = P.buf("mask")
    P.dma("sp", mask[:], mask_d, [], [bmask], bmask)

    L = {}
    NB = 3
    if 1 in phases:
        qk = [alloc(nc, f"lqk{i}", [64, 4, 512], BF16) for i in range(NB)]
        kv = [alloc(nc, f"lkv{i}", [128, 4, 256], BF16) for i in range(NB)]
        bin_ = P.bufs("lin", NB)
        dec = alloc(nc, "ldec", [64, 2, NCH], F32)
        bdec = P.buf("ldec")
        osb = [alloc(nc, f"losb{i}", [128, 4, 128], F32) for i in range(2)]
        bosb = P.bufs("losb", 2)
        P.dma("sp", dec[:], ldec_d, [], [bdec], bdec)
    for xi, X in enumerate(("r", "g") if 1 in phases else ()):
        o = TokCtx()
        o.qi, o.ki, o.kti, o.vi, o.oi = 2 * xi, 2 * xi + 1, 128 * xi, 128 * xi + 64, 64 * xi
        o.scm = [alloc(nc, f"{X}scm{i}", [128, 128], BF16) for i in range(2)]
        o.bscm = P.bufs(X + "scm", 2)
        o.st = alloc(nc, X + "st", [64, 64], F32)
        o.tmp = alloc(nc, X + "tmp", [64, 64], F32)
        o.stb = alloc(nc, X + "stb", [64, 64], BF16)
        o.bst, o.btmp, o.bstb = P.buf(X + "st"), P.buf(X + "tmp"), P.buf(X + "stb")
        o.xi = xi
        L[X] = o
        P.I("dve", "memset", [], [o.bst], o.st[:], 0.0)
        P.I("dve", "memset", [], [o.bstb], o.stb[:], 0.0)

    def lin_load(g):
        s = g % NB
        t0 = g * 512
        P.dma("sp", qk[s][:], lqk_d[:, :, t0:t0 + 512], [], [bin_[s]], bin_[s])
        P.dma("act", kv[s][:], lkv_d[t0:t0 + 512, :].rearrange("(n p) d -> p n d", p=128), [], [bin_[s]], bin_[s],
              partial=True)

    def lin_A(n):
        g, c = n // 4, n % 4
        s = g % NB
        cs = slice(c * 128, (c + 1) * 128)
        for xi, X in enumerate(("r", "g")):
            o = L[X]
            sb = xi * 2 + (n % 2)
            m2 = n % 2
            _mm(P, ps[sb][:, 0:128], qk[s][:, o.ki, cs], qk[s][:, o.qi, cs], True, True, [bin_[s]], [bps[sb]])
            P.I("dve", "tensor_tensor", [bps[sb], bmask], [o.bscm[m2]], out=o.scm[m2][:], in0=ps[sb][:, 0:128],
                in1=mask[:], op=ALU.mult)

    def lin_C(n):
        g, c = n // 4, n % 4
        s = g % NB
        so = g % 2
        cs = slice(c * 128, (c + 1) * 128)
        for xi, X in enumerate(("r", "g")):
            o = L[X]
            ob = 4 + xi * 2 + (n % 2)
            m2 = n % 2
            vv = kv[s][:, c, o.vi:o.vi + 64]
            _mm(P, ps[ob][:, 0:64], o.scm[m2][:], vv, True, False, [o.bscm[m2], bin_[s]], [bps[ob]])
            _mm(P, ps[ob][:, 0:64], qk[s][:, o.qi, cs], o.stb[:], False, True, [bin_[s], o.bstb], [bps[ob]])
            _mm(P, ps[ob][0:64, 64:128], kv[s][:, c, o.kti:o.kti + 64], vv, True, True, [bin_[s]], [bps[ob]])
            P.I("dve", "scalar_tensor_tensor", [bps[ob], bdec, o.btmp], [o.bstb], out=o.stb[:], in0=ps[ob][0:64, 64:128],
                scalar=dec[:, xi, n:n + 1], in1=o.tmp[:], op0=ALU.mult, op1=ALU.add)
            P.I("dve", "scalar_tensor_tensor", [bps[ob], bdec, o.btmp], [o.bst], out=o.st[:], in0=ps[ob][0:64, 64:128],
                scalar=dec[:, xi, n:n + 1], in1=o.tmp[:], op0=ALU.mult, op1=ALU.add)
            P.I("act", "copy", [bps[ob]], [bosb[so]], out=osb[so][:, c, o.oi:o.oi + 64], in_=ps[ob][:, 0:64])
            if n + 1 < NCH:
                P.I("dve", "tensor_scalar", [o.bst, bdec], [o.btmp], out=o.tmp[:], in0=o.st[:],
                    scalar1=dec[:, xi, n + 1:n + 2], scalar2=None, op0=ALU.mult)
        if c == 3:
            P.dma("sp", lo_d[g * 512:(g + 1) * 512, :].rearrange("(n p) d -> p n d", p=128), osb[so][:],
                  [bosb[so]], [], bosb[so], is_out=True)

    if 1 in phases:
        for X in ("r", "g"):
            P.I("dve", "memset", [], [L[X].btmp], L[X].tmp[:], 0.0)
        for g0 in range(NB):
            lin_load(g0)
        lin_A(0)
        for n in range(NCH):
            if n + 1 < NCH:
                lin_A(n + 1)
            lin_C(n)
            if n % 4 == 3 and n // 4 + NB < NQG:
                lin_load(n // 4 + NB)

    if 2 not in phases:
        P.emit()
        return nc
    kn = alloc(nc, "kn", [128, S], BF16)
    kr = alloc(nc, "kr", [64, S], BF16)
    V = alloc(nc, "V", [128, NCH, 128], BF16)
    bkv = P.bufs("kv", NQG)
    kvsem = P.bufs("kvsem", 4)
    qn = [alloc(nc, f"qn{i}", [128, 512], BF16) for i in range(2)]
    qr = [alloc(nc, f"qr{i}", [64, 512], BF16) for i in range(2)]
    bq = P.bufs("q", 2)
    NSC = 4
    pt = [alloc(nc, f"pt{i}", [128, 512], BF16) for i in range(NSC)]
    bpt = P.bufs("pt", NSC)
    psum_t = [alloc(nc, f"ptsum{i}", [128, 512], F32) for i in range(2)]
    bpsum = P.bufs("ptsum", 2)
    rec = [alloc(nc, f"rec{i}", [128, 512], F32) for i in range(2)]
    brec = P.bufs("rec", 2)
    mosb = [alloc(nc, f"mosb{i}", [128, 512], F32) for i in range(2)]
    bmosb = P.bufs("mosb", 2)
    onesf = alloc(nc, "onesf", [128, 128], F32)
    bonesf = P.buf("onesf")
    P.I("pool", "memset", [], [bonesf], onesf[:], 1.0)
    vm_v = vm_d.rearrange("(n p) d -> p n d", p=128)

    def kv_load(g):
        t0 = g * 512
        sb = kvsem[g % 4]
        rd = [bkv[g - 4]] if g >= 4 else []
        P.dma("sp", kn[:, t0:t0 + 512], kn_d[:, t0:t0 + 512], rd, [bkv[g]], sb)
        P.dma("sp", kr[:, t0:t0 + 512], kr_d[:, t0:t0 + 512], [], [bkv[g]], sb, partial=True)
        P.dma("sp", V[:, 4 * g:4 * g + 4, 0:128], vm_v[:, 4 * g:4 * g + 4, :], [], [bkv[g]], sb, partial=True)

    def q_load(g):
        s = g % 2
        t0 = g * 512
        P.dma("sp", qn[s][:], qn_d[:, t0:t0 + 512], [], [bq[s]], bq[s])
        P.dma("sp", qr[s][:], qr_d[:, t0:t0 + 512], [], [bq[s]], bq[s], partial=True)

    LOOK = 3
    SUMB = 6
    blocks = [(g, kb) for g in range(NQG) for kb in range(4 * g + 4)]
    nblk = len(blocks)
    kv_load(0)
    q_load(0)

    def emit_sc(i):
        g, kb = blocks[i]
        s = g % 2
        if kb == 0 and g + 1 < NQG:
            kv_load(g + 1)
            q_load(g + 1)
        j = kb - 4 * g
        c0 = 128 * j if j > 0 else 0
        r = i % NSC
        ks = slice(kb * 128, (kb + 1) * 128)
        kvb = bkv[kb // 4]
        _mm(P, ps[r][:, c0:512], kn[:, ks], qn[s][:, c0:512], True, False, [kvb, bq[s]], [bps[r]])
        _mm(P, ps[r][:, c0:512], kr[:, ks], qr[s][:, c0:512], False, True, [kvb, bq[s]], [bps[r]])
        P.I("act", "activation", [bps[r]], [bpt[r]], out=pt[r][:, c0:512], in_=ps[r][:, c0:512], func=AF.Exp,
            scale=SCALE_MLA)
        if j >= 0:
            P.I("pool", "tensor_tensor", [bpt[r], bmask], [bpt[r]], out=pt[r][:, 128 * j:128 * j + 128],
                in0=pt[r][:, 128 * j:128 * j + 128], in1=mask[:], op=ALU.mult)
        if kb == 0:
            P.I("dve", "tensor_copy", [bpt[r]], [bpsum[s]], out=psum_t[s][:], in_=pt[r][:])
        else:
            P.I("dve", "tensor_tensor", [bpt[r], bpsum[s]], [bpsum[s]], out=psum_t[s][:, c0:512], in0=psum_t[s][:, c0:512],
                in1=pt[r][:, c0:512], op=ALU.add)

    def emit_pv(i):
        g, kb = blocks[i]
        s = g % 2
        j = kb - 4 * g
        c0 = 128 * j if j > 0 else 0
        r = i % NSC
        kvb = bkv[kb // 4]
        ab = 4 + s
        _mm(P, ps[ab][:, c0:512], V[:, kb, 0:128], pt[r][:, c0:512], kb == 0, kb == 4 * g + 3, [bpt[r], kvb], [bps[ab]])
        if kb == 4 * g + 3:
            P.I("pe", "matmul", [bpsum[s], bonesf], [bps[SUMB]], ps[SUMB][:], onesf[:], psum_t[s][:], start=True, stop=True)
            P.I("dve", "reciprocal", [bps[SUMB]], [brec[s]], out=rec[s][:], in_=ps[SUMB][:])
            P.I("dve", "tensor_tensor", [bps[ab], brec[s]], [bmosb[s]], out=mosb[s][:], in0=ps[ab][:], in1=rec[s][:], op=ALU.mult)
            P.dma("sp", mo_d[:, g * 512:(g + 1) * 512], mosb[s][:], [bmosb[s]], [], bmosb[s], is_out=True)

    for i in range(nblk + LOOK):
        if i < nblk:
            emit_sc(i)
        if i >= LOOK:
            emit_pv(i - LOOK)
    P.emit()
    return nc


TWO_PI = 2.0 * np.pi
MAGIC = 12582912.0
CW1 = 6.28125
CW2 = TWO_PI - 6.28125
V_MIX = 0
V_CQ = 8
V_CKV = 11
V_QN = 13
V_KN = 14
V_QR = 15
V_QRS = 16
V_KR = 17
V_KRS = 18
V_INV = 19
V_RO = 20
V_GO = 22
NVEC = 24


def norm_from_psum(P, C, src_aps, src_bufs, K, nparts, ones_ap, n_norm, out_aps, out_buf, gain_cols, stat_bank):
    sb, sp = C.bps[stat_bank], C.ps[stat_bank]
    n = len(src_aps)
    for k in range(n):
        q = k % 2
        P.I("act", "activation", [src_bufs[k]], [C.bsq[q]], out=C.sq[q][0:nparts, :], in_=src_aps[k], func=AF.Square)
        _mm(P, sp[0:nparts, :], ones_ap, C.sq[q][0:nparts, :], k == 0, k == n - 1, [C.bsq[q], C.bones], [sb])
    P.I("act", "activation", [sb, C.bconst], [C.brstd], out=C.rstd[0:nparts, :], in_=sp[0:nparts, :], func=AF.Ln,
        bias=C.epsc[0:nparts, 0:1], scale=1.0 / n_norm)
    P.I("act", "activation", [C.brstd], [C.brstd], out=C.rstd[0:nparts, :], in_=C.rstd[0:nparts, :], func=AF.Exp, scale=-0.5)
    if out_aps is not None:
        for k in range(n):
            P.I("dve", "scalar_tensor_tensor", [src_bufs[k], C.brstd, C.bvec], [out_buf], out=out_aps[k], in0=src_aps[k],
                scalar=C.vec[0:nparts, gain_cols[k]:gain_cols[k] + 1], in1=C.rstd[0:nparts, :], op0=ALU.mult, op1=ALU.mult)


def norm_gen(P, C, src_aps, src_bufs, nparts, ones_ap, n_norm, out_aps, out_buf, gain_cols, stat_bank):
    sb, sp = C.bps[stat_bank], C.ps[stat_bank]
    n = len(src_aps)
    if n <= 2:
        for k in range(n):
            P.I("act", "activation", [src_bufs[k]], [C.bsq[k]], out=C.sq[k][0:nparts, :], in_=src_aps[k], func=AF.Square)
        yield None
        for k in range(n):
            _mm(P, sp[0:nparts, :], ones_ap, C.sq[k][0:nparts, :], k == 0, k == n - 1, [C.bsq[k], C.bones], [sb])
    else:
        yield None
        for k in range(n):
            q = k % 2
            P.I("act", "activation", [src_bufs[k]], [C.bsq[q]], out=C.sq[q][0:nparts, :], in_=src_aps[k], func=AF.Square)
            _mm(P, sp[0:nparts, :], ones_ap, C.sq[q][0:nparts, :], k == 0, k == n - 1, [C.bsq[q], C.bones], [sb])
    P.I("act", "activation", [sb, C.bconst], [C.brstd], out=C.rstd[0:nparts, :], in_=sp[0:nparts, :], func=AF.Ln,
        bias=C.epsc[0:nparts, 0:1], scale=1.0 / n_norm)
    P.I("act", "activation", [C.brstd], [C.brstd], out=C.rstd[0:nparts, :], in_=C.rstd[0:nparts, :], func=AF.Exp, scale=-0.5)
    if out_aps is not None:
        for k in range(n):
            P.I("dve", "scalar_tensor_tensor", [src_bufs[k], C.brstd, C.bvec], [out_buf], out=out_aps[k], in0=src_aps[k],
                scalar=C.vec[0:nparts, gain_cols[k]:gain_cols[k] + 1], in1=C.rstd[0:nparts, :], op0=ALU.mult, op1=ALU.mult)
    yield None


def proj_phase(P, nc, C, d):
    WA = C.WA
    off = [0]

    def carve(n):
        a = off[0]
        off[0] += n
        return WA[:, a:a + n]

    bW = C.bWA
    w_in = carve(8 * IN_COLS).rearrange("p (k c) -> p k c", k=8)
    w_sw = carve(8 * 576).rearrange("p (k c) -> p k c", k=8)
    wuq_n = carve(3 * 512).rearrange("p (k c) -> p k c", k=3)
    wuq_r = carve(3 * 256).rearrange("p (k c) -> p k c", k=3)
    wuq_rs = carve(3 * 256).rearrange("p (k c) -> p k c", k=3)
    wkv_k = carve(2 * 512).rearrange("p (k c) -> p k c", k=2)
    wkv_v = carve(2 * 512).rearrange("p (k c) -> p k c", k=2)
    first = [True]

    def wdma(out, in_):
        P.dma("pool", out, in_, [], [bW], bW, partial=not first[0])
        first[0] = False

    win_d = d["w_in"]
    wdma(w_in, win_d.rearrange("(k p) c -> p k c", p=128))
    for k in range(3):
        rows = slice(k * 128, (k + 1) * 128)
        srcu = d["w_uq"][rows, :].rearrange("p (h c) -> p h c", h=4)
        wdma(wuq_n[:, k, :].rearrange("p (h c) -> p h c", h=4), srcu[:, :, 0:128])
        wdma(wuq_r[:, k, :].rearrange("p (h c) -> p h c", h=4), srcu[:, :, 128:192])
    for k in range(2):
        rows = slice(k * 128, (k + 1) * 128)
        srck = d["w_ukv"][rows, :].rearrange("p (h c) -> p h c", h=4)
        wdma(wkv_k[:, k, :].rearrange("p (h c) -> p h c", h=4), srck[:, :, 0:128])
        wdma(wkv_v[:, k, :].rearrange("p (h c) -> p h c", h=4), srck[:, :, 128:256])
    src4 = w_in[:, :, 0:512].rearrange("p k (h t c) -> p k h t c", h=8, t=2)
    dst4 = w_sw[:, :, 0:512].rearrange("p k (h t c) -> p k h t c", h=8, t=2)
    P.I("dve", "tensor_copy", [bW], [bW], out=dst4[:, :, :, 0, :], in_=src4[:, :, :, 1, :])
    P.I("act", "copy", [bW], [bW], out=dst4[:, :, :, 1, :], in_=src4[:, :, :, 0, :])
    P.I("dve", "tensor_copy", [bW], [bW], out=w_sw[:, :, 512:544], in_=w_in[:, :, C_KR + 32:C_KR + 64])
    P.I("dve", "tensor_copy", [bW], [bW], out=w_sw[:, :, 544:576], in_=w_in[:, :, C_KR:C_KR + 32])
    r4 = wuq_r.rearrange("p k (h c) -> p k h c", h=4)
    rs4 = wuq_rs.rearrange("p k (h c) -> p k h c", h=4)
    P.I("act", "copy", [bW], [bW], out=rs4[:, :, :, 0:32], in_=r4[:, :, :, 32:64])
    P.I("act", "copy", [bW], [bW], out=rs4[:, :, :, 32:64], in_=r4[:, :, :, 0:32])

    def alias(*olds):
        b = Buf("al")
        for o in (olds or (C.bWA, C.bWB, C.bact)):
            for k, v in o.writers.items():
                if k not in b.writers or b.writers[k].idx < v.idx:
                    b.writers[k] = v
            for k, v in o.readers.items():
                if k not in b.readers or b.readers[k].idx < v.idx:
                    b.readers[k] = v
        return b

    assert off[0] % 2 == 0
    WAf = WA.bitcast(F32)
    WBf = C.WB.bitcast(F32)
    offa = [off[0] // 2]
    offb = [0]

    def cf(n, region="b"):
        o_, t_ = (offb, WBf) if region == "b" else (offa, WAf)
        a = o_[0]
        o_[0] += n
        return t_[:, a:a + n]

    xitab = cf(4 * TT, "a").rearrange("p (k c) -> p k c", k=4); bxi = alias()
    Eq = [cf(TT, "a") for _ in range(2)]; Ek = [cf(TT, "a") for _ in range(2)]; bE = [alias() for _ in range(2)]
    assert offa[0] <= 8 * 5632 // 2, offa[0]
    pos = cf(TT); bpos = alias()
    ang = cf(TT); bang = alias()
    tk = cf(TT); btk = alias()
    r1 = cf(TT); br1 = alias()
    Ssb = cf(TT); bS = alias()
    Csb = cf(TT); bC = alias()
    GCq = cf(TT); GSq = cf(TT); bGq = alias()
    GCk = cf(TT); GSk = cf(TT); bGk = alias()
    t1 = [cf(TT) for _ in range(2)]; bt1 = [alias() for _ in range(2)]
    t2 = [cf(TT) for _ in range(2)]; bt2 = [alias() for _ in range(2)]
    sgs = [cf(TT) for _ in range(2)]; bsgs = [alias() for _ in range(2)]
    alow = cf(TT); balow = alias()
    gw = cf(256); bgw = alias()
    tri = cf(128); btri = alias()
    zl = cf(256); bzl = alias()
    decs = cf(8).rearrange("p (k c) -> p k c", k=2); bdecs = alias()
    rstd2 = cf(TT); brstd2 = alias()
    sq2 = []
    for _ in range(2):
        a_ = offb[0]
        offb[0] += TT // 2
        sq2.append(C.WB[:, 2 * a_:2 * a_ + TT])
    bsq2 = [alias(), alias()]
    hT1 = WA[:, off[0] + 8 * TT * 2:off[0] + 8 * TT * 2 + 8 * TT].rearrange("p (k t) -> p k t", k=8)
    assert off[0] + 8 * TT * 2 + 8 * TT <= 8 * 5632
    hT_set = [(C.hT, C.bhT), (hT1, alias())]
    x1s = C.xt[1]
    TBS = [dict(pos=pos, bpos=bpos, S=Ssb, bS=bS, C=Csb, bC=bC, GCq=GCq, GSq=GSq, bGq=bGq, GCk=GCk, GSk=GSk, bGk=bGk),
           dict(pos=x1s[:, 0, :], bpos=alias(C.bx[1]), S=x1s[:, 1, :], bS=alias(C.bx[1]), C=x1s[:, 2, :], bC=alias(C.bx[1]),
                GCq=x1s[:, 3, :], GSq=x1s[:, 4, :], bGq=alias(C.bx[1]), GCk=x1s[:, 5, :], GSk=x1s[:, 6, :],
                bGk=alias(C.bx[1]))]
    assert offb[0] <= 22 * 1024 // 2, offb[0]

    nb = [0]

    def stage_bf(n):
        a = nb[0]
        nb[0] += n
        assert nb[0] <= 22
        return C.act[:, a:a + n, :], alias()

    cqn, bcqn = stage_bf(3)
    ckvn, bckvn = stage_bf(2)
    vst, bvst = stage_bf(4)
    ostage = [stage_bf(1) for _ in range(8)]
    nst = [0]

    def next_stage():
        a = ostage[nst[0] % len(ostage)]
        nst[0] += 1
        return a[0][:, 0, :], a[1]

    P.dma("sp", C.vec[:], d["vecs"], [], [C.bvec], C.bvec)
    P.dma("sp", xitab, d["xitab"].rearrange("p (k c) -> p k c", k=4), [], [bxi], bxi)
    P.dma("sp", gw[0:17, :], d["gw"], [], [bgw], bgw)
    P.dma("sp", tri, d["tri"], [], [btri], btri)
    P.I("pool", "memset", [], [balow], alow[0:32, :], 1.0)

    x_v = d["x1T"].rearrange("(k p) t -> p k t", p=128)

    def load(i):
        P.dma("sp", C.xt[0][:], x_v[:, :, i * TT:(i + 1) * TT], [C.bxd[i]], [C.bx[0]], C.bx[0])

    load(0)
    bank = [0]

    lane = [0]
    bank2 = [0]

    def nb_():
        if lane[0] == 0:
            b = bank[0] % 3
            bank[0] += 1
        else:
            b = 3 + bank2[0] % 3
            bank2[0] += 1
        return b

    def proj_chunk(wt, col0, M=128):
        b = nb_()
        for k in range(8):
            _mm(P, C.ps[b][0:M, :], wt[:, k, col0:col0 + M], HT[0][:, k, :], k == 0, k == 7, [bW, HT[1]], [C.bps[b]])
        return b

    def store(dram_ap, sb_ap, buf):
        P.dma("sp", dram_ap, sb_ap, [buf], [], buf, is_out=True)

    def rope_combine(bx_, bs_, M, cos_ap, sin_ap, rbufs, post_ap, post_bufs, out_ap, out_buf, q):
        P.I("dve", "tensor_tensor", [C.bps[bx_]] + rbufs, [bt1[q]], out=t1[q][0:M, :], in0=C.ps[bx_][0:M, :], in1=cos_ap,
            op=ALU.mult)
        P.I("dve", "tensor_tensor", [C.bps[bs_]] + rbufs, [bt2[q]], out=t2[q][0:M, :], in0=C.ps[bs_][0:M, :], in1=sin_ap,
            op=ALU.mult)
        P.I("dve", "tensor_tensor", [bt1[q], bt2[q]], [bt1[q]], out=t1[q][0:M, :], in0=t1[q][0:M, :], in1=t2[q][0:M, :],
            op=ALU.add)
        P.I("dve", "tensor_tensor", [bt1[q]] + post_bufs, [out_buf], out=out_ap, in0=t1[q][0:M, :], in1=post_ap, op=ALU.mult)

    def prep(i, t):
        xt, bx = C.xt[0], C.bx[0]
        tsl = slice(i * TT, (i + 1) * TT)
        hT_t, bhT_t = hT_set[t]
        T = TBS[t]
        for k in range(8):
            q = k % 2
            P.I("act", "activation", [bx], [bsq2[q]], out=sq2[q], in_=xt[:, k, :], func=AF.Square)
            _mm(P, C.ps[6][:], C.ones[:], sq2[q], k == 0, k == 7, [bsq2[q], C.bones], [C.bps[6]])
        P.I("act", "activation", [C.bps[6], C.bconst], [brstd2], out=rstd2, in_=C.ps[6][:], func=AF.Ln,
            bias=C.epsc[:, 0:1], scale=1.0 / D)
        P.I("act", "activation", [brstd2], [brstd2], out=rstd2, in_=rstd2, func=AF.Exp, scale=-0.5)
        for k in range(8):
            P.I("dve", "scalar_tensor_tensor", [bx, brstd2, C.bvec], [bhT_t], out=hT_t[:, k, :], in0=xt[:, k, :],
                scalar=C.vec[:, V_MIX + k:V_MIX + k + 1], in1=rstd2, op0=ALU.mult, op1=ALU.mult)
        P.dma("sp", T["pos"], d["posf"][:, tsl], [], [T["bpos"]], T["bpos"])
        for (dst, bd, shift) in ((T["S"], T["bS"], 0.0), (T["C"], T["bC"], 0.5 * np.pi)):
            P.I("dve", "tensor_scalar", [T["bpos"], C.bvec], [bang], out=ang, in0=T["pos"], scalar1=C.vec[:, V_INV:V_INV + 1],
                scalar2=shift, op0=ALU.mult, op1=ALU.add)
            P.I("dve", "tensor_scalar", [bang], [btk], out=tk, in0=ang, scalar1=1.0 / TWO_PI, scalar2=MAGIC,
                op0=ALU.mult, op1=ALU.add)
            P.I("dve", "tensor_scalar", [btk], [btk], out=tk, in0=tk, scalar1=-MAGIC, scalar2=None, op0=ALU.add)
            P.I("dve", "scalar_tensor_tensor", [btk, bang], [br1], out=r1, in0=tk, scalar=-CW1, in1=ang,
                op0=ALU.mult, op1=ALU.add)
            P.I("dve", "scalar_tensor_tensor", [btk, br1], [br1], out=r1, in0=tk, scalar=-CW2, in1=r1,
                op0=ALU.mult, op1=ALU.add)
            P.I("dve", "tensor_scalar", [br1], [br1], out=r1, in0=r1, scalar1=-np.pi, scalar2=np.pi, op0=ALU.max, op1=ALU.min)
            P.I("act", "activation", [br1], [bd], out=dst, in_=r1, func=AF.Sin)
        P.I("pool", "tensor_scalar", [T["bC"], C.bvec], [T["bGq"]], out=T["GCq"], in0=T["C"], scalar1=C.vec[:, V_QR:V_QR + 1],
            scalar2=None, op0=ALU.mult)
        P.I("pool", "tensor_scalar", [T["bS"], C.bvec], [T["bGq"]], out=T["GSq"], in0=T["S"], scalar1=C.vec[:, V_QRS:V_QRS + 1],
            scalar2=None, op0=ALU.mult)
        P.I("pool", "tensor_scalar", [T["bC"], C.bvec], [T["bGk"]], out=T["GCk"][0:64, :], in0=T["C"][0:64, :],
            scalar1=C.vec[0:64, V_KR:V_KR + 1], scalar2=None, op0=ALU.mult)
        P.I("pool", "tensor_scalar", [T["bS"], C.bvec], [T["bGk"]], out=T["GSk"][0:64, :], in0=T["S"][0:64, :],
            scalar1=C.vec[0:64, V_KRS:V_KRS + 1], scalar2=None, op0=ALU.mult)

    prep(0, 0)
    for i in range(NT):
        tsl = slice(i * TT, (i + 1) * TT)
        HT = hT_set[i % 2]
        TB = TBS[i % 2]
        if i + 1 < NT:
            load(i + 1)

        def u_ret(c0, tab0, dname, cc):
            def f():
                bxp = proj_chunk(w_in, c0 + cc * 128)
                bsp = proj_chunk(w_sw, c0 + cc * 128)
                o_ap, o_b = next_stage()
                rope_combine(bxp, bsp, 128, TB["C"], TB["S"], [TB["bC"], TB["bS"]], xitab[:, tab0 + cc, :], [bxi], o_ap, o_b, 1)
                store(d[dname][cc * 128:(cc + 1) * 128, tsl], o_ap, o_b)
            return f

        def u_sg(c0, r0, cc):
            def f():
                bp = proj_chunk(w_in, c0 + cc * 128)
                P.I("act", "activation", [C.bps[bp]], [bsgs[cc]], out=sgs[cc], in_=C.ps[bp][:], func=AF.Silu)
                store(d["sgT"][r0 + cc * 128:r0 + (cc + 1) * 128, tsl], sgs[cc], bsgs[cc])
            return f

        def u_v(sub):
            def f():
                b = nb_()
                for (c0, o0) in ((C_RV, 0), (C_GV, 256)):
                    for k in range(8):
                        _mm(P, C.ps[b][:, o0:o0 + 256], HT[0][:, k, sub * 128:(sub + 1) * 128], w_in[:, k, c0:c0 + 256],
                            k == 0, k == 7, [bW, HT[1]], [C.bps[b]])
                P.I("act", "copy", [C.bps[b]], [bvst], out=vst[:, sub, :], in_=C.ps[b][:])
                if sub == 3:
                    store(d["vtok"][i * TT:(i + 1) * TT, :].rearrange("(n p) c -> p n c", p=128), vst, bvst)
            return f

        def u_vm(sub):
            def f():
                b = nb_()
                for k in range(2):
                    _mm(P, C.ps[b][:], ckvn[:, k, sub * 128:(sub + 1) * 128], wkv_v[:, k, :], k == 0, k == 1, [bW, bckvn],
                        [C.bps[b]])
                o_ap, o_b = next_stage()
                P.I("act", "copy", [C.bps[b]], [o_b], out=o_ap, in_=C.ps[b][:])
                store(d["vmtok"][i * TT + sub * 128:i * TT + (sub + 1) * 128, :], o_ap, o_b)
            return f

        def u_gqk(c0, E, dname, cc):
            def f():
                bp = proj_chunk(w_in, c0 + cc * 128)
                o_ap, o_b = next_stage()
                P.I("dve", "tensor_tensor", [C.bps[bp], bE[cc]], [o_b], out=o_ap, in0=C.ps[bp][:], in1=E[cc], op=ALU.mult)
                store(d[dname][cc * 128:(cc + 1) * 128, tsl], o_ap, o_b)
            return f

        fill = [u_ret(C_RQ, 0, "rqT", 0), u_ret(C_RQ, 0, "rqT", 1), u_ret(C_RK, 2, "rkT", 0), u_ret(C_RK, 2, "rkT", 1)]
        fill += [u_sg(C_RG, 0, 0), u_sg(C_RG, 0, 1), u_sg(C_GR, 256, 0), u_sg(C_GR, 256, 1)]
        fill += [u_v(sub) for sub in range(4)]
        if i + 1 < NT:
            fill.insert(2, (lambda ii=i + 1: prep(ii, ii % 2)))
        after = {"ckvn": [u_vm(sub) for sub in range(4)],
                 "E": [u_gqk(C_GQ, Eq, "gqT", 0), u_gqk(C_GQ, Eq, "gqT", 1), u_gqk(C_GK, Ek, "gkT", 0), u_gqk(C_GK, Ek, "gkT", 1)]}

        def lane1():
            ba = proj_chunk(w_in, C_GA, M=16)
            P.I("act", "copy", [C.bps[ba]], [balow], out=alow[0:16, :], in_=C.ps[ba][0:16, :])
            yield None
            bc = [nb_(), nb_()]
            for sub in range(4):
                bz = 7
                P.I("pe", "matmul", [balow, bgw], [C.bps[bz]], C.ps[bz][:, 0:256], alow[0:17, sub * 128:(sub + 1) * 128],
                    gw[0:17, :], start=True, stop=True)
                P.I("act", "activation", [C.bps[bz]], [bzl], out=zl, in_=C.ps[bz][:, 0:256], func=AF.Exp, scale=-1.0)
                P.I("act", "activation", [bzl, C.bconst], [bzl], out=zl, in_=zl, func=AF.Ln, bias=C.epsc[:, 1:2], scale=1.0)
                yield None
                for c in range(2):
                    P.I("pe", "matmul", [bzl, btri], [C.bps[bc[c]]], C.ps[bc[c]][:, sub * 128:(sub + 1) * 128],
                        zl[:, c * 128:(c + 1) * 128], tri, start=True, stop=True)
            for c in range(2):
                pb = C.ps[bc[c]]
                P.I("act", "activation", [C.bps[bc[c]], C.bconst], [bE[c]], out=Eq[c], in_=pb[:], func=AF.Exp,
                    bias=C.epsc[:, 2:3], scale=1.0)
                P.I("act", "activation", [C.bps[bc[c]]], [bE[c]], out=Ek[c], in_=pb[:], func=AF.Exp, scale=-1.0)
                P.I("act", "activation", [C.bps[bc[c]]], [bdecs], out=decs[:, c, :],
                    in_=pb[:].rearrange("p (n t) -> p n t", t=128)[:, :, 127], func=AF.Exp)
            store(d["gdec"][:, i * 4:(i + 1) * 4].rearrange("(c p) n -> p c n", p=128), decs, bdecs)
            yield "E"
            bk_ = [proj_chunk(w_in, C_CKV + k * 128) for k in range(2)]
            yield from norm_gen(P, C, [C.ps[b][:] for b in bk_], [C.bps[b] for b in bk_], 128, C.ones[:], 256,
                                [ckvn[:, k, :] for k in range(2)], bckvn, [V_CKV + k for k in range(2)], 6)
            yield "ckvn"
            bq_ = [proj_chunk(w_in, C_CQ + k * 128) for k in range(3)]
            yield from norm_gen(P, C, [C.ps[b][:] for b in bq_], [C.bps[b] for b in bq_], 128, C.ones[:], 384,
                                [cqn[:, k, :] for k in range(3)], bcqn, [V_CQ + k for k in range(3)], 6)
            for h in range(4):
                for (wt, src_t, src_b, nk, gcol, dname) in ((wuq_n, cqn, bcqn, 3, V_QN, "qnT"), (wkv_k, ckvn, bckvn, 2, V_KN, "knT")):
                    b = nb_()
                    for k in range(nk):
                        _mm(P, C.ps[b][:], wt[:, k, h * 128:(h + 1) * 128], src_t[:, k, :], k == 0, k == nk - 1, [bW, src_b],
                            [C.bps[b]])
                    o_ap, o_b = next_stage()
                    yield from norm_gen(P, C, [C.ps[b][:]], [C.bps[b]], 128, C.ones[:], 128, [o_ap], o_b, [gcol], 7)
                    store(d[dname][h * 128:(h + 1) * 128, tsl], o_ap, o_b)
            for cc in range(2):
                b1, b2 = nb_(), nb_()
                for (b, wt) in ((b1, wuq_r), (b2, wuq_rs)):
                    for k in range(3):
                        _mm(P, C.ps[b][:], wt[:, k, cc * 128:(cc + 1) * 128], cqn[:, k, :], k == 0, k == 2, [bW, bcqn], [C.bps[b]])
                yield from norm_gen(P, C, [C.ps[b1][:]], [C.bps[b1]], 128, C.ones2[:], 64, None, None, None, 7)
                o_ap, o_b = next_stage()
                rope_combine(b1, b2, 128, TB["GCq"], TB["GSq"], [TB["bGq"]], C.rstd[:], [C.brstd], o_ap, o_b, 0)
                store(d["qrT"][cc * 128:(cc + 1) * 128, tsl], o_ap, o_b)
            b1 = proj_chunk(w_in, C_KR, M=64)
            b2 = proj_chunk(w_sw, 512, M=64)
            yield from norm_gen(P, C, [C.ps[b1][0:64, :]], [C.bps[b1]], 64, C.ones[0:64, 0:64], 64, None, None, None, 7)
            o_ap, o_b = next_stage()
            rope_combine(b1, b2, 64, TB["GCk"][0:64, :], TB["GSk"][0:64, :], [TB["bGk"]], C.rstd[0:64, :], [C.brstd], o_ap[0:64, :], o_b, 0)
            store(d["krT"][:, tsl], o_ap[0:64, :], o_b)

        lane[0] = 0
        for tag in lane1():
            if tag is not None:
                fill += after.pop(tag)
            if fill:
                lane[0] = 1
                fill.pop(0)()
                lane[0] = 0
        lane[0] = 1
        for tag in list(after):
            fill += after.pop(tag)
        while fill:
            fill.pop(0)()
        lane[0] = 0


K1_OUTS = dict(rqT=([256, TOK], BF16), rkT=([256, TOK], BF16), sgT=([512, TOK], F32), vtok=([TOK, 512], BF16),
               qnT=([512, TOK], BF16), qrT=([256, TOK], BF16), knT=([512, TOK], BF16), vmtok=([TOK, 512], BF16),
               krT=([64, TOK], BF16), gdec=([256, TOK // 128], F32), gqT=([256, TOK], BF16), gkT=([256, TOK], BF16),
               x1T=([D, TOK], F32))


def build_k1(with_ffn=True):
    nc = bass.Bass("TRN2", target_bir_lowering=False)

    def din(name, shape, dt=F32):
        return nc.dram_tensor(name, shape, dt, kind="ExternalInput").ap()

    xT = din("xT", [D, TOK])
    wgu = din("wgu", [D, 2 * DFF])
    wd = din("wd", [DFF, D])
    g1 = din("g1", [128, 8])
    d = dict(w_in=din("w_in", [D, IN_COLS]), w_uq=din("w_uq", [384, 768]), w_ukv=din("w_ukv", [256, 1024]),
             vecs=din("vecs", [128, NVEC]), xitab=din("xitab", [128, 4 * TT]), gw=din("gw", [17, 256]),
             tri=din("tri", [128, 128]), posf=din("posf", [128, TOK]))
    for name, (shape, dt) in K1_OUTS.items():
        d[name] = nc.dram_tensor(name, shape, dt, kind="ExternalOutput").ap()
    P = Prog(nc)
    C = tok_ctx(P, nc)
    if with_ffn:
        ffn_phase(P, nc, C, xT, d["x1T"], wgu, wd, g1, dst_bufs=C.bxd)
    else:
        d["x1T"] = xT
    proj_phase(P, nc, C, d)
    P.emit()
    return nc


def post_phase(P, nc, C, d):
    WA = C.WA
    bW = C.bWA
    w_out = WA[:, 0:8 * 1024].rearrange("p (k c) -> p k c", k=8)
    P.dma("pool", w_out, d["w_out"].rearrange("(k p) n -> p k n", p=128), [], [bW], bW)
    WBf = C.WB.bitcast(F32)
    offb = [0]

    def cf(n):
        a = offb[0]
        offb[0] += n
        return WBf[:, a:a + n]

    ot = [cf(TT) for _ in range(8)]; bot = P.bufs("ot", 8)
    sg = [cf(TT) for _ in range(4)]; bsg = P.bufs("sg", 4)
    zt = [cf(TT) for _ in range(2)]; bzt = P.bufs("zt", 2)
    cat = C.act[:, 0:8, :]
    bcat = P.buf("cat")
    C.post_wb = bot + bsg + bzt
    C.post_act = [bcat]
    P.dma("sp", C.vec[:], d["vecs"], [], [C.bvec], C.bvec)
    x_v = d["x1T"].rearrange("(k p) t -> p k t", p=128)
    x_o = d["x2T"].rearrange("(k p) t -> p k t", p=128)
    srcs = [("roT", 0), ("roT", 1), ("moT", 0), ("moT", 1), ("moT", 2), ("moT", 3), ("goT", 0), ("goT", 1)]

    def load(i):
        s = i % 2
        tsl = slice(i * TT, (i + 1) * TT)
        P.dma("sp", C.xt[s][:], x_v[:, :, tsl], [], [C.bx[s]], C.bx[s])
        for c, (name, cc) in enumerate(srcs):
            P.dma("act" if c % 2 else "sp", ot[c], d[name][cc * 128:(cc + 1) * 128, tsl], [], [bot[c]], bot[c])
        for c in range(4):
            r0 = (0, 128, 256, 384)[c]
            P.dma("act" if c % 2 else "sp", sg[c], d["sgT"][r0:r0 + 128, tsl], [], [bsg[c]], bsg[c])

    load(0)
    for i in range(NT):
        s = i % 2
        tsl = slice(i * TT, (i + 1) * TT)
        xt, bx = C.xt[s], C.bx[s]
        for n2, (c, gcol, sgi) in enumerate(((0, V_RO, 0), (1, V_RO + 1, 1), (6, V_GO, 2), (7, V_GO + 1, 3))):
            q = n2 % 2
            norm_from_psum(P, C, [ot[c]], [bot[c]], 128, 128, C.ones2[:], 64, [zt[q]], bzt[q], [gcol], 7)
            P.I("dve", "tensor_tensor", [bzt[q], bsg[sgi]], [bcat], out=cat[:, c, :], in0=zt[q], in1=sg[sgi], op=ALU.mult)
        for c in range(2, 6):
            P.I("act", "copy", [bot[c]], [bcat], out=cat[:, c, :], in_=ot[c])
        if i + 1 < NT:
            load(i + 1)
        for m in range(8):
            r = m % 6
            for k in range(8):
                _mm(P, C.ps[r][:], w_out[:, k, m * 128:(m + 1) * 128], cat[:, k, :], k == 0, k == 7, [bW, bcat], [C.bps[r]])
            P.I("dve", "tensor_tensor", [C.bps[r], bx], [bx], out=xt[:, m, :], in0=C.ps[r][:], in1=xt[:, m, :], op=ALU.add)
        P.dma("sp", x_o[:, :, tsl], xt[:], [bx], [C.bxd[i]], bx, is_out=True)


def build_k3():
    nc = bass.Bass("TRN2", target_bir_lowering=False)

    def din(name, shape, dt=F32):
        return nc.dram_tensor(name, shape, dt, kind="ExternalInput").ap()

    d = dict(x1T=din("x1T", [D, TOK]), roT=din("roT", [256, TOK]), goT=din("goT", [256, TOK]), moT=din("moT", [512, TOK]),
             sgT=din("sgT", [512, TOK]), w_out=din("w_out", [D, D]), vecs=din("vecs", [128, NVEC]))
    wgu = din("wgu", [D, 2 * DFF])
    wd = din("wd", [DFF, D])
    g2 = din("g2", [128, 8])
    d["x2T"] = nc.dram_tensor("x2T", [D, TOK], F32, kind="ExternalOutput").ap()
    x3T = nc.dram_tensor("x3T", [D, TOK], F32, kind="ExternalOutput").ap()
    P = Prog(nc)
    C = tok_ctx(P, nc)
    post_phase(P, nc, C, d)
    for dst, srcs in ((C.bWB, C.post_wb), (C.bact, C.post_act)):
        for o in srcs:
            for k, v in o.writers.items():
                if k not in dst.writers or dst.writers[k].idx < v.idx:
                    dst.writers[k] = v
            for k, v in o.readers.items():
                if k not in dst.readers or dst.readers[k].idx < v.idx:
                    dst.readers[k] = v
    ffn_phase(P, nc, C, d["x2T"], x3T, wgu, wd, g2, src_bufs=C.bxd)
    P.emit()
    return nc


_BF = ml_dtypes.bfloat16
_CACHE = {}


def _get(name, fn):
    if name not in _CACHE:
        _CACHE[name] = fn()
    return _CACHE[name]


def _swap(g):
    return np.concatenate([g[32:], g[:32]])


def _consts():
    p = np.arange(128)
    inv = (10000.0 ** (-(np.arange(0, 64, 2, dtype=np.float32)) / 64.0)).astype(np.float32)
    inv_signed = np.where((p % 64) < 32, -1.0, 1.0).astype(np.float32) * inv[p % 32]
    t = (np.arange(TT) % 128 + 1).astype(np.float64)
    xitab = np.zeros((128, 4, TT), np.float32)
    for k in range(4):
        for half in range(2):
            h = (k % 2) * 2 + half
            lg = np.log1p(-2.0 ** (-5.0 - h))
            row = np.exp(lg * t) if k < 2 else np.exp(-lg * t) * 0.125
            xitab[half * 64:(half + 1) * 64, k, :] = row[None, :]
    s_, t_ = np.meshgrid(np.arange(128), np.arange(128), indexing="ij")
    tri = np.where(s_ <= t_, -1.0 / 16.0, 0.0).astype(np.float32)
    mask = np.where(s_ <= t_, 1.0, 0.0).astype(_BF)
    rdec = np.zeros((4, 64, NCH), np.float32)
    for h in range(4):
        rdec[h] = np.exp(np.log1p(-2.0 ** (-5.0 - h)) * 128.0)
    return dict(inv_signed=inv_signed, xitab=xitab.reshape(128, 4 * TT), tri=tri, mask=mask, rdec=rdec)


def _vecs(inp, l, cst):
    v = np.zeros((128, NVEC), np.float32)
    v[:, V_MIX:V_MIX + 8] = inp["mix_norm"][l].reshape(8, 128).T
    v[:, V_CQ:V_CQ + 3] = inp["mla_q_norm"][l].reshape(3, 128).T
    v[:, V_CKV:V_CKV + 2] = inp["mla_kv_norm"][l].reshape(2, 128).T
    v[:, V_QN] = inp["mla_q_nope_norm"][l]
    v[:, V_KN] = inp["mla_k_nope_norm"][l]
    gq = inp["mla_q_rope_norm"][l]
    gk = inp["mla_k_rope_norm"][l]
    v[:, V_QR] = np.tile(gq, 2)
    v[:, V_QRS] = np.tile(_swap(gq), 2)
    v[:64, V_KR] = gk
    v[:64, V_KRS] = _swap(gk)
    v[:, V_INV] = cst["inv_signed"]
    v[:, V_RO:V_RO + 2] = inp["ret_out_norm"][l].reshape(2, 128).T
    v[:, V_GO:V_GO + 2] = inp["gla_out_norm"][l].reshape(2, 128).T
    return v


def _run(nc, in_maps):
    res = run_bass_kernel_spmd(nc, in_maps, core_ids=list(range(NCORE)))
    return res.results


def _cat_tok(res, name, b):
    return np.concatenate([res[b * 4 + q][name] for q in range(4)], axis=1)


def _cat_rows(res, name, b):
    return np.concatenate([res[b * 4 + q][name] for q in range(4)], axis=0)


def kernel(**inp):
    inp = {k: np.asarray(v) for k, v in inp.items()}
    cst = _get("cst", _consts)
    k1 = _get("k1", build_k1)
    k2a = _get("k2a", lambda: build_k2(phases=(1,)))
    k2b = _get("k2b", lambda: build_k2(phases=(2,)))
    k3 = _get("k3", build_k3)
    x = inp["x"]
    posf = inp["positions"].astype(np.float32)
    xT = [np.ascontiguousarray(x[c // 4, (c % 4) * TOK:(c % 4 + 1) * TOK, :].T) for c in range(NCORE)]
    for l in range(DEPTH):
        vecs = _vecs(inp, l, cst)
        gw = np.concatenate([inp["gla_w_gate_up"][l], inp["gla_gate_bias"][l][None, :]], axis=0)
        ims = []
        for c in range(NCORE):
            b, q = c // 4, c % 4
            ims.append(dict(xT=xT[c], wgu=inp["ffn1_w_gate_up"][l], wd=inp["ffn1_w_down"][l],
                            g1=np.ascontiguousarray(inp["ffn1_norm"][l].reshape(8, 128).T),
                            w_in=inp["w_in"][l], w_uq=inp["mla_w_uq"][l], w_ukv=inp["mla_w_ukv"][l], vecs=vecs,
                            xitab=cst["xitab"], gw=gw, tri=cst["tri"],
                            posf=np.ascontiguousarray(np.broadcast_to(posf[b, q * TOK:(q + 1) * TOK][None, :], (128, TOK)))))
        r1 = _run(k1, ims)
        ima, imb = [], []
        for b in range(B):
            rq, rk = _cat_tok(r1, "rqT", b), _cat_tok(r1, "rkT", b)
            gq, gk = _cat_tok(r1, "gqT", b), _cat_tok(r1, "gkT", b)
            vt = _cat_rows(r1, "vtok", b)
            qn, qr = _cat_tok(r1, "qnT", b), _cat_tok(r1, "qrT", b)
            kn, kr = _cat_tok(r1, "knT", b), _cat_tok(r1, "krT", b)
            vm = _cat_rows(r1, "vmtok", b)
            gdec = _cat_tok(r1, "gdec", b)
            for h in range(4):
                hs = slice(h * 64, (h + 1) * 64)
                ima.append(dict(lqk=np.ascontiguousarray(np.stack([rq[hs], rk[hs], gq[hs], gk[hs]], axis=1)),
                                lkv=np.ascontiguousarray(np.concatenate([rk[hs].T, vt[:, h * 64:(h + 1) * 64], gk[hs].T,
                                                                         vt[:, 256 + h * 64:256 + (h + 1) * 64]], axis=1)),
                                ldec=np.ascontiguousarray(np.stack([cst["rdec"][h], gdec[hs]], axis=1)), mask=cst["mask"]))
                imb.append(dict(qn=np.ascontiguousarray(qn[h * 128:(h + 1) * 128]), qr=np.ascontiguousarray(qr[hs]),
                                kn=np.ascontiguousarray(kn[h * 128:(h + 1) * 128]), kr=kr,
                                vm=np.ascontiguousarray(vm[:, h * 128:(h + 1) * 128]), mask=cst["mask"]))
        r2a = _run(k2a, ima)
        r2b = _run(k2b, imb)
        im3 = []
        for c in range(NCORE):
            b, q = c // 4, c % 4
            ts = slice(q * TOK, (q + 1) * TOK)
            roT = np.concatenate([r2a[b * 4 + h]["lo"][ts, 0:64].T for h in range(4)], axis=0)
            goT = np.concatenate([r2a[b * 4 + h]["lo"][ts, 64:128].T for h in range(4)], axis=0)
            moT = np.concatenate([r2b[b * 4 + h]["moT"][:, ts] for h in range(4)], axis=0)
            im3.append(dict(x1T=r1[c]["x1T"], roT=np.ascontiguousarray(roT), goT=np.ascontiguousarray(goT),
                            moT=np.ascontiguousarray(moT), sgT=r1[c]["sgT"], w_out=inp["w_out"][l], vecs=vecs,
                            wgu=inp["ffn2_w_gate_up"][l], wd=inp["ffn2_w_down"][l],
                            g2=np.ascontiguousarray(inp["ffn2_norm"][l].reshape(8, 128).T)))
        r3 = _run(k3, im3)
        xT = [r3[c]["x3T"] for c in range(NCORE)]
        if _CACHE.get("debug") is not None:
            _CACHE["debug"].append(dict(r1=r1, r2a=r2a, r2b=r2b, r3=r3))
    out = np.empty((B, S, D), np.float32)
    for c in range(NCORE):
        out[c // 4, (c % 4) * TOK:(c % 4 + 1) * TOK, :] = xT[c].T
    return out
```

```python
import numpy as np
import ml_dtypes
import concourse.bass as bass
import concourse.mybir as mybir
from concourse.bass_utils import run_bass_kernel_spmd

F32 = mybir.dt.float32
BF16 = mybir.dt.bfloat16
I32 = mybir.dt.int32
AF = mybir.ActivationFunctionType
ALU = mybir.AluOpType

D = 1024
B = 2
S = 16384
DEPTH = 2
DFF = 2816
NCORE = 8
TOK = B * S // NCORE
TT = 512
NT = TOK // TT
EPS = 1e-6
IN_COLS = 2768
C_RQ, C_RK, C_RV, C_RG = 0, 256, 512, 768
C_CQ, C_CKV, C_KR = 1024, 1408, 1664
C_GQ, C_GK, C_GV, C_GA, C_GR = 1728, 1984, 2240, 2496, 2512


class Buf:
    __slots__ = ("name", "writers", "readers", "sem", "dcount", "excl")

    def __init__(self, name, excl=False):
        self.name = name
        self.excl = excl
        self.writers = {}
        self.readers = {}
        self.sem = None
        self.dcount = 0


class Op:
    __slots__ = ("eng", "fn", "deps", "needs_inc", "sem", "count", "is_dma", "idx")

    def __init__(self, eng, fn, is_dma):
        self.eng = eng
        self.fn = fn
        self.deps = []
        self.needs_inc = False
        self.sem = None
        self.count = 0
        self.is_dma = is_dma


ENGS = ("pe", "act", "dve", "pool", "sp")
ROT = 30000


class Prog:
    def __init__(self, nc):
        self.nc = nc
        self.ops = {e: [] for e in ENGS}
        self.nops = 0
        self.dma_sems = []
        self.out_dmas = []

    def buf(self, name):
        return Buf(name)

    def bufs(self, name, n, excl=False):
        return [Buf(f"{name}{i}", excl) for i in range(n)]

    def _dep(self, op, prod, kind):
        if prod is None or prod is op:
            return
        if not prod.is_dma and prod.eng == op.eng and not op.is_dma:
            if op.eng == "pe" or kind != "raw":
                return
        prod.needs_inc = True
        op.deps.append(prod)

    def add(self, eng, fn, reads=(), writes=(), dma_buf=None, is_out=False):
        is_dma = dma_buf is not None
        op = Op(eng, fn, is_dma)
        op.idx = self.nops
        self.nops += 1
        for b in reads:
            for w in b.writers.values():
                self._dep(op, w, "raw")
            if b.excl:
                for r in b.readers.values():
                    self._dep(op, r, "war")
        for b in writes:
            for w in b.writers.values():
                self._dep(op, w, "waw")
            for r in b.readers.values():
                self._dep(op, r, "war")
        if is_dma:
            if dma_buf.sem is None:
                dma_buf.sem = self.nc.semaphore(f"d{len(self.dma_sems)}_{dma_buf.name}").__enter__()
                self.dma_sems.append(dma_buf.sem)
            dma_buf.dcount += 16
            op.sem = dma_buf.sem
            op.count = dma_buf.dcount
            op.needs_inc = True
            key = ("dma", id(dma_buf))
            if is_out:
                self.out_dmas.append(op)
        else:
            key = eng
        for b in reads:
            b.readers[key] = op
        for b in writes:
            b.writers = {key: op}
            b.readers = {}
        self.ops[eng].append(op)
        return op

    def I(self, eng, meth, reads, writes, *args, **kw):
        return self.add(eng, lambda e: getattr(e, meth)(*args, **kw), reads=reads, writes=writes)

    def dma(self, eng, out, in_, reads, writes, dma_buf, partial=False, is_out=False):
        fn = lambda e: e.dma_start(out=out, in_=in_)
        if partial:
            return self.add_partial_write(eng, fn, reads, writes, dma_buf)
        return self.add(eng, fn, reads, writes, dma_buf, is_out)

    def add_partial_write(self, eng, fn, reads=(), writes=(), dma_buf=None):
        saved = [(b, dict(b.writers), dict(b.readers)) for b in writes]
        op = self.add(eng, fn, reads, writes, dma_buf)
        for b, w, r in saved:
            key = ("dma", id(dma_buf)) if dma_buf is not None else eng
            w = dict(w)
            w[key] = op
            b.writers = w
            b.readers = r
        return op

    def emit(self):
        nc = self.nc
        eng_sems = {}
        for e in ENGS:
            cnt = 0
            sems = []
            for op in self.ops[e]:
                if op.is_dma or not op.needs_inc:
                    continue
                k = cnt // ROT
                if k >= len(sems):
                    sems.append(nc.semaphore(f"c_{e}{k}").__enter__())
                op.sem = sems[k]
                op.count = cnt % ROT + 1
                cnt += 1
            eng_sems[e] = sems
        final_waits = [(op.sem, op.count) for op in self.out_dmas]
        fw = {}
        for s, c in final_waits:
            fw[id(s)] = (s, max(c, fw.get(id(s), (s, 0))[1]))

        def run(engname, eng):
            waited = {}
            for op in self.ops[engname]:
                need = {}
                for p in op.deps:
                    k = id(p.sem)
                    if p.count > need.get(k, (None, 0))[1]:
                        need[k] = (p.sem, p.count)
                for k, (s, c) in need.items():
                    if waited.get(k, 0) >= c:
                        continue
                    eng.wait_ge(s, c)
                    waited[k] = c
                ins = op.fn(eng)
                if op.needs_inc:
                    ins.then_inc(op.sem, 16 if op.is_dma else 1)
            if engname == "sp":
                for s, c in fw.values():
                    eng.wait_ge(s, c)

        with nc.Block() as block:
            @block.tensor
            def _(t):
                run("pe", t)

            @block.scalar
            def _(t):
                run("act", t)

            @block.vector
            def _(t):
                run("dve", t)

            @block.gpsimd
            def _(t):
                run("pool", t)

            @block.sync
            def _(t):
                run("sp", t)


def _mm(P, out_ap, lhsT, rhs, start, stop, reads, writes):
    return P.I("pe", "matmul", reads, writes, out_ap, lhsT, rhs, start=start, stop=stop)


class TokCtx:
    pass


def alloc(nc, name, shape, dt):
    return nc.sbuf_tensor("s_" + name, shape, dt).__enter__()


def ffn_phase(P, nc, C, x_src, x_dst, wgu_d, wd_d, gain_d, src_bufs=None, dst_bufs=None):
    WA, WB = C.WA, C.WB
    wgu = WA[:, 0:8 * 5632].rearrange("p (k c) -> p k c", k=8)
    wd = WB[:, 0:22 * 1024].rearrange("p (k c) -> p k c", k=22)
    for k in range(8):
        P.dma("pool", wgu[:, k, :], wgu_d[k * 128:(k + 1) * 128, :], [], [C.bWA], C.bWA, partial=(k > 0))
    wd_v = wd_d.rearrange("(k p) n -> p k n", p=128)
    for k0 in range(0, 22, 11):
        P.dma("pool", wd[:, k0:k0 + 11, :], wd_v[:, k0:k0 + 11, :], [], [C.bWB], C.bWB, partial=(k0 > 0))
    P.dma("sp", C.gain[:, 0:8], gain_d, [], [C.bgain], C.bgain)

    x_src_v = x_src.rearrange("(k p) t -> p k t", p=128)
    x_dst_v = x_dst.rearrange("(k p) t -> p k t", p=128)

    def load(i):
        s = i % 2
        P.dma("sp", C.xt[s][:], x_src_v[:, :, i * TT:(i + 1) * TT], [src_bufs[i]] if src_bufs else [], [C.bx[s]], C.bx[s])

    load(0)
    for i in range(NT):
        s = i % 2
        if i + 1 < NT:
            load(i + 1)
        xt = C.xt[s]
        bx = C.bx[s]
        yb = C.bps[6]
        ps = C.ps[6]
        for k in range(8):
            q = k % 2
            P.I("act", "activation", [bx], [C.bsq[q]], out=C.sq[q][:], in_=xt[:, k, :], func=AF.Square)
            _mm(P, ps[:], C.ones[:], C.sq[q][:], k == 0, k == 7, [C.bsq[q], C.bones], [yb])
        P.I("act", "activation", [yb, C.bconst], [C.brstd], out=C.rstd[:], in_=ps[:], func=AF.Ln,
            bias=C.epsc[:, 0:1], scale=1.0 / D)
        P.I("act", "activation", [C.brstd], [C.brstd], out=C.rstd[:], in_=C.rstd[:], func=AF.Exp, scale=-0.5)
        for k in range(8):
            P.I("dve", "scalar_tensor_tensor", [bx, C.brstd, C.bgain], [C.bhT], out=C.hT[:, k, :], in0=xt[:, k, :],
                scalar=C.gain[:, k:k + 1], in1=C.rstd[:], op0=ALU.mult, op1=ALU.mult)
        for j in range(22):
            r = j % 3
            gb, ub = C.bps[r], C.bps[3 + r]
            gp, up = C.ps[r], C.ps[3 + r]
            for k in range(8):
                _mm(P, gp[:], wgu[:, k, j * 128:(j + 1) * 128], C.hT[:, k, :], k == 0, k == 7, [C.bWA, C.bhT], [gb])
            for k in range(8):
                _mm(P, up[:], wgu[:, k, DFF + j * 128:DFF + (j + 1) * 128], C.hT[:, k, :], k == 0, k == 7,
                    [C.bWA, C.bhT], [ub])
            q = j % 2
            P.I("act", "activation", [gb], [C.bstmp[q]], out=C.stmp[q][:], in_=gp[:], func=AF.Silu)
            P.I("dve", "tensor_tensor", [ub, C.bstmp[q]], [C.bact], out=C.act[:, j, :], in0=up[:], in1=C.stmp[q][:],
                op=ALU.mult)
        for m in range(8):
            r = 6 + (m % 2)
            yb, yp = C.bps[r], C.ps[r]
            for k in range(22):
                _mm(P, yp[:], wd[:, k, m * 128:(m + 1) * 128], C.act[:, k, :], k == 0, k == 21, [C.bWB, C.bact], [yb])
            P.I("dve", "scalar_tensor_tensor", [yb, bx], [bx], out=xt[:, m, :], in0=yp[:], scalar=0.5, in1=xt[:, m, :],
                op0=ALU.mult, op1=ALU.add)
        P.dma("sp", x_dst_v[:, :, i * TT:(i + 1) * TT], xt[:], [bx], [dst_bufs[i]] if dst_bufs else [], bx, is_out=True)


def tok_ctx(P, nc):
    C = TokCtx()
    C.WA = alloc(nc, "WA", [128, 8 * 5632], BF16)
    C.WB = alloc(nc, "WB", [128, 22 * 1024], BF16)
    C.bWA, C.bWB = P.buf("WA"), P.buf("WB")
    C.xt = [alloc(nc, f"xt{i}", [128, 8, TT], F32) for i in range(2)]
    C.bx = P.bufs("x", 2)
    C.hT = alloc(nc, "hT", [128, 8, TT], BF16)
    C.bhT = P.buf("hT")
    C.act = alloc(nc, "act", [128, 22, TT], BF16)
    C.bact = P.buf("act")
    C.sq = [alloc(nc, f"sq{i}", [128, TT], BF16) for i in range(2)]
    C.bsq = P.bufs("sq", 2)
    C.rstd = alloc(nc, "rstd", [128, TT], F32)
    C.brstd = P.buf("rstd")
    C.stmp = [alloc(nc, f"stmp{i}", [128, TT], BF16) for i in range(2)]
    C.bstmp = P.bufs("stmp", 2)
    C.gain = alloc(nc, "gain", [128, 32], F32)
    C.bgain = P.buf("gain")
    C.ones = alloc(nc, "ones", [128, 128], BF16)
    C.bones = P.buf("ones")
    C.epsc = alloc(nc, "epsc", [128, 4], F32)
    C.vec = alloc(nc, "vec", [128, NVEC], F32)
    C.bvec = P.buf("vec")
    C.ones2 = alloc(nc, "ones2", [128, 128], BF16)
    C.bones2 = C.bones
    C.bxd = P.bufs("xd", NT)
    C.bconst = P.buf("const")
    C.ps = [nc.psum_tensor(f"ps{i}", [128, 512], F32).__enter__() for i in range(8)]
    C.bps = P.bufs("ps", 8, excl=True)
    P.I("dve", "memset", [], [C.bones], C.ones[:], 1.0)
    P.I("dve", "memset", [], [C.bconst], C.epsc[:], EPS)
    P.I("dve", "memset", [C.bconst], [C.bconst], C.epsc[:, 1:2], 1.0)
    P.I("dve", "memset", [C.bconst], [C.bconst], C.epsc[:, 2:3], float(np.log(0.125)))
    P.I("pool", "memset", [], [C.bones], C.ones2[:], 0.0)
    P.I("pool", "memset", [C.bones], [C.bones], C.ones2[0:64, 0:64], 1.0)
    P.I("pool", "memset", [C.bones], [C.bones], C.ones2[64:128, 64:128], 1.0)
    return C


def build_k1_test():
    nc = bass.Bass("TRN2", target_bir_lowering=False)
    xT = nc.dram_tensor("xT", [D, TOK], F32, kind="ExternalInput").ap()
    wgu = nc.dram_tensor("wgu", [D, 2 * DFF], F32, kind="ExternalInput").ap()
    wd = nc.dram_tensor("wd", [DFF, D], F32, kind="ExternalInput").ap()
    g1 = nc.dram_tensor("g1", [128, 8], F32, kind="ExternalInput").ap()
    x1T = nc.dram_tensor("x1T", [D, TOK], F32, kind="ExternalOutput").ap()
    P = Prog(nc)
    C = tok_ctx(P, nc)
    ffn_phase(P, nc, C, xT, x1T, wgu, wd, g1)
    P.emit()
    return nc


NQG = S // 512
NCH = S // 128
SCALE_MLA = 192.0 ** -0.5


def build_k2(phases=(1, 2)):
    nc = bass.Bass("TRN2", target_bir_lowering=False)

    def din(name, shape, dt):
        return nc.dram_tensor(name, shape, dt, kind="ExternalInput").ap()

    if 2 in phases:
        qn_d = din("qn", [128, S], BF16)
        qr_d = din("qr", [64, S], BF16)
        kn_d = din("kn", [128, S], BF16)
        kr_d = din("kr", [64, S], BF16)
        vm_d = din("vm", [S, 128], BF16)
        mo_d = nc.dram_tensor("moT", [128, S], F32, kind="ExternalOutput").ap()
    if 1 in phases:
        lqk_d = din("lqk", [64, 4, S], BF16)
        lkv_d = din("lkv", [S, 256], BF16)
        ldec_d = din("ldec", [64, 2, NCH], F32)
        lo_d = nc.dram_tensor("lo", [S, 128], F32, kind="ExternalOutput").ap()
    mask_d = din("mask", [128, 128], BF16)

    P = Prog(nc)
    ps = [nc.psum_tensor(f"ps{i}", [128, 512], F32).__enter__() for i in range(8)]
    bps = P.bufs("ps", 8, excl=True)
    mask = alloc(nc, "mask", [128, 128], BF16)
    bmask = P.buf("mask")
    P.dma("sp", mask[:], mask_d, [], [bmask], bmask)

    L = {}
    NB = 3
    if 1 in phases:
        qk = [alloc(nc, f"lqk{i}", [64, 4, 512], BF16) for i in range(NB)]
        kv = [alloc(nc, f"lkv{i}", [128, 4, 256], BF16) for i in range(NB)]
        bin_ = P.bufs("lin", NB)
        dec = alloc(nc, "ldec", [64, 2, NCH], F32)
        bdec = P.buf("ldec")
        osb = [alloc(nc, f"losb{i}", [128, 4, 128], F32) for i in range(2)]
        bosb = P.bufs("losb", 2)
        P.dma("sp", dec[:], ldec_d, [], [bdec], bdec)
    for xi, X in enumerate(("r", "g") if 1 in phases else ()):
        o = TokCtx()
        o.qi, o.ki, o.kti, o.vi, o.oi = 2 * xi, 2 * xi + 1, 128 * xi, 128 * xi + 64, 64 * xi
        o.scm = [alloc(nc, f"{X}scm{i}", [128, 128], BF16) for i in range(2)]
        o.bscm = P.bufs(X + "scm", 2)
        o.st = alloc(nc, X + "st", [64, 64], F32)
        o.tmp = alloc(nc, X + "tmp", [64, 64], F32)
        o.stb = alloc(nc, X + "stb", [64, 64], BF16)
        o.bst, o.btmp, o.bstb = P.buf(X + "st"), P.buf(X + "tmp"), P.buf(X + "stb")
        o.xi = xi
        L[X] = o
        P.I("dve", "memset", [], [o.bst], o.st[:], 0.0)
        P.I("dve", "memset", [], [o.bstb], o.stb[:], 0.0)

    def lin_load(g):
        s = g % NB
        t0 = g * 512
        P.dma("sp", qk[s][:], lqk_d[:, :, t0:t0 + 512], [], [bin_[s]], bin_[s])
        P.dma("act", kv[s][:], lkv_d[t0:t0 + 512, :].rearrange("(n p) d -> p n d", p=128), [], [bin_[s]], bin_[s],
              partial=True)

    def lin_A(n):
        g, c = n // 4, n % 4
        s = g % NB
        cs = slice(c * 128, (c + 1) * 128)
        for xi, X in enumerate(("r", "g")):
            o = L[X]
            sb = xi * 2 + (n % 2)
            m2 = n % 2
            _mm(P, ps[sb][:, 0:128], qk[s][:, o.ki, cs], qk[s][:, o.qi, cs], True, True, [bin_[s]], [bps[sb]])
            P.I("dve", "tensor_tensor", [bps[sb], bmask], [o.bscm[m2]], out=o.scm[m2][:], in0=ps[sb][:, 0:128],
                in1=mask[:], op=ALU.mult)

    def lin_C(n):
        g, c = n // 4, n % 4
        s = g % NB
        so = g % 2
        cs = slice(c * 128, (c + 1) * 128)
        for xi, X in enumerate(("r", "g")):
            o = L[X]
            ob = 4 + xi * 2 + (n % 2)
            m2 = n % 2
            vv = kv[s][:, c, o.vi:o.vi + 64]
            _mm(P, ps[ob][:, 0:64], o.scm[m2][:], vv, True, False, [o.bscm[m2], bin_[s]], [bps[ob]])
            _mm(P, ps[ob][:, 0:64], qk[s][:, o.qi, cs], o.stb[:], False, True, [bin_[s], o.bstb], [bps[ob]])
            _mm(P, ps[ob][0:64, 64:128], kv[s][:, c, o.kti:o.kti + 64], vv, True, True, [bin_[s]], [bps[ob]])
            P.I("dve", "scalar_tensor_tensor", [bps[ob], bdec, o.btmp], [o.bstb], out=o.stb[:], in0=ps[ob][0:64, 64:128],
                scalar=dec[:, xi, n:n + 1], in1=o.tmp[:], op0=ALU.mult, op1=ALU.add)
            P.I("dve", "scalar_tensor_tensor", [bps[ob], bdec, o.btmp], [o.bst], out=o.st[:], in0=ps[ob][0:64, 64:128],
                scalar=dec[:, xi, n:n + 1], in1=o.tmp[:], op0=ALU.mult, op1=ALU.add)
            P.I("act", "copy", [bps[ob]], [bosb[so]], out=osb[so][:, c, o.oi:o.oi + 64], in_=ps[ob][:, 0:64])
            if n + 1 < NCH:
                P.I("dve", "tensor_scalar", [o.bst, bdec], [o.btmp], out=o.tmp[:], in0=o.st[:],
                    scalar1=dec[:, xi, n + 1:n + 2], scalar2=None, op0=ALU.mult)
        if c == 3:
            P.dma("sp", lo_d[g * 512:(g + 1) * 512, :].rearrange("(n p) d -> p n d", p=128), osb[so][:],
                  [bosb[so]], [], bosb[so], is_out=True)

    if 1 in phases:
        for X in ("r", "g"):
            P.I("dve", "memset", [], [L[X].btmp], L[X].tmp[:], 0.0)
        for g0 in range(NB):
            lin_load(g0)
        lin_A(0)
        for n in range(NCH):
            if n + 1 < NCH:
                lin_A(n + 1)
            lin_C(n)
            if n % 4 == 3 and n // 4 + NB < NQG:
                lin_load(n // 4 + NB)

    if 2 not in phases:
        P.emit()
        return nc
    kn = alloc(nc, "kn", [128, S], BF16)
    kr = alloc(nc, "kr", [64, S], BF16)
    V = alloc(nc, "V", [128, NCH, 128], BF16)
    bkv = P.bufs("kv", NQG)
    kvsem = P.bufs("kvsem", 4)
    qn = [alloc(nc, f"qn{i}", [128, 512], BF16) for i in range(2)]
    qr = [alloc(nc, f"qr{i}", [64, 512], BF16) for i in range(2)]
    bq = P.bufs("q", 2)
    NSC = 4
    pt = [alloc(nc, f"pt{i}", [128, 512], BF16) for i in range(NSC)]
    bpt = P.bufs("pt", NSC)
    psum_t = [alloc(nc, f"ptsum{i}", [128, 512], F32) for i in range(2)]
    bpsum = P.bufs("ptsum", 2)
    rec = [alloc(nc, f"rec{i}", [128, 512], F32) for i in range(2)]
    brec = P.bufs("rec", 2)
    mosb = [alloc(nc, f"mosb{i}", [128, 512], F32) for i in range(2)]
    bmosb = P.bufs("mosb", 2)
    onesf = alloc(nc, "onesf", [128, 128], F32)
    bonesf = P.buf("onesf")
    P.I("pool", "memset", [], [bonesf], onesf[:], 1.0)
    vm_v = vm_d.rearrange("(n p) d -> p n d", p=128)

    def kv_load(g):
        t0 = g * 512
        sb = kvsem[g % 4]
        rd = [bkv[g - 4]] if g >= 4 else []
        P.dma("sp", kn[:, t0:t0 + 512], kn_d[:, t0:t0 + 512], rd, [bkv[g]], sb)
        P.dma("sp", kr[:, t0:t0 + 512], kr_d[:, t0:t0 + 512], [], [bkv[g]], sb, partial=True)
        P.dma("sp", V[:, 4 * g:4 * g + 4, 0:128], vm_v[:, 4 * g:4 * g + 4, :], [], [bkv[g]], sb, partial=True)

    def q_load(g):
        s = g % 2
        t0 = g * 512
        P.dma("sp", qn[s][:], qn_d[:, t0:t0 + 512], [], [bq[s]], bq[s])
        P.dma("sp", qr[s][:], qr_d[:, t0:t0 + 512], [], [bq[s]], bq[s], partial=True)

    LOOK = 3
    SUMB = 6
    blocks = [(g, kb) for g in range(NQG) for kb in range(4 * g + 4)]
    nblk = len(blocks)
    kv_load(0)
    q_load(0)

    def emit_sc(i):
        g, kb = blocks[i]
        s = g % 2
        if kb == 0 and g + 1 < NQG:
            kv_load(g + 1)
            q_load(g + 1)
        j = kb - 4 * g
        c0 = 128 * j if j > 0 else 0
        r = i % NSC
        ks = slice(kb * 128, (kb + 1) * 128)
        kvb = bkv[kb // 4]
        _mm(P, ps[r][:, c0:512], kn[:, ks], qn[s][:, c0:512], True, False, [kvb, bq[s]], [bps[r]])
        _mm(P, ps[r][:, c0:512], kr[:, ks], qr[s][:, c0:512], False, True, [kvb, bq[s]], [bps[r]])
        P.I("act", "activation", [bps[r]], [bpt[r]], out=pt[r][:, c0:512], in_=ps[r][:, c0:512], func=AF.Exp,
            scale=SCALE_MLA)
        if j >= 0:
            P.I("pool", "tensor_tensor", [bpt[r], bmask], [bpt[r]], out=pt[r][:, 128 * j:128 * j + 128],
                in0=pt[r][:, 128 * j:128 * j + 128], in1=mask[:], op=ALU.mult)
        if kb == 0:
            P.I("dve", "tensor_copy", [bpt[r]], [bpsum[s]], out=psum_t[s][:], in_=pt[r][:])
        else:
            P.I("dve", "tensor_tensor", [bpt[r], bpsum[s]], [bpsum[s]], out=psum_t[s][:, c0:512], in0=psum_t[s][:, c0:512],
                in1=pt[r][:, c0:512], op=ALU.add)

    def emit_pv(i):
        g, kb = blocks[i]
        s = g % 2
        j = kb - 4 * g
        c0 = 128 * j if j > 0 else 0
        r = i % NSC
        kvb = bkv[kb // 4]
        ab = 4 + s
        _mm(P, ps[ab][:, c0:512], V[:, kb, 0:128], pt[r][:, c0:512], kb == 0, kb == 4 * g + 3, [bpt[r], kvb], [bps[ab]])
        if kb == 4 * g + 3:
            P.I("pe", "matmul", [bpsum[s], bonesf], [bps[SUMB]], ps[SUMB][:], onesf[:], psum_t[s][:], start=True, stop=True)
            P.I("dve", "reciprocal", [bps[SUMB]], [brec[s]], out=rec[s][:], in_=ps[SUMB][:])
            P.I("dve", "tensor_tensor", [bps[ab], brec[s]], [bmosb[s]], out=mosb[s][:], in0=ps[ab][:], in1=rec[s][:], op=ALU.mult)
            P.dma("sp", mo_d[:, g * 512:(g + 1) * 512], mosb[s][:], [bmosb[s]], [], bmosb[s], is_out=True)

    for i in range(nblk + LOOK):
        if i < nblk:
            emit_sc(i)
        if i >= LOOK:
            emit_pv(i - LOOK)
    P.emit()
    return nc


TWO_PI = 2.0 * np.pi
MAGIC = 12582912.0
CW1 = 6.28125
CW2 = TWO_PI - 6.28125
V_MIX = 0
V_CQ = 8
V_CKV = 11
V_QN = 13
V_KN = 14
V_QR = 15
V_QRS = 16
V_KR = 17
V_KRS = 18
V_INV = 19
V_RO = 20
V_GO = 22
NVEC = 24


def norm_from_psum(P, C, src_aps, src_bufs, K, nparts, ones_ap, n_norm, out_aps, out_buf, gain_cols, stat_bank):
    sb, sp = C.bps[stat_bank], C.ps[stat_bank]
    n = len(src_aps)
    for k in range(n):
        q = k % 2
        P.I("act", "activation", [src_bufs[k]], [C.bsq[q]], out=C.sq[q][0:nparts, :], in_=src_aps[k], func=AF.Square)
        _mm(P, sp[0:nparts, :], ones_ap, C.sq[q][0:nparts, :], k == 0, k == n - 1, [C.bsq[q], C.bones], [sb])
    P.I("act", "activation", [sb, C.bconst], [C.brstd], out=C.rstd[0:nparts, :], in_=sp[0:nparts, :], func=AF.Ln,
        bias=C.epsc[0:nparts, 0:1], scale=1.0 / n_norm)
    P.I("act", "activation", [C.brstd], [C.brstd], out=C.rstd[0:nparts, :], in_=C.rstd[0:nparts, :], func=AF.Exp, scale=-0.5)
    if out_aps is not None:
        for k in range(n):
            P.I("dve", "scalar_tensor_tensor", [src_bufs[k], C.brstd, C.bvec], [out_buf], out=out_aps[k], in0=src_aps[k],
                scalar=C.vec[0:nparts, gain_cols[k]:gain_cols[k] + 1], in1=C.rstd[0:nparts, :], op0=ALU.mult, op1=ALU.mult)


def norm_gen(P, C, src_aps, src_bufs, nparts, ones_ap, n_norm, out_aps, out_buf, gain_cols, stat_bank):
    sb, sp = C.bps[stat_bank], C.ps[stat_bank]
    n = len(src_aps)
    if n <= 2:
        for k in range(n):
            P.I("act", "activation", [src_bufs[k]], [C.bsq[k]], out=C.sq[k][0:nparts, :], in_=src_aps[k], func=AF.Square)
        yield None
        for k in range(n):
            _mm(P, sp[0:nparts, :], ones_ap, C.sq[k][0:nparts, :], k == 0, k == n - 1, [C.bsq[k], C.bones], [sb])
    else:
        yield None
        for k in range(n):
            q = k % 2
            P.I("act", "activation", [src_bufs[k]], [C.bsq[q]], out=C.sq[q][0:nparts, :], in_=src_aps[k], func=AF.Square)
            _mm(P, sp[0:nparts, :], ones_ap, C.sq[q][0:nparts, :], k == 0, k == n - 1, [C.bsq[q], C.bones], [sb])
    P.I("act", "activation", [sb, C.bconst], [C.brstd], out=C.rstd[0:nparts, :], in_=sp[0:nparts, :], func=AF.Ln,
        bias=C.epsc[0:nparts, 0:1], scale=1.0 / n_norm)
    P.I("act", "activation", [C.brstd], [C.brstd], out=C.rstd[0:nparts, :], in_=C.rstd[0:nparts, :], func=AF.Exp, scale=-0.5)
    if out_aps is not None:
        for k in range(n):
            P.I("dve", "scalar_tensor_tensor", [src_bufs[k], C.brstd, C.bvec], [out_buf], out=out_aps[k], in0=src_aps[k],
                scalar=C.vec[0:nparts, gain_cols[k]:gain_cols[k] + 1], in1=C.rstd[0:nparts, :], op0=ALU.mult, op1=ALU.mult)
    yield None


def proj_phase(P, nc, C, d):
    WA = C.WA
    off = [0]

    def carve(n):
        a = off[0]
        off[0] += n
        return WA[:, a:a + n]

    bW = C.bWA
    w_in = carve(8 * IN_COLS).rearrange("p (k c) -> p k c", k=8)
    w_sw = carve(8 * 576).rearrange("p (k c) -> p k c", k=8)
    wuq_n = carve(3 * 512).rearrange("p (k c) -> p k c", k=3)
    wuq_r = carve(3 * 256).rearrange("p (k c) -> p k c", k=3)
    wuq_rs = carve(3 * 256).rearrange("p (k c) -> p k c", k=3)
    wkv_k = carve(2 * 512).rearrange("p (k c) -> p k c", k=2)
    wkv_v = carve(2 * 512).rearrange("p (k c) -> p k c", k=2)
    first = [True]

    def wdma(out, in_):
        P.dma("pool", out, in_, [], [bW], bW, partial=not first[0])
        first[0] = False

    win_d = d["w_in"]
    wdma(w_in, win_d.rearrange("(k p) c -> p k c", p=128))
    for k in range(3):
        rows = slice(k * 128, (k + 1) * 128)
        srcu = d["w_uq"][rows, :].rearrange("p (h c) -> p h c", h=4)
        wdma(wuq_n[:, k, :].rearrange("p (h c) -> p h c", h=4), srcu[:, :, 0:128])
        wdma(wuq_r[:, k, :].rearrange("p (h c) -> p h c", h=4), srcu[:, :, 128:192])
    for k in range(2):
        rows = slice(k * 128, (k + 1) * 128)
        srck = d["w_ukv"][rows, :].rearrange("p (h c) -> p h c", h=4)
        wdma(wkv_k[:, k, :].rearrange("p (h c) -> p h c", h=4), srck[:, :, 0:128])
        wdma(wkv_v[:, k, :].rearrange("p (h c) -> p h c", h=4), srck[:, :, 128:256])
    src4 = w_in[:, :, 0:512].rearrange("p k (h t c) -> p k h t c", h=8, t=2)
    dst4 = w_sw[:, :, 0:512].rearrange("p k (h t c) -> p k h t c", h=8, t=2)
    P.I("dve", "tensor_copy", [bW], [bW], out=dst4[:, :, :, 0, :], in_=src4[:, :, :, 1, :])
    P.I("act", "copy", [bW], [bW], out=dst4[:, :, :, 1, :], in_=src4[:, :, :, 0, :])
    P.I("dve", "tensor_copy", [bW], [bW], out=w_sw[:, :, 512:544], in_=w_in[:, :, C_KR + 32:C_KR + 64])
    P.I("dve", "tensor_copy", [bW], [bW], out=w_sw[:, :, 544:576], in_=w_in[:, :, C_KR:C_KR + 32])
    r4 = wuq_r.rearrange("p k (h c) -> p k h c", h=4)
    rs4 = wuq_rs.rearrange("p k (h c) -> p k h c", h=4)
    P.I("act", "copy", [bW], [bW], out=rs4[:, :, :, 0:32], in_=r4[:, :, :, 32:64])
    P.I("act", "copy", [bW], [bW], out=rs4[:, :, :, 32:64], in_=r4[:, :, :, 0:32])

    def alias(*olds):
        b = Buf("al")
        for o in (olds or (C.bWA, C.bWB, C.bact)):
            for k, v in o.writers.items():
                if k not in b.writers or b.writers[k].idx < v.idx:
                    b.writers[k] = v
            for k, v in o.readers.items():
                if k not in b.readers or b.readers[k].idx < v.idx:
                    b.readers[k] = v
        return b

    assert off[0] % 2 == 0
    WAf = WA.bitcast(F32)
    WBf = C.WB.bitcast(F32)
    offa = [off[0] // 2]
    offb = [0]

    def cf(n, region="b"):
        o_, t_ = (offb, WBf) if region == "b" else (offa, WAf)
        a = o_[0]
        o_[0] += n
        return t_[:, a:a + n]

    xitab = cf(4 * TT, "a").rearrange("p (k c) -> p k c", k=4); bxi = alias()
    Eq = [cf(TT, "a") for _ in range(2)]; Ek = [cf(TT, "a") for _ in range(2)]; bE = [alias() for _ in range(2)]
    assert offa[0] <= 8 * 5632 // 2, offa[0]
    pos = cf(TT); bpos = alias()
    ang = cf(TT); bang = alias()
    tk = cf(TT); btk = alias()
    r1 = cf(TT); br1 = alias()
    Ssb = cf(TT); bS = alias()
    Csb = cf(TT); bC = alias()
    GCq = cf(TT); GSq = cf(TT); bGq = alias()
    GCk = cf(TT); GSk = cf(TT); bGk = alias()
    t1 = [cf(TT) for _ in range(2)]; bt1 = [alias() for _ in range(2)]
    t2 = [cf(TT) for _ in range(2)]; bt2 = [alias() for _ in range(2)]
    sgs = [cf(TT) for _ in range(2)]; bsgs = [alias() for _ in range(2)]
    alow = cf(TT); balow = alias()
    gw = cf(256); bgw = alias()
    tri = cf(128); btri = alias()
    zl = cf(256); bzl = alias()
    decs = cf(8).rearrange("p (k c) -> p k c", k=2); bdecs = alias()
    rstd2 = cf(TT); brstd2 = alias()
    sq2 = []
    for _ in range(2):
        a_ = offb[0]
        offb[0] += TT // 2
        sq2.append(C.WB[:, 2 * a_:2 * a_ + TT])
    bsq2 = [alias(), alias()]
    hT1 = WA[:, off[0] + 8 * TT * 2:off[0] + 8 * TT * 2 + 8 * TT].rearrange("p (k t) -> p k t", k=8)
    assert off[0] + 8 * TT * 2 + 8 * TT <= 8 * 5632
    hT_set = [(C.hT, C.bhT), (hT1, alias())]
    x1s = C.xt[1]
    TBS = [dict(pos=pos, bpos=bpos, S=Ssb, bS=bS, C=Csb, bC=bC, GCq=GCq, GSq=GSq, bGq=bGq, GCk=GCk, GSk=GSk, bGk=bGk),
           dict(pos=x1s[:, 0, :], bpos=alias(C.bx[1]), S=x1s[:, 1, :], bS=alias(C.bx[1]), C=x1s[:, 2, :], bC=alias(C.bx[1]),
                GCq=x1s[:, 3, :], GSq=x1s[:, 4, :], bGq=alias(C.bx[1]), GCk=x1s[:, 5, :], GSk=x1s[:, 6, :],
                bGk=alias(C.bx[1]))]
    assert offb[0] <= 22 * 1024 // 2, offb[0]

    nb = [0]

    def stage_bf(n):
        a = nb[0]
        nb[0] += n
        assert nb[0] <= 22
        return C.act[:, a:a + n, :], alias()

    cqn, bcqn = stage_bf(3)
    ckvn, bckvn = stage_bf(2)
    vst, bvst = stage_bf(4)
    ostage = [stage_bf(1) for _ in range(8)]
    nst = [0]

    def next_stage():
        a = ostage[nst[0] % len(ostage)]
        nst[0] += 1
        return a[0][:, 0, :], a[1]

    P.dma("sp", C.vec[:], d["vecs"], [], [C.bvec], C.bvec)
    P.dma("sp", xitab, d["xitab"].rearrange("p (k c) -> p k c", k=4), [], [bxi], bxi)
    P.dma("sp", gw[0:17, :], d["gw"], [], [bgw], bgw)
    P.dma("sp", tri, d["tri"], [], [btri], btri)
    P.I("pool", "memset", [], [balow], alow[0:32, :], 1.0)

    x_v = d["x1T"].rearrange("(k p) t -> p k t", p=128)

    def load(i):
        P.dma("sp", C.xt[0][:], x_v[:, :, i * TT:(i + 1) * TT], [C.bxd[i]], [C.bx[0]], C.bx[0])

    load(0)
    bank = [0]

    lane = [0]
    bank2 = [0]

    def nb_():
        if lane[0] == 0:
            b = bank[0] % 3
            bank[0] += 1
        else:
            b = 3 + bank2[0] % 3
            bank2[0] += 1
        return b

    def proj_chunk(wt, col0, M=128):
        b = nb_()
        for k in range(8):
            _mm(P, C.ps[b][0:M, :], wt[:, k, col0:col0 + M], HT[0][:, k, :], k == 0, k == 7, [bW, HT[1]], [C.bps[b]])
        return b

    def store(dram_ap, sb_ap, buf):
        P.dma("sp", dram_ap, sb_ap, [buf], [], buf, is_out=True)

    def rope_combine(bx_, bs_, M, cos_ap, sin_ap, rbufs, post_ap, post_bufs, out_ap, out_buf, q):
        P.I("dve", "tensor_tensor", [C.bps[bx_]] + rbufs, [bt1[q]], out=t1[q][0:M, :], in0=C.ps[bx_][0:M, :], in1=cos_ap,
            op=ALU.mult)
        P.I("dve", "tensor_tensor", [C.bps[bs_]] + rbufs, [bt2[q]], out=t2[q][0:M, :], in0=C.ps[bs_][0:M, :], in1=sin_ap,
            op=ALU.mult)
        P.I("dve", "tensor_tensor", [bt1[q], bt2[q]], [bt1[q]], out=t1[q][0:M, :], in0=t1[q][0:M, :], in1=t2[q][0:M, :],
            op=ALU.add)
        P.I("dve", "tensor_tensor", [bt1[q]] + post_bufs, [out_buf], out=out_ap, in0=t1[q][0:M, :], in1=post_ap, op=ALU.mult)

    def prep(i, t):
        xt, bx = C.xt[0], C.bx[0]
        tsl = slice(i * TT, (i + 1) * TT)
        hT_t, bhT_t = hT_set[t]
        T = TBS[t]
        for k in range(8):
            q = k % 2
            P.I("act", "activation", [bx], [bsq2[q]], out=sq2[q], in_=xt[:, k, :], func=AF.Square)
            _mm(P, C.ps[6][:], C.ones[:], sq2[q], k == 0, k == 7, [bsq2[q], C.bones], [C.bps[6]])
        P.I("act", "activation", [C.bps[6], C.bconst], [brstd2], out=rstd2, in_=C.ps[6][:], func=AF.Ln,
            bias=C.epsc[:, 0:1], scale=1.0 / D)
        P.I("act", "activation", [brstd2], [brstd2], out=rstd2, in_=rstd2, func=AF.Exp, scale=-0.5)
        for k in range(8):
            P.I("dve", "scalar_tensor_tensor", [bx, brstd2, C.bvec], [bhT_t], out=hT_t[:, k, :], in0=xt[:, k, :],
                scalar=C.vec[:, V_MIX + k:V_MIX + k + 1], in1=rstd2, op0=ALU.mult, op1=ALU.mult)
        P.dma("sp", T["pos"], d["posf"][:, tsl], [], [T["bpos"]], T["bpos"])
        for (dst, bd, shift) in ((T["S"], T["bS"], 0.0), (T["C"], T["bC"], 0.5 * np.pi)):
            P.I("dve", "tensor_scalar", [T["bpos"], C.bvec], [bang], out=ang, in0=T["pos"], scalar1=C.vec[:, V_INV:V_INV + 1],
                scalar2=shift, op0=ALU.mult, op1=ALU.add)
            P.I("dve", "tensor_scalar", [bang], [btk], out=tk, in0=ang, scalar1=1.0 / TWO_PI, scalar2=MAGIC,
                op0=ALU.mult, op1=ALU.add)
            P.I("dve", "tensor_scalar", [btk], [btk], out=tk, in0=tk, scalar1=-MAGIC, scalar2=None, op0=ALU.add)
            P.I("dve", "scalar_tensor_tensor", [btk, bang], [br1], out=r1, in0=tk, scalar=-CW1, in1=ang,
                op0=ALU.mult, op1=ALU.add)
            P.I("dve", "scalar_tensor_tensor", [btk, br1], [br1], out=r1, in0=tk, scalar=-CW2, in1=r1,
                op0=ALU.mult, op1=ALU.add)
            P.I("dve", "tensor_scalar", [br1], [br1], out=r1, in0=r1, scalar1=-np.pi, scalar2=np.pi, op0=ALU.max, op1=ALU.min)
            P.I("act", "activation", [br1], [bd], out=dst, in_=r1, func=AF.Sin)
        P.I("pool", "tensor_scalar", [T["bC"], C.bvec], [T["bGq"]], out=T["GCq"], in0=T["C"], scalar1=C.vec[:, V_QR:V_QR + 1],
            scalar2=None, op0=ALU.mult)
        P.I("pool", "tensor_scalar", [T["bS"], C.bvec], [T["bGq"]], out=T["GSq"], in0=T["S"], scalar1=C.vec[:, V_QRS:V_QRS + 1],
            scalar2=None, op0=ALU.mult)
        P.I("pool", "tensor_scalar", [T["bC"], C.bvec], [T["bGk"]], out=T["GCk"][0:64, :], in0=T["C"][0:64, :],
            scalar1=C.vec[0:64, V_KR:V_KR + 1], scalar2=None, op0=ALU.mult)
        P.I("pool", "tensor_scalar", [T["bS"], C.bvec], [T["bGk"]], out=T["GSk"][0:64, :], in0=T["S"][0:64, :],
            scalar1=C.vec[0:64, V_KRS:V_KRS + 1], scalar2=None, op0=ALU.mult)

    prep(0, 0)
    for i in range(NT):
        tsl = slice(i * TT, (i + 1) * TT)
        HT = hT_set[i % 2]
        TB = TBS[i % 2]
        if i + 1 < NT:
            load(i + 1)

        def u_ret(c0, tab0, dname, cc):
            def f():
                bxp = proj_chunk(w_in, c0 + cc * 128)
                bsp = proj_chunk(w_sw, c0 + cc * 128)
                o_ap, o_b = next_stage()
                rope_combine(bxp, bsp, 128, TB["C"], TB["S"], [TB["bC"], TB["bS"]], xitab[:, tab0 + cc, :], [bxi], o_ap, o_b, 1)
                store(d[dname][cc * 128:(cc + 1) * 128, tsl], o_ap, o_b)
            return f

        def u_sg(c0, r0, cc):
            def f():
                bp = proj_chunk(w_in, c0 + cc * 128)
                P.I("act", "activation", [C.bps[bp]], [bsgs[cc]], out=sgs[cc], in_=C.ps[bp][:], func=AF.Silu)
                store(d["sgT"][r0 + cc * 128:r0 + (cc + 1) * 128, tsl], sgs[cc], bsgs[cc])
            return f

        def u_v(sub):
            def f():
                b = nb_()
                for (c0, o0) in ((C_RV, 0), (C_GV, 256)):
                    for k in range(8):
                        _mm(P, C.ps[b][:, o0:o0 + 256], HT[0][:, k, sub * 128:(sub + 1) * 128], w_in[:, k, c0:c0 + 256],
                            k == 0, k == 7, [bW, HT[1]], [C.bps[b]])
                P.I("act", "copy", [C.bps[b]], [bvst], out=vst[:, sub, :], in_=C.ps[b][:])
                if sub == 3:
                    store(d["vtok"][i * TT:(i + 1) * TT, :].rearrange("(n p) c -> p n c", p=128), vst, bvst)
            return f

        def u_vm(sub):
            def f():
                b = nb_()
                for k in range(2):
                    _mm(P, C.ps[b][:], ckvn[:, k, sub * 128:(sub + 1) * 128], wkv_v[:, k, :], k == 0, k == 1, [bW, bckvn],
                        [C.bps[b]])
                o_ap, o_b = next_stage()
                P.I("act", "copy", [C.bps[b]], [o_b], out=o_ap, in_=C.ps[b][:])
                store(d["vmtok"][i * TT + sub * 128:i * TT + (sub + 1) * 128, :], o_ap, o_b)
            return f

        def u_gqk(c0, E, dname, cc):
            def f():
                bp = proj_chunk(w_in, c0 + cc * 128)
                o_ap, o_b = next_stage()
                P.I("dve", "tensor_tensor", [C.bps[bp], bE[cc]], [o_b], out=o_ap, in0=C.ps[bp][:], in1=E[cc], op=ALU.mult)
                store(d[dname][cc * 128:(cc + 1) * 128, tsl], o_ap, o_b)
            return f

        fill = [u_ret(C_RQ, 0, "rqT", 0), u_ret(C_RQ, 0, "rqT", 1), u_ret(C_RK, 2, "rkT", 0), u_ret(C_RK, 2, "rkT", 1)]
        fill += [u_sg(C_RG, 0, 0), u_sg(C_RG, 0, 1), u_sg(C_GR, 256, 0), u_sg(C_GR, 256, 1)]
        fill += [u_v(sub) for sub in range(4)]
        if i + 1 < NT:
            fill.insert(2, (lambda ii=i + 1: prep(ii, ii % 2)))
        after = {"ckvn": [u_vm(sub) for sub in range(4)],
                 "E": [u_gqk(C_GQ, Eq, "gqT", 0), u_gqk(C_GQ, Eq, "gqT", 1), u_gqk(C_GK, Ek, "gkT", 0), u_gqk(C_GK, Ek, "gkT", 1)]}

        def lane1():
            ba = proj_chunk(w_in, C_GA, M=16)
            P.I("act", "copy", [C.bps[ba]], [balow], out=alow[0:16, :], in_=C.ps[ba][0:16, :])
            yield None
            bc = [nb_(), nb_()]
            for sub in range(4):
                bz = 7
                P.I("pe", "matmul", [balow, bgw], [C.bps[bz]], C.ps[bz][:, 0:256], alow[0:17, sub * 128:(sub + 1) * 128],
                    gw[0:17, :], start=True, stop=True)
                P.I("act", "activation", [C.bps[bz]], [bzl], out=zl, in_=C.ps[bz][:, 0:256], func=AF.Exp, scale=-1.0)
                P.I("act", "activation", [bzl, C.bconst], [bzl], out=zl, in_=zl, func=AF.Ln, bias=C.epsc[:, 1:2], scale=1.0)
                yield None
                for c in range(2):
                    P.I("pe", "matmul", [bzl, btri], [C.bps[bc[c]]], C.ps[bc[c]][:, sub * 128:(sub + 1) * 128],
                        zl[:, c * 128:(c + 1) * 128], tri, start=True, stop=True)
            for c in range(2):
                pb = C.ps[bc[c]]
                P.I("act", "activation", [C.bps[bc[c]], C.bconst], [bE[c]], out=Eq[c], in_=pb[:], func=AF.Exp,
                    bias=C.epsc[:, 2:3], scale=1.0)
                P.I("act", "activation", [C.bps[bc[c]]], [bE[c]], out=Ek[c], in_=pb[:], func=AF.Exp, scale=-1.0)
                P.I("act", "activation", [C.bps[bc[c]]], [bdecs], out=decs[:, c, :],
                    in_=pb[:].rearrange("p (n t) -> p n t", t=128)[:, :, 127], func=AF.Exp)
            store(d["gdec"][:, i * 4:(i + 1) * 4].rearrange("(c p) n -> p c n", p=128), decs, bdecs)
            yield "E"
            bk_ = [proj_chunk(w_in, C_CKV + k * 128) for k in range(2)]
            yield from norm_gen(P, C, [C.ps[b][:] for b in bk_], [C.bps[b] for b in bk_], 128, C.ones[:], 256,
                                [ckvn[:, k, :] for k in range(2)], bckvn, [V_CKV + k for k in range(2)], 6)
            yield "ckvn"
            bq_ = [proj_chunk(w_in, C_CQ + k * 128) for k in range(3)]
            yield from norm_gen(P, C, [C.ps[b][:] for b in bq_], [C.bps[b] for b in bq_], 128, C.ones[:], 384,
                                [cqn[:, k, :] for k in range(3)], bcqn, [V_CQ + k for k in range(3)], 6)
            for h in range(4):
                for (wt, src_t, src_b, nk, gcol, dname) in ((wuq_n, cqn, bcqn, 3, V_QN, "qnT"), (wkv_k, ckvn, bckvn, 2, V_KN, "knT")):
                    b = nb_()
                    for k in range(nk):
                        _mm(P, C.ps[b][:], wt[:, k, h * 128:(h + 1) * 128], src_t[:, k, :], k == 0, k == nk - 1, [bW, src_b],
                            [C.bps[b]])
                    o_ap, o_b = next_stage()
                    yield from norm_gen(P, C, [C.ps[b][:]], [C.bps[b]], 128, C.ones[:], 128, [o_ap], o_b, [gcol], 7)
                    store(d[dname][h * 128:(h + 1) * 128, tsl], o_ap, o_b)
            for cc in range(2):
                b1, b2 = nb_(), nb_()
                for (b, wt) in ((b1, wuq_r), (b2, wuq_rs)):
                    for k in range(3):
                        _mm(P, C.ps[b][:], wt[:, k, cc * 128:(cc + 1) * 128], cqn[:, k, :], k == 0, k == 2, [bW, bcqn], [C.bps[b]])
                yield from norm_gen(P, C, [C.ps[b1][:]], [C.bps[b1]], 128, C.ones2[:], 64, None, None, None, 7)
                o_ap, o_b = next_stage()
                rope_combine(b1, b2, 128, TB["GCq"], TB["GSq"], [TB["bGq"]], C.rstd[:], [C.brstd], o_ap, o_b, 0)
                store(d["qrT"][cc * 128:(cc + 1) * 128, tsl], o_ap, o_b)
            b1 = proj_chunk(w_in, C_KR, M=64)
            b2 = proj_chunk(w_sw, 512, M=64)
            yield from norm_gen(P, C, [C.ps[b1][0:64, :]], [C.bps[b1]], 64, C.ones[0:64, 0:64], 64, None, None, None, 7)
            o_ap, o_b = next_stage()
            rope_combine(b1, b2, 64, TB["GCk"][0:64, :], TB["GSk"][0:64, :], [TB["bGk"]], C.rstd[0:64, :], [C.brstd], o_ap[0:64, :], o_b, 0)
            store(d["krT"][:, tsl], o_ap[0:64, :], o_b)

        lane[0] = 0
        for tag in lane1():
            if tag is not None:
                fill += after.pop(tag)
            if fill:
                lane[0] = 1
                fill.pop(0)()
                lane[0] = 0
        lane[0] = 1
        for tag in list(after):
            fill += after.pop(tag)
        while fill:
            fill.pop(0)()
        lane[0] = 0


K1_OUTS = dict(rqT=([256, TOK], BF16), rkT=([256, TOK], BF16), sgT=([512, TOK], F32), vtok=([TOK, 512], BF16),
               qnT=([512, TOK], BF16), qrT=([256, TOK], BF16), knT=([512, TOK], BF16), vmtok=([TOK, 512], BF16),
               krT=([64, TOK], BF16), gdec=([256, TOK // 128], F32), gqT=([256, TOK], BF16), gkT=([256, TOK], BF16),
               x1T=([D, TOK], F32))


def build_k1(with_ffn=True):
    nc = bass.Bass("TRN2", target_bir_lowering=False)

    def din(name, shape, dt=F32):
        return nc.dram_tensor(name, shape, dt, kind="ExternalInput").ap()

    xT = din("xT", [D, TOK])
    wgu = din("wgu", [D, 2 * DFF])
    wd = din("wd", [DFF, D])
    g1 = din("g1", [128, 8])
    d = dict(w_in=din("w_in", [D, IN_COLS]), w_uq=din("w_uq", [384, 768]), w_ukv=din("w_ukv", [256, 1024]),
             vecs=din("vecs", [128, NVEC]), xitab=din("xitab", [128, 4 * TT]), gw=din("gw", [17, 256]),
             tri=din("tri", [128, 128]), posf=din("posf", [128, TOK]))
    for name, (shape, dt) in K1_OUTS.items():
        d[name] = nc.dram_tensor(name, shape, dt, kind="ExternalOutput").ap()
    P = Prog(nc)
    C = tok_ctx(P, nc)
    if with_ffn:
        ffn_phase(P, nc, C, xT, d["x1T"], wgu, wd, g1, dst_bufs=C.bxd)
    else:
        d["x1T"] = xT
    proj_phase(P, nc, C, d)
    P.emit()
    return nc


def post_phase(P, nc, C, d):
    WA = C.WA
    bW = C.bWA
    w_out = WA[:, 0:8 * 1024].rearrange("p (k c) -> p k c", k=8)
    P.dma("pool", w_out, d["w_out"].rearrange("(k p) n -> p k n", p=128), [], [bW], bW)
    WBf = C.WB.bitcast(F32)
    offb = [0]

    def cf(n):
        a = offb[0]
        offb[0] += n
        return WBf[:, a:a + n]

    ot = [cf(TT) for _ in range(8)]; bot = P.bufs("ot", 8)
    sg = [cf(TT) for _ in range(4)]; bsg = P.bufs("sg", 4)
    zt = [cf(TT) for _ in range(2)]; bzt = P.bufs("zt", 2)
    cat = C.act[:, 0:8, :]
    bcat = P.buf("cat")
    C.post_wb = bot + bsg + bzt
    C.post_act = [bcat]
    P.dma("sp", C.vec[:], d["vecs"], [], [C.bvec], C.bvec)
    x_v = d["x1T"].rearrange("(k p) t -> p k t", p=128)
    x_o = d["x2T"].rearrange("(k p) t -> p k t", p=128)
    srcs = [("roT", 0), ("roT", 1), ("moT", 0), ("moT", 1), ("moT", 2), ("moT", 3), ("goT", 0), ("goT", 1)]

    def load(i):
        s = i % 2
        tsl = slice(i * TT, (i + 1) * TT)
        P.dma("sp", C.xt[s][:], x_v[:, :, tsl], [], [C.bx[s]], C.bx[s])
        for c, (name, cc) in enumerate(srcs):
            P.dma("act" if c % 2 else "sp", ot[c], d[name][cc * 128:(cc + 1) * 128, tsl], [], [bot[c]], bot[c])
        for c in range(4):
            r0 = (0, 128, 256, 384)[c]
            P.dma("act" if c % 2 else "sp", sg[c], d["sgT"][r0:r0 + 128, tsl], [], [bsg[c]], bsg[c])

    load(0)
    for i in range(NT):
        s = i % 2
        tsl = slice(i * TT, (i + 1) * TT)
        xt, bx = C.xt[s], C.bx[s]
        for n2, (c, gcol, sgi) in enumerate(((0, V_RO, 0), (1, V_RO + 1, 1), (6, V_GO, 2), (7, V_GO + 1, 3))):
            q = n2 % 2
            norm_from_psum(P, C, [ot[c]], [bot[c]], 128, 128, C.ones2[:], 64, [zt[q]], bzt[q], [gcol], 7)
            P.I("dve", "tensor_tensor", [bzt[q], bsg[sgi]], [bcat], out=cat[:, c, :], in0=zt[q], in1=sg[sgi], op=ALU.mult)
        for c in range(2, 6):
            P.I("act", "copy", [bot[c]], [bcat], out=cat[:, c, :], in_=ot[c])
        if i + 1 < NT:
            load(i + 1)
        for m in range(8):
            r = m % 6
            for k in range(8):
                _mm(P, C.ps[r][:], w_out[:, k, m * 128:(m + 1) * 128], cat[:, k, :], k == 0, k == 7, [bW, bcat], [C.bps[r]])
            P.I("dve", "tensor_tensor", [C.bps[r], bx], [bx], out=xt[:, m, :], in0=C.ps[r][:], in1=xt[:, m, :], op=ALU.add)
        P.dma("sp", x_o[:, :, tsl], xt[:], [bx], [C.bxd[i]], bx, is_out=True)


def build_k3():
    nc = bass.Bass("TRN2", target_bir_lowering=False)

    def din(name, shape, dt=F32):
        return nc.dram_tensor(name, shape, dt, kind="ExternalInput").ap()

    d = dict(x1T=din("x1T", [D, TOK]), roT=din("roT", [256, TOK]), goT=din("goT", [256, TOK]), moT=din("moT", [512, TOK]),
             sgT=din("sgT", [512, TOK]), w_out=din("w_out", [D, D]), vecs=din("vecs", [128, NVEC]))
    wgu = din("wgu", [D, 2 * DFF])
    wd = din("wd", [DFF, D])
    g2 = din("g2", [128, 8])
    d["x2T"] = nc.dram_tensor("x2T", [D, TOK], F32, kind="ExternalOutput").ap()
    x3T = nc.dram_tensor("x3T", [D, TOK], F32, kind="ExternalOutput").ap()
    P = Prog(nc)
    C = tok_ctx(P, nc)
    post_phase(P, nc, C, d)
    for dst, srcs in ((C.bWB, C.post_wb), (C.bact, C.post_act)):
        for o in srcs:
            for k, v in o.writers.items():
                if k not in dst.writers or dst.writers[k].idx < v.idx:
                    dst.writers[k] = v
            for k, v in o.readers.items():
                if k not in dst.readers or dst.readers[k].idx < v.idx:
                    dst.readers[k] = v
    ffn_phase(P, nc, C, d["x2T"], x3T, wgu, wd, g2, src_bufs=C.bxd)
    P.emit()
    return nc


_BF = ml_dtypes.bfloat16
_CACHE = {}


def _get(name, fn):
    if name not in _CACHE:
        _CACHE[name] = fn()
    return _CACHE[name]


def _swap(g):
    return np.concatenate([g[32:], g[:32]])


def _consts():
    p = np.arange(128)
    inv = (10000.0 ** (-(np.arange(0, 64, 2, dtype=np.float32)) / 64.0)).astype(np.float32)
    inv_signed = np.where((p % 64) < 32, -1.0, 1.0).astype(np.float32) * inv[p % 32]
    t = (np.arange(TT) % 128 + 1).astype(np.float64)
    xitab = np.zeros((128, 4, TT), np.float32)
    for k in range(4):
        for half in range(2):
            h = (k % 2) * 2 + half
            lg = np.log1p(-2.0 ** (-5.0 - h))
            row = np.exp(lg * t) if k < 2 else np.exp(-lg * t) * 0.125
            xitab[half * 64:(half + 1) * 64, k, :] = row[None, :]
    s_, t_ = np.meshgrid(np.arange(128), np.arange(128), indexing="ij")
    tri = np.where(s_ <= t_, -1.0 / 16.0, 0.0).astype(np.float32)
    mask = np.where(s_ <= t_, 1.0, 0.0).astype(_BF)
    rdec = np.zeros((4, 64, NCH), np.float32)
    for h in range(4):
        rdec[h] = np.exp(np.log1p(-2.0 ** (-5.0 - h)) * 128.0)
    return dict(inv_signed=inv_signed, xitab=xitab.reshape(128, 4 * TT), tri=tri, mask=mask, rdec=rdec)


def _vecs(inp, l, cst):
    v = np.zeros((128, NVEC), np.float32)
    v[:, V_MIX:V_MIX + 8] = inp["mix_norm"][l].reshape(8, 128).T
    v[:, V_CQ:V_CQ + 3] = inp["mla_q_norm"][l].reshape(3, 128).T
    v[:, V_CKV:V_CKV + 2] = inp["mla_kv_norm"][l].reshape(2, 128).T
    v[:, V_QN] = inp["mla_q_nope_norm"][l]
    v[:, V_KN] = inp["mla_k_nope_norm"][l]
    gq = inp["mla_q_rope_norm"][l]
    gk = inp["mla_k_rope_norm"][l]
    v[:, V_QR] = np.tile(gq, 2)
    v[:, V_QRS] = np.tile(_swap(gq), 2)
    v[:64, V_KR] = gk
    v[:64, V_KRS] = _swap(gk)
    v[:, V_INV] = cst["inv_signed"]
    v[:, V_RO:V_RO + 2] = inp["ret_out_norm"][l].reshape(2, 128).T
    v[:, V_GO:V_GO + 2] = inp["gla_out_norm"][l].reshape(2, 128).T
    return v


def _run(nc, in_maps):
    res = run_bass_kernel_spmd(nc, in_maps, core_ids=list(range(NCORE)))
    return res.results


def _cat_tok(res, name, b):
    return np.concatenate([res[b * 4 + q][name] for q in range(4)], axis=1)


def _cat_rows(res, name, b):
    return np.concatenate([res[b * 4 + q][name] for q in range(4)], axis=0)


def kernel(**inp):
    inp = {k: np.asarray(v) for k, v in inp.items()}
    cst = _get("cst", _consts)
    k1 = _get("k1", build_k1)
    k2 = _get("k2", lambda: build_k2(phases=(1, 2)))
    k3 = _get("k3", build_k3)
    x = inp["x"]
    posf = inp["positions"].astype(np.float32)
    xT = [np.ascontiguousarray(x[c // 4, (c % 4) * TOK:(c % 4 + 1) * TOK, :].T) for c in range(NCORE)]
    for l in range(DEPTH):
        vecs = _vecs(inp, l, cst)
        gw = np.concatenate([inp["gla_w_gate_up"][l], inp["gla_gate_bias"][l][None, :]], axis=0)
        ims = []
        for c in range(NCORE):
            b, q = c // 4, c % 4
            ims.append(dict(xT=xT[c], wgu=inp["ffn1_w_gate_up"][l], wd=inp["ffn1_w_down"][l],
                            g1=np.ascontiguousarray(inp["ffn1_norm"][l].reshape(8, 128).T),
                            w_in=inp["w_in"][l], w_uq=inp["mla_w_uq"][l], w_ukv=inp["mla_w_ukv"][l], vecs=vecs,
                            xitab=cst["xitab"], gw=gw, tri=cst["tri"],
                            posf=np.ascontiguousarray(np.broadcast_to(posf[b, q * TOK:(q + 1) * TOK][None, :], (128, TOK)))))
        r1 = _run(k1, ims)
        ima, imb = [], []
        for b in range(B):
            rq, rk = _cat_tok(r1, "rqT", b), _cat_tok(r1, "rkT", b)
            gq, gk = _cat_tok(r1, "gqT", b), _cat_tok(r1, "gkT", b)
            vt = _cat_rows(r1, "vtok", b)
            qn, qr = _cat_tok(r1, "qnT", b), _cat_tok(r1, "qrT", b)
            kn, kr = _cat_tok(r1, "knT", b), _cat_tok(r1, "krT", b)
            vm = _cat_rows(r1, "vmtok", b)
            gdec = _cat_tok(r1, "gdec", b)
            for h in range(4):
                hs = slice(h * 64, (h + 1) * 64)
                ima.append(dict(lqk=np.ascontiguousarray(np.stack([rq[hs], rk[hs], gq[hs], gk[hs]], axis=1)),
                                lkv=np.ascontiguousarray(np.concatenate([rk[hs].T, vt[:, h * 64:(h + 1) * 64], gk[hs].T,
                                                                         vt[:, 256 + h * 64:256 + (h + 1) * 64]], axis=1)),
                                ldec=np.ascontiguousarray(np.stack([cst["rdec"][h], gdec[hs]], axis=1)), mask=cst["mask"]))
                imb.append(dict(qn=np.ascontiguousarray(qn[h * 128:(h + 1) * 128]), qr=np.ascontiguousarray(qr[hs]),
                                kn=np.ascontiguousarray(kn[h * 128:(h + 1) * 128]), kr=kr,
                                vm=np.ascontiguousarray(vm[:, h * 128:(h + 1) * 128]), mask=cst["mask"]))
        r2a = _run(k2, [dict(a_, **b_) for a_, b_ in zip(ima, imb)])
        r2b = r2a
        im3 = []
        for c in range(NCORE):
            b, q = c // 4, c % 4
            ts = slice(q * TOK, (q + 1) * TOK)
            roT = np.concatenate([r2a[b * 4 + h]["lo"][ts, 0:64].T for h in range(4)], axis=0)
            goT = np.concatenate([r2a[b * 4 + h]["lo"][ts, 64:128].T for h in range(4)], axis=0)
            moT = np.concatenate([r2b[b * 4 + h]["moT"][:, ts] for h in range(4)], axis=0)
            im3.append(dict(x1T=r1[c]["x1T"], roT=np.ascontiguousarray(roT), goT=np.ascontiguousarray(goT),
                            moT=np.ascontiguousarray(moT), sgT=r1[c]["sgT"], w_out=inp["w_out"][l], vecs=vecs,
                            wgu=inp["ffn2_w_gate_up"][l], wd=inp["ffn2_w_down"][l],
                            g2=np.ascontiguousarray(inp["ffn2_norm"][l].reshape(8, 128).T)))
        r3 = _run(k3, im3)
        xT = [r3[c]["x3T"] for c in range(NCORE)]
        if _CACHE.get("debug") is not None:
            _CACHE["debug"].append(dict(r1=r1, r2a=r2a, r2b=r2b, r3=r3))
    out = np.empty((B, S, D), np.float32)
    for c in range(NCORE):
        out[c // 4, (c % 4) * TOK:(c % 4 + 1) * TOK, :] = xT[c].T
    return out
```

```python
import numpy as np
import ml_dtypes
import concourse.bass as bass
import concourse.mybir as mybir
from concourse.bass_utils import run_bass_kernel_spmd

F32 = mybir.dt.float32
BF16 = mybir.dt.bfloat16
I32 = mybir.dt.int32
AF = mybir.ActivationFunctionType
ALU = mybir.AluOpType

D = 1024
B = 2
S = 16384
DEPTH = 2
DFF = 2816
NCORE = 8
TOK = B * S // NCORE
TT = 512
NT = TOK // TT
EPS = 1e-6
IN_COLS = 2768
C_RQ, C_RK, C_RV, C_RG = 0, 256, 512, 768
C_CQ, C_CKV, C_KR = 1024, 1408, 1664
C_GQ, C_GK, C_GV, C_GA, C_GR = 1728, 1984, 2240, 2496, 2512


class Buf:
    __slots__ = ("name", "writers", "readers", "sem", "dcount", "excl")

    def __init__(self, name, excl=False):
        self.name = name
        self.excl = excl
        self.writers = {}
        self.readers = {}
        self.sem = None
        self.dcount = 0


class Op:
    __slots__ = ("eng", "fn", "deps", "needs_inc", "sem", "count", "is_dma", "idx")

    def __init__(self, eng, fn, is_dma):
        self.eng = eng
        self.fn = fn
        self.deps = []
        self.needs_inc = False
        self.sem = None
        self.count = 0
        self.is_dma = is_dma


ENGS = ("pe", "act", "dve", "pool", "sp")
ROT = 30000


class Prog:
    def __init__(self, nc):
        self.nc = nc
        self.ops = {e: [] for e in ENGS}
        self.nops = 0
        self.dma_sems = []
        self.out_dmas = []

    def buf(self, name):
        return Buf(name)

    def bufs(self, name, n, excl=False):
        return [Buf(f"{name}{i}", excl) for i in range(n)]

    def _dep(self, op, prod, kind):
        if prod is None or prod is op:
            return
        if not prod.is_dma and prod.eng == op.eng and not op.is_dma:
            if op.eng == "pe" or kind != "raw":
                return
        prod.needs_inc = True
        op.deps.append(prod)

    def add(self, eng, fn, reads=(), writes=(), dma_buf=None, is_out=False):
        is_dma = dma_buf is not None
        op = Op(eng, fn, is_dma)
        op.idx = self.nops
        self.nops += 1
        for b in reads:
            for w in b.writers.values():
                self._dep(op, w, "raw")
            if b.excl:
                for r in b.readers.values():
                    self._dep(op, r, "war")
        for b in writes:
            for w in b.writers.values():
                self._dep(op, w, "waw")
            for r in b.readers.values():
                self._dep(op, r, "war")
        if is_dma:
            if dma_buf.sem is None:
                dma_buf.sem = self.nc.semaphore(f"d{len(self.dma_sems)}_{dma_buf.name}").__enter__()
                self.dma_sems.append(dma_buf.sem)
            dma_buf.dcount += 16
            op.sem = dma_buf.sem
            op.count = dma_buf.dcount
            op.needs_inc = True
            key = ("dma", id(dma_buf))
            if is_out:
                self.out_dmas.append(op)
        else:
            key = eng
        for b in reads:
            b.readers[key] = op
        for b in writes:
            b.writers = {key: op}
            b.readers = {}
        self.ops[eng].append(op)
        return op

    def I(self, eng, meth, reads, writes, *args, **kw):
        return self.add(eng, lambda e: getattr(e, meth)(*args, **kw), reads=reads, writes=writes)

    def dma(self, eng, out, in_, reads, writes, dma_buf, partial=False, is_out=False):
        fn = lambda e: e.dma_start(out=out, in_=in_)
        if partial:
            return self.add_partial_write(eng, fn, reads, writes, dma_buf)
        return self.add(eng, fn, reads, writes, dma_buf, is_out)

    def add_partial_write(self, eng, fn, reads=(), writes=(), dma_buf=None):
        saved = [(b, dict(b.writers), dict(b.readers)) for b in writes]
        op = self.add(eng, fn, reads, writes, dma_buf)
        for b, w, r in saved:
            key = ("dma", id(dma_buf)) if dma_buf is not None else eng
            w = dict(w)
            w[key] = op
            b.writers = w
            b.readers = r
        return op

    def emit(self):
        nc = self.nc
        eng_sems = {}
        for e in ENGS:
            cnt = 0
            sems = []
            for op in self.ops[e]:
                if op.is_dma or not op.needs_inc:
                    continue
                k = cnt // ROT
                if k >= len(sems):
                    sems.append(nc.semaphore(f"c_{e}{k}").__enter__())
                op.sem = sems[k]
                op.count = cnt % ROT + 1
                cnt += 1
            eng_sems[e] = sems
        final_waits = [(op.sem, op.count) for op in self.out_dmas]
        fw = {}
        for s, c in final_waits:
            fw[id(s)] = (s, max(c, fw.get(id(s), (s, 0))[1]))

        def run(engname, eng):
            waited = {}
            for op in self.ops[engname]:
                need = {}
                for p in op.deps:
                    k = id(p.sem)
                    if p.count > need.get(k, (None, 0))[1]:
                        need[k] = (p.sem, p.count)
                for k, (s, c) in need.items():
                    if waited.get(k, 0) >= c:
                        continue
                    eng.wait_ge(s, c)
                    waited[k] = c
                ins = op.fn(eng)
                if op.needs_inc:
                    ins.then_inc(op.sem, 16 if op.is_dma else 1)
            if engname == "sp":
                for s, c in fw.values():
                    eng.wait_ge(s, c)

        with nc.Block() as block:
            @block.tensor
            def _(t):
                run("pe", t)

            @block.scalar
            def _(t):
                run("act", t)

            @block.vector
            def _(t):
                run("dve", t)

            @block.gpsimd
            def _(t):
                run("pool", t)

            @block.sync
            def _(t):
                run("sp", t)


def _mm(P, out_ap, lhsT, rhs, start, stop, reads, writes):
    return P.I("pe", "matmul", reads, writes, out_ap, lhsT, rhs, start=start, stop=stop)


class TokCtx:
    pass


def alloc(nc, name, shape, dt):
    return nc.sbuf_tensor("s_" + name, shape, dt).__enter__()


def ffn_phase(P, nc, C, x_src, x_dst, wgu_d, wd_d, gain_d, src_bufs=None, dst_bufs=None):
    WA, WB = C.WA, C.WB
    wgu = WA[:, 0:8 * 5632].rearrange("p (k c) -> p k c", k=8)
    wd = WB[:, 0:22 * 1024].rearrange("p (k c) -> p k c", k=22)
    for k in range(8):
        P.dma("pool", wgu[:, k, :], wgu_d[k * 128:(k + 1) * 128, :], [], [C.bWA], C.bWA, partial=(k > 0))
    wd_v = wd_d.rearrange("(k p) n -> p k n", p=128)
    for k0 in range(0, 22, 11):
        P.dma("pool", wd[:, k0:k0 + 11, :], wd_v[:, k0:k0 + 11, :], [], [C.bWB], C.bWB, partial=(k0 > 0))
    P.dma("sp", C.gain[:, 0:8], gain_d, [], [C.bgain], C.bgain)

    x_src_v = x_src.rearrange("(k p) t -> p k t", p=128)
    x_dst_v = x_dst.rearrange("(k p) t -> p k t", p=128)

    def load(i):
        s = i % 2
        P.dma("sp", C.xt[s][:], x_src_v[:, :, i * TT:(i + 1) * TT], [src_bufs[i]] if src_bufs else [], [C.bx[s]], C.bx[s])

    load(0)
    for i in range(NT):
        s = i % 2
        if i + 1 < NT:
            load(i + 1)
        xt = C.xt[s]
        bx = C.bx[s]
        yb = C.bps[6]
        ps = C.ps[6]
        for k in range(8):
            q = k % 2
            P.I("act", "activation", [bx], [C.bsq[q]], out=C.sq[q][:], in_=xt[:, k, :], func=AF.Square)
            _mm(P, ps[:], C.ones[:], C.sq[q][:], k == 0, k == 7, [C.bsq[q], C.bones], [yb])
        P.I("act", "activation", [yb, C.bconst], [C.brstd], out=C.rstd[:], in_=ps[:], func=AF.Ln,
            bias=C.epsc[:, 0:1], scale=1.0 / D)
        P.I("act", "activation", [C.brstd], [C.brstd], out=C.rstd[:], in_=C.rstd[:], func=AF.Exp, scale=-0.5)
        for k in range(8):
            P.I("dve", "scalar_tensor_tensor", [bx, C.brstd, C.bgain], [C.bhT], out=C.hT[:, k, :], in0=xt[:, k, :],
                scalar=C.gain[:, k:k + 1], in1=C.rstd[:], op0=ALU.mult, op1=ALU.mult)
        for j in range(22):
            r = j % 3
            gb, ub = C.bps[r], C.bps[3 + r]
            gp, up = C.ps[r], C.ps[3 + r]
            for k in range(8):
                _mm(P, gp[:], wgu[:, k, j * 128:(j + 1) * 128], C.hT[:, k, :], k == 0, k == 7, [C.bWA, C.bhT], [gb])
            for k in range(8):
                _mm(P, up[:], wgu[:, k, DFF + j * 128:DFF + (j + 1) * 128], C.hT[:, k, :], k == 0, k == 7,
                    [C.bWA, C.bhT], [ub])
            q = j % 2
            P.I("act", "activation", [gb], [C.bstmp[q]], out=C.stmp[q][:], in_=gp[:], func=AF.Silu)
            P.I("dve", "tensor_tensor", [ub, C.bstmp[q]], [C.bact], out=C.act[:, j, :], in0=up[:], in1=C.stmp[q][:],
                op=ALU.mult)
        for m in range(8):
            r = 6 + (m % 2)
            yb, yp = C.bps[r], C.ps[r]
            for k in range(22):
                _mm(P, yp[:], wd[:, k, m * 128:(m + 1) * 128], C.act[:, k, :], k == 0, k == 21, [C.bWB, C.bact], [yb])
            P.I("dve", "scalar_tensor_tensor", [yb, bx], [bx], out=xt[:, m, :], in0=yp[:], scalar=0.5, in1=xt[:, m, :],
                op0=ALU.mult, op1=ALU.add)
        P.dma("sp", x_dst_v[:, :, i * TT:(i + 1) * TT], xt[:], [bx], [dst_bufs[i]] if dst_bufs else [], bx, is_out=True)


def tok_ctx(P, nc):
    C = TokCtx()
    C.WA = alloc(nc, "WA", [128, 8 * 5632], BF16)
    C.WB = alloc(nc, "WB", [128, 22 * 1024], BF16)
    C.bWA, C.bWB = P.buf("WA"), P.buf("WB")
    C.xt = [alloc(nc, f"xt{i}", [128, 8, TT], F32) for i in range(2)]
    C.bx = P.bufs("x", 2)
    C.hT = alloc(nc, "hT", [128, 8, TT], BF16)
    C.bhT = P.buf("hT")
    C.act = alloc(nc, "act", [128, 22, TT], BF16)
    C.bact = P.buf("act")
    C.sq = [alloc(nc, f"sq{i}", [128, TT], BF16) for i in range(2)]
    C.bsq = P.bufs("sq", 2)
    C.rstd = alloc(nc, "rstd", [128, TT], F32)
    C.brstd = P.buf("rstd")
    C.stmp = [alloc(nc, f"stmp{i}", [128, TT], BF16) for i in range(2)]
    C.bstmp = P.bufs("stmp", 2)
    C.gain = alloc(nc, "gain", [128, 32], F32)
    C.bgain = P.buf("gain")
    C.ones = alloc(nc, "ones", [128, 128], BF16)
    C.bones = P.buf("ones")
    C.epsc = alloc(nc, "epsc", [128, 4], F32)
    C.vec = alloc(nc, "vec", [128, NVEC], F32)
    C.bvec = P.buf("vec")
    C.ones2 = alloc(nc, "ones2", [128, 128], BF16)
    C.bones2 = C.bones
    C.bxd = P.bufs("xd", NT)
    C.bconst = P.buf("const")
    C.ps = [nc.psum_tensor(f"ps{i}", [128, 512], F32).__enter__() for i in range(8)]
    C.bps = P.bufs("ps", 8, excl=True)
    P.I("dve", "memset", [], [C.bones], C.ones[:], 1.0)
    P.I("dve", "memset", [], [C.bconst], C.epsc[:], EPS)
    P.I("dve", "memset", [C.bconst], [C.bconst], C.epsc[:, 1:2], 1.0)
    P.I("dve", "memset", [C.bconst], [C.bconst], C.epsc[:, 2:3], float(np.log(0.125)))
    P.I("pool", "memset", [], [C.bones], C.ones2[:], 0.0)
    P.I("pool", "memset", [C.bones], [C.bones], C.ones2[0:64, 0:64], 1.0)
    P.I("pool", "memset", [C.bones], [C.bones], C.ones2[64:128, 64:128], 1.0)
    return C


def build_k1_test():
    nc = bass.Bass("TRN2", target_bir_lowering=False)
    xT = nc.dram_tensor("xT", [D, TOK], F32, kind="ExternalInput").ap()
    wgu = nc.dram_tensor("wgu", [D, 2 * DFF], F32, kind="ExternalInput").ap()
    wd = nc.dram_tensor("wd", [DFF, D], F32, kind="ExternalInput").ap()
    g1 = nc.dram_tensor("g1", [128, 8], F32, kind="ExternalInput").ap()
    x1T = nc.dram_tensor("x1T", [D, TOK], F32, kind="ExternalOutput").ap()
    P = Prog(nc)
    C = tok_ctx(P, nc)
    ffn_phase(P, nc, C, xT, x1T, wgu, wd, g1)
    P.emit()
    return nc


NQG = S // 512
NCH = S // 128
SCALE_MLA = 192.0 ** -0.5


def build_k2(phases=(1, 2)):
    nc = bass.Bass("TRN2", target_bir_lowering=False)

    def din(name, shape, dt):
        return nc.dram_tensor(name, shape, dt, kind="ExternalInput").ap()

    if 2 in phases:
        qn_d = din("qn", [128, S], BF16)
        qr_d = din("qr", [64, S], BF16)
        kn_d = din("kn", [128, S], BF16)
        kr_d = din("kr", [64, S], BF16)
        vm_d = din("vm", [S, 128], BF16)
        mo_d = nc.dram_tensor("moT", [128, S], F32, kind="ExternalOutput").ap()
    if 1 in phases:
        lqk_d = din("lqk", [64, 4, S], BF16)
        lkv_d = din("lkv", [S, 256], BF16)
        ldec_d = din("ldec", [64, 2, NCH], F32)
        lo_d = nc.dram_tensor("lo", [S, 128], F32, kind="ExternalOutput").ap()
    mask_d = din("mask", [128, 128], BF16)

    P = Prog(nc)
    ps = [nc.psum_tensor(f"ps{i}", [128, 512], F32).__enter__() for i in range(8)]
    bps = P.bufs("ps", 8, excl=True)
    mask = alloc(nc, "mask", [128, 128], BF16)
    bmask = P.buf("mask")
    P.dma("sp", mask[:], mask_d, [], [bmask], bmask)

    L = {}
    NB = 3
    if 1 in phases:
        qk = [alloc(nc, f"lqk{i}", [64, 4, 512], BF16) for i in range(NB)]
        kv = [alloc(nc, f"lkv{i}", [128, 4, 256], BF16) for i in range(NB)]
        bin_ = P.bufs("lin", NB)
        dec = alloc(nc, "ldec", [64, 2, NCH], F32)
        bdec = P.buf("ldec")
        osb = [alloc(nc, f"losb{i}", [128, 4, 128], F32) for i in range(2)]
        bosb = P.bufs("losb", 2)
        P.dma("sp", dec[:], ldec_d, [], [bdec], bdec)
    for xi, X in enumerate(("r", "g") if 1 in phases else ()):
        o = TokCtx()
        o.qi, o.ki, o.kti, o.vi, o.oi = 2 * xi, 2 * xi + 1, 128 * xi, 128 * xi + 64, 64 * xi
        o.scm = [alloc(nc, f"{X}scm{i}", [128, 128], BF16) for i in range(2)]
        o.bscm = P.bufs(X + "scm", 2)
        o.st = alloc(nc, X + "st", [64, 64], F32)
        o.tmp = alloc(nc, X + "tmp", [64, 64], F32)
        o.stb = alloc(nc, X + "stb", [64, 64], BF16)
        o.bst, o.btmp, o.bstb = P.buf(X + "st"), P.buf(X + "tmp"), P.buf(X + "stb")
        o.xi = xi
        L[X] = o
        P.I("dve", "memset", [], [o.bst], o.st[:], 0.0)
        P.I("dve", "memset", [], [o.bstb], o.stb[:], 0.0)

    def lin_load(g):
        s = g % NB
        t0 = g * 512
        P.dma("sp", qk[s][:], lqk_d[:, :, t0:t0 + 512], [], [bin_[s]], bin_[s])
        P.dma("act", kv[s][:], lkv_d[t0:t0 + 512, :].rearrange("(n p) d -> p n d", p=128), [], [bin_[s]], bin_[s],
              partial=True)

    def lin_A(n):
        g, c = n // 4, n % 4
        s = g % NB
        cs = slice(c * 128, (c + 1) * 128)
        for xi, X in enumerate(("r", "g")):
            o = L[X]
            sb = xi * 2 + (n % 2)
            m2 = n % 2
            _mm(P, ps[sb][:, 0:128], qk[s][:, o.ki, cs], qk[s][:, o.qi, cs], True, True, [bin_[s]], [bps[sb]])
            P.I("dve", "tensor_tensor", [bps[sb], bmask], [o.bscm[m2]], out=o.scm[m2][:], in0=ps[sb][:, 0:128],
                in1=mask[:], op=ALU.mult)

    def lin_C(n):
        g, c = n // 4, n % 4
        s = g % NB
        so = g % 2
        cs = slice(c * 128, (c + 1) * 128)
        for xi, X in enumerate(("r", "g")):
            o = L[X]
            ob = 4 + xi * 2 + (n % 2)
            m2 = n % 2
            vv = kv[s][:, c, o.vi:o.vi + 64]
            _mm(P, ps[ob][:, 0:64], o.scm[m2][:], vv, True, False, [o.bscm[m2], bin_[s]], [bps[ob]])
            _mm(P, ps[ob][:, 0:64], qk[s][:, o.qi, cs], o.stb[:], False, True, [bin_[s], o.bstb], [bps[ob]])
            _mm(P, ps[ob][0:64, 64:128], kv[s][:, c, o.kti:o.kti + 64], vv, True, True, [bin_[s]], [bps[ob]])
            P.I("dve", "scalar_tensor_tensor", [bps[ob], bdec, o.btmp], [o.bstb], out=o.stb[:], in0=ps[ob][0:64, 64:128],
                scalar=dec[:, xi, n:n + 1], in1=o.tmp[:], op0=ALU.mult, op1=ALU.add)
            P.I("dve", "scalar_tensor_tensor", [bps[ob], bdec, o.btmp], [o.bst], out=o.st[:], in0=ps[ob][0:64, 64:128],
                scalar=dec[:, xi, n:n + 1], in1=o.tmp[:], op0=ALU.mult, op1=ALU.add)
            P.I("act", "copy", [bps[ob]], [bosb[so]], out=osb[so][:, c, o.oi:o.oi + 64], in_=ps[ob][:, 0:64])
            if n + 1 < NCH:
                P.I("dve", "tensor_scalar", [o.bst, bdec], [o.btmp], out=o.tmp[:], in0=o.st[:],
                    scalar1=dec[:, xi, n + 1:n + 2], scalar2=None, op0=ALU.mult)
        if c == 3:
            P.dma("sp", lo_d[g * 512:(g + 1) * 512, :].rearrange("(n p) d -> p n d", p=128), osb[so][:],
                  [bosb[so]], [], bosb[so], is_out=True)

    if 1 in phases:
        for X in ("r", "g"):
            P.I("dve", "memset", [], [L[X].btmp], L[X].tmp[:], 0.0)
        for g0 in range(NB):
            lin_load(g0)
        lin_A(0)
        for n in range(NCH):
            if n + 1 < NCH:
                lin_A(n + 1)
            lin_C(n)
            if n % 4 == 3 and n // 4 + NB < NQG:
                lin_load(n // 4 + NB)

    if 2 not in phases:
        P.emit()
        return nc
    kn = alloc(nc, "kn", [128, S], BF16)
    kr = alloc(nc, "kr", [64, S], BF16)
    V = alloc(nc, "V", [128, NCH, 128], BF16)
    bkv = P.bufs("kv", NQG)
    kvsem = P.bufs("kvsem", 4)
    qn = [alloc(nc, f"qn{i}", [128, 512], BF16) for i in range(2)]
    qr = [alloc(nc, f"qr{i}", [64, 512], BF16) for i in range(2)]
    bq = P.bufs("q", 2)
    NSC = 4
    pt = [alloc(nc, f"pt{i}", [128, 512], BF16) for i in range(NSC)]
    bpt = P.bufs("pt", NSC)
    psum_t = [alloc(nc, f"ptsum{i}", [128, 512], F32) for i in range(2)]
    bpsum = P.bufs("ptsum", 2)
    rec = [alloc(nc, f"rec{i}", [128, 512], F32) for i in range(2)]
    brec = P.bufs("rec", 2)
    mosb = [alloc(nc, f"mosb{i}", [128, 512], F32) for i in range(2)]
    bmosb = P.bufs("mosb", 2)
    onesf = alloc(nc, "onesf", [128, 128], F32)
    bonesf = P.buf("onesf")
    P.I("pool", "memset", [], [bonesf], onesf[:], 1.0)
    vm_v = vm_d.rearrange("(n p) d -> p n d", p=128)

    def kv_load(g):
        t0 = g * 512
        sb = kvsem[g % 4]
        rd = [bkv[g - 4]] if g >= 4 else []
        P.dma("sp", kn[:, t0:t0 + 512], kn_d[:, t0:t0 + 512], rd, [bkv[g]], sb)
        P.dma("sp", kr[:, t0:t0 + 512], kr_d[:, t0:t0 + 512], [], [bkv[g]], sb, partial=True)
        P.dma("sp", V[:, 4 * g:4 * g + 4, 0:128], vm_v[:, 4 * g:4 * g + 4, :], [], [bkv[g]], sb, partial=True)

    def q_load(g):
        s = g % 2
        t0 = g * 512
        P.dma("sp", qn[s][:], qn_d[:, t0:t0 + 512], [], [bq[s]], bq[s])
        P.dma("sp", qr[s][:], qr_d[:, t0:t0 + 512], [], [bq[s]], bq[s], partial=True)

    LOOK = 3
    SUMB = 6
    blocks = [(g, kb) for g in range(NQG) for kb in range(4 * g + 4)]
    nblk = len(blocks)
    kv_load(0)
    q_load(0)

    def emit_sc(i):
        g, kb = blocks[i]
        s = g % 2
        if kb == 0 and g + 1 < NQG:
            kv_load(g + 1)
            q_load(g + 1)
        j = kb - 4 * g
        c0 = 128 * j if j > 0 else 0
        r = i % NSC
        ks = slice(kb * 128, (kb + 1) * 128)
        kvb = bkv[kb // 4]
        _mm(P, ps[r][:, c0:512], kn[:, ks], qn[s][:, c0:512], True, False, [kvb, bq[s]], [bps[r]])
        _mm(P, ps[r][:, c0:512], kr[:, ks], qr[s][:, c0:512], False, True, [kvb, bq[s]], [bps[r]])
        P.I("act", "activation", [bps[r]], [bpt[r]], out=pt[r][:, c0:512], in_=ps[r][:, c0:512], func=AF.Exp,
            scale=SCALE_MLA)
        if j >= 0:
            P.I("pool", "tensor_tensor", [bpt[r], bmask], [bpt[r]], out=pt[r][:, 128 * j:128 * j + 128],
                in0=pt[r][:, 128 * j:128 * j + 128], in1=mask[:], op=ALU.mult)
        if kb == 0:
            P.I("dve", "tensor_copy", [bpt[r]], [bpsum[s]], out=psum_t[s][:], in_=pt[r][:])
        else:
            P.I("dve", "tensor_tensor", [bpt[r], bpsum[s]], [bpsum[s]], out=psum_t[s][:, c0:512], in0=psum_t[s][:, c0:512],
                in1=pt[r][:, c0:512], op=ALU.add)

    def emit_pv(i):
        g, kb = blocks[i]
        s = g % 2
        j = kb - 4 * g
        c0 = 128 * j if j > 0 else 0
        r = i % NSC
        kvb = bkv[kb // 4]
        ab = 4 + s
        _mm(P, ps[ab][:, c0:512], V[:, kb, 0:128], pt[r][:, c0:512], kb == 0, kb == 4 * g + 3, [bpt[r], kvb], [bps[ab]])
        if kb == 4 * g + 3:
            P.I("pe", "matmul", [bpsum[s], bonesf], [bps[SUMB]], ps[SUMB][:], onesf[:], psum_t[s][:], start=True, stop=True)
            P.I("dve", "reciprocal", [bps[SUMB]], [brec[s]], out=rec[s][:], in_=ps[SUMB][:])
            P.I("dve", "tensor_tensor", [bps[ab], brec[s]], [bmosb[s]], out=mosb[s][:], in0=ps[ab][:], in1=rec[s][:], op=ALU.mult)
            P.dma("sp", mo_d[:, g * 512:(g + 1) * 512], mosb[s][:], [bmosb[s]], [], bmosb[s], is_out=True)

    for i in range(nblk + LOOK):
        if i < nblk:
            emit_sc(i)
        if i >= LOOK:
            emit_pv(i - LOOK)
    P.emit()
    return nc


TWO_PI = 2.0 * np.pi
MAGIC = 12582912.0
CW1 = 6.28125
CW2 = TWO_PI - 6.28125
V_MIX = 0
V_CQ = 8
V_CKV = 11
V_QN = 13
V_KN = 14
V_QR = 15
V_QRS = 16
V_KR = 17
V_KRS = 18
V_INV = 19
V_RO = 20
V_GO = 22
NVEC = 24


def norm_from_psum(P, C, src_aps, src_bufs, K, nparts, ones_ap, n_norm, out_aps, out_buf, gain_cols, stat_bank):
    sb, sp = C.bps[stat_bank], C.ps[stat_bank]
    n = len(src_aps)
    for k in range(n):
        q = k % 2
        P.I("act", "activation", [src_bufs[k]], [C.bsq[q]], out=C.sq[q][0:nparts, :], in_=src_aps[k], func=AF.Square)
        _mm(P, sp[0:nparts, :], ones_ap, C.sq[q][0:nparts, :], k == 0, k == n - 1, [C.bsq[q], C.bones], [sb])
    P.I("act", "activation", [sb, C.bconst], [C.brstd], out=C.rstd[0:nparts, :], in_=sp[0:nparts, :], func=AF.Ln,
        bias=C.epsc[0:nparts, 0:1], scale=1.0 / n_norm)
    P.I("act", "activation", [C.brstd], [C.brstd], out=C.rstd[0:nparts, :], in_=C.rstd[0:nparts, :], func=AF.Exp, scale=-0.5)
    if out_aps is not None:
        for k in range(n):
            P.I("dve", "scalar_tensor_tensor", [src_bufs[k], C.brstd, C.bvec], [out_buf], out=out_aps[k], in0=src_aps[k],
                scalar=C.vec[0:nparts, gain_cols[k]:gain_cols[k] + 1], in1=C.rstd[0:nparts, :], op0=ALU.mult, op1=ALU.mult)


def norm_gen(P, C, src_aps, src_bufs, nparts, ones_ap, n_norm, out_aps, out_buf, gain_cols, stat_bank):
    sb, sp = C.bps[stat_bank], C.ps[stat_bank]
    n = len(src_aps)
    if n <= 2:
        for k in range(n):
            P.I("act", "activation", [src_bufs[k]], [C.bsq[k]], out=C.sq[k][0:nparts, :], in_=src_aps[k], func=AF.Square)
        yield None
        for k in range(n):
            _mm(P, sp[0:nparts, :], ones_ap, C.sq[k][0:nparts, :], k == 0, k == n - 1, [C.bsq[k], C.bones], [sb])
    else:
        yield None
        for k in range(n):
            q = k % 2
            P.I("act", "activation", [src_bufs[k]], [C.bsq[q]], out=C.sq[q][0:nparts, :], in_=src_aps[k], func=AF.Square)
            _mm(P, sp[0:nparts, :], ones_ap, C.sq[q][0:nparts, :], k == 0, k == n - 1, [C.bsq[q], C.bones], [sb])
    P.I("act", "activation", [sb, C.bconst], [C.brstd], out=C.rstd[0:nparts, :], in_=sp[0:nparts, :], func=AF.Ln,
        bias=C.epsc[0:nparts, 0:1], scale=1.0 / n_norm)
    P.I("act", "activation", [C.brstd], [C.brstd], out=C.rstd[0:nparts, :], in_=C.rstd[0:nparts, :], func=AF.Exp, scale=-0.5)
    if out_aps is not None:
        for k in range(n):
            P.I("dve", "scalar_tensor_tensor", [src_bufs[k], C.brstd, C.bvec], [out_buf], out=out_aps[k], in0=src_aps[k],
                scalar=C.vec[0:nparts, gain_cols[k]:gain_cols[k] + 1], in1=C.rstd[0:nparts, :], op0=ALU.mult, op1=ALU.mult)
    yield None


def proj_phase(P, nc, C, d):
    WA = C.WA
    off = [0]

    def carve(n):
        a = off[0]
        off[0] += n
        return WA[:, a:a + n]

    bW = C.bWA
    w_in = carve(8 * IN_COLS).rearrange("p (k c) -> p k c", k=8)
    w_sw = carve(8 * 576).rearrange("p (k c) -> p k c", k=8)
    wuq_n = carve(3 * 512).rearrange("p (k c) -> p k c", k=3)
    wuq_r = carve(3 * 256).rearrange("p (k c) -> p k c", k=3)
    wuq_rs = carve(3 * 256).rearrange("p (k c) -> p k c", k=3)
    wkv_k = carve(2 * 512).rearrange("p (k c) -> p k c", k=2)
    wkv_v = carve(2 * 512).rearrange("p (k c) -> p k c", k=2)
    first = [True]

    def wdma(out, in_):
        P.dma("pool", out, in_, [], [bW], bW, partial=not first[0])
        first[0] = False

    win_d = d["w_in"]
    wdma(w_in, win_d.rearrange("(k p) c -> p k c", p=128))
    for k in range(3):
        rows = slice(k * 128, (k + 1) * 128)
        srcu = d["w_uq"][rows, :].rearrange("p (h c) -> p h c", h=4)
        wdma(wuq_n[:, k, :].rearrange("p (h c) -> p h c", h=4), srcu[:, :, 0:128])
        wdma(wuq_r[:, k, :].rearrange("p (h c) -> p h c", h=4), srcu[:, :, 128:192])
    for k in range(2):
        rows = slice(k * 128, (k + 1) * 128)
        srck = d["w_ukv"][rows, :].rearrange("p (h c) -> p h c", h=4)
        wdma(wkv_k[:, k, :].rearrange("p (h c) -> p h c", h=4), srck[:, :, 0:128])
        wdma(wkv_v[:, k, :].rearrange("p (h c) -> p h c", h=4), srck[:, :, 128:256])
    src4 = w_in[:, :, 0:512].rearrange("p k (h t c) -> p k h t c", h=8, t=2)
    dst4 = w_sw[:, :, 0:512].rearrange("p k (h t c) -> p k h t c", h=8, t=2)
    P.I("dve", "tensor_copy", [bW], [bW], out=dst4[:, :, :, 0, :], in_=src4[:, :, :, 1, :])
    P.I("act", "copy", [bW], [bW], out=dst4[:, :, :, 1, :], in_=src4[:, :, :, 0, :])
    P.I("dve", "tensor_copy", [bW], [bW], out=w_sw[:, :, 512:544], in_=w_in[:, :, C_KR + 32:C_KR + 64])
    P.I("dve", "tensor_copy", [bW], [bW], out=w_sw[:, :, 544:576], in_=w_in[:, :, C_KR:C_KR + 32])
    r4 = wuq_r.rearrange("p k (h c) -> p k h c", h=4)
    rs4 = wuq_rs.rearrange("p k (h c) -> p k h c", h=4)
    P.I("act", "copy", [bW], [bW], out=rs4[:, :, :, 0:32], in_=r4[:, :, :, 32:64])
    P.I("act", "copy", [bW], [bW], out=rs4[:, :, :, 32:64], in_=r4[:, :, :, 0:32])

    def alias(*olds):
        b = Buf("al")
        for o in (olds or (C.bWA, C.bWB, C.bact)):
            for k, v in o.writers.items():
                if k not in b.writers or b.writers[k].idx < v.idx:
                    b.writers[k] = v
            for k, v in o.readers.items():
                if k not in b.readers or b.readers[k].idx < v.idx:
                    b.readers[k] = v
        return b

    assert off[0] % 2 == 0
    WAf = WA.bitcast(F32)
    WBf = C.WB.bitcast(F32)
    offa = [off[0] // 2]
    offb = [0]

    def cf(n, region="b"):
        o_, t_ = (offb, WBf) if region == "b" else (offa, WAf)
        a = o_[0]
        o_[0] += n
        return t_[:, a:a + n]

    xitab = cf(4 * TT, "a").rearrange("p (k c) -> p k c", k=4); bxi = alias()
    Eq = [cf(TT, "a") for _ in range(2)]; Ek = [cf(TT, "a") for _ in range(2)]; bE = [alias() for _ in range(2)]
    assert offa[0] <= 8 * 5632 // 2, offa[0]
    pos = cf(TT); bpos = alias()
    ang = cf(TT); bang = alias()
    tk = cf(TT); btk = alias()
    r1 = cf(TT); br1 = alias()
    Ssb = cf(TT); bS = alias()
    Csb = cf(TT); bC = alias()
    GCq = cf(TT); GSq = cf(TT); bGq = alias()
    GCk = cf(TT); GSk = cf(TT); bGk = alias()
    t1 = [cf(TT) for _ in range(2)]; bt1 = [alias() for _ in range(2)]
    t2 = [cf(TT) for _ in range(2)]; bt2 = [alias() for _ in range(2)]
    sgs = [cf(TT) for _ in range(2)]; bsgs = [alias() for _ in range(2)]
    alow = cf(TT); balow = alias()
    gw = cf(256); bgw = alias()
    tri = cf(128); btri = alias()
    zl = cf(256); bzl = alias()
    decs = cf(8).rearrange("p (k c) -> p k c", k=2); bdecs = alias()
    rstd2 = cf(TT); brstd2 = alias()
    sq2 = []
    for _ in range(2):
        a_ = offb[0]
        offb[0] += TT // 2
        sq2.append(C.WB[:, 2 * a_:2 * a_ + TT])
    bsq2 = [alias(), alias()]
    hT1 = WA[:, off[0] + 8 * TT * 2:off[0] + 8 * TT * 2 + 8 * TT].rearrange("p (k t) -> p k t", k=8)
    assert off[0] + 8 * TT * 2 + 8 * TT <= 8 * 5632
    hT_set = [(C.hT, C.bhT), (hT1, alias())]
    x1s = C.xt[1]
    TBS = [dict(pos=pos, bpos=bpos, S=Ssb, bS=bS, C=Csb, bC=bC, GCq=GCq, GSq=GSq, bGq=bGq, GCk=GCk, GSk=GSk, bGk=bGk),
           dict(pos=x1s[:, 0, :], bpos=alias(C.bx[1]), S=x1s[:, 1, :], bS=alias(C.bx[1]), C=x1s[:, 2, :], bC=alias(C.bx[1]),
                GCq=x1s[:, 3, :], GSq=x1s[:, 4, :], bGq=alias(C.bx[1]), GCk=x1s[:, 5, :], GSk=x1s[:, 6, :],
                bGk=alias(C.bx[1]))]
    assert offb[0] <= 22 * 1024 // 2, offb[0]

    nb = [0]

    def stage_bf(n):
        a = nb[0]
        nb[0] += n
        assert nb[0] <= 22
        return C.act[:, a:a + n, :], alias()

    cqn, bcqn = stage_bf(3)
    ckvn, bckvn = stage_bf(2)
    vst, bvst = stage_bf(4)
    ostage = [stage_bf(1) for _ in range(8)]
    nst = [0]

    def next_stage():
        a = ostage[nst[0] % len(ostage)]
        nst[0] += 1
        return a[0][:, 0, :], a[1]

    P.dma("sp", C.vec[:], d["vecs"], [], [C.bvec], C.bvec)
    P.dma("sp", xitab, d["xitab"].rearrange("p (k c) -> p k c", k=4), [], [bxi], bxi)
    P.dma("sp", gw[0:17, :], d["gw"], [], [bgw], bgw)
    P.dma("sp", tri, d["tri"], [], [btri], btri)
    P.I("pool", "memset", [], [balow], alow[0:32, :], 1.0)

    x_v = d["x1T"].rearrange("(k p) t -> p k t", p=128)

    def load(i):
        P.dma("sp", C.xt[0][:], x_v[:, :, i * TT:(i + 1) * TT], [C.bxd[i]], [C.bx[0]], C.bx[0])

    load(0)
    bank = [0]

    lane = [0]
    bank2 = [0]

    def nb_():
        if lane[0] == 0:
            b = bank[0] % 3
            bank[0] += 1
        else:
            b = 3 + bank2[0] % 3
            bank2[0] += 1
        return b

    def proj_chunk(wt, col0, M=128):
        b = nb_()
        for k in range(8):
            _mm(P, C.ps[b][0:M, :], wt[:, k, col0:col0 + M], HT[0][:, k, :], k == 0, k == 7, [bW, HT[1]], [C.bps[b]])
        return b

    def store(dram_ap, sb_ap, buf):
        P.dma("sp", dram_ap, sb_ap, [buf], [], buf, is_out=True)

    def rope_combine(bx_, bs_, M, cos_ap, sin_ap, rbufs, post_ap, post_bufs, out_ap, out_buf, q):
        P.I("dve", "tensor_tensor", [C.bps[bx_]] + rbufs, [bt1[q]], out=t1[q][0:M, :], in0=C.ps[bx_][0:M, :], in1=cos_ap,
            op=ALU.mult)
        P.I("dve", "tensor_tensor", [C.bps[bs_]] + rbufs, [bt2[q]], out=t2[q][0:M, :], in0=C.ps[bs_][0:M, :], in1=sin_ap,
            op=ALU.mult)
        P.I("dve", "tensor_tensor", [bt1[q], bt2[q]], [bt1[q]], out=t1[q][0:M, :], in0=t1[q][0:M, :], in1=t2[q][0:M, :],
            op=ALU.add)
        P.I("dve", "tensor_tensor", [bt1[q]] + post_bufs, [out_buf], out=out_ap, in0=t1[q][0:M, :], in1=post_ap, op=ALU.mult)

    def prep(i, t):
        xt, bx = C.xt[0], C.bx[0]
        tsl = slice(i * TT, (i + 1) * TT)
        hT_t, bhT_t = hT_set[t]
        T = TBS[t]
        for k in range(8):
            q = k % 2
            P.I("act", "activation", [bx], [bsq2[q]], out=sq2[q], in_=xt[:, k, :], func=AF.Square)
            _mm(P, C.ps[6][:], C.ones[:], sq2[q], k == 0, k == 7, [bsq2[q], C.bones], [C.bps[6]])
        P.I("act", "activation", [C.bps[6], C.bconst], [brstd2], out=rstd2, in_=C.ps[6][:], func=AF.Ln,
            bias=C.epsc[:, 0:1], scale=1.0 / D)
        P.I("act", "activation", [brstd2], [brstd2], out=rstd2, in_=rstd2, func=AF.Exp, scale=-0.5)
        for k in range(8):
            P.I("dve", "scalar_tensor_tensor", [bx, brstd2, C.bvec], [bhT_t], out=hT_t[:, k, :], in0=xt[:, k, :],
                scalar=C.vec[:, V_MIX + k:V_MIX + k + 1], in1=rstd2, op0=ALU.mult, op1=ALU.mult)
        P.dma("sp", T["pos"], d["posf"][:, tsl], [], [T["bpos"]], T["bpos"])
        for (dst, bd, shift) in ((T["S"], T["bS"], 0.0), (T["C"], T["bC"], 0.5 * np.pi)):
            P.I("dve", "tensor_scalar", [T["bpos"], C.bvec], [bang], out=ang, in0=T["pos"], scalar1=C.vec[:, V_INV:V_INV + 1],
                scalar2=shift, op0=ALU.mult, op1=ALU.add)
            P.I("dve", "tensor_scalar", [bang], [btk], out=tk, in0=ang, scalar1=1.0 / TWO_PI, scalar2=MAGIC,
                op0=ALU.mult, op1=ALU.add)
            P.I("dve", "tensor_scalar", [btk], [btk], out=tk, in0=tk, scalar1=-MAGIC, scalar2=None, op0=ALU.add)
            P.I("dve", "scalar_tensor_tensor", [btk, bang], [br1], out=r1, in0=tk, scalar=-CW1, in1=ang,
                op0=ALU.mult, op1=ALU.add)
            P.I("dve", "scalar_tensor_tensor", [btk, br1], [br1], out=r1, in0=tk, scalar=-CW2, in1=r1,
                op0=ALU.mult, op1=ALU.add)
            P.I("dve", "tensor_scalar", [br1], [br1], out=r1, in0=r1, scalar1=-np.pi, scalar2=np.pi, op0=ALU.max, op1=ALU.min)
            P.I("act", "activation", [br1], [bd], out=dst, in_=r1, func=AF.Sin)
        P.I("pool", "tensor_scalar", [T["bC"], C.bvec], [T["bGq"]], out=T["GCq"], in0=T["C"], scalar1=C.vec[:, V_QR:V_QR + 1],
            scalar2=None, op0=ALU.mult)
        P.I("pool", "tensor_scalar", [T["bS"], C.bvec], [T["bGq"]], out=T["GSq"], in0=T["S"], scalar1=C.vec[:, V_QRS:V_QRS + 1],
            scalar2=None, op0=ALU.mult)
        P.I("pool", "tensor_scalar", [T["bC"], C.bvec], [T["bGk"]], out=T["GCk"][0:64, :], in0=T["C"][0:64, :],
            scalar1=C.vec[0:64, V_KR:V_KR + 1], scalar2=None, op0=ALU.mult)
        P.I("pool", "tensor_scalar", [T["bS"], C.bvec], [T["bGk"]], out=T["GSk"][0:64, :], in0=T["S"][0:64, :],
            scalar1=C.vec[0:64, V_KRS:V_KRS + 1], scalar2=None, op0=ALU.mult)

    prep(0, 0)
    for i in range(NT):
        tsl = slice(i * TT, (i + 1) * TT)
        HT = hT_set[i % 2]
        TB = TBS[i % 2]
        if i + 1 < NT:
            load(i + 1)

        def u_ret(c0, tab0, dname, cc):
            def f():
                bxp = proj_chunk(w_in, c0 + cc * 128)
                bsp = proj_chunk(w_sw, c0 + cc * 128)
                o_ap, o_b = next_stage()
                rope_combine(bxp, bsp, 128, TB["C"], TB["S"], [TB["bC"], TB["bS"]], xitab[:, tab0 + cc, :], [bxi], o_ap, o_b, 1)
                store(d[dname][cc * 128:(cc + 1) * 128, tsl], o_ap, o_b)
            return f

        def u_sg(c0, r0, cc):
            def f():
                bp = proj_chunk(w_in, c0 + cc * 128)
                P.I("act", "activation", [C.bps[bp]], [bsgs[cc]], out=sgs[cc], in_=C.ps[bp][:], func=AF.Silu)
                store(d["sgT"][r0 + cc * 128:r0 + (cc + 1) * 128, tsl], sgs[cc], bsgs[cc])
            return f

        def u_v(sub):
            def f():
                b = nb_()
                for (c0, o0) in ((C_RV, 0), (C_GV, 256)):
                    for k in range(8):
                        _mm(P, C.ps[b][:, o0:o0 + 256], HT[0][:, k, sub * 128:(sub + 1) * 128], w_in[:, k, c0:c0 + 256],
                            k == 0, k == 7, [bW, HT[1]], [C.bps[b]])
                P.I("act", "copy", [C.bps[b]], [bvst], out=vst[:, sub, :], in_=C.ps[b][:])
                if sub == 3:
                    store(d["vtok"][i * TT:(i + 1) * TT, :].rearrange("(n p) c -> p n c", p=128), vst, bvst)
            return f

        def u_vm(sub):
            def f():
                b = nb_()
                for k in range(2):
                    _mm(P, C.ps[b][:], ckvn[:, k, sub * 128:(sub + 1) * 128], wkv_v[:, k, :], k == 0, k == 1, [bW, bckvn],
                        [C.bps[b]])
                o_ap, o_b = next_stage()
                P.I("act", "copy", [C.bps[b]], [o_b], out=o_ap, in_=C.ps[b][:])
                store(d["vmtok"][i * TT + sub * 128:i * TT + (sub + 1) * 128, :], o_ap, o_b)
            return f

        def u_gqk(c0, E, dname, cc):
            def f():
                bp = proj_chunk(w_in, c0 + cc * 128)
                o_ap, o_b = next_stage()
                P.I("dve", "tensor_tensor", [C.bps[bp], bE[cc]], [o_b], out=o_ap, in0=C.ps[bp][:], in1=E[cc], op=ALU.mult)
                store(d[dname][cc * 128:(cc + 1) * 128, tsl], o_ap, o_b)
            return f

        fill = [u_ret(C_RQ, 0, "rqT", 0), u_ret(C_RQ, 0, "rqT", 1), u_ret(C_RK, 2, "rkT", 0), u_ret(C_RK, 2, "rkT", 1)]
        fill += [u_sg(C_RG, 0, 0), u_sg(C_RG, 0, 1), u_sg(C_GR, 256, 0), u_sg(C_GR, 256, 1)]
        fill += [u_v(sub) for sub in range(4)]
        if i + 1 < NT:
            fill.insert(2, (lambda ii=i + 1: prep(ii, ii % 2)))
        after = {"ckvn": [u_vm(sub) for sub in range(4)],
                 "E": [u_gqk(C_GQ, Eq, "gqT", 0), u_gqk(C_GQ, Eq, "gqT", 1), u_gqk(C_GK, Ek, "gkT", 0), u_gqk(C_GK, Ek, "gkT", 1)]}

        def lane1():
            ba = proj_chunk(w_in, C_GA, M=16)
            P.I("act", "copy", [C.bps[ba]], [balow], out=alow[0:16, :], in_=C.ps[ba][0:16, :])
            yield None
            bc = [nb_(), nb_()]
            for sub in range(4):
                bz = 7
                P.I("pe", "matmul", [balow, bgw], [C.bps[bz]], C.ps[bz][:, 0:256], alow[0:17, sub * 128:(sub + 1) * 128],
                    gw[0:17, :], start=True, stop=True)
                P.I("act", "activation", [C.bps[bz]], [bzl], out=zl, in_=C.ps[bz][:, 0:256], func=AF.Exp, scale=-1.0)
                P.I("act", "activation", [bzl, C.bconst], [bzl], out=zl, in_=zl, func=AF.Ln, bias=C.epsc[:, 1:2], scale=1.0)
                yield None
                for c in range(2):
                    P.I("pe", "matmul", [bzl, btri], [C.bps[bc[c]]], C.ps[bc[c]][:, sub * 128:(sub + 1) * 128],
                        zl[:, c * 128:(c + 1) * 128], tri, start=True, stop=True)
            for c in range(2):
                pb = C.ps[bc[c]]
                P.I("act", "activation", [C.bps[bc[c]], C.bconst], [bE[c]], out=Eq[c], in_=pb[:], func=AF.Exp,
                    bias=C.epsc[:, 2:3], scale=1.0)
                P.I("act", "activation", [C.bps[bc[c]]], [bE[c]], out=Ek[c], in_=pb[:], func=AF.Exp, scale=-1.0)
                P.I("act", "activation", [C.bps[bc[c]]], [bdecs], out=decs[:, c, :],
                    in_=pb[:].rearrange("p (n t) -> p n t", t=128)[:, :, 127], func=AF.Exp)
            store(d["gdec"][:, i * 4:(i + 1) * 4].rearrange("(c p) n -> p c n", p=128), decs, bdecs)
            yield "E"
            bk_ = [proj_chunk(w_in, C_CKV + k * 128) for k in range(2)]
            yield from norm_gen(P, C, [C.ps[b][:] for b in bk_], [C.bps[b] for b in bk_], 128, C.ones[:], 256,
                                [ckvn[:, k, :] for k in range(2)], bckvn, [V_CKV + k for k in range(2)], 6)
            yield "ckvn"
            bq_ = [proj_chunk(w_in, C_CQ + k * 128) for k in range(3)]
            yield from norm_gen(P, C, [C.ps[b][:] for b in bq_], [C.bps[b] for b in bq_], 128, C.ones[:], 384,
                                [cqn[:, k, :] for k in range(3)], bcqn, [V_CQ + k for k in range(3)], 6)
            for h in range(4):
                for (wt, src_t, src_b, nk, gcol, dname) in ((wuq_n, cqn, bcqn, 3, V_QN, "qnT"), (wkv_k, ckvn, bckvn, 2, V_KN, "knT")):
                    b = nb_()
                    for k in range(nk):
                        _mm(P, C.ps[b][:], wt[:, k, h * 128:(h + 1) * 128], src_t[:, k, :], k == 0, k == nk - 1, [bW, src_b],
                            [C.bps[b]])
                    o_ap, o_b = next_stage()
                    yield from norm_gen(P, C, [C.ps[b][:]], [C.bps[b]], 128, C.ones[:], 128, [o_ap], o_b, [gcol], 7)
                    store(d[dname][h * 128:(h + 1) * 128, tsl], o_ap, o_b)
            for cc in range(2):
                b1, b2 = nb_(), nb_()
                for (b, wt) in ((b1, wuq_r), (b2, wuq_rs)):
                    for k in range(3):
                        _mm(P, C.ps[b][:], wt[:, k, cc * 128:(cc + 1) * 128], cqn[:, k, :], k == 0, k == 2, [bW, bcqn], [C.bps[b]])
                yield from norm_gen(P, C, [C.ps[b1][:]], [C.bps[b1]], 128, C.ones2[:], 64, None, None, None, 7)
                o_ap, o_b = next_stage()
                rope_combine(b1, b2, 128, TB["GCq"], TB["GSq"], [TB["bGq"]], C.rstd[:], [C.brstd], o_ap, o_b, 0)
                store(d["qrT"][cc * 128:(cc + 1) * 128, tsl], o_ap, o_b)
            b1 = proj_chunk(w_in, C_KR, M=64)
            b2 = proj_chunk(w_sw, 512, M=64)
            yield from norm_gen(P, C, [C.ps[b1][0:64, :]], [C.bps[b1]], 64, C.ones[0:64, 0:64], 64, None, None, None, 7)
            o_ap, o_b = next_stage()
            rope_combine(b1, b2, 64, TB["GCk"][0:64, :], TB["GSk"][0:64, :], [TB["bGk"]], C.rstd[0:64, :], [C.brstd], o_ap[0:64, :], o_b, 0)
            store(d["krT"][:, tsl], o_ap[0:64, :], o_b)

        lane[0] = 0
        for tag in lane1():
            if tag is not None:
                fill += after.pop(tag)
            if fill:
                lane[0] = 1
                fill.pop(0)()
                lane[0] = 0
        lane[0] = 1
        for tag in list(after):
            fill += after.pop(tag)
        while fill:
            fill.pop(0)()
        lane[0] = 0


K1_OUTS = dict(rqT=([256, TOK], BF16), rkT=([256, TOK], BF16), sgT=([512, TOK], F32), vtok=([TOK, 512], BF16),
               qnT=([512, TOK], BF16), qrT=([256, TOK], BF16), knT=([512, TOK], BF16), vmtok=([TOK, 512], BF16),
               krT=([64, TOK], BF16), gdec=([256, TOK // 128], F32), gqT=([256, TOK], BF16), gkT=([256, TOK], BF16),
               x1T=([D, TOK], F32))


def build_k1(with_ffn=True):
    nc = bass.Bass("TRN2", target_bir_lowering=False)

    def din(name, shape, dt=F32):
        return nc.dram_tensor(name, shape, dt, kind="ExternalInput").ap()

    xT = din("xT", [D, TOK])
    wgu = din("wgu", [D, 2 * DFF])
    wd = din("wd", [DFF, D])
    g1 = din("g1", [128, 8])
    d = dict(w_in=din("w_in", [D, IN_COLS]), w_uq=din("w_uq", [384, 768]), w_ukv=din("w_ukv", [256, 1024]),
             vecs=din("vecs", [128, NVEC]), xitab=din("xitab", [128, 4 * TT]), gw=din("gw", [17, 256]),
             tri=din("tri", [128, 128]), posf=din("posf", [128, TOK]))
    for name, (shape, dt) in K1_OUTS.items():
        d[name] = nc.dram_tensor(name, shape, dt, kind="ExternalOutput").ap()
    P = Prog(nc)
    C = tok_ctx(P, nc)
    if with_ffn:
        ffn_phase(P, nc, C, xT, d["x1T"], wgu, wd, g1, dst_bufs=C.bxd)
    else:
        d["x1T"] = xT
    proj_phase(P, nc, C, d)
    P.emit()
    return nc


def post_phase(P, nc, C, d):
    WA = C.WA
    bW = C.bWA
    w_out = WA[:, 0:8 * 1024].rearrange("p (k c) -> p k c", k=8)
    P.dma("pool", w_out, d["w_out"].rearrange("(k p) n -> p k n", p=128), [], [bW], bW)
    WBf = C.WB.bitcast(F32)
    offb = [0]

    def cf(n):
        a = offb[0]
        offb[0] += n
        return WBf[:, a:a + n]

    ot = [cf(TT) for _ in range(8)]; bot = P.bufs("ot", 8)
    sg = [cf(TT) for _ in range(4)]; bsg = P.bufs("sg", 4)
    zt = [cf(TT) for _ in range(2)]; bzt = P.bufs("zt", 2)
    cat = C.act[:, 0:8, :]
    bcat = P.buf("cat")
    C.post_wb = bot + bsg + bzt
    C.post_act = [bcat]
    P.dma("sp", C.vec[:], d["vecs"], [], [C.bvec], C.bvec)
    x_v = d["x1T"].rearrange("(k p) t -> p k t", p=128)
    x_o = d["x2T"].rearrange("(k p) t -> p k t", p=128)
    srcs = [("roT", 0), ("roT", 1), ("moT", 0), ("moT", 1), ("moT", 2), ("moT", 3), ("goT", 0), ("goT", 1)]

    def load(i):
        s = i % 2
        tsl = slice(i * TT, (i + 1) * TT)
        P.dma("sp", C.xt[s][:], x_v[:, :, tsl], [], [C.bx[s]], C.bx[s])
        for c, (name, cc) in enumerate(srcs):
            P.dma("act" if c % 2 else "sp", ot[c], d[name][cc * 128:(cc + 1) * 128, tsl], [], [bot[c]], bot[c])
        for c in range(4):
            r0 = (0, 128, 256, 384)[c]
            P.dma("act" if c % 2 else "sp", sg[c], d["sgT"][r0:r0 + 128, tsl], [], [bsg[c]], bsg[c])

    load(0)
    for i in range(NT):
        s = i % 2
        tsl = slice(i * TT, (i + 1) * TT)
        xt, bx = C.xt[s], C.bx[s]
        for n2, (c, gcol, sgi) in enumerate(((0, V_RO, 0), (1, V_RO + 1, 1), (6, V_GO, 2), (7, V_GO + 1, 3))):
            q = n2 % 2
            norm_from_psum(P, C, [ot[c]], [bot[c]], 128, 128, C.ones2[:], 64, [zt[q]], bzt[q], [gcol], 7)
            P.I("dve", "tensor_tensor", [bzt[q], bsg[sgi]], [bcat], out=cat[:, c, :], in0=zt[q], in1=sg[sgi], op=ALU.mult)
        for c in range(2, 6):
            P.I("act", "copy", [bot[c]], [bcat], out=cat[:, c, :], in_=ot[c])
        if i + 1 < NT:
            load(i + 1)
        for m in range(8):
            r = m % 6
            for k in range(8):
                _mm(P, C.ps[r][:], w_out[:, k, m * 128:(m + 1) * 128], cat[:, k, :], k == 0, k == 7, [bW, bcat], [C.bps[r]])
            P.I("dve", "tensor_tensor", [C.bps[r], bx], [bx], out=xt[:, m, :], in0=C.ps[r][:], in1=xt[:, m, :], op=ALU.add)
        P.dma("sp", x_o[:, :, tsl], xt[:], [bx], [C.bxd[i]], bx, is_out=True)


def build_k3():
    nc = bass.Bass("TRN2", target_bir_lowering=False)

    def din(name, shape, dt=F32):
        return nc.dram_tensor(name, shape, dt, kind="ExternalInput").ap()

    d = dict(x1T=din("x1T", [D, TOK]), roT=din("roT", [256, TOK]), goT=din("goT", [256, TOK]), moT=din("moT", [512, TOK]),
             sgT=din("sgT", [512, TOK]), w_out=din("w_out", [D, D]), vecs=din("vecs", [128, NVEC]))
    wgu = din("wgu", [D, 2 * DFF])
    wd = din("wd", [DFF, D])
    g2 = din("g2", [128, 8])
    d["x2T"] = nc.dram_tensor("x2T", [D, TOK], F32, kind="ExternalOutput").ap()
    x3T = nc.dram_tensor("x3T", [D, TOK], F32, kind="ExternalOutput").ap()
    P = Prog(nc)
    C = tok_ctx(P, nc)
    post_phase(P, nc, C, d)
    for dst, srcs in ((C.bWB, C.post_wb), (C.bact, C.post_act)):
        for o in srcs:
            for k, v in o.writers.items():
                if k not in dst.writers or dst.writers[k].idx < v.idx:
                    dst.writers[k] = v
            for k, v in o.readers.items():
                if k not in dst.readers or dst.readers[k].idx < v.idx:
                    dst.readers[k] = v
    ffn_phase(P, nc, C, d["x2T"], x3T, wgu, wd, g2, src_bufs=C.bxd)
    P.emit()
    return nc


def build_k31():
    nc = bass.Bass("TRN2", target_bir_lowering=False)

    def din(name, shape, dt=F32):
        return nc.dram_tensor(name, shape, dt, kind="ExternalInput").ap()

    dp = dict(x1T=din("p_x1T", [D, TOK]), roT=din("roT", [256, TOK]), goT=din("goT", [256, TOK]), moT=din("moT", [512, TOK]),
              sgT=din("p_sgT", [512, TOK]), w_out=din("w_out", [D, D]), vecs=din("p_vecs", [128, NVEC]))
    wgu2 = din("p_wgu", [D, 2 * DFF])
    wd2 = din("p_wd", [DFF, D])
    g2 = din("p_g2", [128, 8])
    dp["x2T"] = nc.dram_tensor("x2T", [D, TOK], F32, kind="ExternalOutput").ap()
    x3T = nc.dram_tensor("x3T", [D, TOK], F32, kind="ExternalOutput").ap()
    wgu = din("wgu", [D, 2 * DFF])
    wd = din("wd", [DFF, D])
    g1 = din("g1", [128, 8])
    d = dict(w_in=din("w_in", [D, IN_COLS]), w_uq=din("w_uq", [384, 768]), w_ukv=din("w_ukv", [256, 1024]),
             vecs=din("vecs", [128, NVEC]), xitab=din("xitab", [128, 4 * TT]), gw=din("gw", [17, 256]),
             tri=din("tri", [128, 128]), posf=din("posf", [128, TOK]))
    for name, (shape, dt) in K1_OUTS.items():
        d[name] = nc.dram_tensor(name, shape, dt, kind="ExternalOutput").ap()
    P = Prog(nc)
    C = tok_ctx(P, nc)
    post_phase(P, nc, C, dp)
    for dst, srcs in ((C.bWB, C.post_wb), (C.bact, C.post_act)):
        for o in srcs:
            for k, v in o.writers.items():
                if k not in dst.writers or dst.writers[k].idx < v.idx:
                    dst.writers[k] = v
            for k, v in o.readers.items():
                if k not in dst.readers or dst.readers[k].idx < v.idx:
                    dst.readers[k] = v
    bxdB = P.bufs("xdB", NT)
    bxdC = P.bufs("xdC", NT)
    ffn_phase(P, nc, C, dp["x2T"], x3T, wgu2, wd2, g2, src_bufs=C.bxd, dst_bufs=bxdB)
    ffn_phase(P, nc, C, x3T, d["x1T"], wgu, wd, g1, src_bufs=bxdB, dst_bufs=bxdC)
    C.bxd = bxdC
    proj_phase(P, nc, C, d)
    P.emit()
    return nc


_BF = ml_dtypes.bfloat16
_CACHE = {}


def _get(name, fn):
    if name not in _CACHE:
        _CACHE[name] = fn()
    return _CACHE[name]


def _swap(g):
    return np.concatenate([g[32:], g[:32]])


def _consts():
    p = np.arange(128)
    inv = (10000.0 ** (-(np.arange(0, 64, 2, dtype=np.float32)) / 64.0)).astype(np.float32)
    inv_signed = np.where((p % 64) < 32, -1.0, 1.0).astype(np.float32) * inv[p % 32]
    t = (np.arange(TT) % 128 + 1).astype(np.float64)
    xitab = np.zeros((128, 4, TT), np.float32)
    for k in range(4):
        for half in range(2):
            h = (k % 2) * 2 + half
            lg = np.log1p(-2.0 ** (-5.0 - h))
            row = np.exp(lg * t) if k < 2 else np.exp(-lg * t) * 0.125
            xitab[half * 64:(half + 1) * 64, k, :] = row[None, :]
    s_, t_ = np.meshgrid(np.arange(128), np.arange(128), indexing="ij")
    tri = np.where(s_ <= t_, -1.0 / 16.0, 0.0).astype(np.float32)
    mask = np.where(s_ <= t_, 1.0, 0.0).astype(_BF)
    rdec = np.zeros((4, 64, NCH), np.float32)
    for h in range(4):
        rdec[h] = np.exp(np.log1p(-2.0 ** (-5.0 - h)) * 128.0)
    return dict(inv_signed=inv_signed, xitab=xitab.reshape(128, 4 * TT), tri=tri, mask=mask, rdec=rdec)


def _vecs(inp, l, cst):
    v = np.zeros((128, NVEC), np.float32)
    v[:, V_MIX:V_MIX + 8] = inp["mix_norm"][l].reshape(8, 128).T
    v[:, V_CQ:V_CQ + 3] = inp["mla_q_norm"][l].reshape(3, 128).T
    v[:, V_CKV:V_CKV + 2] = inp["mla_kv_norm"][l].reshape(2, 128).T
    v[:, V_QN] = inp["mla_q_nope_norm"][l]
    v[:, V_KN] = inp["mla_k_nope_norm"][l]
    gq = inp["mla_q_rope_norm"][l]
    gk = inp["mla_k_rope_norm"][l]
    v[:, V_QR] = np.tile(gq, 2)
    v[:, V_QRS] = np.tile(_swap(gq), 2)
    v[:64, V_KR] = gk
    v[:64, V_KRS] = _swap(gk)
    v[:, V_INV] = cst["inv_signed"]
    v[:, V_RO:V_RO + 2] = inp["ret_out_norm"][l].reshape(2, 128).T
    v[:, V_GO:V_GO + 2] = inp["gla_out_norm"][l].reshape(2, 128).T
    return v


def _run(nc, in_maps):
    res = run_bass_kernel_spmd(nc, in_maps, core_ids=list(range(NCORE)))
    return res.results


def _cat_tok(res, name, b):
    return np.concatenate([res[b * 4 + q][name] for q in range(4)], axis=1)


def _cat_rows(res, name, b):
    return np.concatenate([res[b * 4 + q][name] for q in range(4)], axis=0)


def kernel(**inp):
    inp = {k: np.asarray(v) for k, v in inp.items()}
    cst = _get("cst", _consts)
    k1 = _get("k1", build_k1)
    k2 = _get("k2", lambda: build_k2(phases=(1, 2)))
    k3 = _get("k3", build_k3)
    x = inp["x"]
    posf = inp["positions"].astype(np.float32)
    xT = [np.ascontiguousarray(x[c // 4, (c % 4) * TOK:(c % 4 + 1) * TOK, :].T) for c in range(NCORE)]
    k31 = _get("k31", build_k31)

    def k1_inputs(l, c):
        b, q = c // 4, c % 4
        return dict(wgu=inp["ffn1_w_gate_up"][l], wd=inp["ffn1_w_down"][l],
                    g1=np.ascontiguousarray(inp["ffn1_norm"][l].reshape(8, 128).T),
                    w_in=inp["w_in"][l], w_uq=inp["mla_w_uq"][l], w_ukv=inp["mla_w_ukv"][l], vecs=_vecs(inp, l, cst),
                    xitab=cst["xitab"],
                    gw=np.concatenate([inp["gla_w_gate_up"][l], inp["gla_gate_bias"][l][None, :]], axis=0), tri=cst["tri"],
                    posf=np.ascontiguousarray(np.broadcast_to(posf[b, q * TOK:(q + 1) * TOK][None, :], (128, TOK))))

    r1 = _run(k1, [dict(k1_inputs(0, c), xT=xT[c]) for c in range(NCORE)])
    for l in range(DEPTH):
        vecs = _vecs(inp, l, cst)
        ima, imb = [], []
        for b in range(B):
            rq, rk = _cat_tok(r1, "rqT", b), _cat_tok(r1, "rkT", b)
            gq, gk = _cat_tok(r1, "gqT", b), _cat_tok(r1, "gkT", b)
            vt = _cat_rows(r1, "vtok", b)
            qn, qr = _cat_tok(r1, "qnT", b), _cat_tok(r1, "qrT", b)
            kn, kr = _cat_tok(r1, "knT", b), _cat_tok(r1, "krT", b)
            vm = _cat_rows(r1, "vmtok", b)
            gdec = _cat_tok(r1, "gdec", b)
            for h in range(4):
                hs = slice(h * 64, (h + 1) * 64)
                ima.append(dict(lqk=np.ascontiguousarray(np.stack([rq[hs], rk[hs], gq[hs], gk[hs]], axis=1)),
                                lkv=np.ascontiguousarray(np.concatenate([rk[hs].T, vt[:, h * 64:(h + 1) * 64], gk[hs].T,
                                                                         vt[:, 256 + h * 64:256 + (h + 1) * 64]], axis=1)),
                                ldec=np.ascontiguousarray(np.stack([cst["rdec"][h], gdec[hs]], axis=1)), mask=cst["mask"]))
                imb.append(dict(qn=np.ascontiguousarray(qn[h * 128:(h + 1) * 128]), qr=np.ascontiguousarray(qr[hs]),
                                kn=np.ascontiguousarray(kn[h * 128:(h + 1) * 128]), kr=kr,
                                vm=np.ascontiguousarray(vm[:, h * 128:(h + 1) * 128]), mask=cst["mask"]))
        r2 = _run(k2, [dict(a_, **b_) for a_, b_ in zip(ima, imb)])
        im3 = []
        for c in range(NCORE):
            b, q = c // 4, c % 4
            ts = slice(q * TOK, (q + 1) * TOK)
            roT = np.concatenate([r2[b * 4 + h]["lo"][ts, 0:64].T for h in range(4)], axis=0)
            goT = np.concatenate([r2[b * 4 + h]["lo"][ts, 64:128].T for h in range(4)], axis=0)
            moT = np.concatenate([r2[b * 4 + h]["moT"][:, ts] for h in range(4)], axis=0)
            im3.append(dict(x1T=r1[c]["x1T"], roT=np.ascontiguousarray(roT), goT=np.ascontiguousarray(goT),
                            moT=np.ascontiguousarray(moT), sgT=r1[c]["sgT"], w_out=inp["w_out"][l], vecs=vecs,
                            wgu=inp["ffn2_w_gate_up"][l], wd=inp["ffn2_w_down"][l],
                            g2=np.ascontiguousarray(inp["ffn2_norm"][l].reshape(8, 128).T)))
        if l + 1 < DEPTH:
            ims = []
            for c in range(NCORE):
                m = im3[c]
                ims.append(dict(k1_inputs(l + 1, c), p_x1T=m["x1T"], roT=m["roT"], goT=m["goT"], moT=m["moT"], p_sgT=m["sgT"],
                                w_out=m["w_out"], p_vecs=m["vecs"], p_wgu=m["wgu"], p_wd=m["wd"], p_g2=m["g2"]))
            r1 = _run(k31, ims)
        else:
            r3 = _run(k3, im3)
            xT = [r3[c]["x3T"] for c in range(NCORE)]
    out = np.empty((B, S, D), np.float32)
    for c in range(NCORE):
        out[c // 4, (c % 4) * TOK:(c % 4 + 1) * TOK, :] = xT[c].T
    return out
```
